# Optimizing a Trainium2 kernel written in Bass

```python
import jax, jax.numpy as jnp
from jax import lax
import numpy as np

D_MODEL = 2048
BATCH = 4
SEQ = 2048
DEPTH = 1
DEC_BATCH = 8
DEC_SEQ = 4
PAST_LEN = 16384
PAGE_SIZE = 128

D_MIX = D_MODEL
D_ATTN = D_MIX // 2
D_CONV = D_MIX - D_ATTN
HEAD_DIM = 128
N_HEADS = D_ATTN // HEAD_DIM
N_KV_HEADS = 2
GROUP = N_HEADS // N_KV_HEADS
ROPE_DIM = HEAD_DIM // 4
ROPE_THETA = 500000.0
N_IDX_HEADS = 16
IDX_DIM = 64
IDX_ROPE_DIM = IDX_DIM // 4
TOPK_MAX = 256
CONV_W = 3
N_CONV_GROUPS = 8
Q_BLOCK = 128
EPS = 1e-6

SPLITS = (
    ("q", N_HEADS * HEAD_DIM),
    ("k", N_KV_HEADS * HEAD_DIM),
    ("v", N_KV_HEADS * HEAD_DIM),
    ("z_attn", D_ATTN),
    ("q_idx", N_IDX_HEADS * IDX_DIM),
    ("k_idx", IDX_DIM),
    ("w_idx", N_IDX_HEADS),
    ("h", D_CONV),
    ("b", D_CONV),
    ("c", D_CONV),
    ("z_conv", D_CONV),
)
D_IN = sum(n for _, n in SPLITS)

kernel_name = "hymba_dsa_shortconv_step"


def rmsnorm(x, g):
    xf = x.astype(jnp.float32)
    y = xf * lax.rsqrt(jnp.mean(xf * xf, axis=-1, keepdims=True) + EPS)
    return (y * g.astype(jnp.float32)).astype(x.dtype)


def partial_rope(x, pos, rot_dim):
    half = rot_dim // 2
    inv = ROPE_THETA ** (-jnp.arange(half, dtype=jnp.float32) / half)
    ang = pos.astype(jnp.float32)[:, None] * inv[None, :]
    cos = jnp.cos(ang)[:, None, :]
    sin = jnp.sin(ang)[:, None, :]
    xr = x[..., :rot_dim].astype(jnp.float32)
    x1, x2 = xr[..., :half], xr[..., half:]
    rot = jnp.concatenate([x1 * cos - x2 * sin, x2 * cos + x1 * sin], axis=-1)
    return jnp.concatenate([rot.astype(x.dtype), x[..., rot_dim:]], axis=-1)


def split_proj(p):
    offs = [0] + [int(v) for v in np.cumsum([n for _, n in SPLITS])]
    return {name: p[..., offs[i]:offs[i + 1]] for i, (name, _) in enumerate(SPLITS)}


take_rows = jax.vmap(lambda a, i: a[i])


def index_scores(qi, wi, ki, q_pos, k_pos):
    dots = jnp.einsum("bthd,bld->bthl", qi.astype(jnp.float32), ki.astype(jnp.float32))
    s = jnp.einsum("bth,bthl->btl", wi.astype(jnp.float32), jax.nn.relu(dots))
    return jnp.where(k_pos[None, None, :] <= q_pos[None, :, None], s, -jnp.inf)


def sparse_attend(q, k_sel, v_sel, valid):
    Bn, T = q.shape[:2]
    qg = q.reshape(Bn, T, N_KV_HEADS, GROUP, HEAD_DIM)
    logits = jnp.einsum("btkgd,btnkd->btkgn", qg, k_sel,
                        preferred_element_type=jnp.float32) * (HEAD_DIM ** -0.5)
    logits = jnp.where(valid[:, :, None, None, :], logits, -jnp.inf)
    probs = jax.nn.softmax(logits, axis=-1).astype(v_sel.dtype)
    o = jnp.einsum("btkgn,btnkd->btkgd", probs, v_sel)
    return o.reshape(Bn, T, N_HEADS * HEAD_DIM)


def prompt_sparse_attention(q, k, v, qi, ki, wi):
    Bn, S = q.shape[:2]
    topk = min(TOPK_MAX, S // 4)
    nb = S // Q_BLOCK
    k_pos = jnp.arange(S)

    def blk(args):
        qb, qib, wib, tb = args
        sc = index_scores(qib, wib, ki, tb, k_pos)
        _, idx = lax.top_k(sc, topk)
        valid = idx <= tb[None, :, None]
        return sparse_attend(qb, take_rows(k, idx), take_rows(v, idx), valid)

    def to_blocks(a):
        return a.reshape(Bn, nb, Q_BLOCK, *a.shape[2:]).swapaxes(0, 1)

    out = lax.map(blk, (to_blocks(q), to_blocks(qi), to_blocks(wi), k_pos.reshape(nb, Q_BLOCK)))
    return out.swapaxes(0, 1).reshape(Bn, S, D_ATTN)


def sample_sparse_attention(q, k_new, v_new, qi, ki_new, wi, cache_k, cache_v, cache_kidx, page_table):
    Bd, T = q.shape[:2]
    past = page_table.shape[1] * PAGE_SIZE
    L = past + T
    topk = min(TOPK_MAX, L // 4)
    ki_past = cache_kidx[page_table].reshape(Bd, past, IDX_DIM)
    ki_all = jnp.concatenate([ki_past, ki_new], axis=1)
    q_pos = past + jnp.arange(T)
    sc = index_scores(qi, wi, ki_all, q_pos, jnp.arange(L))
    _, idx = lax.top_k(sc, topk)
    valid = idx <= q_pos[None, :, None]
    in_past = idx < past
    pidx = jnp.minimum(idx, past - 1)
    phys = jax.vmap(lambda pt, i: pt[i])(page_table, pidx // PAGE_SIZE)
    off = pidx % PAGE_SIZE
    nidx = jnp.clip(idx - past, 0, T - 1)

    def sel(pool, new):
        return jnp.where(in_past[..., None, None], pool[phys, off], take_rows(new, nidx))

    return sparse_attend(q, sel(cache_k, k_new), sel(cache_v, v_new), valid)


def short_conv(u, buf, w_conv):
    T = u.shape[1]
    padded = jnp.concatenate([buf, u], axis=1)
    y = padded[:, 0:T] * w_conv[0]
    for j in range(1, CONV_W):
        y = y + padded[:, j:j + T] * w_conv[j]
    return y, padded[:, -(CONV_W - 1):]


def mixer_layer(x, pos, conv_buf, attn_fn, norm_in, w_in, g_q, g_k, g_kidx, w_conv, w_out):
    Bn, T, _ = x.shape
    p = split_proj(rmsnorm(x, norm_in) @ w_in)
    q = partial_rope(rmsnorm(p["q"].reshape(Bn, T, N_HEADS, HEAD_DIM), g_q), pos, ROPE_DIM)
    k = partial_rope(rmsnorm(p["k"].reshape(Bn, T, N_KV_HEADS, HEAD_DIM), g_k), pos, ROPE_DIM)
    v = p["v"].reshape(Bn, T, N_KV_HEADS, HEAD_DIM)
    qi = partial_rope(p["q_idx"].reshape(Bn, T, N_IDX_HEADS, IDX_DIM), pos, IDX_ROPE_DIM)
    ki = partial_rope(rmsnorm(p["k_idx"], g_kidx)[:, :, None, :], pos, IDX_ROPE_DIM)[:, :, 0]
    wi = p["w_idx"] * ((N_IDX_HEADS ** -0.5) * (IDX_DIM ** -0.5))
    attn = attn_fn(q, k, v, qi, ki, wi)
    conv, new_buf = short_conv(p["c"] * p["h"], conv_buf, w_conv)
    mix = jnp.concatenate([attn * jax.nn.silu(p["z_attn"]),
                           p["b"] * conv * jax.nn.silu(p["z_conv"])], axis=-1)
    return x + mix @ w_out, k, v, ki, new_buf


def setup_inputs(seed: int = 0) -> dict:
    key = jax.random.key(seed)
    ks = jax.random.split(key, 16)
    f32 = jnp.float32
    n_pages = PAST_LEN // PAGE_SIZE
    n_used = DEC_BATCH * n_pages
    n_pool = n_used + max(1, n_used // 4)

    def nrm(k, shape, s):
        return jax.random.normal(k, shape, f32) * s

    page_table = jax.random.permutation(ks[6], n_pool)[:n_used].reshape(DEC_BATCH, n_pages).astype(jnp.int32)
    return {
        "x_prompt": nrm(ks[0], (BATCH, SEQ, D_MODEL), 1.0),
        "x_sample": nrm(ks[1], (DEC_BATCH, DEC_SEQ, D_MODEL), 1.0),
        "cache_k": nrm(ks[2], (DEPTH, n_pool, PAGE_SIZE, N_KV_HEADS, HEAD_DIM), 1.0),
        "cache_v": nrm(ks[3], (DEPTH, n_pool, PAGE_SIZE, N_KV_HEADS, HEAD_DIM), 1.0),
        "cache_kidx": nrm(ks[4], (DEPTH, n_pool, PAGE_SIZE, IDX_DIM), 1.0),
        "state_conv": nrm(ks[5], (DEPTH, DEC_BATCH, CONV_W - 1, D_CONV), 1.0),
        "page_table": page_table,
        "norm_in": 1.0 + nrm(ks[7], (DEPTH, D_MODEL), 0.02),
        "w_in": nrm(ks[8], (DEPTH, D_MODEL, D_IN), D_MODEL ** -0.5),
        "g_q": 1.0 + nrm(ks[9], (DEPTH, HEAD_DIM), 0.02),
        "g_k": 1.0 + nrm(ks[10], (DEPTH, HEAD_DIM), 0.02),
        "g_kidx": 1.0 + nrm(ks[11], (DEPTH, IDX_DIM), 0.02),
        "w_conv": nrm(ks[12], (DEPTH, CONV_W, D_CONV), CONV_W ** -0.5),
        "w_out": nrm(ks[13], (DEPTH, D_MIX, D_MODEL), D_MIX ** -0.5),
    }


def reference(x_prompt, x_sample, cache_k, cache_v, cache_kidx, state_conv, page_table,
              norm_in, w_in, g_q, g_k, g_kidx, w_conv, w_out):
    Bp, S, _ = x_prompt.shape
    Bd, T, _ = x_sample.shape
    past = page_table.shape[1] * PAGE_SIZE
    pos_p = jnp.arange(S)
    pos_s = past + jnp.arange(T)
    hp, hs = x_prompt, x_sample
    kp, vp, kip, cp, ks_, vs, kis, cs = [], [], [], [], [], [], [], []
    for l in range(DEPTH):
        lw = (norm_in[l], w_in[l], g_q[l], g_k[l], g_kidx[l], w_conv[l], w_out[l])
        zero_buf = jnp.zeros((Bp, CONV_W - 1, D_CONV), x_prompt.dtype)
        hp, k1, v1, ki1, c1 = mixer_layer(hp, pos_p, zero_buf, prompt_sparse_attention, *lw)
        ck, cv, cki = cache_k[l], cache_v[l], cache_kidx[l]

        def attn_s(q, k, v, qi, ki, wi, ck=ck, cv=cv, cki=cki):
            return sample_sparse_attention(q, k, v, qi, ki, wi, ck, cv, cki, page_table)

        hs, k2, v2, ki2, c2 = mixer_layer(hs, pos_s, state_conv[l], attn_s, *lw)
        kp.append(k1); vp.append(v1); kip.append(ki1); cp.append(c1)
        ks_.append(k2); vs.append(v2); kis.append(ki2); cs.append(c2)
    return (hp, hs,
            jnp.stack(kp), jnp.stack(vp), jnp.stack(kip), jnp.stack(cp),
            jnp.stack(ks_), jnp.stack(vs), jnp.stack(kis), jnp.stack(cs))
```

```python
import numpy as np
import concourse.bass as bass
import concourse.mybir as mybir
from concourse.bass_utils import run_bass_kernel_spmd

F32 = mybir.dt.float32
BF16 = mybir.dt.bfloat16
I32 = mybir.dt.int32
AF = mybir.ActivationFunctionType
ALU = mybir.AluOpType
AX = mybir.AxisListType

D = 2048
S = 2048
NB_ = 4
DEC_B = 8
DEC_T = 4
PAST = 16384
PAGE = 128
NPAGES = PAST // PAGE
D_IN = 7760
HD = 128
NH = 8
NKV = 2
IDXH = 16
IDXD = 64
TOPK = 256
EPS = 1e-6
THETA = 500000.0
KBIS = 14
NEG = -30000.0
C_Q, C_K, C_V, C_ZA, C_QI, C_KI, C_WI, C_H, C_B, C_C, C_ZC = 0, 1024, 1280, 1536, 2560, 3584, 3648, 3664, 4688, 5712, 6736
T_OWN = ([0, 3, 4, 7, 8, 11, 12, 15], [1, 2, 5, 6, 9, 10, 13, 14])
ENGS = ("sync", "act", "dve", "pool", "pe")
NOSELF = ("pe",)


class Prog:
    def __init__(self, nc):
        self.nc = nc
        self.ops = {e: [] for e in ENGS}
        self.count = {e: 0 for e in ENGS}
        self.dcount = {}
        self.last_write = {}
        self.readers = {}
        self.waited = {e: {} for e in ENGS}
        self.pending_barrier = {e: None for e in ENGS}
        import os
        self.limit = int(os.environ.get('DBG_OPS', '100000000'))
        self.total = 0

    def _deps(self, eng, reads, writes):
        deps = {}

        def need(sv):
            if sv is None:
                return
            k, v = sv
            if deps.get(k, 0) < v:
                deps[k] = v

        for r in reads:
            need(self.last_write.get(r))
        for w in writes:
            need(self.last_write.get(w))
            for rd in self.readers.get(w, ()):
                need(rd)
        pb = self.pending_barrier[eng]
        if pb is not None:
            for kv in pb:
                need(kv)
            self.pending_barrier[eng] = None
        out = []
        for k, v in deps.items():
            if k == eng and eng in NOSELF:
                continue
            if self.waited[eng].get(k, 0) >= v:
                continue
            self.waited[eng][k] = v
            out.append((k, v))
        return out

    def _commit(self, me, reads, writes):
        for r in reads:
            self.readers.setdefault(r, []).append(me)
        for w in writes:
            self.last_write[w] = me
            self.readers[w] = []

    def op(self, eng, fn, reads=(), writes=()):
        self.total += 1
        if self.total > self.limit:
            return
        reads = tuple(reads)
        writes = tuple(writes)
        waits = self._deps(eng, reads, writes)
        self.count[eng] += 1
        self.ops[eng].append(("op", fn, waits))
        self._commit((eng, self.count[eng]), reads, writes)

    def dma(self, queue, fn, dsem, reads=(), writes=()):
        self.total += 1
        if self.total > self.limit:
            return
        reads = tuple(reads)
        writes = tuple(writes)
        key = ("d", dsem)
        waits = self._deps(queue, reads, writes)
        self.dcount[key] = self.dcount.get(key, 0) + 1
        self.ops[queue].append(("dma", fn, waits, key))
        self._commit((key, 16 * self.dcount[key]), reads, writes)

    def alias(self, new_key, old_keys):
        lst = self.readers.setdefault(new_key, [])
        for ok in old_keys:
            lw = self.last_write.get(ok)
            if lw is not None:
                lst.append(lw)
            lst.extend(self.readers.get(ok, ()))

    def barrier(self):
        snap = [(e, self.count[e]) for e in ENGS if self.count[e] > 0]
        snap += [(k, 16 * n) for k, n in self.dcount.items()]
        for e in ENGS:
            self.pending_barrier[e] = list(snap)

    def wait_all_dma(self, eng="sync"):
        waits = [(k, 16 * n) for k, n in self.dcount.items()]
        self.ops[eng].append(("waitonly", None, waits))

    def emit(self):
        nc = self.nc
        from contextlib import ExitStack
        with ExitStack() as es:
            sems = {}
            for e in ENGS:
                sems[e] = es.enter_context(nc.semaphore("s_" + e))
            for i, k in enumerate(self.dcount.keys()):
                sems[k] = es.enter_context(nc.semaphore("d_%d" % i))
            block = es.enter_context(nc.Block())

            def run(engname, h):
                for rec in self.ops[engname]:
                    kind, fn, waits = rec[0], rec[1], rec[2]
                    for k, v in waits:
                        h.wait_ge(sems[k], v)
                    if kind == "op":
                        fn(h).then_inc(sems[engname], 1)
                    elif kind == "dma":
                        fn(h).then_inc(sems[rec[3]], 16)

            @block.sync
            def _(h):
                run("sync", h)

            @block.scalar
            def _(h):
                run("act", h)

            @block.vector
            def _(h):
                run("dve", h)

            @block.gpsimd
            def _(h):
                run("pool", h)

            @block.tensor
            def _(h):
                run("pe", h)


def build_program(stop=99):
    nc = bass.Bass("TRN2", target_bir_lowering=False)
    P = Prog(nc)

    def din(name, shape, dt=F32):
        return nc.dram_tensor(name, list(shape), dt, kind="ExternalInput").ap()

    def dout(name, shape, dt=F32):
        return nc.dram_tensor(name, list(shape), dt, kind="ExternalOutput").ap()

    xall = din("xall", [2176, D])
    w_in = din("w_in", [D, D_IN])
    w_out = din("w_out", [D, D])
    norm_in = din("norm_in", [128, 16])
    g_q = din("g_q", [HD])
    g_k = din("g_k", [HD])
    g_ki = din("g_ki", [IDXD])
    w_conv = din("w_conv", [128, 3, 8])
    cos32 = din("cos32", [2176, 16])
    sin32 = din("sin32", [2176, 16])
    cos16 = din("cos16", [2176, 8])
    sin16 = din("sin16", [2176, 8])
    cmask_d = din("cmask", [128, 512])
    pmask_d = din("pmask", [128, 512])
    pow2_d = din("pow2", [KBIS])
    stc = din("stc", [128, 2, 8])
    ptab = din("ptab", [128, 1], I32)
    cm4_d = din("cm4", [4, 4])
    cache_k = din("cache_k", [1280, PAGE * 256])
    cache_v = din("cache_v", [1280, PAGE * 256])
    cache_ki = din("cache_ki", [1280, PAGE * 64])

    y_own = dout("y_own", [1024, D])
    y_s = dout("y_s", [4, D])
    k_own = dout("k_own", [1024, 256])
    v_own = dout("v_own", [1024, 256])
    ki_own = dout("ki_own", [1024, 64])
    conv_p = dout("conv_p", [128, 2, 8])
    k_s = dout("k_s", [4, 256])
    v_s = dout("v_s", [4, 256])
    ki_s = dout("ki_s", [4, 64])
    conv_s = dout("conv_s", [128, 2, 8])
    import os
    DUMP = os.environ.get("DBG_DUMP") == "1"
    if DUMP:
        dbg_I = dout("dbg_I", [128, 516])
        dbg_sm5 = dout("dbg_sm5", [128, 64])
        dbg_NB = dout("dbg_NB", [128, 516])
        dbg_att = dout("dbg_att", [128, 128])
        dbg_O = dout("dbg_O", [16, 132])
        dbg_Sm = dout("dbg_Sm", [128, 256])
        dbg_Wbc = dout("dbg_Wbc", [128, 64])

    BASE = 16640
    LIMIT = 229376
    cur = [BASE]
    cnt = [0]

    def alloc_at(off, shape, dt):
        cnt[0] += 1
        return nc.alloc_sbuf_tensor_at("t%d" % cnt[0], list(shape), dt, offset=off).ap()

    def nbytes(shape, dt):
        n = int(np.prod(shape[1:])) * (2 if dt == BF16 else 4)
        return (n + 63) // 64 * 64

    def sb(shape, dt):
        off = cur[0]
        cur[0] += nbytes(shape, dt)
        assert cur[0] <= LIMIT, ("SBUF overflow", cur[0])
        return alloc_at(off, shape, dt)

    ident_f = sb([128, 128], F32)
    ident_b = sb([128, 128], BF16)
    ID4 = sb([128, 512], BF16)
    ONES = sb([128, 128], BF16)
    gq_bc = sb([128, 128], F32)
    gk_bc = sb([128, 128], F32)
    gki_bc = sb([128, 64], F32)
    COS32 = sb([128, 17, 16], F32)
    SIN32 = sb([128, 17, 16], F32)
    COS16 = sb([128, 17, 8], F32)
    SIN16 = sb([128, 17, 8], F32)
    wcT = sb([128, 3, 8], F32)
    cmask = sb([128, 512], F32)
    pmask = sb([128, 512], F32)
    pow2 = sb([128, KBIS], F32)
    WIDX = sb([128, 9, 16], F32)
    ginT = sb([128, 16], F32)
    SCT = sb([128, 2, 8], F32)
    CO = sb([128, 2, 8], F32)
    COs = sb([128, 2, 8], F32)
    sml = sb([128, 80], F32)
    sml1 = sb([128, 8], F32)
    rt = sb([128, 4, 32], F32)
    bis = sb([128, 16], F32)
    Wt = sb([128, KBIS], F32)
    o_QT = cur[0]
    QT = sb([128, 8, 1024], BF16)
    KT = sb([128, 2, 2048], BF16)
    Vb = sb([128, 16, 256], BF16)
    QIT = sb([128, 8, 1024], BF16)
    KIT = sb([128, 2, 2048], BF16)
    o_MIXA = cur[0]
    MIXA = sb([128, 8, 1024], BF16)
    xstb = [alloc_at(o_MIXA + 8192 * j, [128, 2048], F32) for j in range(2)]
    QT_s = sb([128, 8, 32], BF16)
    KT_s = sb([128, 2, 32], BF16)
    V_s = sb([128, 256], BF16)
    QIT_s = sb([128, 8, 32], BF16)
    KIT_s = sb([128, 32], BF16)
    SZA_s = sb([128, 8, 32], BF16)
    MIX_s = sb([128, 16, 32], BF16)
    Us = sb([128, 8], F32)
    cvs = sb([128, 8], F32)
    XOFF = cur[0]
    xnT_oth = sb([128, 16, 1024], BF16)
    SZA = alloc_at(XOFF, [128, 8, 1024], BF16)
    MIXC = alloc_at(XOFF + 16384, [128, 8, 1024], BF16)
    ROFF = cur[0]
    RSIZE = LIMIT - ROFF
    print('SBUF: XOFF', XOFF, 'ROFF', ROFF, 'RSIZE', RSIZE)
    assert RSIZE >= 79872, RSIZE

    class Region:
        def __init__(self):
            self.o = ROFF

        def sb(self, shape, dt):
            off = self.o
            self.o += nbytes(shape, dt)
            assert self.o <= LIMIT, ("region overflow", self.o - ROFF)
            return alloc_at(off, shape, dt)

    R1 = Region()
    xnT_own = R1.sb([128, 16, 1152], BF16)
    o_x = R1.o
    xst = R1.sb([128, 2048], F32)
    o_xs = R1.o
    xs = R1.sb([128, 2048], BF16)
    Wr = [R1.sb([128, 16, 256], BF16) for _ in range(3)]
    o_tq = R1.o
    NJ = 5
    tQ = [R1.sb([128, 256], F32) for _ in range(NJ)]
    tQ2 = [R1.sb([128, 256], F32) for _ in range(NJ)]
    kbj = [R1.sb([128, 256], BF16) for _ in range(NJ)]
    rtj = [R1.sb([128, 4, 32], F32) for _ in range(NJ)]
    tA = [alloc_at(o_MIXA + 2048 * j, [128, 512], F32) for j in range(2)]
    tB = [alloc_at(o_MIXA + 4096 + 2048 * j, [128, 512], F32) for j in range(2)]
    Wr.append(alloc_at(o_MIXA + 8192, [128, 16, 256], BF16))
    U = alloc_at(o_x, [128, 8, 130], F32)
    xsb = [xs, alloc_at(o_x, [128, 2048], BF16)]
    CV = alloc_at(o_xs, [128, 8, 128], F32)

    ps = [nc.alloc_psum_tensor("ps%d" % i, [128, 512], F32).ap() for i in range(8)]
    psb = [p.bitcast(BF16) for p in ps]

    def ACT(out, in_, func, r, w, **kw):
        P.op("act", lambda e: e.activation(out=out, in_=in_, func=func, **kw), r, w)

    def TS(eng, out, in0, s1, s2, op0, op1, r, w, accum=None):
        if op1 is None:
            P.op(eng, lambda e: e.tensor_scalar(out=out, in0=in0, scalar1=s1, scalar2=None, op0=op0), r, w)
        elif accum is None:
            P.op(eng, lambda e: e.tensor_scalar(out=out, in0=in0, scalar1=s1, scalar2=s2, op0=op0, op1=op1), r, w)
        else:
            P.op(eng, lambda e: e.tensor_scalar(out=out, in0=in0, scalar1=s1, scalar2=s2, op0=op0, op1=op1,
                                                accum_out=accum), r, w)

    def TT(eng, out, in0, in1, op, r, w):
        P.op(eng, lambda e: e.tensor_tensor(out=out, in0=in0, in1=in1, op=op), r, w)

    def STT(out, in0, scalar, in1, op0, op1, r, w):
        P.op("dve", lambda e: e.scalar_tensor_tensor(out=out, in0=in0, scalar=scalar, in1=in1, op0=op0, op1=op1), r, w)

    def CP(eng, out, in_, r, w):
        P.op(eng, lambda e: e.tensor_copy(out=out, in_=in_), r, w)

    def RED(out, in_, op, r, w):
        P.op("dve", lambda e: e.tensor_reduce(out=out, in_=in_, axis=AX.X, op=op), r, w)

    def RCP(out, in_, r, w):
        P.op("dve", lambda e: e.reciprocal(out=out, in_=in_), r, w)

    def MM(out, lhsT, rhs, start, stop, r, w):
        P.op("pe", lambda e: e.matmul(out, lhsT=lhsT, rhs=rhs, start=start, stop=stop), r, w)

    def TR(out, in_, ident, r, w):
        P.op("pe", lambda e: e.transpose(out=out, in_=in_, identity=ident), r, w)

    def DMA(queue, out, in_, dsem, r, w, **kw):
        P.dma(queue, lambda e: e.dma_start(out=out, in_=in_, **kw), dsem, r, w)

    DMA("sync", ginT, norm_in, "c3", [], ["gin"])

    def load_consts():
        DMA("sync", gq_bc, g_q.partition_broadcast(128), "c0", [], ["gq"])
        DMA("sync", gk_bc, g_k.partition_broadcast(128), "c1", [], ["gk"])
        DMA("sync", gki_bc, g_ki.partition_broadcast(128), "c2", [], ["gki"])
        DMA("sync", COS32[:, 0:16, :], cos32[0:2048, :].rearrange("(t p) c -> p t c", p=128), "c4", [], ["cs0"])
        DMA("sync", SIN32[:, 0:16, :], sin32[0:2048, :].rearrange("(t p) c -> p t c", p=128), "c5", [], ["cs1"])
        DMA("sync", COS16[:, 0:16, :], cos16[0:2048, :].rearrange("(t p) c -> p t c", p=128), "c6", [], ["cs2"])
        DMA("sync", SIN16[:, 0:16, :], sin16[0:2048, :].rearrange("(t p) c -> p t c", p=128), "c7", [], ["cs3"])
        DMA("sync", COS32[:, 16, :], cos32[2048:2176, :], "c8", [], ["cs0e"])
        DMA("sync", SIN32[:, 16, :], sin32[2048:2176, :], "c9", [], ["cs1e"])
        DMA("sync", COS16[:, 16, :], cos16[2048:2176, :], "c10", [], ["cs2e"])
        DMA("sync", SIN16[:, 16, :], sin16[2048:2176, :], "c11", [], ["cs3e"])
        DMA("sync", cmask, cmask_d, "c12", [], ["cmask"])
        DMA("sync", pmask, pmask_d, "c13", [], ["pmask"])
        DMA("sync", pow2, pow2_d.partition_broadcast(128), "c14", [], ["pow2"])
        DMA("sync", wcT, w_conv, "c15", [], ["wcT"])
        DMA("sync", SCT, stc, "c16", [], ["SCT"])

    CONST_KEYS = ["gq", "gk", "gki", "gin", "cs0", "cs1", "cs2", "cs3", "cs0e", "cs1e", "cs2e", "cs3e",
                  "cmask", "pmask", "pow2", "wcT", "SCT"]
    P.op("pool", lambda e: e.memset(ident_f, 0.0), [], ["identf"])
    P.op("pool", lambda e: e.affine_select(out=ident_f, in_=ident_f, pattern=[[-1, 128]], compare_op=ALU.not_equal,
                                           fill=1.0, base=0, channel_multiplier=1), ["identf"], ["identf"])
    CP("pool", ident_b, ident_f, ["identf"], ["identb"])
    for a in range(4):
        CP("pool", ID4[:, a * 128:(a + 1) * 128], ident_f, ["identf"], ["ID4"])
    P.op("pool", lambda e: e.memset(ONES, 1.0), [], ["ONES"])

    if stop < 1:
        load_consts()
        P.wait_all_dma('sync'); P.emit(); return nc
    def tile_src(tt):
        if tt == 16:
            return xall[2048:2176, :], 128
        return xall[tt * 128:(tt + 1) * 128, :], 128

    def xn_dst(tt):
        if tt == 16:
            return xnT_own, 1024, 128
        if tt % 2 == 0:
            return xnT_own, (tt // 2) * 128, 128
        return xnT_oth, (tt // 2) * 128, 128

    p1_state = [0]

    def p1_A(tt):
        src, np_ = tile_src(tt)
        pb_ = tt % 2
        xst_, xs_ = xstb[pb_], xsb[pb_]
        kx, kxs, ksm1 = "xst%d" % pb_, "xs%d" % pb_, "sml1_%d" % pb_
        sm1 = sml1[:, 4 * pb_:4 * pb_ + 4]
        DMA("pool", xst_[:np_, :], src, "xld%d" % pb_, [], [kx])
        ACT(xs_[:np_, :], xst_[:np_, :], AF.Square, [kx], [kxs, ksm1], accum_out=sm1[:np_, 0:1])
        TS("dve", sm1[:np_, 1:2], sm1[:np_, 0:1], 1.0 / D, EPS, ALU.mult, ALU.add, [ksm1], [ksm1])
        ACT(sm1[:np_, 2:3], sm1[:np_, 1:2], AF.Sqrt, [ksm1], [ksm1])
        RCP(sm1[:np_, 3:4], sm1[:np_, 2:3], [ksm1], [ksm1])
        TS("dve", xs_[:np_, :], xst_[:np_, :], sm1[:np_, 3:4], None, ALU.mult, None, [kx, ksm1], [kxs])

    def p1_B(tt):
        src, np_ = tile_src(tt)
        pb_ = tt % 2
        xs_ = xsb[pb_]
        kxs = "xs%d" % pb_
        dst, t0, _ = xn_dst(tt)
        for half in range(2):
            bk = 2 + (2 * tt + half) % 2
            for j in range(8):
                kt = half * 8 + j
                TR(psb[bk][:, j * 128:j * 128 + np_], xs_[:np_, kt * 128:(kt + 1) * 128], ident_b[:np_, :np_],
                   [kxs, "identb"], ["ps%d" % bk])
            src_ps = psb[bk].rearrange("p (j t) -> p j t", j=8)[:, :, 0:np_]
            ACT(dst[:, half * 8:half * 8 + 8, t0:t0 + np_], src_ps, AF.Copy, ["ps%d" % bk], [("xnTraw", tt)])
        TT("pool", dst[:, :, t0:t0 + np_], dst[:, :, t0:t0 + np_], ginT.unsqueeze(2).to_broadcast([128, 16, np_]), ALU.mult,
           [("xnTraw", tt), "gin"], [("xnT", tt)])

    def p1_tile(tt):
        if tt == 0:
            p1_A(0)
        if tt + 1 <= 16:
            p1_A(tt + 1)
        p1_B(tt)

    def p1_upto(tt):
        while p1_state[0] <= min(tt, 16):
            p1_tile(p1_state[0])
            p1_state[0] += 1

    if stop < 2:
        load_consts()
        p1_upto(16)
        P.wait_all_dma('sync'); P.emit(); return nc

    wctr = [0]

    def win_cols(c0, n):
        return w_in[:, c0:c0 + n].rearrange("(kt p) c -> p kt c", p=128)

    own_tiles = [2 * i for i in range(8)] + [16]
    all_tiles = list(range(17))
    bkc = [0]
    trc = [0]
    tac = [0]

    def xn_src(tt, kt):
        dst, t0, n = xn_dst(tt)
        return dst[:, kt, t0:t0 + n], n

    def own_idx(tt):
        return tt // 2

    CK = ["cs0", "cs1", "cs2", "cs3", "cs0e", "cs1e", "cs2e", "cs3e"]

    chunks = [("k", C_K, 256, all_tiles, 2, 128, 16, gk_bc, True, 0),
              ("v", C_V, 256, all_tiles, 0, 0, 0, None, False, 0),
              ("kiw", C_KI, 80, all_tiles, 1, 64, 8, gki_bc, True, 0)]
    for c in range(4):
        chunks.append(("q", C_Q + 256 * c, 256, own_tiles, 2, 128, 16, gq_bc, True, c))
    for c in range(4):
        chunks.append(("qi", C_QI + 256 * c, 256, own_tiles, 4, 64, 8, None, False, c))

    wslot = {}

    def issue_w(ci):
        if ci >= len(chunks) or ci in wslot:
            return
        sidx = ci if ci < 3 else (ci - 3) % 2
        wslot[ci] = sidx
        DMA("pool", Wr[sidx][:, :, 0:chunks[ci][2]], win_cols(chunks[ci][1], chunks[ci][2]), "w%d" % sidx, [], ["W%d" % sidx])

    def job(ci, tt, js):
        kind, c0, ncols, _, nh, hd, rh, g_bc, do_norm, cidx = chunks[ci]
        sw = wslot[ci]
        bk = bkc[0] % 2
        bkc[0] += 1
        pk = "ps%d" % bk
        for kt in range(16):
            lhs, n = xn_src(tt, kt)
            MM(ps[bk][:, 0:ncols], lhs, Wr[sw][:, kt, 0:ncols], kt == 0, kt == 15, [("xnT", tt), "W%d" % sw], [pk])
        own = (tt == 16) or (tt % 2 == 0)
        i = own_idx(tt)
        tq, tq2, kb_, rt_ = tQ[js], tQ2[js], kbj[js], rtj[js]
        kq, kq2, kkb, krt, ksm = "tQ%d" % js, "tQ2%d" % js, "kb%d" % js, "rt%d" % js, "sml%d" % js
        sm = sml[:, 16 * js:16 * js + 16]
        if kind == "v":
            if own:
                ACT(tq[:, 0:256], ps[bk][:, 0:256], AF.Copy, [pk], [kq])
            if tt == 16:
                ACT(V_s[:, :], ps[bk][:, 0:256], AF.Copy, [pk], ["Vs"])
            else:
                ACT(Vb[:, tt, :], ps[bk][:, 0:256], AF.Copy, [pk], [("V", tt)])
            if own:
                for _ in range(6):
                    yield
                if tt == 16:
                    DMA("sync", v_s, tq[0:4, 0:256], "ko%d" % js, [kq], [])
                else:
                    DMA("sync", v_own[i * 128:(i + 1) * 128, :], tq[:, 0:256], "ko%d" % js, [kq], [])
            return
        nd = nh * hd
        ACT(tq[:, 0:ncols], ps[bk][:, 0:ncols], AF.Copy, [pk], [kq])
        if do_norm:
            for h in range(nh):
                ACT(tq2[:, h * hd:(h + 1) * hd], ps[bk][:, h * hd:(h + 1) * hd], AF.Square, [pk], [kq2, ksm],
                    accum_out=sm[:, h:h + 1])
            yield
            TS("dve", sm[:, 4:4 + nh], sm[:, 0:nh], 1.0 / hd, EPS, ALU.mult, ALU.add, [ksm], [ksm])
            yield
            ACT(sm[:, 8:8 + nh], sm[:, 4:4 + nh], AF.Sqrt, [ksm], [ksm])
            yield
            RCP(sm[:, 12:12 + nh], sm[:, 8:8 + nh], [ksm], [ksm])
            yield
            for h in range(nh):
                STT(tq2[:, h * hd:(h + 1) * hd], tq[:, h * hd:(h + 1) * hd], sm[:, 12 + h:13 + h], g_bc[:, 0:hd],
                    ALU.mult, ALU.mult, [kq, ksm, "gq", "gk", "gki"], [kq2])
                yield
            src, ksrc = tq2, kq2
        else:
            yield
            src, ksrc = tq, kq
        if kind == "kiw":
            if tt == 16:
                TS("dve", WIDX[:, 8, :], tq[:, 64:80], 1.0 / 32.0, None, ALU.mult, None, [kq], [("WIDX", 8)])
            elif own:
                TS("dve", WIDX[:, i, :], tq[:, 64:80], 1.0 / 32.0, None, ALU.mult, None, [kq], [("WIDX", i)])
        s3 = src[:, 0:nd].rearrange("p (h d) -> p h d", h=nh)
        x1 = s3[:, :, 0:rh]
        x2 = s3[:, :, rh:2 * rh]
        if rh == 16:
            cT, sT = COS32[:, tt, :], SIN32[:, tt, :]
        else:
            cT, sT = COS16[:, tt, :], SIN16[:, tt, :]
        cb = cT.unsqueeze(1).to_broadcast([128, nh, rh])
        sbb = sT.unsqueeze(1).to_broadcast([128, nh, rh])
        r4 = [rt_[:, j, 0:nh * rh].rearrange("p (h r) -> p h r", h=nh) for j in range(4)]
        TT("dve", r4[0], x1, cb, ALU.mult, [ksrc] + CK, [krt])
        TT("dve", r4[1], x2, sbb, ALU.mult, [ksrc] + CK, [krt])
        yield
        TT("dve", r4[2], x2, cb, ALU.mult, [ksrc] + CK, [krt])
        TT("dve", r4[3], x1, sbb, ALU.mult, [ksrc] + CK, [krt])
        yield
        TT("dve", x1, r4[0], r4[1], ALU.subtract, [krt], [ksrc])
        TT("dve", x2, r4[2], r4[3], ALU.add, [krt], [ksrc])
        yield
        if kind == "kiw":
            P.op("pool", lambda e: e.memset(kb_[:, 64:192], 0.0), [], [kkb])
            CP("dve", kb_[:, 0:64], src[:, 0:64], [ksrc], [kkb])
            ACT(kb_[:, 192:256], src[:, 0:64], AF.Copy, [ksrc], [kkb])
            nblk = 2
        else:
            CP("dve", kb_[:, 0:256], src[:, 0:256], [ksrc], [kkb])
            nblk = 2
        yield
        tb = 2 + trc[0] % 2
        trc[0] += 1
        for j in range(nblk):
            TR(psb[tb][:, j * 128:(j + 1) * 128], kb_[:, j * 128:(j + 1) * 128], ident_b, [kkb, "identb"], ["ps%d" % tb])
        for j in range(nblk):
            if kind == "k":
                dst, wk = (KT_s[:, j, 0:32], ("KTs", j)) if tt == 16 else (KT[:, j, tt * 128:(tt + 1) * 128], ("KT", tt))
            elif kind == "kiw":
                if tt == 16:
                    if j == 1:
                        continue
                    dst, wk = KIT_s[:, 0:32], "KITs"
                else:
                    dst, wk = KIT[:, j, tt * 128:(tt + 1) * 128], ("KIT", tt, j)
            elif kind == "q":
                hh = 2 * cidx + j
                dst, wk = (QT_s[:, hh, 0:32], ("QTs", hh)) if tt == 16 else (QT[:, hh, i * 128:(i + 1) * 128], ("QT", i, hh))
            else:
                hh = 2 * cidx + j
                dst, wk = (QIT_s[:, hh, 0:32], ("QITs", hh)) if tt == 16 else (QIT[:, hh, i * 128:(i + 1) * 128], ("QIT", i, hh))
            ncp = dst.shape[-1]
            ACT(dst, psb[tb][:, j * 128:j * 128 + ncp], AF.Copy, ["ps%d" % tb], [wk])
        if own and kind in ("k", "kiw"):
            for _ in range(6):
                yield
            if kind == "k":
                if tt == 16:
                    DMA("sync", k_s, src[0:4, 0:256], "ko%d" % js, [ksrc], [])
                else:
                    DMA("sync", k_own[i * 128:(i + 1) * 128, :], src[:, 0:256], "ko%d" % js, [ksrc], [])
            else:
                if tt == 16:
                    DMA("sync", ki_s, src[0:4, 0:64], "ko%d" % js, [ksrc], [])
                else:
                    DMA("sync", ki_own[i * 128:(i + 1) * 128, :], src[:, 0:64], "ko%d" % js, [ksrc], [])

    def all_jobs():
        for tt in all_tiles:
            for ci in range(3):
                yield (ci, tt)
        for ci in range(3, len(chunks)):
            for tt in chunks[ci][3]:
                yield (ci, tt)

    bkB = [0]
    wB = [0]

    def gen_B():
        oth_keys = [("xnT", tt) for tt in range(1, 16, 2)]
        tok_chunks = [(0, 512), (512, 512), (1024, 32)]

        def fproj(s, sub, c0, n):
            bk = 4 + bkB[0] % 4
            bkB[0] += 1
            tts = [("xnT", 2 * i) for i in range(8)] if c0 < 1024 else [("xnT", 16)]
            for kt in range(16):
                MM(ps[bk][:, 0:n], Wr[s][:, kt, sub * 128:(sub + 1) * 128], xnT_own[:, kt, c0:c0 + n], kt == 0, kt == 15,
                   tts + ["W%d" % s], ["ps%d" % bk])
            return bk

        for f2 in range(4):
            s = 2 + wB[0] % 2
            wB[0] += 1
            DMA("pool", Wr[s][:, :, 0:256], win_cols(C_ZA + 256 * f2, 256), "w%d" % s, [], ["W%d" % s])
            for sub in range(2):
                f = 2 * f2 + sub
                for (c0, n) in tok_chunks:
                    bk = fproj(s, sub, c0, n)
                    if c0 < 1024:
                        ACT(SZA[:, f, c0:c0 + n], ps[bk][:, 0:n], AF.Silu, ["ps%d" % bk] + oth_keys, [("SZA", f, c0)])
                        yield
                    else:
                        ACT(SZA_s[:, f, 0:32], ps[bk][:, 0:32], AF.Silu, ["ps%d" % bk], [("SZAs", f)])
                    yield

        sA, sB = 2, 3

        def load_A(f_):
            DMA("pool", Wr[sA][:, :, 0:128], win_cols(C_H + 128 * f_, 128), "w%d" % sA, [], ["W%d" % sA])
            DMA("pool", Wr[sA][:, :, 128:256], win_cols(C_C + 128 * f_, 128), "w%d" % sA, [], ["W%d" % sA])

        def load_B(f_):
            DMA("pool", Wr[sB][:, :, 0:128], win_cols(C_B + 128 * f_, 128), "w%d" % sB, [], ["W%d" % sB])
            DMA("pool", Wr[sB][:, :, 128:256], win_cols(C_ZC + 128 * f_, 128), "w%d" % sB, [], ["W%d" % sB])

        load_A(0)
        load_B(0)
        for f in range(8):
            for ci, (c0, n) in enumerate(tok_chunks):
                bh = fproj(sA, 0, c0, n)
                yield
                bc = fproj(sA, 1, c0, n)
                yield
                a = tac[0] % 2
                tac[0] += 1
                ACT(tA[a][:, 0:n], ps[bh][:, 0:n], AF.Copy, ["ps%d" % bh], ["tA%d" % a])
                if c0 < 1024:
                    TT("dve", U[:, 4 * ci:4 * ci + 4, 2:130], tA[a][:, 0:512].rearrange("p (i t) -> p i t", i=4),
                       ps[bc][:, 0:512].rearrange("p (i t) -> p i t", i=4), ALU.mult, ["tA%d" % a, "ps%d" % bc], [("U", ci)])
                else:
                    TT("dve", tB[a][:, 0:32], tA[a][:, 0:32], ps[bc][:, 0:32], ALU.mult, ["tA%d" % a, "ps%d" % bc], ["tB%d" % a])
                    CP("dve", U[:, :, 0:2], tB[a][:, 4:20].rearrange("p (i two) -> p i two", two=2), ["tB%d" % a], [("U", "h")])
                    CP("dve", Us[:, 2:6], tB[a][:, 0:4], ["tB%d" % a], ["Us"])
                    CP("dve", Us[:, 0:2], SCT[:, :, f], ["SCT"], ["Us"])
            if f + 1 < 8:
                load_A(f + 1)
            ukeys = [("U", 0), ("U", 1), ("U", "h")]
            TS("dve", CV, U[:, :, 0:128], wcT[:, 0, f:f + 1], None, ALU.mult, None, ukeys + ["wcT"], ["CV"])
            STT(CV, U[:, :, 1:129], wcT[:, 1, f:f + 1], CV, ALU.mult, ALU.add, ukeys + ["wcT", "CV"], ["CV"])
            STT(CV, U[:, :, 2:130], wcT[:, 2, f:f + 1], CV, ALU.mult, ALU.add, ukeys + ["wcT", "CV"], ["CV"])
            TS("dve", cvs[:, 0:4], Us[:, 0:4], wcT[:, 0, f:f + 1], None, ALU.mult, None, ["Us", "wcT"], ["cvs"])
            STT(cvs[:, 0:4], Us[:, 1:5], wcT[:, 1, f:f + 1], cvs[:, 0:4], ALU.mult, ALU.add, ["Us", "wcT", "cvs"], ["cvs"])
            STT(cvs[:, 0:4], Us[:, 2:6], wcT[:, 2, f:f + 1], cvs[:, 0:4], ALU.mult, ALU.add, ["Us", "wcT", "cvs"], ["cvs"])
            CP("dve", CO[:, :, f], U[:, 7, 128:130], ukeys, [("CO", f)])
            CP("dve", COs[:, :, f], Us[:, 4:6], ["Us"], [("COs", f)])
            for ci, (c0, n) in enumerate(tok_chunks):
                bb = fproj(sB, 0, c0, n)
                yield
                bz = fproj(sB, 1, c0, n)
                yield
                a = tac[0] % 2
                tac[0] += 1
                ACT(tA[a][:, 0:n], ps[bz][:, 0:n], AF.Silu, ["ps%d" % bz], ["tA%d" % a])
                if c0 < 1024:
                    TT("dve", tB[a][:, 0:512], CV[:, 4 * ci:4 * ci + 4, :].rearrange("p i t -> p (i t)"), ps[bb][:, 0:512], ALU.mult,
                       ["CV", "ps%d" % bb], ["tB%d" % a])
                    TT("dve", MIXC[:, f, c0:c0 + 512], tB[a][:, 0:512], tA[a][:, 0:512], ALU.mult,
                       ["tA%d" % a, "tB%d" % a] + oth_keys, [("MIXC", f, ci)])
                else:
                    TT("dve", tB[a][:, 0:4], cvs[:, 0:4], ps[bb][:, 0:4], ALU.mult, ["cvs", "ps%d" % bb], ["tB%d" % a])
                    TT("dve", MIX_s[:, 8 + f, 0:4], tB[a][:, 0:4], tA[a][:, 0:4], ALU.mult, ["tA%d" % a, "tB%d" % a], [("MIXs", 8 + f)])
            if f + 1 < 8:
                load_B(f + 1)
        DMA("sync", conv_p, CO, "cvo", [("CO", f) for f in range(8)], [])
        DMA("sync", conv_s, COs, "cvo2", [("COs", f) for f in range(8)], [])


    issue_w(0)
    issue_w(1)
    issue_w(2)
    p1_upto(2)
    load_consts()
    free_slots = list(range(NJ))
    active = []
    p1_alias_done = [False]
    gB = [None]
    gB_done = [False]
    want_B = [False]

    def make_B():
        okeys = [("xnT", t_) for t_ in range(1, 16, 2)]
        for f_ in range(8):
            for c_ in (0, 512):
                P.alias(("SZA", f_, c_), okeys)
            for c_ in range(2):
                P.alias(("MIXC", f_, c_), okeys)
        gB[0] = gen_B()

    jit = all_jobs()
    pending = next(jit, None)
    while pending is not None or active:
        while pending is not None and free_slots:
            ci, tt = pending
            p1_upto(tt + 1)
            if p1_state[0] > 16 and not p1_alias_done[0]:
                p1_alias_done[0] = True
                for _k in ("tA0", "tA1", "tB0", "tB1"):
                    P.alias(_k, ["xst0"])
                P.alias("W3", ["xst1"])
                for _k in (("U", 0), ("U", 1), ("U", "h")):
                    P.alias(_k, ["xs1"])
                P.alias("CV", ["xs0"])
            if ci >= 3:
                want_B[0] = True
            if ci >= 3:
                issue_w(ci)
                issue_w(ci + 1)
            js = free_slots.pop(0)
            g_ = job(ci, tt, js)
            try:
                next(g_)
                active.append((g_, js))
            except StopIteration:
                free_slots.append(js)
            pending = next(jit, None)
        for ent in list(active):
            g, js = ent
            try:
                next(g)
            except StopIteration:
                active.remove(ent)
                free_slots.append(js)
        if want_B[0] and gB[0] is None:
            make_B()
        if gB[0] is not None and not gB_done[0]:
            try:
                next(gB[0])
            except StopIteration:
                gB_done[0] = True
    if gB[0] is None:
        make_B()
    if not gB_done[0]:
        for _ in gB[0]:
            pass

    if stop < 4:
        P.wait_all_dma('sync'); P.emit(); return nc
    P.barrier()

    R2 = Region()
    IS = [R2.sb([128, 2048], F32) for _ in range(3)]
    NBm = [R2.sb([128, 2048], BF16) for _ in range(3)]
    junk = R2.sb([128, 2048], BF16)
    Rh = [R2.sb([128, 512], BF16) for _ in range(6)]
    Dg = [R2.sb([128, 16, 128], BF16) for _ in range(2)]
    PT = [R2.sb([128, 512], BF16) for _ in range(5)]
    rz = R2.sb([128, 512], F32)
    otmp = R2.sb([128, 512], F32)
    t256 = R2.sb([128, 256], F32)
    cmaskb = R2.sb([128, 512], BF16)
    Zacc = [R2.sb([128, 512], F32) for _ in range(2)]
    ONESF = R2.sb([128, 128], F32)
    otm = [R2.sb([128, 512], F32) for _ in range(2)]
    rzl = [R2.sb([128, 512], F32) for _ in range(2)]
    rzz = [R2.sb([128, 512], F32) for _ in range(2)]
    rhc = [0]
    ptc = [0]
    dbk = [0]
    stc_ = [0]
    SCALE = float(HD) ** -0.5
    CP("pool", cmaskb, cmask, ["cmask"], ["cmaskb"])
    P.op("pool", lambda e: e.memset(ONESF, 1.0), [], ["ONESF"])

    def indexer(i):
        ib = i % 2
        isb = i % 3
        nk = 256 * (i + 1)
        par = i % 2
        for h in range(16):
            TS("pool", Dg[ib][:, h, :], ident_b, WIDX[:, i, h:h + 1], 0.0, ALU.mult, ALU.add, ["identb", ("WIDX", i)], [("Dg", ib)])
        nch = (nk + 511) // 512
        for c in range(nch):
            k0 = 512 * c
            ncol = min(512, nk - k0)
            m0 = nk - 256
            has_mask = (k0 + ncol > m0)
            prev = None
            for h in range(16):
                hp = h % 2
                bk = (0, 1, 7)[dbk[0] % 3]
                dbk[0] += 1
                MM(ps[bk][:, 0:ncol], QIT[:, h // 2, i * 128:(i + 1) * 128],
                   KIT[:, hp, k0:k0 + ncol], True, True,
                   [("QIT", i, h // 2)] + [("KIT", tt, hp) for tt in range(k0 // 128, (k0 + ncol) // 128)], ["ps%d" % bk])
                r = rhc[0] % 6
                rhc[0] += 1
                ACT(Rh[r][:, 0:ncol], ps[bk][:, 0:ncol], AF.Relu, ["ps%d" % bk], ["Rh%d" % r])
                if prev is not None:
                    ph, pr = prev
                    MM(ps[2][:, 0:ncol], Dg[ib][:, ph, :], Rh[pr][:, 0:ncol], ph == 0, False, [("Dg", ib), "Rh%d" % pr], ["ps2"])
                prev = (h, r)
                yield
            ph, pr = prev
            MM(ps[2][:, 0:ncol], Dg[ib][:, ph, :], Rh[pr][:, 0:ncol], False, not has_mask, [("Dg", ib), "Rh%d" % pr], ["ps2"])
            if has_mask:
                lo = m0 - k0
                MM(ps[2][:, lo:lo + 256], ident_b, cmaskb[:, par * 256:(par + 1) * 256], False, True, ["identb", "cmaskb"], ["ps2"])
            ACT(IS[isb][:, k0:k0 + ncol], ps[2][:, 0:ncol], AF.Copy, ["ps2"], [("IS", isb)])

    def bisect(i):
        ib = i % 3
        nk = 256 * (i + 1)
        par = i % 2
        isk = ("IS", ib)
        X = IS[ib][:, 0:nk]
        RED(bis[:, 0:1], X, ALU.max, [isk], ["bis"])
        TT("dve", t256, IS[ib][:, nk - 256:nk], pmask[:, par * 256:(par + 1) * 256], ALU.add, [isk, "pmask"], ["t256"])
        RED(bis[:, 1:2], t256, ALU.min, ["t256"], ["bis"])
        if nk > 256:
            RED(bis[:, 2:3], IS[ib][:, 0:nk - 256], ALU.min, [isk], ["bis"])
            TT("dve", bis[:, 1:2], bis[:, 1:2], bis[:, 2:3], ALU.min, ["bis"], ["bis"])
        TT("dve", bis[:, 3:4], bis[:, 0:1], bis[:, 1:2], ALU.subtract, ["bis"], ["bis"])
        STT(bis[:, 4:5], bis[:, 3:4], -(2.0 ** -10), bis[:, 1:2], ALU.mult, ALU.add, ["bis"], ["bis"])
        TS("dve", bis[:, 4:5], bis[:, 4:5], -1e-6, None, ALU.add, None, ["bis"], ["bis"])
        TT("dve", bis[:, 5:6], bis[:, 0:1], bis[:, 4:5], ALU.subtract, ["bis"], ["bis"])
        TS("dve", Wt, pow2, bis[:, 5:6], None, ALU.mult, None, ["pow2", "bis"], ["Wt"])
        TT("dve", bis[:, 6:7], bis[:, 4:5], Wt[:, 0:1], ALU.add, ["bis", "Wt"], ["bis"])
        for k in range(KBIS):
            TS("dve", junk[:, 0:nk], X, bis[:, 6:7], None, ALU.is_ge, ALU.add, [isk, "bis"], ["junk", "bis"], accum=bis[:, 7:8])
            TS("dve", bis[:, 8:9], bis[:, 7:8], float(TOPK), 0.5, ALU.is_ge, ALU.subtract, ["bis"], ["bis"])
            if k < KBIS - 1:
                STT(bis[:, 6:7], bis[:, 8:9], Wt[:, k:k + 1], bis[:, 6:7], ALU.mult, ALU.add, ["bis", "Wt"], ["bis"])
            else:
                TS("dve", bis[:, 8:9], bis[:, 8:9], -0.5, None, ALU.add, None, ["bis"], ["bis"])
                STT(bis[:, 9:10], bis[:, 8:9], Wt[:, k:k + 1], bis[:, 6:7], ALU.mult, ALU.add, ["bis", "Wt"], ["bis"])
        TS("dve", NBm[ib][:, 0:nk], X, bis[:, 9:10], NEG, ALU.is_lt, ALU.mult, [isk, "bis"], [("NB", ib)])

    def attend_main(i):
        ib = i % 3
        nsl = 2 * i + 2
        seq = [(g, s_) for g in range(2) for s_ in range(nsl)]
        pend = None

        def s_mm(g, s_):
            sb_ = (3, 6)[stc_[0] % 2]
            stc_[0] += 1
            MM(ps[sb_].rearrange("p (h t) -> p h t", h=4), KT[:, g, s_ * 128:(s_ + 1) * 128],
               QT[:, 4 * g:4 * g + 4, i * 128:(i + 1) * 128], True, False,
               [("KT", s_)] + [("QT", i, 4 * g + hh) for hh in range(4)], ["ps%d" % sb_])
            MM(ps[sb_], NBm[ib][:, s_ * 128:(s_ + 1) * 128], ID4, False, True, [("NB", ib), "ID4"], ["ps%d" % sb_])
            r = ptc[0] % 5
            ptc[0] += 1
            ACT(PT[r], ps[sb_], AF.Exp, ["ps%d" % sb_], ["PT%d" % r], scale=SCALE)
            return r

        def pv_mm(g, s_, r):
            MM(ps[4], Vb[:, s_, g * 128:(g + 1) * 128], PT[r], s_ == 0, s_ == nsl - 1, [("V", s_), "PT%d" % r], ["ps4"])
            if s_ % 2 == 0:
                MM(ps[5], ONES, PT[r], s_ == 0, False, ["ONES", "PT%d" % r], ["ps5"])
            elif s_ == 1:
                CP("pool", Zacc[g], PT[r], ["PT%d" % r], ["Zacc%d" % g])
            else:
                TT("pool", Zacc[g], Zacc[g], PT[r], ALU.add, ["PT%d" % r, "Zacc%d" % g], ["Zacc%d" % g])
            if s_ == nsl - 1:
                MM(ps[5], ONESF, Zacc[g], False, True, ["ONESF", "Zacc%d" % g], ["ps5"])
                ACT(otm[g], ps[4], AF.Copy, ["ps4"], ["otm%d" % g])
                ACT(rzl[g], ps[5], AF.Ln, ["ps5"], ["rzl%d" % g])
                ACT(rzz[g], rzl[g], AF.Exp, ["rzl%d" % g], ["rzz%d" % g], scale=-1.0)
                TT("pool", otm[g], otm[g], rzz[g], ALU.mult, ["otm%d" % g, "rzz%d" % g], ["otm%d" % g])
                TT("pool", MIXA[:, 4 * g:4 * g + 4, i * 128:(i + 1) * 128], otm[g].rearrange("p (h t) -> p h t", h=4),
                   SZA[:, 4 * g:4 * g + 4, i * 128:(i + 1) * 128], ALU.mult,
                   ["otm%d" % g] + [("SZA", 4 * g + hh, (i // 4) * 512) for hh in range(4)], [("MIXA", i, g)])

        for (g, s_) in seq:
            r = s_mm(g, s_)
            if pend is not None:
                pv_mm(*pend)
            pend = (g, s_, r)
            yield
        pv_mm(*pend)

    def attend_epi(i):
        pass

    def drain(g_):
        for _ in g_:
            pass

    drain(indexer(0))
    bisect(0)
    drain(indexer(1))
    drain(indexer(2))
    for i in range(8):
        if i + 1 < 8:
            bisect(i + 1)
        ga = attend_main(i)
        if i + 3 < 8:
            gi = indexer(i + 3)
            n_att = 2 * (2 * i + 2)
            n_idx = 16 * ((256 * (i + 4) + 511) // 512)
            gi_alive = True
            acc = 0
            for _ in ga:
                acc += n_idx
                while acc >= n_att:
                    acc -= n_att
                    if gi_alive:
                        try:
                            next(gi)
                        except StopIteration:
                            gi_alive = False
            if gi_alive:
                drain(gi)
        else:
            drain(ga)

    if stop < 5:
        P.wait_all_dma('sync'); P.emit(); return nc
    P.barrier()

    R4 = Region()
    I_all = R4.sb([128, 129, 4], F32)
    NBs = R4.sb([128, 129, 4], F32)
    junk_s = R4.sb([128, 132], F32)
    idx16 = R4.sb([128, 16], I32)
    ptb = R4.sb([128, 1], I32)
    Q64 = R4.sb([128, 8, 2, 4], BF16)
    Q64z = R4.sb([128, 2, 64], BF16)
    cnt_b = R4.sb([128, 4], BF16)
    idx4 = R4.sb([128, 4], I32)
    sm5 = R4.sb([128, 64], F32)
    Dw = R4.sb([128, 64], F32)
    Wbc = R4.sb([128, 64], F32)
    dg4 = R4.sb([128, 8], F32)
    ones_f = R4.sb([128, 128], F32)
    cm4 = R4.sb([128, 4], F32)
    Vn = R4.sb([128, 2, 130], BF16)
    Pn = R4.sb([128, 2, 16], BF16)
    Sn = R4.sb([128, 2, 16], F32)
    att = R4.sb([128, 128], F32)
    o_sub = R4.o
    KI = R4.sb([128, 128, 64], F32)
    KIb = R4.sb([128, 128, 64], BF16)
    kiTg = [R4.sb([128, 8, 128], BF16) for _ in range(2)]
    tmpi = R4.sb([128, 8, 64], F32)
    R5 = Region()
    R5.o = o_sub
    Kc = [R5.sb([128, 8, 256], F32) for _ in range(3)]
    Vc = [R5.sb([128, 8, 256], F32) for _ in range(3)]
    Kb = [R5.sb([128, 8, 256], BF16) for _ in range(2)]
    KTc = [R5.sb([128, 8, 2, 128], BF16) for _ in range(2)]
    Vb2 = [R5.sb([128, 8, 2, 130], BF16) for _ in range(3)]
    Sm = R5.sb([128, 8, 2, 16], F32)
    Pb = [R5.sb([128, 8, 2, 16], BF16) for _ in range(2)]

    def IDMA(out, in_, idx_ap, dsem, r, w):
        P.dma("pool", lambda e: e.indirect_dma_start(out=out, out_offset=None, in_=in_,
                                                     in_offset=bass.IndirectOffsetOnAxis(ap=idx_ap, axis=0)), dsem, r, w)

    sstage = [0]

    def gen_sample():
        DMA("sync", ptb, ptab, "s0", [], ["ptb"])
        DMA("sync", cm4[0:4, :], cm4_d, "s1", [], ["cm4"])
        cki4 = cache_ki.rearrange("p (c x) -> (p c) x", c=4)
        for q4 in range(4):
            TS("dve", idx4[:, q4:q4 + 1], ptb[:, 0:1], 4.0, float(q4), ALU.mult, ALU.add, ["ptb"], [("idx4", q4)])
            IDMA(KI[:, q4 * 32:(q4 + 1) * 32, :].rearrange("p j d -> p (j d)"), cki4, idx4[:, q4:q4 + 1], "s2_%d" % q4,
                 [("idx4", q4)], [("KI", q4)])
        for c in range(16):
            TS("dve", idx16[:, c:c + 1], ptb[:, 0:1], 16.0, float(c), ALU.mult, ALU.add, ["ptb"], ["idx16"])
        P.op("pool", lambda e: e.memset(ones_f, 1.0), [], ["ones_f"])
        P.op("pool", lambda e: e.memset(I_all[:, 128, :], -1e30), [], ["I128"])
        P.op("pool", lambda e: e.memset(Vn, 1.0), [], ["Vn"])
        CP("dve", Q64[0:64, :, 0, :], QIT_s[0:64, :, 0:4], [("QITs", j) for j in range(8)], ["Q64a"])
        DMA("sync", Q64[0:64, :, 1, :], QIT_s[64:128, :, 0:4], "s3", [("QITs", j) for j in range(8)], ["Q64b"])
        P.op("pool", lambda e: e.memset(Q64z, 0.0), [], ["Q64z"])
        CP("dve", Q64z[0:64, 0, :], Q64[0:64].rearrange("p a b t -> p (a b t)"), ["Q64a", "Q64b", "Q64z"], ["Q64z"])
        DMA("sync", Q64z[64:128, 1, :], Q64[0:64].rearrange("p a b t -> p (a b t)"), "s3b", ["Q64a", "Q64b", "Q64z"], ["Q64zb"])
        TT("dve", Dw[0:4, :].rearrange("p (h t) -> p h t", h=16), WIDX[0:4, 8, :].unsqueeze(2).to_broadcast([4, 16, 4]),
           ident_f[0:4, 0:4].unsqueeze(1).to_broadcast([4, 16, 4]), ALU.mult, [("WIDX", 8), "identf"], ["Dw"])
        MM(ps[4][:, 0:64], ones_f[0:4, :], Dw[0:4, :], True, True, ["ones_f", "Dw"], ["ps4"])
        CP("dve", Wbc, ps[4][:, 0:64], ["ps4"], ["Wbc"])
        for q4 in range(4):
            if q4 % 2 == 1:
                CP("pool", KIb[:, q4 * 32:(q4 + 1) * 32, :], KI[:, q4 * 32:(q4 + 1) * 32, :], [("KI", q4)], [("KIb", q4)])
            else:
                ACT(KIb[:, q4 * 32:(q4 + 1) * 32, :], KI[:, q4 * 32:(q4 + 1) * 32, :], AF.Copy, [("KI", q4)], [("KIb", q4)])
        wb3 = Wbc.unsqueeze(1).to_broadcast([128, 8, 64])
        for gi in range(16):
            j0 = gi * 8
            tb = gi % 2
            for jp in range(4):
                TR(psb[tb][:, jp * 128:(jp + 1) * 128], KIb[:, j0 + 2 * jp:j0 + 2 * jp + 2, :].rearrange("p a d -> p (a d)"), ident_b,
                   [("KIb", j0 // 32), "identb"], ["ps%d" % tb])
            ACT(kiTg[tb][:, 0:4, :], psb[tb][:, 0:512].rearrange("p (j t) -> p j t", j=4), AF.Copy, ["ps%d" % tb], ["kiTg%d" % tb])
            db = 2 + gi % 2
            for jj in range(8):
                MM(ps[db][:, jj * 64:(jj + 1) * 64], kiTg[tb][:, jj // 2, :], Q64z[:, jj % 2, :], True, True,
                   ["kiTg%d" % tb, "Q64z", "Q64zb"], ["ps%d" % db])
            STT(tmpi, ps[db].rearrange("p (j c) -> p j c", j=8), 0.0, wb3, ALU.max, ALU.mult, ["ps%d" % db, "Wbc"], ["tmpi"])
            RED(I_all[:, j0:j0 + 8, :], tmpi.rearrange("p j (h t) -> p j t h", h=16), ALU.add, ["tmpi"], [("Ia", gi)])
            yield
        MM(ps[4][0:4, 0:64], KIT_s[:, 0:4], Q64z[:, 0, :], True, True, ["KITs", "Q64z", "Q64zb"], ["ps4"])
        STT(tmpi[0:4, 0, :], ps[4][0:4, 0:64], 0.0, Wbc[0:4, :], ALU.max, ALU.mult, ["ps4", "Wbc"], ["tmpi"])
        RED(sm5[0:4, 0:4], tmpi[0:4, 0, :].rearrange("p (h t) -> p t h", h=16), ALU.add, ["tmpi"], ["sm5"])
        TT("dve", I_all[0:4, 128, :], sm5[0:4, 0:4], cm4[0:4, :], ALU.add, ["sm5", "cm4", "I128"], ["I128"])
        IA_KEYS = [("Ia", gi) for gi in range(16)] + ["I128"]

        def bcast4(dst, src_col4, negate=False):
            P.op("pe", lambda e: e.transpose(out=ps[5][0:4, 0:128], in_=src_col4, identity=ident_f), ["sm5", "identf"], ["ps5"])
            RED(dg4[0:4, 0:1], ps[5][0:4, 0:128], ALU.max, ["ps5"], ["dg4"])
            TS("dve", dg4[0:4, 4:8], ident_f[0:4, 0:4], dg4[0:4, 0:1], None, ALU.mult, None, ["dg4", "identf"], ["dg4b"])
            MM(ps[5][:, 256:260], ones_f[0:4, :], dg4[0:4, 4:8], True, True, ["ones_f", "dg4b"], ["ps5"])
            TS("dve", dst, ps[5][:, 256:260], -1.0 if negate else 1.0, None, ALU.mult, None, ["ps5"], ["sm5"])

        Iv = I_all.rearrange("p s t -> p t s")
        RED(sm5[:, 8:12], Iv, ALU.max, IA_KEYS, ["sm5"])
        bcast4(sm5[:, 12:16], sm5[:, 8:12])
        P.op("dve", lambda e: e.tensor_reduce(out=sm5[:, 8:12], in_=Iv[:, :, 0:128], axis=AX.X, op=ALU.min, negate=True), IA_KEYS + ["sm5"], ["sm5"])
        bcast4(sm5[:, 16:20], sm5[:, 8:12], negate=True)
        TT("dve", sm5[:, 20:24], sm5[:, 12:16], sm5[:, 16:20], ALU.subtract, ["sm5"], ["sm5"])
        STT(sm5[:, 24:28], sm5[:, 20:24], -(2.0 ** -10), sm5[:, 16:20], ALU.mult, ALU.add, ["sm5"], ["sm5"])
        TS("dve", sm5[:, 24:28], sm5[:, 24:28], -1e-6, None, ALU.add, None, ["sm5"], ["sm5"])
        TT("dve", sm5[:, 28:32], sm5[:, 12:16], sm5[:, 24:28], ALU.subtract, ["sm5"], ["sm5"])
        STT(sm5[:, 32:36], sm5[:, 28:32], 0.5, sm5[:, 24:28], ALU.mult, ALU.add, ["sm5"], ["sm5"])
        def bisect_s():
            KS = 14
            for k in range(KS):
                for t in range(4):
                    TS("dve", junk_s[:, 0:129], I_all[:, :, t], sm5[:, 32 + t:33 + t], None, ALU.is_ge, ALU.add, IA_KEYS + ["sm5"],
                       ["junk_s", "cntp"], accum=cnt_b[:, t:t + 1])
                MM(ps[4][:, 0:4], ONES, cnt_b, True, True, ["ONES", "cntp"], ["ps4"])
                TS("dve", sm5[:, 40:44], ps[4][:, 0:4], float(TOPK), 0.5, ALU.is_ge, ALU.subtract, ["ps4"], ["sm5"])
                TT("dve", sm5[:, 44:48], sm5[:, 40:44], sm5[:, 28:32], ALU.mult, ["sm5"], ["sm5"])
                yield
                if k < KS - 1:
                    STT(sm5[:, 32:36], sm5[:, 44:48], 2.0 ** -(k + 1), sm5[:, 32:36], ALU.mult, ALU.add, ["sm5"], ["sm5"])
                else:
                    STT(sm5[:, 48:52], sm5[:, 28:32], -0.5, sm5[:, 44:48], ALU.mult, ALU.add, ["sm5"], ["sm5"])
                    STT(sm5[:, 52:56], sm5[:, 48:52], 2.0 ** -(k + 1), sm5[:, 32:36], ALU.mult, ALU.add, ["sm5"], ["sm5"])
            for t in range(4):
                TS("dve", NBs[:, :, t], I_all[:, :, t], sm5[:, 52 + t:53 + t], NEG, ALU.is_lt, ALU.mult, IA_KEYS + ["sm5"], ["NBs"])


        if DUMP:
            DMA('sync', dbg_I, I_all.rearrange('p s t -> p (s t)'), 'dd0', IA_KEYS, [])
            DMA('sync', dbg_sm5, sm5, 'dd1', ['sm5'], [])
            DMA('sync', dbg_NB, NBs.rearrange('p s t -> p (s t)'), 'dd2', ['NBs'], [])
            DMA('sync', dbg_Wbc, Wbc, 'dd3', ['Wbc'], [])
        NEWK = ["Kc0", "Kc1", "Kc2", "Vc0", "Vc1", "Vc2", "Kb0", "Kb1", ("KTc", 0, 0), ("KTc", 0, 1), ("KTc", 1, 0), ("KTc", 1, 1),
                "Vb2_0", "Vb2_1", "Vb2_2", "Sm", "Pb0", "Pb1"]
        OLDK = [("KI", 0), ("KI", 1), ("KI", 2), ("KI", 3), ("KIb", 0), ("KIb", 1), ("KIb", 2), ("KIb", 3), "kiTg0", "kiTg1", "tmpi"]
        for _nk in NEWK:
            P.alias(_nk, OLDK)
        for c3 in range(3):
            P.op("pool", (lambda c3: (lambda e: e.memset(Vb2[c3], 1.0)))(c3), [], ["Vb2_%d" % c3])
        ck2 = cache_k.rearrange("p (c x) -> (p c) x", c=16)
        cv2 = cache_v.rearrange("p (c x) -> (p c) x", c=16)
        QsR = [QT_s[:, 4 * g:4 * g + 4, 0:4] for g in range(2)]

        def front(c):
            b2, b3 = c % 2, c % 3
            IDMA(Kc[b3].rearrange("p j d -> p (j d)"), ck2, idx16[:, c:c + 1], "s4_%d" % b3, ["idx16"], ["Kc%d" % b3])
            IDMA(Vc[b3].rearrange("p j d -> p (j d)"), cv2, idx16[:, c:c + 1], "s5_%d" % b3, ["idx16"], ["Vc%d" % b3])
            ACT(Kb[b2], Kc[b3], AF.Copy, ["Kc%d" % b3], ["Kb%d" % b2])
            CP("dve", Vb2[b3][:, :, :, 0:128], Vc[b3].rearrange("p j (g d) -> p j g d", g=2), ["Vc%d" % b3, "Vb2_%d" % b3], ["Vb2_%d" % b3])
            yield
            for half in range(2):
                tb = half
                for jj in range(4):
                    for g in range(2):
                        TR(psb[tb][:, (jj * 2 + g) * 128:(jj * 2 + g + 1) * 128], Kb[b2][:, half * 4 + jj, g * 128:(g + 1) * 128], ident_b,
                           ["Kb%d" % b2, "identb"], ["ps%d" % tb])
                ACT(KTc[b2][:, half * 4:half * 4 + 4, :, :], psb[tb].rearrange("p (j g t) -> p j g t", j=4, g=2), AF.Copy,
                    ["ps%d" % tb], [("KTc", b2, half)])
                yield
            sbk = 2 + b2
            for jj in range(8):
                for g in range(2):
                    MM(ps[sbk][:, (jj * 2 + g) * 16:(jj * 2 + g + 1) * 16].rearrange("p (h t) -> p h t", h=4), KTc[b2][:, jj, g, :], QsR[g], True, True,
                       [("KTc", b2, jj // 4)] + [("QTs", 4 * g + hh) for hh in range(4)], ["ps%d" % sbk])
            yield

        def back(c):
            b2, b3 = c % 2, c % 3
            sbk = 2 + b2
            TT("dve", Sm.rearrange("p j g (h t) -> p j (g h) t", h=4), ps[sbk][:, 0:256].rearrange("p (j gh t) -> p j gh t", j=8, t=4),
               NBs[:, c * 8:(c + 1) * 8, :].unsqueeze(2).to_broadcast([128, 8, 8, 4]), ALU.add, ["ps%d" % sbk, "NBs"], ["Sm"])
            ACT(Pb[b2], Sm, AF.Exp, ["Sm"], ["Pb%d" % b2], scale=SCALE)
            for jj in range(8):
                for g in range(2):
                    MM(ps[4 + g][0:16, 0:129], Pb[b2][:, jj, g, :], Vb2[b3][:, jj, g, 0:129], c == 0 and jj == 0, False,
                       ["Pb%d" % b2, "Vb2_%d" % b3], ["ps%d" % (4 + g)])
            yield

        yield from front(0)
        yield from front(1)
        yield from bisect_s()
        sstage[0] = 1
        for c in range(16):
            yield from back(c)
            if c + 2 < 16:
                yield from front(c + 2)
        for g in range(2):
            MM(ps[2][0:4, g * 16:(g + 1) * 16].rearrange("p (h t) -> p h t", h=4), KT_s[:, g, 0:4], QsR[g], True, True,
               [("KTs", g)] + [("QTs", 4 * g + hh) for hh in range(4)], ["ps2"])
            CP("dve", Vn[0:4, g, 0:128], V_s[0:4, g * 128:(g + 1) * 128], ["Vs", "Vn"], ["Vn"])
        TT("dve", Sn[0:4].rearrange("p g (h t) -> p (g h) t", h=4), ps[2][0:4, 0:32].rearrange("p (gh t) -> p gh t", t=4),
           NBs[0:4, 128, :].unsqueeze(1).to_broadcast([4, 8, 4]), ALU.add, ["ps2", "NBs"], ["Sn"])
        ACT(Pn[0:4], Sn[0:4], AF.Exp, ["Sn"], ["Pn"], scale=SCALE)
        for g in range(2):
            MM(ps[4 + g][0:16, 0:129], Pn[0:4, g, :], Vn[0:4, g, 0:129], False, True, ["Pn", "Vn"], ["ps%d" % (4 + g)])
            RCP(sm5[0:16, 60:61], ps[4 + g][0:16, 128:129], ["ps%d" % (4 + g)], ["sm5r"])
            TS("dve", att[0:16, :], ps[4 + g][0:16, 0:128], sm5[0:16, 60:61], None, ALU.mult, None, ["ps%d" % (4 + g), "sm5r"], ["att"])
            P.op("pe", lambda e: e.transpose(out=ps[0][:, 0:16], in_=att[0:16, :], identity=ident_f[0:16, 0:16]), ["att", "identf"], ["ps0"])
            TT("dve", MIX_s[:, 4 * g:4 * g + 4, 0:4], ps[0][:, 0:16].rearrange("p (h t) -> p h t", h=4), SZA_s[:, 4 * g:4 * g + 4, 0:4], ALU.mult,
               ["ps0"] + [("SZAs", 4 * g + hh) for hh in range(4)], [("MIXs", 4 * g + hh) for hh in range(4)])

        if DUMP:
            DMA('sync', dbg_att, att, 'dd4', ['att'], [])
            CP('dve', junk_s[0:16, 0:132], ps[5][0:16, 0:132], ['ps5'], ['junk_s'])
            DMA('sync', dbg_O, junk_s[0:16, 0:132], 'dd5', ['junk_s'], [])
            DMA('sync', dbg_Sm, Sm.rearrange('p j g c -> p (j g c)'), 'dd6', ['Sm'], [])

    if stop < 6:
        for _ in gen_sample():
            pass
        P.wait_all_dma('sync'); P.emit(); return nc

    R3 = Region()
    R3.o = o_QT
    WO = [R3.sb([128, 16, 512], BF16) for _ in range(3)]
    xr = [R3.sb([128, 512], F32) for _ in range(2)]
    yst = [R3.sb([128, 512], F32) for _ in range(2)]
    wo_slot = {0: 0, 1: 1, 2: 2, 3: 0}
    assert R3.o <= o_MIXA, (R3.o, o_MIXA)
    xc = [0]
    obk = [0]

    def out_step(c, i):
        s_ = wo_slot[c]
        bk = 6 + obk[0] % 2
        obk[0] += 1
        a_ = xc[0] % 2
        xc[0] += 1
        if i < 8:
            np_ = 128
            DMA("sync", xr[a_], xall[(2 * i) * 128:(2 * i + 1) * 128, c * 512:(c + 1) * 512], "xr%d" % a_, [], ["xr%d" % a_])
        else:
            np_ = 4
            DMA("sync", xr[a_][0:4, :], xall[2048:2052, c * 512:(c + 1) * 512], "xr%d" % a_, [], ["xr%d" % a_])
        for ft in range(16):
            if i < 8:
                lhs = (MIXA if ft < 8 else MIXC)[:, ft % 8, i * 128:(i + 1) * 128]
                rk = [("MIXA", i, 0), ("MIXA", i, 1)] if ft < 8 else [("MIXC", ft - 8, i // 4)]
            else:
                lhs = MIX_s[:, ft, 0:4]
                rk = [("MIXs", ft)]
            MM(ps[bk][:np_, :], lhs, WO[s_][:, ft, :], ft == 0, ft == 15, rk + ["WO%d" % s_], ["ps%d" % bk])
            if ft % 4 == 3 and ft < 15:
                yield
        TT("dve", yst[a_][:np_, :], ps[bk][:np_, :], xr[a_][:np_, :], ALU.add, ["ps%d" % bk, "xr%d" % a_], ["yst%d" % a_])
        if i < 8:
            DMA("sync", y_own[i * 128:(i + 1) * 128, c * 512:(c + 1) * 512], yst[a_], "yo%d" % a_, ["yst%d" % a_], [])
        else:
            DMA("sync", y_s[:, c * 512:(c + 1) * 512], yst[a_][0:4, :], "yo%d" % a_, ["yst%d" % a_], [])
        yield

    def load_wo(c):
        s_ = wo_slot[c]
        DMA("pool", WO[s_], w_out[:, c * 512:(c + 1) * 512].rearrange("(ft p) c -> p ft c", p=128), "wo%d" % s_, [], ["WO%d" % s_])

    def gen_out():
        load_wo(0)
        load_wo(1)
        load_wo(2)
        for c in range(4):
            for i in range(8):
                yield from out_step(c, i)
            if c == 0:
                load_wo(3)

    gs = gen_sample()
    go = gen_out()
    gs_alive, go_alive = True, True
    while gs_alive or go_alive:
        if gs_alive:
            try:
                next(gs)
            except StopIteration:
                gs_alive = False
        n_out = 3 if (sstage[0] == 0 and gs_alive) else 1
        for _ in range(n_out):
            if go_alive:
                try:
                    next(go)
                except StopIteration:
                    go_alive = False
    for _ in out_step(1, 8):
        pass
    wo_slot[0] = 1
    load_wo(0)
    for c_ in (2, 3, 0):
        for _ in out_step(c_, 8):
            pass

    P.wait_all_dma("sync")
    P.emit()
    return nc


_NC_CACHE = {}


def _rope_tables(pos, half):
    inv = (np.float32(THETA) ** (-(np.arange(half, dtype=np.float32) / np.float32(half)))).astype(np.float32)
    ang = pos.astype(np.float32)[:, None] * inv[None, :]
    return np.cos(ang).astype(np.float32), np.sin(ang).astype(np.float32)


def kernel(x_prompt, x_sample, cache_k, cache_v, cache_kidx, state_conv, page_table,
           norm_in, w_in, g_q, g_k, g_kidx, w_conv, w_out):
    x_prompt = np.asarray(x_prompt, np.float32)
    x_sample = np.asarray(x_sample, np.float32)
    if "nc" not in _NC_CACHE:
        _NC_CACHE["nc"] = build_program()
    nc = _NC_CACHE["nc"]
    ck = np.ascontiguousarray(np.asarray(cache_k, np.float32)[0].reshape(1280, PAGE * 256))
    cv = np.ascontiguousarray(np.asarray(cache_v, np.float32)[0].reshape(1280, PAGE * 256))
    cki = np.ascontiguousarray(np.asarray(cache_kidx, np.float32)[0].reshape(1280, PAGE * 64))
    w_in0 = np.ascontiguousarray(np.asarray(w_in, np.float32)[0])
    w_out0 = np.ascontiguousarray(np.asarray(w_out, np.float32)[0])
    pow2 = (2.0 ** -(np.arange(KBIS, dtype=np.float32) + 1)).astype(np.float32)
    in_maps = []
    slots_all = []
    for c in range(8):
        b, j = c // 2, c % 2
        own = T_OWN[j]
        oth = T_OWN[1 - j]
        slots = []
        for i in range(8):
            slots += [own[i], oth[i]]
        slots_all.append(slots)
        xall = np.zeros((2176, D), np.float32)
        pos = np.zeros(2176, np.int64)
        for s_, t in enumerate(slots):
            xall[s_ * 128:(s_ + 1) * 128] = x_prompt[b, t * 128:(t + 1) * 128]
            pos[s_ * 128:(s_ + 1) * 128] = np.arange(t * 128, (t + 1) * 128)
        xall[2048:2052] = x_sample[c]
        pos[2048:2052] = PAST + np.arange(4)
        for i in range(8):
            t = own[i]
            if t > 0:
                xall[2052 + 2 * i:2054 + 2 * i] = x_prompt[b, t * 128 - 2:t * 128]
        c32, s32 = _rope_tables(pos, 16)
        c16, s16 = _rope_tables(pos, 8)
        cm = np.zeros((128, 512), np.float32)
        tri = np.where(np.arange(128)[None, :] <= np.arange(128)[:, None], 0.0, -30000.0).astype(np.float32)
        for par in range(2):
            cm[:, par * 256:par * 256 + 128] = tri
            other_before = oth[par] < own[par]
            cm[:, par * 256 + 128:par * 256 + 256] = 0.0 if other_before else -30000.0
        pm = (-2.0 * cm).astype(np.float32)
        in_maps.append(dict(
            xall=xall, w_in=w_in0, w_out=w_out0,
            norm_in=np.ascontiguousarray(np.asarray(norm_in, np.float32)[0].reshape(16, 128).T),
            g_q=np.ascontiguousarray(np.asarray(g_q, np.float32)[0]),
            g_k=np.ascontiguousarray(np.asarray(g_k, np.float32)[0]),
            g_ki=np.ascontiguousarray(np.asarray(g_kidx, np.float32)[0]),
            w_conv=np.ascontiguousarray(np.asarray(w_conv, np.float32)[0].reshape(3, 8, 128).transpose(2, 0, 1)),
            cos32=c32, sin32=s32, cos16=c16, sin16=s16, cmask=cm, pmask=pm, pow2=pow2,
            stc=np.ascontiguousarray(np.asarray(state_conv, np.float32)[0, c].reshape(2, 8, 128).transpose(2, 0, 1)),
            ptab=np.ascontiguousarray(np.asarray(page_table, np.int32)[c].reshape(128, 1)),
            cache_k=ck, cache_v=cv, cache_ki=cki,
            cm4=np.where(np.arange(4)[:, None] <= np.arange(4)[None, :], 0.0, -1e30).astype(np.float32),
        ))
    res = run_bass_kernel_spmd(nc, in_maps, core_ids=list(range(8)))
    R = res.results
    y_prompt = np.zeros((NB_, S, D), np.float32)
    y_sample = np.zeros((DEC_B, DEC_T, D), np.float32)
    k_p = np.zeros((1, NB_, S, NKV, HD), np.float32)
    v_p = np.zeros((1, NB_, S, NKV, HD), np.float32)
    ki_p = np.zeros((1, NB_, S, IDXD), np.float32)
    conv_p = np.zeros((1, NB_, 2, 1024), np.float32)
    k_s = np.zeros((1, DEC_B, DEC_T, NKV, HD), np.float32)
    v_s = np.zeros((1, DEC_B, DEC_T, NKV, HD), np.float32)
    ki_s = np.zeros((1, DEC_B, DEC_T, IDXD), np.float32)
    conv_s = np.zeros((1, DEC_B, 2, 1024), np.float32)
    for c in range(8):
        b, j = c // 2, c % 2
        r = R[c]
        for i, t in enumerate(T_OWN[j]):
            sl = slice(t * 128, (t + 1) * 128)
            y_prompt[b, sl] = r["y_own"][i * 128:(i + 1) * 128]
            k_p[0, b, sl] = r["k_own"][i * 128:(i + 1) * 128].reshape(128, NKV, HD)
            v_p[0, b, sl] = r["v_own"][i * 128:(i + 1) * 128].reshape(128, NKV, HD)
            ki_p[0, b, sl] = r["ki_own"][i * 128:(i + 1) * 128]
        if j == 0:
            conv_p[0, b] = r["conv_p"].transpose(1, 2, 0).reshape(2, 1024)
        y_sample[c] = r["y_s"]
        k_s[0, c] = r["k_s"].reshape(4, NKV, HD)
        v_s[0, c] = r["v_s"].reshape(4, NKV, HD)
        ki_s[0, c] = r["ki_s"]
        conv_s[0, c] = r["conv_s"].transpose(1, 2, 0).reshape(2, 1024)
    return (y_prompt, y_sample, k_p, v_p, ki_p, conv_p, k_s, v_s, ki_s, conv_s)
```

```python
import numpy as np
import concourse.bass as bass
import concourse.mybir as mybir
from concourse.bass_utils import run_bass_kernel_spmd

F32 = mybir.dt.float32
BF16 = mybir.dt.bfloat16
I32 = mybir.dt.int32
AF = mybir.ActivationFunctionType
ALU = mybir.AluOpType
AX = mybir.AxisListType

D = 2048
S = 2048
NB_ = 4
DEC_B = 8
DEC_T = 4
PAST = 16384
PAGE = 128
NPAGES = PAST // PAGE
D_IN = 7760
HD = 128
NH = 8
NKV = 2
IDXH = 16
IDXD = 64
TOPK = 256
EPS = 1e-6
THETA = 500000.0
KBIS = 14
NEG = -30000.0
C_Q, C_K, C_V, C_ZA, C_QI, C_KI, C_WI, C_H, C_B, C_C, C_ZC = 0, 1024, 1280, 1536, 2560, 3584, 3648, 3664, 4688, 5712, 6736
T_OWN = ([0, 3, 4, 7, 8, 11, 12, 15], [1, 2, 5, 6, 9, 10, 13, 14])
ENGS = ("sync", "act", "dve", "pool", "pe")
NOSELF = ("pe",)


class Prog:
    def __init__(self, nc):
        self.nc = nc
        self.ops = {e: [] for e in ENGS}
        self.count = {e: 0 for e in ENGS}
        self.dcount = {}
        self.last_write = {}
        self.readers = {}
        self.waited = {e: {} for e in ENGS}
        self.pending_barrier = {e: None for e in ENGS}
        import os
        self.limit = int(os.environ.get('DBG_OPS', '100000000'))
        self.total = 0

    def _deps(self, eng, reads, writes):
        deps = {}

        def need(sv):
            if sv is None:
                return
            k, v = sv
            if deps.get(k, 0) < v:
                deps[k] = v

        for r in reads:
            need(self.last_write.get(r))
        for w in writes:
            need(self.last_write.get(w))
            for rd in self.readers.get(w, ()):
                need(rd)
        pb = self.pending_barrier[eng]
        if pb is not None:
            for kv in pb:
                need(kv)
            self.pending_barrier[eng] = None
        out = []
        for k, v in deps.items():
            if k == eng and eng in NOSELF:
                continue
            if self.waited[eng].get(k, 0) >= v:
                continue
            self.waited[eng][k] = v
            out.append((k, v))
        return out

    def _commit(self, me, reads, writes):
        for r in reads:
            self.readers.setdefault(r, []).append(me)
        for w in writes:
            self.last_write[w] = me
            self.readers[w] = []

    def op(self, eng, fn, reads=(), writes=()):
        self.total += 1
        if self.total > self.limit:
            return
        reads = tuple(reads)
        writes = tuple(writes)
        waits = self._deps(eng, reads, writes)
        self.count[eng] += 1
        self.ops[eng].append(("op", fn, waits))
        self._commit((eng, self.count[eng]), reads, writes)

    def dma(self, queue, fn, dsem, reads=(), writes=()):
        self.total += 1
        if self.total > self.limit:
            return
        reads = tuple(reads)
        writes = tuple(writes)
        key = ("d", dsem)
        waits = self._deps(queue, reads, writes)
        self.dcount[key] = self.dcount.get(key, 0) + 1
        self.ops[queue].append(("dma", fn, waits, key))
        self._commit((key, 16 * self.dcount[key]), reads, writes)

    def alias(self, new_key, old_keys):
        lst = self.readers.setdefault(new_key, [])
        for ok in old_keys:
            lw = self.last_write.get(ok)
            if lw is not None:
                lst.append(lw)
            lst.extend(self.readers.get(ok, ()))

    def barrier(self):
        snap = [(e, self.count[e]) for e in ENGS if self.count[e] > 0]
        snap += [(k, 16 * n) for k, n in self.dcount.items()]
        for e in ENGS:
            self.pending_barrier[e] = list(snap)

    def wait_all_dma(self, eng="sync"):
        waits = [(k, 16 * n) for k, n in self.dcount.items()]
        self.ops[eng].append(("waitonly", None, waits))

    def emit(self):
        nc = self.nc
        from contextlib import ExitStack
        with ExitStack() as es:
            sems = {}
            for e in ENGS:
                sems[e] = es.enter_context(nc.semaphore("s_" + e))
            for i, k in enumerate(self.dcount.keys()):
                sems[k] = es.enter_context(nc.semaphore("d_%d" % i))
            block = es.enter_context(nc.Block())

            def run(engname, h):
                for rec in self.ops[engname]:
                    kind, fn, waits = rec[0], rec[1], rec[2]
                    for k, v in waits:
                        h.wait_ge(sems[k], v)
                    if kind == "op":
                        fn(h).then_inc(sems[engname], 1)
                    elif kind == "dma":
                        fn(h).then_inc(sems[rec[3]], 16)

            @block.sync
            def _(h):
                run("sync", h)

            @block.scalar
            def _(h):
                run("act", h)

            @block.vector
            def _(h):
                run("dve", h)

            @block.gpsimd
            def _(h):
                run("pool", h)

            @block.tensor
            def _(h):
                run("pe", h)


def build_program(stop=99):
    nc = bass.Bass("TRN2", target_bir_lowering=False)
    P = Prog(nc)

    def din(name, shape, dt=F32):
        return nc.dram_tensor(name, list(shape), dt, kind="ExternalInput").ap()

    def dout(name, shape, dt=F32):
        return nc.dram_tensor(name, list(shape), dt, kind="ExternalOutput").ap()

    xall = din("xall", [2176, D])
    w_in = din("w_in", [D, D_IN])
    w_out = din("w_out", [D, D])
    norm_in = din("norm_in", [128, 16])
    g_q = din("g_q", [HD])
    g_k = din("g_k", [HD])
    g_ki = din("g_ki", [IDXD])
    w_conv = din("w_conv", [128, 3, 8])
    cos32 = din("cos32", [2176, 16])
    sin32 = din("sin32", [2176, 16])
    cos16 = din("cos16", [2176, 8])
    sin16 = din("sin16", [2176, 8])
    cmask_d = din("cmask", [128, 512])
    pmask_d = din("pmask", [128, 512])
    pow2_d = din("pow2", [KBIS])
    stc = din("stc", [128, 2, 8])
    ptab = din("ptab", [128, 1], I32)
    cm4_d = din("cm4", [4, 4])
    cache_k = din("cache_k", [1280, PAGE * 256])
    cache_v = din("cache_v", [1280, PAGE * 256])
    cache_ki = din("cache_ki", [1280, PAGE * 64])

    y_own = dout("y_own", [1024, D])
    y_s = dout("y_s", [4, D])
    k_own = dout("k_own", [1024, 256])
    v_own = dout("v_own", [1024, 256])
    ki_own = dout("ki_own", [1024, 64])
    conv_p = dout("conv_p", [128, 2, 8])
    k_s = dout("k_s", [4, 256])
    v_s = dout("v_s", [4, 256])
    ki_s = dout("ki_s", [4, 64])
    conv_s = dout("conv_s", [128, 2, 8])
    import os
    DUMP = os.environ.get("DBG_DUMP") == "1"
    if DUMP:
        dbg_I = dout("dbg_I", [128, 516])
        dbg_sm5 = dout("dbg_sm5", [128, 64])
        dbg_NB = dout("dbg_NB", [128, 516])
        dbg_att = dout("dbg_att", [128, 128])
        dbg_O = dout("dbg_O", [16, 132])
        dbg_Sm = dout("dbg_Sm", [128, 256])
        dbg_Wbc = dout("dbg_Wbc", [128, 64])

    BASE = 16640
    LIMIT = 229376
    cur = [BASE]
    cnt = [0]

    def alloc_at(off, shape, dt):
        cnt[0] += 1
        return nc.alloc_sbuf_tensor_at("t%d" % cnt[0], list(shape), dt, offset=off).ap()

    def nbytes(shape, dt):
        n = int(np.prod(shape[1:])) * (2 if dt == BF16 else 4)
        return (n + 63) // 64 * 64

    def sb(shape, dt):
        off = cur[0]
        cur[0] += nbytes(shape, dt)
        assert cur[0] <= LIMIT, ("SBUF overflow", cur[0])
        return alloc_at(off, shape, dt)

    ident_f = sb([128, 128], F32)
    ident_b = sb([128, 128], BF16)
    ID4 = sb([128, 512], BF16)
    ONES = sb([128, 128], BF16)
    gq_bc = sb([128, 128], F32)
    gk_bc = sb([128, 128], F32)
    gki_bc = sb([128, 64], F32)
    COS32 = sb([128, 17, 16], F32)
    SIN32 = sb([128, 17, 16], F32)
    COS16 = sb([128, 17, 8], F32)
    SIN16 = sb([128, 17, 8], F32)
    wcT = sb([128, 3, 8], F32)
    cmask = sb([128, 512], F32)
    pmask = sb([128, 512], F32)
    pow2 = sb([128, KBIS], F32)
    WIDX = sb([128, 9, 16], F32)
    ginT = sb([128, 16], F32)
    SCT = sb([128, 2, 8], F32)
    CO = sb([128, 2, 8], F32)
    COs = sb([128, 2, 8], F32)
    sml = sb([128, 80], F32)
    sml1 = sb([128, 8], F32)
    rt = sb([128, 4, 32], F32)
    bis = sb([128, 16], F32)
    Wt = sb([128, KBIS], F32)
    o_QT = cur[0]
    QT = sb([128, 8, 1024], BF16)
    KT = sb([128, 2, 2048], BF16)
    Vb = sb([128, 16, 256], BF16)
    QIT = sb([128, 8, 1024], BF16)
    KIT = sb([128, 2, 2048], BF16)
    o_MIXA = cur[0]
    MIXA = sb([128, 8, 1024], BF16)
    xstb = [alloc_at(o_MIXA + 8192 * j, [128, 2048], F32) for j in range(2)]
    QT_s = sb([128, 8, 32], BF16)
    KT_s = sb([128, 2, 32], BF16)
    V_s = sb([128, 256], BF16)
    QIT_s = sb([128, 8, 32], BF16)
    KIT_s = sb([128, 32], BF16)
    SZA_s = sb([128, 8, 32], BF16)
    MIX_s = sb([128, 16, 32], BF16)
    Us = sb([128, 8], F32)
    cvs = sb([128, 8], F32)
    XOFF = cur[0]
    xnT_oth = sb([128, 16, 1024], BF16)
    SZA = alloc_at(XOFF, [128, 8, 1024], BF16)
    MIXC = alloc_at(XOFF + 16384, [128, 8, 1024], BF16)
    ROFF = cur[0]
    RSIZE = LIMIT - ROFF
    print('SBUF: XOFF', XOFF, 'ROFF', ROFF, 'RSIZE', RSIZE)
    assert RSIZE >= 79872, RSIZE

    class Region:
        def __init__(self):
            self.o = ROFF

        def sb(self, shape, dt):
            off = self.o
            self.o += nbytes(shape, dt)
            assert self.o <= LIMIT, ("region overflow", self.o - ROFF)
            return alloc_at(off, shape, dt)

    R1 = Region()
    xnT_own = R1.sb([128, 16, 1152], BF16)
    o_x = R1.o
    xst = R1.sb([128, 2048], F32)
    o_xs = R1.o
    xs = R1.sb([128, 2048], BF16)
    Wr = [R1.sb([128, 16, 256], BF16) for _ in range(3)]
    o_tq = R1.o
    NJ = 5
    tQ = [R1.sb([128, 256], F32) for _ in range(NJ)]
    tQ2 = [R1.sb([128, 256], F32) for _ in range(NJ)]
    kbj = [R1.sb([128, 256], BF16) for _ in range(NJ)]
    rtj = [R1.sb([128, 4, 32], F32) for _ in range(NJ)]
    tA = [alloc_at(o_MIXA + 2048 * j, [128, 512], F32) for j in range(2)]
    tB = [alloc_at(o_MIXA + 4096 + 2048 * j, [128, 512], F32) for j in range(2)]
    Wr.append(alloc_at(o_MIXA + 8192, [128, 16, 256], BF16))
    U = alloc_at(o_x, [128, 8, 130], F32)
    xsb = [xs, alloc_at(o_x, [128, 2048], BF16)]
    CV = alloc_at(o_xs, [128, 8, 128], F32)

    ps = [nc.alloc_psum_tensor("ps%d" % i, [128, 512], F32).ap() for i in range(8)]
    psb = [p.bitcast(BF16) for p in ps]

    def ACT(out, in_, func, r, w, **kw):
        P.op("act", lambda e: e.activation(out=out, in_=in_, func=func, **kw), r, w)

    def TS(eng, out, in0, s1, s2, op0, op1, r, w, accum=None):
        if op1 is None:
            P.op(eng, lambda e: e.tensor_scalar(out=out, in0=in0, scalar1=s1, scalar2=None, op0=op0), r, w)
        elif accum is None:
            P.op(eng, lambda e: e.tensor_scalar(out=out, in0=in0, scalar1=s1, scalar2=s2, op0=op0, op1=op1), r, w)
        else:
            P.op(eng, lambda e: e.tensor_scalar(out=out, in0=in0, scalar1=s1, scalar2=s2, op0=op0, op1=op1,
                                                accum_out=accum), r, w)

    def TT(eng, out, in0, in1, op, r, w):
        P.op(eng, lambda e: e.tensor_tensor(out=out, in0=in0, in1=in1, op=op), r, w)

    def STT(out, in0, scalar, in1, op0, op1, r, w):
        P.op("dve", lambda e: e.scalar_tensor_tensor(out=out, in0=in0, scalar=scalar, in1=in1, op0=op0, op1=op1), r, w)

    def CP(eng, out, in_, r, w):
        P.op(eng, lambda e: e.tensor_copy(out=out, in_=in_), r, w)

    def RED(out, in_, op, r, w):
        P.op("dve", lambda e: e.tensor_reduce(out=out, in_=in_, axis=AX.X, op=op), r, w)

    def RCP(out, in_, r, w):
        P.op("dve", lambda e: e.reciprocal(out=out, in_=in_), r, w)

    def MM(out, lhsT, rhs, start, stop, r, w):
        P.op("pe", lambda e: e.matmul(out, lhsT=lhsT, rhs=rhs, start=start, stop=stop), r, w)

    def TR(out, in_, ident, r, w):
        P.op("pe", lambda e: e.transpose(out=out, in_=in_, identity=ident), r, w)

    def DMA(queue, out, in_, dsem, r, w, **kw):
        P.dma(queue, lambda e: e.dma_start(out=out, in_=in_, **kw), dsem, r, w)

    DMA("sync", ginT, norm_in, "c3", [], ["gin"])

    def load_consts():
        DMA("sync", gq_bc, g_q.partition_broadcast(128), "c0", [], ["gq"])
        DMA("sync", gk_bc, g_k.partition_broadcast(128), "c1", [], ["gk"])
        DMA("sync", gki_bc, g_ki.partition_broadcast(128), "c2", [], ["gki"])
        DMA("sync", COS32[:, 0:16, :], cos32[0:2048, :].rearrange("(t p) c -> p t c", p=128), "c4", [], ["cs0"])
        DMA("sync", SIN32[:, 0:16, :], sin32[0:2048, :].rearrange("(t p) c -> p t c", p=128), "c5", [], ["cs1"])
        DMA("sync", COS16[:, 0:16, :], cos16[0:2048, :].rearrange("(t p) c -> p t c", p=128), "c6", [], ["cs2"])
        DMA("sync", SIN16[:, 0:16, :], sin16[0:2048, :].rearrange("(t p) c -> p t c", p=128), "c7", [], ["cs3"])
        DMA("sync", COS32[:, 16, :], cos32[2048:2176, :], "c8", [], ["cs0e"])
        DMA("sync", SIN32[:, 16, :], sin32[2048:2176, :], "c9", [], ["cs1e"])
        DMA("sync", COS16[:, 16, :], cos16[2048:2176, :], "c10", [], ["cs2e"])
        DMA("sync", SIN16[:, 16, :], sin16[2048:2176, :], "c11", [], ["cs3e"])
        DMA("sync", cmask, cmask_d, "c12", [], ["cmask"])
        DMA("sync", pmask, pmask_d, "c13", [], ["pmask"])
        DMA("sync", pow2, pow2_d.partition_broadcast(128), "c14", [], ["pow2"])
        DMA("sync", wcT, w_conv, "c15", [], ["wcT"])
        DMA("sync", SCT, stc, "c16", [], ["SCT"])

    CONST_KEYS = ["gq", "gk", "gki", "gin", "cs0", "cs1", "cs2", "cs3", "cs0e", "cs1e", "cs2e", "cs3e",
                  "cmask", "pmask", "pow2", "wcT", "SCT"]
    P.op("pool", lambda e: e.memset(ident_f, 0.0), [], ["identf"])
    P.op("pool", lambda e: e.affine_select(out=ident_f, in_=ident_f, pattern=[[-1, 128]], compare_op=ALU.not_equal,
                                           fill=1.0, base=0, channel_multiplier=1), ["identf"], ["identf"])
    CP("pool", ident_b, ident_f, ["identf"], ["identb"])
    for a in range(4):
        CP("pool", ID4[:, a * 128:(a + 1) * 128], ident_f, ["identf"], ["ID4"])
    P.op("pool", lambda e: e.memset(ONES, 1.0), [], ["ONES"])

    if stop < 1:
        load_consts()
        P.wait_all_dma('sync'); P.emit(); return nc
    def tile_src(tt):
        if tt == 16:
            return xall[2048:2176, :], 128
        return xall[tt * 128:(tt + 1) * 128, :], 128

    def xn_dst(tt):
        if tt == 16:
            return xnT_own, 1024, 128
        if tt % 2 == 0:
            return xnT_own, (tt // 2) * 128, 128
        return xnT_oth, (tt // 2) * 128, 128

    p1_state = [0]

    def p1_A(tt):
        src, np_ = tile_src(tt)
        pb_ = tt % 2
        xst_, xs_ = xstb[pb_], xsb[pb_]
        kx, kxs, ksm1 = "xst%d" % pb_, "xs%d" % pb_, "sml1_%d" % pb_
        sm1 = sml1[:, 4 * pb_:4 * pb_ + 4]
        DMA("sync", xst_[:np_, :], src, "xld%d" % pb_, [], [kx])
        ACT(xs_[:np_, :], xst_[:np_, :], AF.Square, [kx], [kxs, ksm1], accum_out=sm1[:np_, 0:1])
        TS("dve", sm1[:np_, 1:2], sm1[:np_, 0:1], 1.0 / D, EPS, ALU.mult, ALU.add, [ksm1], [ksm1])
        ACT(sm1[:np_, 2:3], sm1[:np_, 1:2], AF.Sqrt, [ksm1], [ksm1])
        RCP(sm1[:np_, 3:4], sm1[:np_, 2:3], [ksm1], [ksm1])
        TS("dve", xs_[:np_, :], xst_[:np_, :], sm1[:np_, 3:4], None, ALU.mult, None, [kx, ksm1], [kxs])

    def p1_B(tt):
        src, np_ = tile_src(tt)
        pb_ = tt % 2
        xs_ = xsb[pb_]
        kxs = "xs%d" % pb_
        dst, t0, _ = xn_dst(tt)
        for half in range(2):
            bk = 2 + (2 * tt + half) % 2
            for j in range(8):
                kt = half * 8 + j
                TR(psb[bk][:, j * 128:j * 128 + np_], xs_[:np_, kt * 128:(kt + 1) * 128], ident_b[:np_, :np_],
                   [kxs, "identb"], ["ps%d" % bk])
            src_ps = psb[bk].rearrange("p (j t) -> p j t", j=8)[:, :, 0:np_]
            ACT(dst[:, half * 8:half * 8 + 8, t0:t0 + np_], src_ps, AF.Copy, ["ps%d" % bk], [("xnTraw", tt)])
        TT("pool", dst[:, :, t0:t0 + np_], dst[:, :, t0:t0 + np_], ginT.unsqueeze(2).to_broadcast([128, 16, np_]), ALU.mult,
           [("xnTraw", tt), "gin"], [("xnT", tt)])

    def p1_tile(tt):
        if tt == 0:
            p1_A(0)
        if tt + 1 <= 16:
            p1_A(tt + 1)
        p1_B(tt)

    def p1_upto(tt):
        while p1_state[0] <= min(tt, 16):
            p1_tile(p1_state[0])
            p1_state[0] += 1

    if stop < 2:
        load_consts()
        p1_upto(16)
        P.wait_all_dma('sync'); P.emit(); return nc

    wctr = [0]

    def win_cols(c0, n):
        return w_in[:, c0:c0 + n].rearrange("(kt p) c -> p kt c", p=128)

    own_tiles = [2 * i for i in range(8)] + [16]
    all_tiles = list(range(17))
    bkc = [0]
    trc = [0]
    tac = [0]

    def xn_src(tt, kt):
        dst, t0, n = xn_dst(tt)
        return dst[:, kt, t0:t0 + n], n

    def own_idx(tt):
        return tt // 2

    CK = ["cs0", "cs1", "cs2", "cs3", "cs0e", "cs1e", "cs2e", "cs3e"]

    chunks = [("k", C_K, 256, all_tiles, 2, 128, 16, gk_bc, True, 0),
              ("v", C_V, 256, all_tiles, 0, 0, 0, None, False, 0),
              ("kiw", C_KI, 80, all_tiles, 1, 64, 8, gki_bc, True, 0)]
    for c in range(4):
        chunks.append(("q", C_Q + 256 * c, 256, own_tiles, 2, 128, 16, gq_bc, True, c))
    for c in range(4):
        chunks.append(("qi", C_QI + 256 * c, 256, own_tiles, 4, 64, 8, None, False, c))

    wslot = {}

    def issue_w(ci):
        if ci >= len(chunks) or ci in wslot:
            return
        sidx = ci if ci < 3 else (ci - 3) % 2
        wslot[ci] = sidx
        DMA("pool", Wr[sidx][:, :, 0:chunks[ci][2]], win_cols(chunks[ci][1], chunks[ci][2]), "w%d" % sidx, [], ["W%d" % sidx])

    def job(ci, tt, js):
        kind, c0, ncols, _, nh, hd, rh, g_bc, do_norm, cidx = chunks[ci]
        sw = wslot[ci]
        bk = bkc[0] % 2
        bkc[0] += 1
        pk = "ps%d" % bk
        for kt in range(16):
            lhs, n = xn_src(tt, kt)
            MM(ps[bk][:, 0:ncols], lhs, Wr[sw][:, kt, 0:ncols], kt == 0, kt == 15, [("xnT", tt), "W%d" % sw], [pk])
        own = (tt == 16) or (tt % 2 == 0)
        i = own_idx(tt)
        tq, tq2, kb_, rt_ = tQ[js], tQ2[js], kbj[js], rtj[js]
        kq, kq2, kkb, krt, ksm = "tQ%d" % js, "tQ2%d" % js, "kb%d" % js, "rt%d" % js, "sml%d" % js
        sm = sml[:, 16 * js:16 * js + 16]
        if kind == "v":
            if own:
                ACT(tq[:, 0:256], ps[bk][:, 0:256], AF.Copy, [pk], [kq])
            if tt == 16:
                ACT(V_s[:, :], ps[bk][:, 0:256], AF.Copy, [pk], ["Vs"])
            else:
                ACT(Vb[:, tt, :], ps[bk][:, 0:256], AF.Copy, [pk], [("V", tt)])
            if own:
                for _ in range(14):
                    yield
                if tt == 16:
                    DMA("sync", v_s, tq[0:4, 0:256], "ko%d" % js, [kq], [])
                else:
                    DMA("sync", v_own[i * 128:(i + 1) * 128, :], tq[:, 0:256], "ko%d" % js, [kq], [])
            return
        nd = nh * hd
        ACT(tq[:, 0:ncols], ps[bk][:, 0:ncols], AF.Copy, [pk], [kq])
        if do_norm:
            for h in range(nh):
                ACT(tq2[:, h * hd:(h + 1) * hd], ps[bk][:, h * hd:(h + 1) * hd], AF.Square, [pk], [kq2, ksm],
                    accum_out=sm[:, h:h + 1])
            yield
            TS("dve", sm[:, 4:4 + nh], sm[:, 0:nh], 1.0 / hd, EPS, ALU.mult, ALU.add, [ksm], [ksm])
            yield
            ACT(sm[:, 8:8 + nh], sm[:, 4:4 + nh], AF.Sqrt, [ksm], [ksm])
            yield
            RCP(sm[:, 12:12 + nh], sm[:, 8:8 + nh], [ksm], [ksm])
            yield
            for h in range(nh):
                STT(tq2[:, h * hd:(h + 1) * hd], tq[:, h * hd:(h + 1) * hd], sm[:, 12 + h:13 + h], g_bc[:, 0:hd],
                    ALU.mult, ALU.mult, [kq, ksm, "gq", "gk", "gki"], [kq2])
                yield
            src, ksrc = tq2, kq2
        else:
            yield
            src, ksrc = tq, kq
        if kind == "kiw":
            if tt == 16:
                TS("dve", WIDX[:, 8, :], tq[:, 64:80], 1.0 / 32.0, None, ALU.mult, None, [kq], [("WIDX", 8)])
            elif own:
                TS("dve", WIDX[:, i, :], tq[:, 64:80], 1.0 / 32.0, None, ALU.mult, None, [kq], [("WIDX", i)])
        s3 = src[:, 0:nd].rearrange("p (h d) -> p h d", h=nh)
        x1 = s3[:, :, 0:rh]
        x2 = s3[:, :, rh:2 * rh]
        if rh == 16:
            cT, sT = COS32[:, tt, :], SIN32[:, tt, :]
        else:
            cT, sT = COS16[:, tt, :], SIN16[:, tt, :]
        cb = cT.unsqueeze(1).to_broadcast([128, nh, rh])
        sbb = sT.unsqueeze(1).to_broadcast([128, nh, rh])
        r4 = [rt_[:, j, 0:nh * rh].rearrange("p (h r) -> p h r", h=nh) for j in range(4)]
        TT("dve", r4[0], x1, cb, ALU.mult, [ksrc] + CK, [krt])
        TT("dve", r4[1], x2, sbb, ALU.mult, [ksrc] + CK, [krt])
        yield
        TT("dve", r4[2], x2, cb, ALU.mult, [ksrc] + CK, [krt])
        TT("dve", r4[3], x1, sbb, ALU.mult, [ksrc] + CK, [krt])
        yield
        TT("dve", x1, r4[0], r4[1], ALU.subtract, [krt], [ksrc])
        TT("dve", x2, r4[2], r4[3], ALU.add, [krt], [ksrc])
        yield
        if kind == "kiw":
            P.op("pool", lambda e: e.memset(kb_[:, 64:192], 0.0), [], [kkb])
            CP("dve", kb_[:, 0:64], src[:, 0:64], [ksrc], [kkb])
            ACT(kb_[:, 192:256], src[:, 0:64], AF.Copy, [ksrc], [kkb])
            nblk = 2
        else:
            CP("dve", kb_[:, 0:256], src[:, 0:256], [ksrc], [kkb])
            nblk = 2
        yield
        tb = 2 + trc[0] % 2
        trc[0] += 1
        for j in range(nblk):
            TR(psb[tb][:, j * 128:(j + 1) * 128], kb_[:, j * 128:(j + 1) * 128], ident_b, [kkb, "identb"], ["ps%d" % tb])
        for j in range(nblk):
            if kind == "k":
                dst, wk = (KT_s[:, j, 0:32], ("KTs", j)) if tt == 16 else (KT[:, j, tt * 128:(tt + 1) * 128], ("KT", tt))
            elif kind == "kiw":
                if tt == 16:
                    if j == 1:
                        continue
                    dst, wk = KIT_s[:, 0:32], "KITs"
                else:
                    dst, wk = KIT[:, j, tt * 128:(tt + 1) * 128], ("KIT", tt, j)
            elif kind == "q":
                hh = 2 * cidx + j
                dst, wk = (QT_s[:, hh, 0:32], ("QTs", hh)) if tt == 16 else (QT[:, hh, i * 128:(i + 1) * 128], ("QT", i, hh))
            else:
                hh = 2 * cidx + j
                dst, wk = (QIT_s[:, hh, 0:32], ("QITs", hh)) if tt == 16 else (QIT[:, hh, i * 128:(i + 1) * 128], ("QIT", i, hh))
            ncp = dst.shape[-1]
            ACT(dst, psb[tb][:, j * 128:j * 128 + ncp], AF.Copy, ["ps%d" % tb], [wk])
        if own and kind in ("k", "kiw"):
            for _ in range(14):
                yield
            if kind == "k":
                if tt == 16:
                    DMA("sync", k_s, src[0:4, 0:256], "ko%d" % js, [ksrc], [])
                else:
                    DMA("sync", k_own[i * 128:(i + 1) * 128, :], src[:, 0:256], "ko%d" % js, [ksrc], [])
            else:
                if tt == 16:
                    DMA("sync", ki_s, src[0:4, 0:64], "ko%d" % js, [ksrc], [])
                else:
                    DMA("sync", ki_own[i * 128:(i + 1) * 128, :], src[:, 0:64], "ko%d" % js, [ksrc], [])

    def all_jobs():
        for tt in all_tiles:
            for ci in range(3):
                yield (ci, tt)
        for ci in range(3, len(chunks)):
            for tt in chunks[ci][3]:
                yield (ci, tt)

    bkB = [0]
    wB = [0]

    def gen_B():
        oth_keys = [("xnT", tt) for tt in range(1, 16, 2)]
        tok_chunks = [(0, 512), (512, 512), (1024, 32)]

        def fproj(s, sub, c0, n):
            bk = 4 + bkB[0] % 4
            bkB[0] += 1
            tts = [("xnT", 2 * i) for i in range(8)] if c0 < 1024 else [("xnT", 16)]
            for kt in range(16):
                MM(ps[bk][:, 0:n], Wr[s][:, kt, sub * 128:(sub + 1) * 128], xnT_own[:, kt, c0:c0 + n], kt == 0, kt == 15,
                   tts + ["W%d" % s], ["ps%d" % bk])
            return bk

        for f2 in range(4):
            s = 2 + wB[0] % 2
            wB[0] += 1
            DMA("pool", Wr[s][:, :, 0:256], win_cols(C_ZA + 256 * f2, 256), "w%d" % s, [], ["W%d" % s])
            for sub in range(2):
                f = 2 * f2 + sub
                for (c0, n) in tok_chunks:
                    bk = fproj(s, sub, c0, n)
                    if c0 < 1024:
                        ACT(SZA[:, f, c0:c0 + n], ps[bk][:, 0:n], AF.Silu, ["ps%d" % bk] + oth_keys, [("SZA", f, c0)])
                        yield
                    else:
                        ACT(SZA_s[:, f, 0:32], ps[bk][:, 0:32], AF.Silu, ["ps%d" % bk], [("SZAs", f)])
                    yield

        sA, sB = 2, 3

        def load_A(f_):
            DMA("pool", Wr[sA][:, :, 0:128], win_cols(C_H + 128 * f_, 128), "w%d" % sA, [], ["W%d" % sA])
            DMA("pool", Wr[sA][:, :, 128:256], win_cols(C_C + 128 * f_, 128), "w%d" % sA, [], ["W%d" % sA])

        def load_B(f_):
            DMA("pool", Wr[sB][:, :, 0:128], win_cols(C_B + 128 * f_, 128), "w%d" % sB, [], ["W%d" % sB])
            DMA("pool", Wr[sB][:, :, 128:256], win_cols(C_ZC + 128 * f_, 128), "w%d" % sB, [], ["W%d" % sB])

        load_A(0)
        load_B(0)
        for f in range(8):
            for ci, (c0, n) in enumerate(tok_chunks):
                bh = fproj(sA, 0, c0, n)
                yield
                bc = fproj(sA, 1, c0, n)
                yield
                a = tac[0] % 2
                tac[0] += 1
                ACT(tA[a][:, 0:n], ps[bh][:, 0:n], AF.Copy, ["ps%d" % bh], ["tA%d" % a])
                if c0 < 1024:
                    TT("dve", U[:, 4 * ci:4 * ci + 4, 2:130], tA[a][:, 0:512].rearrange("p (i t) -> p i t", i=4),
                       ps[bc][:, 0:512].rearrange("p (i t) -> p i t", i=4), ALU.mult, ["tA%d" % a, "ps%d" % bc], [("U", ci)])
                else:
                    TT("dve", tB[a][:, 0:32], tA[a][:, 0:32], ps[bc][:, 0:32], ALU.mult, ["tA%d" % a, "ps%d" % bc], ["tB%d" % a])
                    CP("dve", U[:, :, 0:2], tB[a][:, 4:20].rearrange("p (i two) -> p i two", two=2), ["tB%d" % a], [("U", "h")])
                    CP("dve", Us[:, 2:6], tB[a][:, 0:4], ["tB%d" % a], ["Us"])
                    CP("dve", Us[:, 0:2], SCT[:, :, f], ["SCT"], ["Us"])
            if f + 1 < 8:
                load_A(f + 1)
            ukeys = [("U", 0), ("U", 1), ("U", "h")]
            TS("dve", CV, U[:, :, 0:128], wcT[:, 0, f:f + 1], None, ALU.mult, None, ukeys + ["wcT"], ["CV"])
            STT(CV, U[:, :, 1:129], wcT[:, 1, f:f + 1], CV, ALU.mult, ALU.add, ukeys + ["wcT", "CV"], ["CV"])
            STT(CV, U[:, :, 2:130], wcT[:, 2, f:f + 1], CV, ALU.mult, ALU.add, ukeys + ["wcT", "CV"], ["CV"])
            TS("dve", cvs[:, 0:4], Us[:, 0:4], wcT[:, 0, f:f + 1], None, ALU.mult, None, ["Us", "wcT"], ["cvs"])
            STT(cvs[:, 0:4], Us[:, 1:5], wcT[:, 1, f:f + 1], cvs[:, 0:4], ALU.mult, ALU.add, ["Us", "wcT", "cvs"], ["cvs"])
            STT(cvs[:, 0:4], Us[:, 2:6], wcT[:, 2, f:f + 1], cvs[:, 0:4], ALU.mult, ALU.add, ["Us", "wcT", "cvs"], ["cvs"])
            CP("dve", CO[:, :, f], U[:, 7, 128:130], ukeys, [("CO", f)])
            CP("dve", COs[:, :, f], Us[:, 4:6], ["Us"], [("COs", f)])
            for ci, (c0, n) in enumerate(tok_chunks):
                bb = fproj(sB, 0, c0, n)
                yield
                bz = fproj(sB, 1, c0, n)
                yield
                a = tac[0] % 2
                tac[0] += 1
                ACT(tA[a][:, 0:n], ps[bz][:, 0:n], AF.Silu, ["ps%d" % bz], ["tA%d" % a])
                if c0 < 1024:
                    TT("dve", tB[a][:, 0:512], CV[:, 4 * ci:4 * ci + 4, :].rearrange("p i t -> p (i t)"), ps[bb][:, 0:512], ALU.mult,
                       ["CV", "ps%d" % bb], ["tB%d" % a])
                    TT("dve", MIXC[:, f, c0:c0 + 512], tB[a][:, 0:512], tA[a][:, 0:512], ALU.mult,
                       ["tA%d" % a, "tB%d" % a] + oth_keys, [("MIXC", f, ci)])
                else:
                    TT("dve", tB[a][:, 0:4], cvs[:, 0:4], ps[bb][:, 0:4], ALU.mult, ["cvs", "ps%d" % bb], ["tB%d" % a])
                    TT("dve", MIX_s[:, 8 + f, 0:4], tB[a][:, 0:4], tA[a][:, 0:4], ALU.mult, ["tA%d" % a, "tB%d" % a], [("MIXs", 8 + f)])
            if f + 1 < 8:
                load_B(f + 1)
        DMA("sync", conv_p, CO, "cvo", [("CO", f) for f in range(8)], [])
        DMA("sync", conv_s, COs, "cvo2", [("COs", f) for f in range(8)], [])


    issue_w(0)
    issue_w(1)
    issue_w(2)
    p1_upto(2)
    load_consts()
    free_slots = list(range(NJ))
    active = []
    p1_alias_done = [False]
    gB = [None]
    gB_done = [False]
    want_B = [False]

    def make_B():
        okeys = [("xnT", t_) for t_ in range(1, 16, 2)]
        for f_ in range(8):
            for c_ in (0, 512):
                P.alias(("SZA", f_, c_), okeys)
            for c_ in range(2):
                P.alias(("MIXC", f_, c_), okeys)
        gB[0] = gen_B()

    jit = all_jobs()
    pending = next(jit, None)
    while pending is not None or active:
        while pending is not None and free_slots:
            ci, tt = pending
            p1_upto(tt + 1)
            if p1_state[0] > 16 and not p1_alias_done[0]:
                p1_alias_done[0] = True
                for _k in ("tA0", "tA1", "tB0", "tB1"):
                    P.alias(_k, ["xst0"])
                P.alias("W3", ["xst1"])
                for _k in (("U", 0), ("U", 1), ("U", "h")):
                    P.alias(_k, ["xs1"])
                P.alias("CV", ["xs0"])
            if ci >= 3:
                want_B[0] = True
            if ci >= 3:
                issue_w(ci)
                issue_w(ci + 1)
            js = free_slots.pop(0)
            g_ = job(ci, tt, js)
            try:
                next(g_)
                active.append((g_, js))
            except StopIteration:
                free_slots.append(js)
            pending = next(jit, None)
        for ent in list(active):
            g, js = ent
            try:
                next(g)
            except StopIteration:
                active.remove(ent)
                free_slots.append(js)
        if want_B[0] and gB[0] is None:
            make_B()
        if gB[0] is not None and not gB_done[0]:
            try:
                next(gB[0])
            except StopIteration:
                gB_done[0] = True
    if gB[0] is None:
        make_B()
    if not gB_done[0]:
        for _ in gB[0]:
            pass

    if stop < 4:
        P.wait_all_dma('sync'); P.emit(); return nc
    P.barrier()

    R2 = Region()
    IS = [R2.sb([128, 2048], F32) for _ in range(3)]
    NBm = [R2.sb([128, 2048], BF16) for _ in range(3)]
    junk = R2.sb([128, 2048], BF16)
    Rh = [R2.sb([128, 512], BF16) for _ in range(6)]
    Dg = [R2.sb([128, 16, 128], BF16) for _ in range(2)]
    PT = [R2.sb([128, 512], BF16) for _ in range(5)]
    rz = R2.sb([128, 512], F32)
    otmp = R2.sb([128, 512], F32)
    t256 = R2.sb([128, 256], F32)
    cmaskb = R2.sb([128, 512], BF16)
    Zacc = [R2.sb([128, 512], F32) for _ in range(2)]
    ONESF = R2.sb([128, 128], F32)
    otm = [R2.sb([128, 512], F32) for _ in range(2)]
    rzl = [R2.sb([128, 512], F32) for _ in range(2)]
    rzz = [R2.sb([128, 512], F32) for _ in range(2)]
    rhc = [0]
    ptc = [0]
    dbk = [0]
    stc_ = [0]
    SCALE = float(HD) ** -0.5
    CP("pool", cmaskb, cmask, ["cmask"], ["cmaskb"])
    P.op("pool", lambda e: e.memset(ONESF, 1.0), [], ["ONESF"])

    def indexer(i):
        ib = i % 2
        isb = i % 3
        nk = 256 * (i + 1)
        par = i % 2
        for h in range(16):
            TS("pool", Dg[ib][:, h, :], ident_b, WIDX[:, i, h:h + 1], 0.0, ALU.mult, ALU.add, ["identb", ("WIDX", i)], [("Dg", ib)])
        nch = (nk + 511) // 512
        for c in range(nch):
            k0 = 512 * c
            ncol = min(512, nk - k0)
            m0 = nk - 256
            has_mask = (k0 + ncol > m0)
            prev = None
            for h in range(16):
                hp = h % 2
                bk = (0, 1, 7)[dbk[0] % 3]
                dbk[0] += 1
                MM(ps[bk][:, 0:ncol], QIT[:, h // 2, i * 128:(i + 1) * 128],
                   KIT[:, hp, k0:k0 + ncol], True, True,
                   [("QIT", i, h // 2)] + [("KIT", tt, hp) for tt in range(k0 // 128, (k0 + ncol) // 128)], ["ps%d" % bk])
                r = rhc[0] % 6
                rhc[0] += 1
                ACT(Rh[r][:, 0:ncol], ps[bk][:, 0:ncol], AF.Relu, ["ps%d" % bk], ["Rh%d" % r])
                if prev is not None:
                    ph, pr = prev
                    MM(ps[2][:, 0:ncol], Dg[ib][:, ph, :], Rh[pr][:, 0:ncol], ph == 0, False, [("Dg", ib), "Rh%d" % pr], ["ps2"])
                prev = (h, r)
                yield
            ph, pr = prev
            MM(ps[2][:, 0:ncol], Dg[ib][:, ph, :], Rh[pr][:, 0:ncol], False, not has_mask, [("Dg", ib), "Rh%d" % pr], ["ps2"])
            if has_mask:
                lo = m0 - k0
                MM(ps[2][:, lo:lo + 256], ident_b, cmaskb[:, par * 256:(par + 1) * 256], False, True, ["identb", "cmaskb"], ["ps2"])
            ACT(IS[isb][:, k0:k0 + ncol], ps[2][:, 0:ncol], AF.Copy, ["ps2"], [("IS", isb)])

    def bisect(i):
        ib = i % 3
        nk = 256 * (i + 1)
        par = i % 2
        isk = ("IS", ib)
        X = IS[ib][:, 0:nk]
        RED(bis[:, 0:1], X, ALU.max, [isk], ["bis"])
        TT("dve", t256, IS[ib][:, nk - 256:nk], pmask[:, par * 256:(par + 1) * 256], ALU.add, [isk, "pmask"], ["t256"])
        RED(bis[:, 1:2], t256, ALU.min, ["t256"], ["bis"])
        if nk > 256:
            RED(bis[:, 2:3], IS[ib][:, 0:nk - 256], ALU.min, [isk], ["bis"])
            TT("dve", bis[:, 1:2], bis[:, 1:2], bis[:, 2:3], ALU.min, ["bis"], ["bis"])
        TT("dve", bis[:, 3:4], bis[:, 0:1], bis[:, 1:2], ALU.subtract, ["bis"], ["bis"])
        STT(bis[:, 4:5], bis[:, 3:4], -(2.0 ** -10), bis[:, 1:2], ALU.mult, ALU.add, ["bis"], ["bis"])
        TS("dve", bis[:, 4:5], bis[:, 4:5], -1e-6, None, ALU.add, None, ["bis"], ["bis"])
        TT("dve", bis[:, 5:6], bis[:, 0:1], bis[:, 4:5], ALU.subtract, ["bis"], ["bis"])
        TS("dve", Wt, pow2, bis[:, 5:6], None, ALU.mult, None, ["pow2", "bis"], ["Wt"])
        TT("dve", bis[:, 6:7], bis[:, 4:5], Wt[:, 0:1], ALU.add, ["bis", "Wt"], ["bis"])
        for k in range(KBIS):
            TS("dve", junk[:, 0:nk], X, bis[:, 6:7], None, ALU.is_ge, ALU.add, [isk, "bis"], ["junk", "bis"], accum=bis[:, 7:8])
            TS("dve", bis[:, 8:9], bis[:, 7:8], float(TOPK), 0.5, ALU.is_ge, ALU.subtract, ["bis"], ["bis"])
            if k < KBIS - 1:
                STT(bis[:, 6:7], bis[:, 8:9], Wt[:, k:k + 1], bis[:, 6:7], ALU.mult, ALU.add, ["bis", "Wt"], ["bis"])
            else:
                TS("dve", bis[:, 8:9], bis[:, 8:9], -0.5, None, ALU.add, None, ["bis"], ["bis"])
                STT(bis[:, 9:10], bis[:, 8:9], Wt[:, k:k + 1], bis[:, 6:7], ALU.mult, ALU.add, ["bis", "Wt"], ["bis"])
        TS("dve", NBm[ib][:, 0:nk], X, bis[:, 9:10], NEG, ALU.is_lt, ALU.mult, [isk, "bis"], [("NB", ib)])

    def attend_main(i):
        ib = i % 3
        nsl = 2 * i + 2
        seq = [(g, s_) for g in range(2) for s_ in range(nsl)]
        pend = None

        def s_mm(g, s_):
            sb_ = (3, 6)[stc_[0] % 2]
            stc_[0] += 1
            MM(ps[sb_].rearrange("p (h t) -> p h t", h=4), KT[:, g, s_ * 128:(s_ + 1) * 128],
               QT[:, 4 * g:4 * g + 4, i * 128:(i + 1) * 128], True, False,
               [("KT", s_)] + [("QT", i, 4 * g + hh) for hh in range(4)], ["ps%d" % sb_])
            MM(ps[sb_], NBm[ib][:, s_ * 128:(s_ + 1) * 128], ID4, False, True, [("NB", ib), "ID4"], ["ps%d" % sb_])
            r = ptc[0] % 5
            ptc[0] += 1
            ACT(PT[r], ps[sb_], AF.Exp, ["ps%d" % sb_], ["PT%d" % r], scale=SCALE)
            return r

        def pv_mm(g, s_, r):
            MM(ps[4], Vb[:, s_, g * 128:(g + 1) * 128], PT[r], s_ == 0, s_ == nsl - 1, [("V", s_), "PT%d" % r], ["ps4"])
            if s_ % 2 == 0:
                MM(ps[5], ONES, PT[r], s_ == 0, False, ["ONES", "PT%d" % r], ["ps5"])
            elif s_ == 1:
                CP("pool", Zacc[g], PT[r], ["PT%d" % r], ["Zacc%d" % g])
            else:
                TT("pool", Zacc[g], Zacc[g], PT[r], ALU.add, ["PT%d" % r, "Zacc%d" % g], ["Zacc%d" % g])
            if s_ == nsl - 1:
                MM(ps[5], ONESF, Zacc[g], False, True, ["ONESF", "Zacc%d" % g], ["ps5"])
                ACT(otm[g], ps[4], AF.Copy, ["ps4"], ["otm%d" % g])
                ACT(rzl[g], ps[5], AF.Ln, ["ps5"], ["rzl%d" % g])
                ACT(rzz[g], rzl[g], AF.Exp, ["rzl%d" % g], ["rzz%d" % g], scale=-1.0)
                TT("pool", otm[g], otm[g], rzz[g], ALU.mult, ["otm%d" % g, "rzz%d" % g], ["otm%d" % g])
                TT("pool", MIXA[:, 4 * g:4 * g + 4, i * 128:(i + 1) * 128], otm[g].rearrange("p (h t) -> p h t", h=4),
                   SZA[:, 4 * g:4 * g + 4, i * 128:(i + 1) * 128], ALU.mult,
                   ["otm%d" % g] + [("SZA", 4 * g + hh, (i // 4) * 512) for hh in range(4)], [("MIXA", i, g)])

        for (g, s_) in seq:
            r = s_mm(g, s_)
            if pend is not None:
                pv_mm(*pend)
            pend = (g, s_, r)
            yield
        pv_mm(*pend)

    def attend_epi(i):
        pass

    def drain(g_):
        for _ in g_:
            pass

    drain(indexer(0))
    bisect(0)
    drain(indexer(1))
    drain(indexer(2))
    for i in range(8):
        if i + 1 < 8:
            bisect(i + 1)
        ga = attend_main(i)
        if i + 3 < 8:
            gi = indexer(i + 3)
            n_att = 2 * (2 * i + 2)
            n_idx = 16 * ((256 * (i + 4) + 511) // 512)
            gi_alive = True
            acc = 0
            for _ in ga:
                acc += n_idx
                while acc >= n_att:
                    acc -= n_att
                    if gi_alive:
                        try:
                            next(gi)
                        except StopIteration:
                            gi_alive = False
            if gi_alive:
                drain(gi)
        else:
            drain(ga)

    if stop < 5:
        P.wait_all_dma('sync'); P.emit(); return nc
    P.barrier()

    R4 = Region()
    I_all = R4.sb([128, 129, 4], F32)
    NBs = R4.sb([128, 129, 4], F32)
    junk_s = R4.sb([128, 132], F32)
    idx16 = R4.sb([128, 16], I32)
    ptb = R4.sb([128, 1], I32)
    Q64 = R4.sb([128, 8, 2, 4], BF16)
    Q64z = R4.sb([128, 2, 64], BF16)
    cnt_b = R4.sb([128, 4], BF16)
    idx4 = R4.sb([128, 4], I32)
    sm5 = R4.sb([128, 64], F32)
    Dw = R4.sb([128, 64], F32)
    Wbc = R4.sb([128, 64], F32)
    dg4 = R4.sb([128, 8], F32)
    ones_f = R4.sb([128, 128], F32)
    cm4 = R4.sb([128, 4], F32)
    Vn = R4.sb([128, 2, 130], BF16)
    Pn = R4.sb([128, 2, 16], BF16)
    Sn = R4.sb([128, 2, 16], F32)
    att = R4.sb([128, 128], F32)
    o_sub = R4.o
    KI = R4.sb([128, 128, 64], F32)
    KIb = R4.sb([128, 128, 64], BF16)
    kiTg = [R4.sb([128, 8, 128], BF16) for _ in range(2)]
    tmpi = R4.sb([128, 8, 64], F32)
    R5 = Region()
    R5.o = o_sub
    Kc = [R5.sb([128, 8, 256], F32) for _ in range(3)]
    Vc = [R5.sb([128, 8, 256], F32) for _ in range(3)]
    Kb = [R5.sb([128, 8, 256], BF16) for _ in range(2)]
    KTc = [R5.sb([128, 8, 2, 128], BF16) for _ in range(2)]
    Vb2 = [R5.sb([128, 8, 2, 130], BF16) for _ in range(3)]
    Sm = R5.sb([128, 8, 2, 16], F32)
    Pb = [R5.sb([128, 8, 2, 16], BF16) for _ in range(2)]

    def IDMA(out, in_, idx_ap, dsem, r, w):
        P.dma("pool", lambda e: e.indirect_dma_start(out=out, out_offset=None, in_=in_,
                                                     in_offset=bass.IndirectOffsetOnAxis(ap=idx_ap, axis=0)), dsem, r, w)

    sstage = [0]

    def gen_sample():
        DMA("sync", ptb, ptab, "s0", [], ["ptb"])
        DMA("sync", cm4[0:4, :], cm4_d, "s1", [], ["cm4"])
        cki4 = cache_ki.rearrange("p (c x) -> (p c) x", c=4)
        for q4 in range(4):
            TS("dve", idx4[:, q4:q4 + 1], ptb[:, 0:1], 4.0, float(q4), ALU.mult, ALU.add, ["ptb"], [("idx4", q4)])
            IDMA(KI[:, q4 * 32:(q4 + 1) * 32, :].rearrange("p j d -> p (j d)"), cki4, idx4[:, q4:q4 + 1], "s2_%d" % q4,
                 [("idx4", q4)], [("KI", q4)])
        for c in range(16):
            TS("dve", idx16[:, c:c + 1], ptb[:, 0:1], 16.0, float(c), ALU.mult, ALU.add, ["ptb"], ["idx16"])
        P.op("pool", lambda e: e.memset(ones_f, 1.0), [], ["ones_f"])
        P.op("pool", lambda e: e.memset(I_all[:, 128, :], -1e30), [], ["I128"])
        P.op("pool", lambda e: e.memset(Vn, 1.0), [], ["Vn"])
        CP("dve", Q64[0:64, :, 0, :], QIT_s[0:64, :, 0:4], [("QITs", j) for j in range(8)], ["Q64a"])
        DMA("sync", Q64[0:64, :, 1, :], QIT_s[64:128, :, 0:4], "s3", [("QITs", j) for j in range(8)], ["Q64b"])
        P.op("pool", lambda e: e.memset(Q64z, 0.0), [], ["Q64z"])
        CP("dve", Q64z[0:64, 0, :], Q64[0:64].rearrange("p a b t -> p (a b t)"), ["Q64a", "Q64b", "Q64z"], ["Q64z"])
        DMA("sync", Q64z[64:128, 1, :], Q64[0:64].rearrange("p a b t -> p (a b t)"), "s3b", ["Q64a", "Q64b", "Q64z"], ["Q64zb"])
        TT("dve", Dw[0:4, :].rearrange("p (h t) -> p h t", h=16), WIDX[0:4, 8, :].unsqueeze(2).to_broadcast([4, 16, 4]),
           ident_f[0:4, 0:4].unsqueeze(1).to_broadcast([4, 16, 4]), ALU.mult, [("WIDX", 8), "identf"], ["Dw"])
        MM(ps[4][:, 0:64], ones_f[0:4, :], Dw[0:4, :], True, True, ["ones_f", "Dw"], ["ps4"])
        CP("dve", Wbc, ps[4][:, 0:64], ["ps4"], ["Wbc"])
        for q4 in range(4):
            if q4 % 2 == 1:
                CP("pool", KIb[:, q4 * 32:(q4 + 1) * 32, :], KI[:, q4 * 32:(q4 + 1) * 32, :], [("KI", q4)], [("KIb", q4)])
            else:
                ACT(KIb[:, q4 * 32:(q4 + 1) * 32, :], KI[:, q4 * 32:(q4 + 1) * 32, :], AF.Copy, [("KI", q4)], [("KIb", q4)])
        wb3 = Wbc.unsqueeze(1).to_broadcast([128, 8, 64])
        for gi in range(16):
            j0 = gi * 8
            tb = gi % 2
            for jp in range(4):
                TR(psb[tb][:, jp * 128:(jp + 1) * 128], KIb[:, j0 + 2 * jp:j0 + 2 * jp + 2, :].rearrange("p a d -> p (a d)"), ident_b,
                   [("KIb", j0 // 32), "identb"], ["ps%d" % tb])
            ACT(kiTg[tb][:, 0:4, :], psb[tb][:, 0:512].rearrange("p (j t) -> p j t", j=4), AF.Copy, ["ps%d" % tb], ["kiTg%d" % tb])
            db = 2 + gi % 2
            for jj in range(8):
                MM(ps[db][:, jj * 64:(jj + 1) * 64], kiTg[tb][:, jj // 2, :], Q64z[:, jj % 2, :], True, True,
                   ["kiTg%d" % tb, "Q64z", "Q64zb"], ["ps%d" % db])
            STT(tmpi, ps[db].rearrange("p (j c) -> p j c", j=8), 0.0, wb3, ALU.max, ALU.mult, ["ps%d" % db, "Wbc"], ["tmpi"])
            RED(I_all[:, j0:j0 + 8, :], tmpi.rearrange("p j (h t) -> p j t h", h=16), ALU.add, ["tmpi"], [("Ia", gi)])
            yield
        MM(ps[4][0:4, 0:64], KIT_s[:, 0:4], Q64z[:, 0, :], True, True, ["KITs", "Q64z", "Q64zb"], ["ps4"])
        STT(tmpi[0:4, 0, :], ps[4][0:4, 0:64], 0.0, Wbc[0:4, :], ALU.max, ALU.mult, ["ps4", "Wbc"], ["tmpi"])
        RED(sm5[0:4, 0:4], tmpi[0:4, 0, :].rearrange("p (h t) -> p t h", h=16), ALU.add, ["tmpi"], ["sm5"])
        TT("dve", I_all[0:4, 128, :], sm5[0:4, 0:4], cm4[0:4, :], ALU.add, ["sm5", "cm4", "I128"], ["I128"])
        IA_KEYS = [("Ia", gi) for gi in range(16)] + ["I128"]

        def bcast4(dst, src_col4, negate=False):
            P.op("pe", lambda e: e.transpose(out=ps[5][0:4, 0:128], in_=src_col4, identity=ident_f), ["sm5", "identf"], ["ps5"])
            RED(dg4[0:4, 0:1], ps[5][0:4, 0:128], ALU.max, ["ps5"], ["dg4"])
            TS("dve", dg4[0:4, 4:8], ident_f[0:4, 0:4], dg4[0:4, 0:1], None, ALU.mult, None, ["dg4", "identf"], ["dg4b"])
            MM(ps[5][:, 256:260], ones_f[0:4, :], dg4[0:4, 4:8], True, True, ["ones_f", "dg4b"], ["ps5"])
            TS("dve", dst, ps[5][:, 256:260], -1.0 if negate else 1.0, None, ALU.mult, None, ["ps5"], ["sm5"])

        Iv = I_all.rearrange("p s t -> p t s")
        RED(sm5[:, 8:12], Iv, ALU.max, IA_KEYS, ["sm5"])
        bcast4(sm5[:, 12:16], sm5[:, 8:12])
        P.op("dve", lambda e: e.tensor_reduce(out=sm5[:, 8:12], in_=Iv[:, :, 0:128], axis=AX.X, op=ALU.min, negate=True), IA_KEYS + ["sm5"], ["sm5"])
        bcast4(sm5[:, 16:20], sm5[:, 8:12], negate=True)
        TT("dve", sm5[:, 20:24], sm5[:, 12:16], sm5[:, 16:20], ALU.subtract, ["sm5"], ["sm5"])
        STT(sm5[:, 24:28], sm5[:, 20:24], -(2.0 ** -10), sm5[:, 16:20], ALU.mult, ALU.add, ["sm5"], ["sm5"])
        TS("dve", sm5[:, 24:28], sm5[:, 24:28], -1e-6, None, ALU.add, None, ["sm5"], ["sm5"])
        TT("dve", sm5[:, 28:32], sm5[:, 12:16], sm5[:, 24:28], ALU.subtract, ["sm5"], ["sm5"])
        STT(sm5[:, 32:36], sm5[:, 28:32], 0.5, sm5[:, 24:28], ALU.mult, ALU.add, ["sm5"], ["sm5"])
        def bisect_s():
            KS = 14
            for k in range(KS):
                for t in range(4):
                    TS("dve", junk_s[:, 0:129], I_all[:, :, t], sm5[:, 32 + t:33 + t], None, ALU.is_ge, ALU.add, IA_KEYS + ["sm5"],
                       ["junk_s", "cntp"], accum=cnt_b[:, t:t + 1])
                MM(ps[4][:, 0:4], ONES, cnt_b, True, True, ["ONES", "cntp"], ["ps4"])
                TS("dve", sm5[:, 40:44], ps[4][:, 0:4], float(TOPK), 0.5, ALU.is_ge, ALU.subtract, ["ps4"], ["sm5"])
                TT("dve", sm5[:, 44:48], sm5[:, 40:44], sm5[:, 28:32], ALU.mult, ["sm5"], ["sm5"])
                yield
                if k < KS - 1:
                    STT(sm5[:, 32:36], sm5[:, 44:48], 2.0 ** -(k + 1), sm5[:, 32:36], ALU.mult, ALU.add, ["sm5"], ["sm5"])
                else:
                    STT(sm5[:, 48:52], sm5[:, 28:32], -0.5, sm5[:, 44:48], ALU.mult, ALU.add, ["sm5"], ["sm5"])
                    STT(sm5[:, 52:56], sm5[:, 48:52], 2.0 ** -(k + 1), sm5[:, 32:36], ALU.mult, ALU.add, ["sm5"], ["sm5"])
            for t in range(4):
                TS("dve", NBs[:, :, t], I_all[:, :, t], sm5[:, 52 + t:53 + t], NEG, ALU.is_lt, ALU.mult, IA_KEYS + ["sm5"], ["NBs"])


        if DUMP:
            DMA('sync', dbg_I, I_all.rearrange('p s t -> p (s t)'), 'dd0', IA_KEYS, [])
            DMA('sync', dbg_sm5, sm5, 'dd1', ['sm5'], [])
            DMA('sync', dbg_NB, NBs.rearrange('p s t -> p (s t)'), 'dd2', ['NBs'], [])
            DMA('sync', dbg_Wbc, Wbc, 'dd3', ['Wbc'], [])
        NEWK = ["Kc0", "Kc1", "Kc2", "Vc0", "Vc1", "Vc2", "Kb0", "Kb1", ("KTc", 0, 0), ("KTc", 0, 1), ("KTc", 1, 0), ("KTc", 1, 1),
                "Vb2_0", "Vb2_1", "Vb2_2", "Sm", "Pb0", "Pb1"]
        OLDK = [("KI", 0), ("KI", 1), ("KI", 2), ("KI", 3), ("KIb", 0), ("KIb", 1), ("KIb", 2), ("KIb", 3), "kiTg0", "kiTg1", "tmpi"]
        for _nk in NEWK:
            P.alias(_nk, OLDK)
        for c3 in range(3):
            P.op("pool", (lambda c3: (lambda e: e.memset(Vb2[c3], 1.0)))(c3), [], ["Vb2_%d" % c3])
        ck2 = cache_k.rearrange("p (c x) -> (p c) x", c=16)
        cv2 = cache_v.rearrange("p (c x) -> (p c) x", c=16)
        QsR = [QT_s[:, 4 * g:4 * g + 4, 0:4] for g in range(2)]

        def front(c):
            b2, b3 = c % 2, c % 3
            IDMA(Kc[b3].rearrange("p j d -> p (j d)"), ck2, idx16[:, c:c + 1], "s4_%d" % b3, ["idx16"], ["Kc%d" % b3])
            IDMA(Vc[b3].rearrange("p j d -> p (j d)"), cv2, idx16[:, c:c + 1], "s5_%d" % b3, ["idx16"], ["Vc%d" % b3])
            ACT(Kb[b2], Kc[b3], AF.Copy, ["Kc%d" % b3], ["Kb%d" % b2])
            CP("dve", Vb2[b3][:, :, :, 0:128], Vc[b3].rearrange("p j (g d) -> p j g d", g=2), ["Vc%d" % b3, "Vb2_%d" % b3], ["Vb2_%d" % b3])
            yield
            for half in range(2):
                tb = half
                for jj in range(4):
                    for g in range(2):
                        TR(psb[tb][:, (jj * 2 + g) * 128:(jj * 2 + g + 1) * 128], Kb[b2][:, half * 4 + jj, g * 128:(g + 1) * 128], ident_b,
                           ["Kb%d" % b2, "identb"], ["ps%d" % tb])
                ACT(KTc[b2][:, half * 4:half * 4 + 4, :, :], psb[tb].rearrange("p (j g t) -> p j g t", j=4, g=2), AF.Copy,
                    ["ps%d" % tb], [("KTc", b2, half)])
                yield
            sbk = 2 + b2
            for jj in range(8):
                for g in range(2):
                    MM(ps[sbk][:, (jj * 2 + g) * 16:(jj * 2 + g + 1) * 16].rearrange("p (h t) -> p h t", h=4), KTc[b2][:, jj, g, :], QsR[g], True, True,
                       [("KTc", b2, jj // 4)] + [("QTs", 4 * g + hh) for hh in range(4)], ["ps%d" % sbk])
            yield

        def back(c):
            b2, b3 = c % 2, c % 3
            sbk = 2 + b2
            TT("dve", Sm.rearrange("p j g (h t) -> p j (g h) t", h=4), ps[sbk][:, 0:256].rearrange("p (j gh t) -> p j gh t", j=8, t=4),
               NBs[:, c * 8:(c + 1) * 8, :].unsqueeze(2).to_broadcast([128, 8, 8, 4]), ALU.add, ["ps%d" % sbk, "NBs"], ["Sm"])
            ACT(Pb[b2], Sm, AF.Exp, ["Sm"], ["Pb%d" % b2], scale=SCALE)
            for jj in range(8):
                for g in range(2):
                    MM(ps[4 + g][0:16, 0:129], Pb[b2][:, jj, g, :], Vb2[b3][:, jj, g, 0:129], c == 0 and jj == 0, False,
                       ["Pb%d" % b2, "Vb2_%d" % b3], ["ps%d" % (4 + g)])
            yield

        yield from front(0)
        yield from front(1)
        yield from bisect_s()
        sstage[0] = 1
        for c in range(16):
            yield from back(c)
            if c + 2 < 16:
                yield from front(c + 2)
        for g in range(2):
            MM(ps[2][0:4, g * 16:(g + 1) * 16].rearrange("p (h t) -> p h t", h=4), KT_s[:, g, 0:4], QsR[g], True, True,
               [("KTs", g)] + [("QTs", 4 * g + hh) for hh in range(4)], ["ps2"])
            CP("dve", Vn[0:4, g, 0:128], V_s[0:4, g * 128:(g + 1) * 128], ["Vs", "Vn"], ["Vn"])
        TT("dve", Sn[0:4].rearrange("p g (h t) -> p (g h) t", h=4), ps[2][0:4, 0:32].rearrange("p (gh t) -> p gh t", t=4),
           NBs[0:4, 128, :].unsqueeze(1).to_broadcast([4, 8, 4]), ALU.add, ["ps2", "NBs"], ["Sn"])
        ACT(Pn[0:4], Sn[0:4], AF.Exp, ["Sn"], ["Pn"], scale=SCALE)
        for g in range(2):
            MM(ps[4 + g][0:16, 0:129], Pn[0:4, g, :], Vn[0:4, g, 0:129], False, True, ["Pn", "Vn"], ["ps%d" % (4 + g)])
            RCP(sm5[0:16, 60:61], ps[4 + g][0:16, 128:129], ["ps%d" % (4 + g)], ["sm5r"])
            TS("dve", att[0:16, :], ps[4 + g][0:16, 0:128], sm5[0:16, 60:61], None, ALU.mult, None, ["ps%d" % (4 + g), "sm5r"], ["att"])
            P.op("pe", lambda e: e.transpose(out=ps[0][:, 0:16], in_=att[0:16, :], identity=ident_f[0:16, 0:16]), ["att", "identf"], ["ps0"])
            TT("dve", MIX_s[:, 4 * g:4 * g + 4, 0:4], ps[0][:, 0:16].rearrange("p (h t) -> p h t", h=4), SZA_s[:, 4 * g:4 * g + 4, 0:4], ALU.mult,
               ["ps0"] + [("SZAs", 4 * g + hh) for hh in range(4)], [("MIXs", 4 * g + hh) for hh in range(4)])

        if DUMP:
            DMA('sync', dbg_att, att, 'dd4', ['att'], [])
            CP('dve', junk_s[0:16, 0:132], ps[5][0:16, 0:132], ['ps5'], ['junk_s'])
            DMA('sync', dbg_O, junk_s[0:16, 0:132], 'dd5', ['junk_s'], [])
            DMA('sync', dbg_Sm, Sm.rearrange('p j g c -> p (j g c)'), 'dd6', ['Sm'], [])

    if stop < 6:
        for _ in gen_sample():
            pass
        P.wait_all_dma('sync'); P.emit(); return nc

    R3 = Region()
    R3.o = o_QT
    WO = [R3.sb([128, 16, 512], BF16) for _ in range(3)]
    xr = [R3.sb([128, 512], F32) for _ in range(2)]
    yst = [R3.sb([128, 512], F32) for _ in range(2)]
    wo_slot = {0: 0, 1: 1, 2: 2, 3: 0}
    assert R3.o <= o_MIXA, (R3.o, o_MIXA)
    xc = [0]
    obk = [0]

    def out_step(c, i):
        s_ = wo_slot[c]
        bk = 6 + obk[0] % 2
        obk[0] += 1
        a_ = xc[0] % 2
        xc[0] += 1
        if i < 8:
            np_ = 128
            DMA("sync", xr[a_], xall[(2 * i) * 128:(2 * i + 1) * 128, c * 512:(c + 1) * 512], "xr%d" % a_, [], ["xr%d" % a_])
        else:
            np_ = 4
            DMA("sync", xr[a_][0:4, :], xall[2048:2052, c * 512:(c + 1) * 512], "xr%d" % a_, [], ["xr%d" % a_])
        for ft in range(16):
            if i < 8:
                lhs = (MIXA if ft < 8 else MIXC)[:, ft % 8, i * 128:(i + 1) * 128]
                rk = [("MIXA", i, 0), ("MIXA", i, 1)] if ft < 8 else [("MIXC", ft - 8, i // 4)]
            else:
                lhs = MIX_s[:, ft, 0:4]
                rk = [("MIXs", ft)]
            MM(ps[bk][:np_, :], lhs, WO[s_][:, ft, :], ft == 0, ft == 15, rk + ["WO%d" % s_], ["ps%d" % bk])
            if ft % 4 == 3 and ft < 15:
                yield
        TT("dve", yst[a_][:np_, :], ps[bk][:np_, :], xr[a_][:np_, :], ALU.add, ["ps%d" % bk, "xr%d" % a_], ["yst%d" % a_])
        if i < 8:
            DMA("sync", y_own[i * 128:(i + 1) * 128, c * 512:(c + 1) * 512], yst[a_], "yo%d" % a_, ["yst%d" % a_], [])
        else:
            DMA("sync", y_s[:, c * 512:(c + 1) * 512], yst[a_][0:4, :], "yo%d" % a_, ["yst%d" % a_], [])
        yield

    def load_wo(c):
        s_ = wo_slot[c]
        DMA("pool", WO[s_], w_out[:, c * 512:(c + 1) * 512].rearrange("(ft p) c -> p ft c", p=128), "wo%d" % s_, [], ["WO%d" % s_])

    def gen_out():
        load_wo(0)
        load_wo(1)
        load_wo(2)
        for c in range(4):
            for i in range(8):
                yield from out_step(c, i)
            if c == 0:
                load_wo(3)

    gs = gen_sample()
    go = gen_out()
    gs_alive, go_alive = True, True
    while gs_alive or go_alive:
        if gs_alive:
            try:
                next(gs)
            except StopIteration:
                gs_alive = False
        n_out = 3 if (sstage[0] == 0 and gs_alive) else 1
        for _ in range(n_out):
            if go_alive:
                try:
                    next(go)
                except StopIteration:
                    go_alive = False
    for _ in out_step(1, 8):
        pass
    wo_slot[0] = 1
    load_wo(0)
    for c_ in (2, 3, 0):
        for _ in out_step(c_, 8):
            pass

    P.wait_all_dma("sync")
    P.emit()
    return nc


_NC_CACHE = {}


def _rope_tables(pos, half):
    inv = (np.float32(THETA) ** (-(np.arange(half, dtype=np.float32) / np.float32(half)))).astype(np.float32)
    ang = pos.astype(np.float32)[:, None] * inv[None, :]
    return np.cos(ang).astype(np.float32), np.sin(ang).astype(np.float32)


def kernel(x_prompt, x_sample, cache_k, cache_v, cache_kidx, state_conv, page_table,
           norm_in, w_in, g_q, g_k, g_kidx, w_conv, w_out):
    x_prompt = np.asarray(x_prompt, np.float32)
    x_sample = np.asarray(x_sample, np.float32)
    if "nc" not in _NC_CACHE:
        _NC_CACHE["nc"] = build_program()
    nc = _NC_CACHE["nc"]
    ck = np.ascontiguousarray(np.asarray(cache_k, np.float32)[0].reshape(1280, PAGE * 256))
    cv = np.ascontiguousarray(np.asarray(cache_v, np.float32)[0].reshape(1280, PAGE * 256))
    cki = np.ascontiguousarray(np.asarray(cache_kidx, np.float32)[0].reshape(1280, PAGE * 64))
    w_in0 = np.ascontiguousarray(np.asarray(w_in, np.float32)[0])
    w_out0 = np.ascontiguousarray(np.asarray(w_out, np.float32)[0])
    pow2 = (2.0 ** -(np.arange(KBIS, dtype=np.float32) + 1)).astype(np.float32)
    in_maps = []
    slots_all = []
    for c in range(8):
        b, j = c // 2, c % 2
        own = T_OWN[j]
        oth = T_OWN[1 - j]
        slots = []
        for i in range(8):
            slots += [own[i], oth[i]]
        slots_all.append(slots)
        xall = np.zeros((2176, D), np.float32)
        pos = np.zeros(2176, np.int64)
        for s_, t in enumerate(slots):
            xall[s_ * 128:(s_ + 1) * 128] = x_prompt[b, t * 128:(t + 1) * 128]
            pos[s_ * 128:(s_ + 1) * 128] = np.arange(t * 128, (t + 1) * 128)
        xall[2048:2052] = x_sample[c]
        pos[2048:2052] = PAST + np.arange(4)
        for i in range(8):
            t = own[i]
            if t > 0:
                xall[2052 + 2 * i:2054 + 2 * i] = x_prompt[b, t * 128 - 2:t * 128]
        c32, s32 = _rope_tables(pos, 16)
        c16, s16 = _rope_tables(pos, 8)
        cm = np.zeros((128, 512), np.float32)
        tri = np.where(np.arange(128)[None, :] <= np.arange(128)[:, None], 0.0, -30000.0).astype(np.float32)
        for par in range(2):
            cm[:, par * 256:par * 256 + 128] = tri
            other_before = oth[par] < own[par]
            cm[:, par * 256 + 128:par * 256 + 256] = 0.0 if other_before else -30000.0
        pm = (-2.0 * cm).astype(np.float32)
        in_maps.append(dict(
            xall=xall, w_in=w_in0, w_out=w_out0,
            norm_in=np.ascontiguousarray(np.asarray(norm_in, np.float32)[0].reshape(16, 128).T),
            g_q=np.ascontiguousarray(np.asarray(g_q, np.float32)[0]),
            g_k=np.ascontiguousarray(np.asarray(g_k, np.float32)[0]),
            g_ki=np.ascontiguousarray(np.asarray(g_kidx, np.float32)[0]),
            w_conv=np.ascontiguousarray(np.asarray(w_conv, np.float32)[0].reshape(3, 8, 128).transpose(2, 0, 1)),
            cos32=c32, sin32=s32, cos16=c16, sin16=s16, cmask=cm, pmask=pm, pow2=pow2,
            stc=np.ascontiguousarray(np.asarray(state_conv, np.float32)[0, c].reshape(2, 8, 128).transpose(2, 0, 1)),
            ptab=np.ascontiguousarray(np.asarray(page_table, np.int32)[c].reshape(128, 1)),
            cache_k=ck, cache_v=cv, cache_ki=cki,
            cm4=np.where(np.arange(4)[:, None] <= np.arange(4)[None, :], 0.0, -1e30).astype(np.float32),
        ))
    res = run_bass_kernel_spmd(nc, in_maps, core_ids=list(range(8)))
    R = res.results
    y_prompt = np.zeros((NB_, S, D), np.float32)
    y_sample = np.zeros((DEC_B, DEC_T, D), np.float32)
    k_p = np.zeros((1, NB_, S, NKV, HD), np.float32)
    v_p = np.zeros((1, NB_, S, NKV, HD), np.float32)
    ki_p = np.zeros((1, NB_, S, IDXD), np.float32)
    conv_p = np.zeros((1, NB_, 2, 1024), np.float32)
    k_s = np.zeros((1, DEC_B, DEC_T, NKV, HD), np.float32)
    v_s = np.zeros((1, DEC_B, DEC_T, NKV, HD), np.float32)
    ki_s = np.zeros((1, DEC_B, DEC_T, IDXD), np.float32)
    conv_s = np.zeros((1, DEC_B, 2, 1024), np.float32)
    for c in range(8):
        b, j = c // 2, c % 2
        r = R[c]
        for i, t in enumerate(T_OWN[j]):
            sl = slice(t * 128, (t + 1) * 128)
            y_prompt[b, sl] = r["y_own"][i * 128:(i + 1) * 128]
            k_p[0, b, sl] = r["k_own"][i * 128:(i + 1) * 128].reshape(128, NKV, HD)
            v_p[0, b, sl] = r["v_own"][i * 128:(i + 1) * 128].reshape(128, NKV, HD)
            ki_p[0, b, sl] = r["ki_own"][i * 128:(i + 1) * 128]
        if j == 0:
            conv_p[0, b] = r["conv_p"].transpose(1, 2, 0).reshape(2, 1024)
        y_sample[c] = r["y_s"]
        k_s[0, c] = r["k_s"].reshape(4, NKV, HD)
        v_s[0, c] = r["v_s"].reshape(4, NKV, HD)
        ki_s[0, c] = r["ki_s"]
        conv_s[0, c] = r["conv_s"].transpose(1, 2, 0).reshape(2, 1024)
    return (y_prompt, y_sample, k_p, v_p, ki_p, conv_p, k_s, v_s, ki_s, conv_s)
```

```python
import numpy as np
import concourse.bass as bass
import concourse.mybir as mybir
from concourse.bass_utils import run_bass_kernel_spmd

F32 = mybir.dt.float32
BF16 = mybir.dt.bfloat16
I32 = mybir.dt.int32
AF = mybir.ActivationFunctionType
ALU = mybir.AluOpType
AX = mybir.AxisListType

D = 2048
S = 2048
NB_ = 4
DEC_B = 8
DEC_T = 4
PAST = 16384
PAGE = 128
NPAGES = PAST // PAGE
D_IN = 7760
HD = 128
NH = 8
NKV = 2
IDXH = 16
IDXD = 64
TOPK = 256
EPS = 1e-6
THETA = 500000.0
KBIS = 14
NEG = -30000.0
C_Q, C_K, C_V, C_ZA, C_QI, C_KI, C_WI, C_H, C_B, C_C, C_ZC = 0, 1024, 1280, 1536, 2560, 3584, 3648, 3664, 4688, 5712, 6736
T_OWN = ([0, 3, 4, 7, 8, 11, 12, 15], [1, 2, 5, 6, 9, 10, 13, 14])
ENGS = ("sync", "act", "dve", "pool", "pe")
NOSELF = ("pe",)


class Prog:
    def __init__(self, nc):
        self.nc = nc
        self.ops = {e: [] for e in ENGS}
        self.count = {e: 0 for e in ENGS}
        self.dcount = {}
        self.last_write = {}
        self.readers = {}
        self.waited = {e: {} for e in ENGS}
        self.pending_barrier = {e: None for e in ENGS}
        import os
        self.limit = int(os.environ.get('DBG_OPS', '100000000'))
        self.total = 0

    def _deps(self, eng, reads, writes):
        deps = {}

        def need(sv):
            if sv is None:
                return
            k, v = sv
            if deps.get(k, 0) < v:
                deps[k] = v

        for r in reads:
            need(self.last_write.get(r))
        for w in writes:
            need(self.last_write.get(w))
            for rd in self.readers.get(w, ()):
                need(rd)
        pb = self.pending_barrier[eng]
        if pb is not None:
            for kv in pb:
                need(kv)
            self.pending_barrier[eng] = None
        out = []
        for k, v in deps.items():
            if k == eng and eng in NOSELF:
                continue
            if self.waited[eng].get(k, 0) >= v:
                continue
            self.waited[eng][k] = v
            out.append((k, v))
        return out

    def _commit(self, me, reads, writes):
        for r in reads:
            self.readers.setdefault(r, []).append(me)
        for w in writes:
            self.last_write[w] = me
            self.readers[w] = []

    def op(self, eng, fn, reads=(), writes=()):
        self.total += 1
        if self.total > self.limit:
            return
        reads = tuple(reads)
        writes = tuple(writes)
        waits = self._deps(eng, reads, writes)
        self.count[eng] += 1
        self.ops[eng].append(("op", fn, waits))
        self._commit((eng, self.count[eng]), reads, writes)

    def dma(self, queue, fn, dsem, reads=(), writes=()):
        self.total += 1
        if self.total > self.limit:
            return
        reads = tuple(reads)
        writes = tuple(writes)
        key = ("d", dsem)
        waits = self._deps(queue, reads, writes)
        self.dcount[key] = self.dcount.get(key, 0) + 1
        self.ops[queue].append(("dma", fn, waits, key))
        self._commit((key, 16 * self.dcount[key]), reads, writes)

    def alias(self, new_key, old_keys):
        lst = self.readers.setdefault(new_key, [])
        for ok in old_keys:
            lw = self.last_write.get(ok)
            if lw is not None:
                lst.append(lw)
            lst.extend(self.readers.get(ok, ()))

    def barrier(self):
        snap = [(e, self.count[e]) for e in ENGS if self.count[e] > 0]
        snap += [(k, 16 * n) for k, n in self.dcount.items()]
        for e in ENGS:
            self.pending_barrier[e] = list(snap)

    def wait_all_dma(self, eng="sync"):
        waits = [(k, 16 * n) for k, n in self.dcount.items()]
        self.ops[eng].append(("waitonly", None, waits))

    def emit(self):
        nc = self.nc
        from contextlib import ExitStack
        with ExitStack() as es:
            sems = {}
            for e in ENGS:
                sems[e] = es.enter_context(nc.semaphore("s_" + e))
            for i, k in enumerate(self.dcount.keys()):
                sems[k] = es.enter_context(nc.semaphore("d_%d" % i))
            block = es.enter_context(nc.Block())

            def run(engname, h):
                for rec in self.ops[engname]:
                    kind, fn, waits = rec[0], rec[1], rec[2]
                    for k, v in waits:
                        h.wait_ge(sems[k], v)
                    if kind == "op":
                        fn(h).then_inc(sems[engname], 1)
                    elif kind == "dma":
                        fn(h).then_inc(sems[rec[3]], 16)

            @block.sync
            def _(h):
                run("sync", h)

            @block.scalar
            def _(h):
                run("act", h)

            @block.vector
            def _(h):
                run("dve", h)

            @block.gpsimd
            def _(h):
                run("pool", h)

            @block.tensor
            def _(h):
                run("pe", h)


def build_program(stop=99):
    nc = bass.Bass("TRN2", target_bir_lowering=False)
    P = Prog(nc)

    def din(name, shape, dt=F32):
        return nc.dram_tensor(name, list(shape), dt, kind="ExternalInput").ap()

    def dout(name, shape, dt=F32):
        return nc.dram_tensor(name, list(shape), dt, kind="ExternalOutput").ap()

    xall = din("xall", [2176, D])
    w_in = din("w_in", [D, D_IN])
    w_out = din("w_out", [D, D])
    norm_in = din("norm_in", [128, 16])
    g_q = din("g_q", [HD])
    g_k = din("g_k", [HD])
    g_ki = din("g_ki", [IDXD])
    w_conv = din("w_conv", [128, 3, 8])
    cos32 = din("cos32", [2176, 16])
    sin32 = din("sin32", [2176, 16])
    cos16 = din("cos16", [2176, 8])
    sin16 = din("sin16", [2176, 8])
    cmask_d = din("cmask", [128, 512])
    pmask_d = din("pmask", [128, 512])
    pow2_d = din("pow2", [KBIS])
    stc = din("stc", [128, 2, 8])
    ptab = din("ptab", [128, 1], I32)
    cm4_d = din("cm4", [4, 4])
    cache_k = din("cache_k", [1280, PAGE * 256])
    cache_v = din("cache_v", [1280, PAGE * 256])
    cache_ki = din("cache_ki", [1280, PAGE * 64])

    y_own = dout("y_own", [1024, D])
    y_s = dout("y_s", [4, D])
    k_own = dout("k_own", [1024, 256])
    v_own = dout("v_own", [1024, 256])
    ki_own = dout("ki_own", [1024, 64])
    conv_p = dout("conv_p", [128, 2, 8])
    k_s = dout("k_s", [4, 256])
    v_s = dout("v_s", [4, 256])
    ki_s = dout("ki_s", [4, 64])
    conv_s = dout("conv_s", [128, 2, 8])
    import os
    DUMP = os.environ.get("DBG_DUMP") == "1"
    if DUMP:
        dbg_I = dout("dbg_I", [128, 516])
        dbg_sm5 = dout("dbg_sm5", [128, 64])
        dbg_NB = dout("dbg_NB", [128, 516])
        dbg_att = dout("dbg_att", [128, 128])
        dbg_O = dout("dbg_O", [16, 132])
        dbg_Sm = dout("dbg_Sm", [128, 256])
        dbg_Wbc = dout("dbg_Wbc", [128, 64])

    BASE = 16640
    LIMIT = 229376
    cur = [BASE]
    cnt = [0]

    def alloc_at(off, shape, dt):
        cnt[0] += 1
        return nc.alloc_sbuf_tensor_at("t%d" % cnt[0], list(shape), dt, offset=off).ap()

    def nbytes(shape, dt):
        n = int(np.prod(shape[1:])) * (2 if dt == BF16 else 4)
        return (n + 63) // 64 * 64

    def sb(shape, dt):
        off = cur[0]
        cur[0] += nbytes(shape, dt)
        assert cur[0] <= LIMIT, ("SBUF overflow", cur[0])
        return alloc_at(off, shape, dt)

    ident_f = sb([128, 128], F32)
    ident_b = sb([128, 128], BF16)
    ID4 = sb([128, 512], BF16)
    ONES = sb([128, 128], BF16)
    gq_bc = sb([128, 128], F32)
    gk_bc = sb([128, 128], F32)
    gki_bc = sb([128, 64], F32)
    COS32 = sb([128, 17, 16], F32)
    SIN32 = sb([128, 17, 16], F32)
    COS16 = sb([128, 17, 8], F32)
    SIN16 = sb([128, 17, 8], F32)
    wcT = sb([128, 3, 8], F32)
    cmask = sb([128, 512], F32)
    pmask = sb([128, 512], F32)
    pow2 = sb([128, KBIS], F32)
    WIDX = sb([128, 9, 16], F32)
    ginT = sb([128, 16], F32)
    SCT = sb([128, 2, 8], F32)
    CO = sb([128, 2, 8], F32)
    COs = sb([128, 2, 8], F32)
    sml = sb([128, 80], F32)
    sml1 = sb([128, 8], F32)
    rt = sb([128, 4, 32], F32)
    bis = sb([128, 16], F32)
    Wt = sb([128, KBIS], F32)
    o_QT = cur[0]
    QT = sb([128, 8, 1024], BF16)
    KT = sb([128, 2, 2048], BF16)
    Vb = sb([128, 16, 256], BF16)
    QIT = sb([128, 8, 1024], BF16)
    KIT = sb([128, 2, 2048], BF16)
    o_MIXA = cur[0]
    MIXA = sb([128, 8, 1024], BF16)
    xstb = [alloc_at(o_MIXA + 8192 * j, [128, 2048], F32) for j in range(2)]
    QT_s = sb([128, 8, 32], BF16)
    KT_s = sb([128, 2, 32], BF16)
    V_s = sb([128, 256], BF16)
    QIT_s = sb([128, 8, 32], BF16)
    KIT_s = sb([128, 32], BF16)
    SZA_s = sb([128, 8, 32], BF16)
    MIX_s = sb([128, 16, 32], BF16)
    Us = sb([128, 8], F32)
    cvs = sb([128, 8], F32)
    XOFF = cur[0]
    xnT_oth = sb([128, 16, 1024], BF16)
    SZA = alloc_at(XOFF, [128, 8, 1024], BF16)
    MIXC = alloc_at(XOFF + 16384, [128, 8, 1024], BF16)
    ROFF = cur[0]
    RSIZE = LIMIT - ROFF
    print('SBUF: XOFF', XOFF, 'ROFF', ROFF, 'RSIZE', RSIZE)
    assert RSIZE >= 79872, RSIZE

    class Region:
        def __init__(self):
            self.o = ROFF

        def sb(self, shape, dt):
            off = self.o
            self.o += nbytes(shape, dt)
            assert self.o <= LIMIT, ("region overflow", self.o - ROFF)
            return alloc_at(off, shape, dt)

    R1 = Region()
    xnT_own = R1.sb([128, 16, 1152], BF16)
    o_x = R1.o
    xst = R1.sb([128, 2048], F32)
    o_xs = R1.o
    xs = R1.sb([128, 2048], BF16)
    Wr = [R1.sb([128, 16, 256], BF16) for _ in range(3)]
    o_tq = R1.o
    NJ = 5
    tQ = [R1.sb([128, 256], F32) for _ in range(NJ)]
    tQ2 = [R1.sb([128, 256], F32) for _ in range(NJ)]
    kbj = [R1.sb([128, 256], BF16) for _ in range(NJ)]
    rtj = [R1.sb([128, 4, 32], F32) for _ in range(NJ)]
    tA = [alloc_at(o_MIXA + 2048 * j, [128, 512], F32) for j in range(2)]
    tB = [alloc_at(o_MIXA + 4096 + 2048 * j, [128, 512], F32) for j in range(2)]
    Wr.append(alloc_at(o_MIXA + 8192, [128, 16, 256], BF16))
    U = alloc_at(o_x, [128, 8, 130], F32)
    xsb = [xs, alloc_at(o_x, [128, 2048], BF16)]
    CV = alloc_at(o_xs, [128, 8, 128], F32)

    ps = [nc.alloc_psum_tensor("ps%d" % i, [128, 512], F32).ap() for i in range(8)]
    psb = [p.bitcast(BF16) for p in ps]

    def ACT(out, in_, func, r, w, **kw):
        P.op("act", lambda e: e.activation(out=out, in_=in_, func=func, **kw), r, w)

    def TS(eng, out, in0, s1, s2, op0, op1, r, w, accum=None):
        if op1 is None:
            P.op(eng, lambda e: e.tensor_scalar(out=out, in0=in0, scalar1=s1, scalar2=None, op0=op0), r, w)
        elif accum is None:
            P.op(eng, lambda e: e.tensor_scalar(out=out, in0=in0, scalar1=s1, scalar2=s2, op0=op0, op1=op1), r, w)
        else:
            P.op(eng, lambda e: e.tensor_scalar(out=out, in0=in0, scalar1=s1, scalar2=s2, op0=op0, op1=op1,
                                                accum_out=accum), r, w)

    def TT(eng, out, in0, in1, op, r, w):
        P.op(eng, lambda e: e.tensor_tensor(out=out, in0=in0, in1=in1, op=op), r, w)

    def STT(out, in0, scalar, in1, op0, op1, r, w):
        P.op("dve", lambda e: e.scalar_tensor_tensor(out=out, in0=in0, scalar=scalar, in1=in1, op0=op0, op1=op1), r, w)

    def CP(eng, out, in_, r, w):
        P.op(eng, lambda e: e.tensor_copy(out=out, in_=in_), r, w)

    def RED(out, in_, op, r, w):
        P.op("dve", lambda e: e.tensor_reduce(out=out, in_=in_, axis=AX.X, op=op), r, w)

    def RCP(out, in_, r, w):
        P.op("dve", lambda e: e.reciprocal(out=out, in_=in_), r, w)

    def MM(out, lhsT, rhs, start, stop, r, w):
        P.op("pe", lambda e: e.matmul(out, lhsT=lhsT, rhs=rhs, start=start, stop=stop), r, w)

    def TR(out, in_, ident, r, w):
        P.op("pe", lambda e: e.transpose(out=out, in_=in_, identity=ident), r, w)

    def DMA(queue, out, in_, dsem, r, w, **kw):
        P.dma(queue, lambda e: e.dma_start(out=out, in_=in_, **kw), dsem, r, w)

    DMA("sync", ginT, norm_in, "c3", [], ["gin"])

    def load_consts():
        DMA("sync", gq_bc, g_q.partition_broadcast(128), "c0", [], ["gq"])
        DMA("sync", gk_bc, g_k.partition_broadcast(128), "c1", [], ["gk"])
        DMA("sync", gki_bc, g_ki.partition_broadcast(128), "c2", [], ["gki"])
        DMA("sync", COS32[:, 0:16, :], cos32[0:2048, :].rearrange("(t p) c -> p t c", p=128), "c4", [], ["cs0"])
        DMA("sync", SIN32[:, 0:16, :], sin32[0:2048, :].rearrange("(t p) c -> p t c", p=128), "c5", [], ["cs1"])
        DMA("sync", COS16[:, 0:16, :], cos16[0:2048, :].rearrange("(t p) c -> p t c", p=128), "c6", [], ["cs2"])
        DMA("sync", SIN16[:, 0:16, :], sin16[0:2048, :].rearrange("(t p) c -> p t c", p=128), "c7", [], ["cs3"])
        DMA("sync", COS32[:, 16, :], cos32[2048:2176, :], "c8", [], ["cs0e"])
        DMA("sync", SIN32[:, 16, :], sin32[2048:2176, :], "c9", [], ["cs1e"])
        DMA("sync", COS16[:, 16, :], cos16[2048:2176, :], "c10", [], ["cs2e"])
        DMA("sync", SIN16[:, 16, :], sin16[2048:2176, :], "c11", [], ["cs3e"])
        DMA("sync", cmask, cmask_d, "c12", [], ["cmask"])
        DMA("sync", pmask, pmask_d, "c13", [], ["pmask"])
        DMA("sync", pow2, pow2_d.partition_broadcast(128), "c14", [], ["pow2"])
        DMA("sync", wcT, w_conv, "c15", [], ["wcT"])
        DMA("sync", SCT, stc, "c16", [], ["SCT"])

    CONST_KEYS = ["gq", "gk", "gki", "gin", "cs0", "cs1", "cs2", "cs3", "cs0e", "cs1e", "cs2e", "cs3e",
                  "cmask", "pmask", "pow2", "wcT", "SCT"]
    P.op("pool", lambda e: e.memset(ident_f, 0.0), [], ["identf"])
    P.op("pool", lambda e: e.affine_select(out=ident_f, in_=ident_f, pattern=[[-1, 128]], compare_op=ALU.not_equal,
                                           fill=1.0, base=0, channel_multiplier=1), ["identf"], ["identf"])
    CP("pool", ident_b, ident_f, ["identf"], ["identb"])
    for a in range(4):
        CP("pool", ID4[:, a * 128:(a + 1) * 128], ident_f, ["identf"], ["ID4"])
    P.op("pool", lambda e: e.memset(ONES, 1.0), [], ["ONES"])

    if stop < 1:
        load_consts()
        P.wait_all_dma('sync'); P.emit(); return nc
    def tile_src(tt):
        if tt == 16:
            return xall[2048:2176, :], 128
        return xall[tt * 128:(tt + 1) * 128, :], 128

    def xn_dst(tt):
        if tt == 16:
            return xnT_own, 1024, 128
        if tt % 2 == 0:
            return xnT_own, (tt // 2) * 128, 128
        return xnT_oth, (tt // 2) * 128, 128

    p1_state = [0]

    def p1_A(tt):
        src, np_ = tile_src(tt)
        pb_ = tt % 2
        xst_, xs_ = xstb[pb_], xsb[pb_]
        kx, kxs, ksm1 = "xst%d" % pb_, "xs%d" % pb_, "sml1_%d" % pb_
        sm1 = sml1[:, 4 * pb_:4 * pb_ + 4]
        DMA("sync", xst_[:np_, :], src, "xld%d" % pb_, [], [kx])
        ACT(xs_[:np_, :], xst_[:np_, :], AF.Square, [kx], [kxs, ksm1], accum_out=sm1[:np_, 0:1])
        TS("dve", sm1[:np_, 1:2], sm1[:np_, 0:1], 1.0 / D, EPS, ALU.mult, ALU.add, [ksm1], [ksm1])
        ACT(sm1[:np_, 2:3], sm1[:np_, 1:2], AF.Sqrt, [ksm1], [ksm1])
        RCP(sm1[:np_, 3:4], sm1[:np_, 2:3], [ksm1], [ksm1])
        TS("dve", xs_[:np_, :], xst_[:np_, :], sm1[:np_, 3:4], None, ALU.mult, None, [kx, ksm1], [kxs])

    def p1_B(tt):
        src, np_ = tile_src(tt)
        pb_ = tt % 2
        xs_ = xsb[pb_]
        kxs = "xs%d" % pb_
        dst, t0, _ = xn_dst(tt)
        for half in range(2):
            bk = 2 + (2 * tt + half) % 2
            for j in range(8):
                kt = half * 8 + j
                TR(psb[bk][:, j * 128:j * 128 + np_], xs_[:np_, kt * 128:(kt + 1) * 128], ident_b[:np_, :np_],
                   [kxs, "identb"], ["ps%d" % bk])
            src_ps = psb[bk].rearrange("p (j t) -> p j t", j=8)[:, :, 0:np_]
            ACT(dst[:, half * 8:half * 8 + 8, t0:t0 + np_], src_ps, AF.Copy, ["ps%d" % bk], [("xnTraw", tt)])
        TT("pool", dst[:, :, t0:t0 + np_], dst[:, :, t0:t0 + np_], ginT.unsqueeze(2).to_broadcast([128, 16, np_]), ALU.mult,
           [("xnTraw", tt), "gin"], [("xnT", tt)])

    def p1_tile(tt):
        if tt == 0:
            p1_A(0)
        if tt + 1 <= 16:
            p1_A(tt + 1)
        p1_B(tt)

    def p1_upto(tt):
        while p1_state[0] <= min(tt, 16):
            p1_tile(p1_state[0])
            p1_state[0] += 1

    if stop < 2:
        load_consts()
        p1_upto(16)
        P.wait_all_dma('sync'); P.emit(); return nc

    wctr = [0]

    def win_cols(c0, n):
        return w_in[:, c0:c0 + n].rearrange("(kt p) c -> p kt c", p=128)

    own_tiles = [2 * i for i in range(8)] + [16]
    all_tiles = list(range(17))
    bkc = [0]
    trc = [0]
    tac = [0]

    def xn_src(tt, kt):
        dst, t0, n = xn_dst(tt)
        return dst[:, kt, t0:t0 + n], n

    def own_idx(tt):
        return tt // 2

    CK = ["cs0", "cs1", "cs2", "cs3", "cs0e", "cs1e", "cs2e", "cs3e"]

    chunks = [("k", C_K, 256, all_tiles, 2, 128, 16, gk_bc, True, 0),
              ("v", C_V, 256, all_tiles, 0, 0, 0, None, False, 0),
              ("kiw", C_KI, 80, all_tiles, 1, 64, 8, gki_bc, True, 0)]
    for c in range(4):
        chunks.append(("q", C_Q + 256 * c, 256, own_tiles, 2, 128, 16, gq_bc, True, c))
    for c in range(4):
        chunks.append(("qi", C_QI + 256 * c, 256, own_tiles, 4, 64, 8, None, False, c))

    wslot = {}

    def issue_w(ci):
        if ci >= len(chunks) or ci in wslot:
            return
        sidx = ci if ci < 3 else (ci - 3) % 2
        wslot[ci] = sidx
        DMA("pool", Wr[sidx][:, :, 0:chunks[ci][2]], win_cols(chunks[ci][1], chunks[ci][2]), "w%d" % sidx, [], ["W%d" % sidx])

    def job(ci, tt, js):
        kind, c0, ncols, _, nh, hd, rh, g_bc, do_norm, cidx = chunks[ci]
        sw = wslot[ci]
        bk = bkc[0] % 2
        bkc[0] += 1
        pk = "ps%d" % bk
        for kt in range(16):
            lhs, n = xn_src(tt, kt)
            MM(ps[bk][:, 0:ncols], lhs, Wr[sw][:, kt, 0:ncols], kt == 0, kt == 15, [("xnT", tt), "W%d" % sw], [pk])
        own = (tt == 16) or (tt % 2 == 0)
        i = own_idx(tt)
        tq, tq2, kb_, rt_ = tQ[js], tQ2[js], kbj[js], rtj[js]
        kq, kq2, kkb, krt, ksm = "tQ%d" % js, "tQ2%d" % js, "kb%d" % js, "rt%d" % js, "sml%d" % js
        sm = sml[:, 16 * js:16 * js + 16]
        if kind == "v":
            if own:
                ACT(tq[:, 0:256], ps[bk][:, 0:256], AF.Copy, [pk], [kq])
            if tt == 16:
                ACT(V_s[:, :], ps[bk][:, 0:256], AF.Copy, [pk], ["Vs"])
            else:
                ACT(Vb[:, tt, :], ps[bk][:, 0:256], AF.Copy, [pk], [("V", tt)])
            if own:
                for _ in range(6):
                    yield
                if tt == 16:
                    DMA("sync", v_s, tq[0:4, 0:256], "ko%d" % js, [kq], [])
                else:
                    DMA("sync", v_own[i * 128:(i + 1) * 128, :], tq[:, 0:256], "ko%d" % js, [kq], [])
            return
        nd = nh * hd
        ACT(tq[:, 0:ncols], ps[bk][:, 0:ncols], AF.Copy, [pk], [kq])
        if do_norm:
            for h in range(nh):
                ACT(tq2[:, h * hd:(h + 1) * hd], ps[bk][:, h * hd:(h + 1) * hd], AF.Square, [pk], [kq2, ksm],
                    accum_out=sm[:, h:h + 1])
            yield
            TS("dve", sm[:, 4:4 + nh], sm[:, 0:nh], 1.0 / hd, EPS, ALU.mult, ALU.add, [ksm], [ksm])
            yield
            ACT(sm[:, 8:8 + nh], sm[:, 4:4 + nh], AF.Sqrt, [ksm], [ksm])
            yield
            RCP(sm[:, 12:12 + nh], sm[:, 8:8 + nh], [ksm], [ksm])
            yield
            for h in range(nh):
                STT(tq2[:, h * hd:(h + 1) * hd], tq[:, h * hd:(h + 1) * hd], sm[:, 12 + h:13 + h], g_bc[:, 0:hd],
                    ALU.mult, ALU.mult, [kq, ksm, "gq", "gk", "gki"], [kq2])
                yield
            src, ksrc = tq2, kq2
        else:
            yield
            src, ksrc = tq, kq
        if kind == "kiw":
            if tt == 16:
                TS("dve", WIDX[:, 8, :], tq[:, 64:80], 1.0 / 32.0, None, ALU.mult, None, [kq], [("WIDX", 8)])
            elif own:
                TS("dve", WIDX[:, i, :], tq[:, 64:80], 1.0 / 32.0, None, ALU.mult, None, [kq], [("WIDX", i)])
        s3 = src[:, 0:nd].rearrange("p (h d) -> p h d", h=nh)
        x1 = s3[:, :, 0:rh]
        x2 = s3[:, :, rh:2 * rh]
        if rh == 16:
            cT, sT = COS32[:, tt, :], SIN32[:, tt, :]
        else:
            cT, sT = COS16[:, tt, :], SIN16[:, tt, :]
        cb = cT.unsqueeze(1).to_broadcast([128, nh, rh])
        sbb = sT.unsqueeze(1).to_broadcast([128, nh, rh])
        r4 = [rt_[:, j, 0:nh * rh].rearrange("p (h r) -> p h r", h=nh) for j in range(4)]
        TT("dve", r4[0], x1, cb, ALU.mult, [ksrc] + CK, [krt])
        TT("dve", r4[1], x2, sbb, ALU.mult, [ksrc] + CK, [krt])
        yield
        TT("dve", r4[2], x2, cb, ALU.mult, [ksrc] + CK, [krt])
        TT("dve", r4[3], x1, sbb, ALU.mult, [ksrc] + CK, [krt])
        yield
        TT("dve", x1, r4[0], r4[1], ALU.subtract, [krt], [ksrc])
        TT("dve", x2, r4[2], r4[3], ALU.add, [krt], [ksrc])
        yield
        if kind == "kiw":
            P.op("pool", lambda e: e.memset(kb_[:, 64:192], 0.0), [], [kkb])
            CP("dve", kb_[:, 0:64], src[:, 0:64], [ksrc], [kkb])
            ACT(kb_[:, 192:256], src[:, 0:64], AF.Copy, [ksrc], [kkb])
            nblk = 2
        else:
            CP("dve", kb_[:, 0:256], src[:, 0:256], [ksrc], [kkb])
            nblk = 2
        yield
        tb = 2 + trc[0] % 2
        trc[0] += 1
        for j in range(nblk):
            TR(psb[tb][:, j * 128:(j + 1) * 128], kb_[:, j * 128:(j + 1) * 128], ident_b, [kkb, "identb"], ["ps%d" % tb])
        for j in range(nblk):
            if kind == "k":
                dst, wk = (KT_s[:, j, 0:32], ("KTs", j)) if tt == 16 else (KT[:, j, tt * 128:(tt + 1) * 128], ("KT", tt))
            elif kind == "kiw":
                if tt == 16:
                    if j == 1:
                        continue
                    dst, wk = KIT_s[:, 0:32], "KITs"
                else:
                    dst, wk = KIT[:, j, tt * 128:(tt + 1) * 128], ("KIT", tt, j)
            elif kind == "q":
                hh = 2 * cidx + j
                dst, wk = (QT_s[:, hh, 0:32], ("QTs", hh)) if tt == 16 else (QT[:, hh, i * 128:(i + 1) * 128], ("QT", i, hh))
            else:
                hh = 2 * cidx + j
                dst, wk = (QIT_s[:, hh, 0:32], ("QITs", hh)) if tt == 16 else (QIT[:, hh, i * 128:(i + 1) * 128], ("QIT", i, hh))
            ncp = dst.shape[-1]
            ACT(dst, psb[tb][:, j * 128:j * 128 + ncp], AF.Copy, ["ps%d" % tb], [wk])
        if own and kind in ("k", "kiw"):
            for _ in range(6):
                yield
            if kind == "k":
                if tt == 16:
                    DMA("sync", k_s, src[0:4, 0:256], "ko%d" % js, [ksrc], [])
                else:
                    DMA("sync", k_own[i * 128:(i + 1) * 128, :], src[:, 0:256], "ko%d" % js, [ksrc], [])
            else:
                if tt == 16:
                    DMA("sync", ki_s, src[0:4, 0:64], "ko%d" % js, [ksrc], [])
                else:
                    DMA("sync", ki_own[i * 128:(i + 1) * 128, :], src[:, 0:64], "ko%d" % js, [ksrc], [])

    def all_jobs():
        for tt in all_tiles:
            for ci in range(3):
                yield (ci, tt)
        for ci in range(3, len(chunks)):
            for tt in chunks[ci][3]:
                yield (ci, tt)

    bkB = [0]
    wB = [0]

    def gen_B():
        oth_keys = [("xnT", tt) for tt in range(1, 16, 2)]
        tok_chunks = [(0, 512), (512, 512), (1024, 32)]

        def fproj(s, sub, c0, n):
            bk = 4 + bkB[0] % 4
            bkB[0] += 1
            tts = [("xnT", 2 * i) for i in range(8)] if c0 < 1024 else [("xnT", 16)]
            for kt in range(16):
                MM(ps[bk][:, 0:n], Wr[s][:, kt, sub * 128:(sub + 1) * 128], xnT_own[:, kt, c0:c0 + n], kt == 0, kt == 15,
                   tts + ["W%d" % s], ["ps%d" % bk])
            return bk

        for f2 in range(4):
            s = 2 + wB[0] % 2
            wB[0] += 1
            DMA("pool", Wr[s][:, :, 0:256], win_cols(C_ZA + 256 * f2, 256), "w%d" % s, [], ["W%d" % s])
            for sub in range(2):
                f = 2 * f2 + sub
                for (c0, n) in tok_chunks:
                    bk = fproj(s, sub, c0, n)
                    if c0 < 1024:
                        ACT(SZA[:, f, c0:c0 + n], ps[bk][:, 0:n], AF.Silu, ["ps%d" % bk] + oth_keys, [("SZA", f, c0)])
                        yield
                    else:
                        ACT(SZA_s[:, f, 0:32], ps[bk][:, 0:32], AF.Silu, ["ps%d" % bk], [("SZAs", f)])
                    yield

        sA, sB = 2, 3

        def load_A(f_):
            DMA("pool", Wr[sA][:, :, 0:128], win_cols(C_H + 128 * f_, 128), "w%d" % sA, [], ["W%d" % sA])
            DMA("pool", Wr[sA][:, :, 128:256], win_cols(C_C + 128 * f_, 128), "w%d" % sA, [], ["W%d" % sA])

        def load_B(f_):
            DMA("pool", Wr[sB][:, :, 0:128], win_cols(C_B + 128 * f_, 128), "w%d" % sB, [], ["W%d" % sB])
            DMA("pool", Wr[sB][:, :, 128:256], win_cols(C_ZC + 128 * f_, 128), "w%d" % sB, [], ["W%d" % sB])

        load_A(0)
        load_B(0)
        for f in range(8):
            for ci, (c0, n) in enumerate(tok_chunks):
                bh = fproj(sA, 0, c0, n)
                yield
                bc = fproj(sA, 1, c0, n)
                yield
                a = tac[0] % 2
                tac[0] += 1
                ACT(tA[a][:, 0:n], ps[bh][:, 0:n], AF.Copy, ["ps%d" % bh], ["tA%d" % a])
                if c0 < 1024:
                    TT("dve", U[:, 4 * ci:4 * ci + 4, 2:130], tA[a][:, 0:512].rearrange("p (i t) -> p i t", i=4),
                       ps[bc][:, 0:512].rearrange("p (i t) -> p i t", i=4), ALU.mult, ["tA%d" % a, "ps%d" % bc], [("U", ci)])
                else:
                    TT("dve", tB[a][:, 0:32], tA[a][:, 0:32], ps[bc][:, 0:32], ALU.mult, ["tA%d" % a, "ps%d" % bc], ["tB%d" % a])
                    CP("dve", U[:, :, 0:2], tB[a][:, 4:20].rearrange("p (i two) -> p i two", two=2), ["tB%d" % a], [("U", "h")])
                    CP("dve", Us[:, 2:6], tB[a][:, 0:4], ["tB%d" % a], ["Us"])
                    CP("dve", Us[:, 0:2], SCT[:, :, f], ["SCT"], ["Us"])
            if f + 1 < 8:
                load_A(f + 1)
            ukeys = [("U", 0), ("U", 1), ("U", "h")]
            TS("dve", CV, U[:, :, 0:128], wcT[:, 0, f:f + 1], None, ALU.mult, None, ukeys + ["wcT"], ["CV"])
            STT(CV, U[:, :, 1:129], wcT[:, 1, f:f + 1], CV, ALU.mult, ALU.add, ukeys + ["wcT", "CV"], ["CV"])
            STT(CV, U[:, :, 2:130], wcT[:, 2, f:f + 1], CV, ALU.mult, ALU.add, ukeys + ["wcT", "CV"], ["CV"])
            TS("dve", cvs[:, 0:4], Us[:, 0:4], wcT[:, 0, f:f + 1], None, ALU.mult, None, ["Us", "wcT"], ["cvs"])
            STT(cvs[:, 0:4], Us[:, 1:5], wcT[:, 1, f:f + 1], cvs[:, 0:4], ALU.mult, ALU.add, ["Us", "wcT", "cvs"], ["cvs"])
            STT(cvs[:, 0:4], Us[:, 2:6], wcT[:, 2, f:f + 1], cvs[:, 0:4], ALU.mult, ALU.add, ["Us", "wcT", "cvs"], ["cvs"])
            CP("dve", CO[:, :, f], U[:, 7, 128:130], ukeys, [("CO", f)])
            CP("dve", COs[:, :, f], Us[:, 4:6], ["Us"], [("COs", f)])
            for ci, (c0, n) in enumerate(tok_chunks):
                bb = fproj(sB, 0, c0, n)
                yield
                bz = fproj(sB, 1, c0, n)
                yield
                a = tac[0] % 2
                tac[0] += 1
                ACT(tA[a][:, 0:n], ps[bz][:, 0:n], AF.Silu, ["ps%d" % bz], ["tA%d" % a])
                if c0 < 1024:
                    TT("dve", tB[a][:, 0:512], CV[:, 4 * ci:4 * ci + 4, :].rearrange("p i t -> p (i t)"), ps[bb][:, 0:512], ALU.mult,
                       ["CV", "ps%d" % bb], ["tB%d" % a])
                    TT("dve", MIXC[:, f, c0:c0 + 512], tB[a][:, 0:512], tA[a][:, 0:512], ALU.mult,
                       ["tA%d" % a, "tB%d" % a] + oth_keys, [("MIXC", f, ci)])
                else:
                    TT("dve", tB[a][:, 0:4], cvs[:, 0:4], ps[bb][:, 0:4], ALU.mult, ["cvs", "ps%d" % bb], ["tB%d" % a])
                    TT("dve", MIX_s[:, 8 + f, 0:4], tB[a][:, 0:4], tA[a][:, 0:4], ALU.mult, ["tA%d" % a, "tB%d" % a], [("MIXs", 8 + f)])
            if f + 1 < 8:
                load_B(f + 1)
        DMA("sync", conv_p, CO, "cvo", [("CO", f) for f in range(8)], [])
        DMA("sync", conv_s, COs, "cvo2", [("COs", f) for f in range(8)], [])


    issue_w(0)
    issue_w(1)
    issue_w(2)
    p1_upto(2)
    load_consts()
    free_slots = list(range(NJ))
    active = []
    p1_alias_done = [False]
    gB = [None]
    gB_done = [False]
    want_B = [False]

    def make_B():
        okeys = [("xnT", t_) for t_ in range(1, 16, 2)]
        for f_ in range(8):
            for c_ in (0, 512):
                P.alias(("SZA", f_, c_), okeys)
            for c_ in range(2):
                P.alias(("MIXC", f_, c_), okeys)
        gB[0] = gen_B()

    jit = all_jobs()
    pending = next(jit, None)
    while pending is not None or active:
        while pending is not None and free_slots:
            ci, tt = pending
            p1_upto(tt + 2)
            if p1_state[0] > 16 and not p1_alias_done[0]:
                p1_alias_done[0] = True
                for _k in ("tA0", "tA1", "tB0", "tB1"):
                    P.alias(_k, ["xst0"])
                P.alias("W3", ["xst1"])
                for _k in (("U", 0), ("U", 1), ("U", "h")):
                    P.alias(_k, ["xs1"])
                P.alias("CV", ["xs0"])
            if ci >= 3:
                want_B[0] = True
            if ci >= 3:
                issue_w(ci)
                issue_w(ci + 1)
            js = free_slots.pop(0)
            g_ = job(ci, tt, js)
            try:
                next(g_)
                active.append((g_, js))
            except StopIteration:
                free_slots.append(js)
            pending = next(jit, None)
        for ent in list(active):
            g, js = ent
            try:
                next(g)
            except StopIteration:
                active.remove(ent)
                free_slots.append(js)
        if want_B[0] and gB[0] is None:
            make_B()
        if gB[0] is not None and not gB_done[0]:
            try:
                next(gB[0])
            except StopIteration:
                gB_done[0] = True
    if gB[0] is None:
        make_B()
    if not gB_done[0]:
        for _ in gB[0]:
            pass

    if stop < 4:
        P.wait_all_dma('sync'); P.emit(); return nc
    P.barrier()

    R2 = Region()
    IS = [R2.sb([128, 2048], F32) for _ in range(3)]
    NBm = [R2.sb([128, 2048], BF16) for _ in range(3)]
    junk = R2.sb([128, 2048], BF16)
    Rh = [R2.sb([128, 512], BF16) for _ in range(6)]
    Dg = [R2.sb([128, 16, 128], BF16) for _ in range(2)]
    PT = [R2.sb([128, 512], BF16) for _ in range(5)]
    rz = R2.sb([128, 512], F32)
    otmp = R2.sb([128, 512], F32)
    t256 = R2.sb([128, 256], F32)
    cmaskb = R2.sb([128, 512], BF16)
    Zacc = [R2.sb([128, 512], F32) for _ in range(2)]
    ONESF = R2.sb([128, 128], F32)
    otm = [R2.sb([128, 512], F32) for _ in range(2)]
    rzl = [R2.sb([128, 512], F32) for _ in range(2)]
    rzz = [R2.sb([128, 512], F32) for _ in range(2)]
    rhc = [0]
    ptc = [0]
    dbk = [0]
    stc_ = [0]
    SCALE = float(HD) ** -0.5
    CP("pool", cmaskb, cmask, ["cmask"], ["cmaskb"])
    P.op("pool", lambda e: e.memset(ONESF, 1.0), [], ["ONESF"])

    def indexer(i):
        ib = i % 2
        isb = i % 3
        nk = 256 * (i + 1)
        par = i % 2
        for h in range(16):
            TS("pool", Dg[ib][:, h, :], ident_b, WIDX[:, i, h:h + 1], 0.0, ALU.mult, ALU.add, ["identb", ("WIDX", i)], [("Dg", ib)])
        nch = (nk + 511) // 512
        for c in range(nch):
            k0 = 512 * c
            ncol = min(512, nk - k0)
            m0 = nk - 256
            has_mask = (k0 + ncol > m0)
            prev = None
            for h in range(16):
                hp = h % 2
                bk = (0, 1, 7)[dbk[0] % 3]
                dbk[0] += 1
                MM(ps[bk][:, 0:ncol], QIT[:, h // 2, i * 128:(i + 1) * 128],
                   KIT[:, hp, k0:k0 + ncol], True, True,
                   [("QIT", i, h // 2)] + [("KIT", tt, hp) for tt in range(k0 // 128, (k0 + ncol) // 128)], ["ps%d" % bk])
                r = rhc[0] % 6
                rhc[0] += 1
                ACT(Rh[r][:, 0:ncol], ps[bk][:, 0:ncol], AF.Relu, ["ps%d" % bk], ["Rh%d" % r])
                if prev is not None:
                    ph, pr = prev
                    MM(ps[2][:, 0:ncol], Dg[ib][:, ph, :], Rh[pr][:, 0:ncol], ph == 0, False, [("Dg", ib), "Rh%d" % pr], ["ps2"])
                prev = (h, r)
                yield
            ph, pr = prev
            MM(ps[2][:, 0:ncol], Dg[ib][:, ph, :], Rh[pr][:, 0:ncol], False, not has_mask, [("Dg", ib), "Rh%d" % pr], ["ps2"])
            if has_mask:
                lo = m0 - k0
                MM(ps[2][:, lo:lo + 256], ident_b, cmaskb[:, par * 256:(par + 1) * 256], False, True, ["identb", "cmaskb"], ["ps2"])
            ACT(IS[isb][:, k0:k0 + ncol], ps[2][:, 0:ncol], AF.Copy, ["ps2"], [("IS", isb)])

    def bisect(i):
        ib = i % 3
        nk = 256 * (i + 1)
        par = i % 2
        isk = ("IS", ib)
        X = IS[ib][:, 0:nk]
        RED(bis[:, 0:1], X, ALU.max, [isk], ["bis"])
        TT("dve", t256, IS[ib][:, nk - 256:nk], pmask[:, par * 256:(par + 1) * 256], ALU.add, [isk, "pmask"], ["t256"])
        RED(bis[:, 1:2], t256, ALU.min, ["t256"], ["bis"])
        if nk > 256:
            RED(bis[:, 2:3], IS[ib][:, 0:nk - 256], ALU.min, [isk], ["bis"])
            TT("dve", bis[:, 1:2], bis[:, 1:2], bis[:, 2:3], ALU.min, ["bis"], ["bis"])
        TT("dve", bis[:, 3:4], bis[:, 0:1], bis[:, 1:2], ALU.subtract, ["bis"], ["bis"])
        STT(bis[:, 4:5], bis[:, 3:4], -(2.0 ** -10), bis[:, 1:2], ALU.mult, ALU.add, ["bis"], ["bis"])
        TS("dve", bis[:, 4:5], bis[:, 4:5], -1e-6, None, ALU.add, None, ["bis"], ["bis"])
        TT("dve", bis[:, 5:6], bis[:, 0:1], bis[:, 4:5], ALU.subtract, ["bis"], ["bis"])
        TS("dve", Wt, pow2, bis[:, 5:6], None, ALU.mult, None, ["pow2", "bis"], ["Wt"])
        TT("dve", bis[:, 6:7], bis[:, 4:5], Wt[:, 0:1], ALU.add, ["bis", "Wt"], ["bis"])
        for k in range(KBIS):
            TS("dve", junk[:, 0:nk], X, bis[:, 6:7], None, ALU.is_ge, ALU.add, [isk, "bis"], ["junk", "bis"], accum=bis[:, 7:8])
            TS("dve", bis[:, 8:9], bis[:, 7:8], float(TOPK), 0.5, ALU.is_ge, ALU.subtract, ["bis"], ["bis"])
            if k < KBIS - 1:
                STT(bis[:, 6:7], bis[:, 8:9], Wt[:, k:k + 1], bis[:, 6:7], ALU.mult, ALU.add, ["bis", "Wt"], ["bis"])
            else:
                TS("dve", bis[:, 8:9], bis[:, 8:9], -0.5, None, ALU.add, None, ["bis"], ["bis"])
                STT(bis[:, 9:10], bis[:, 8:9], Wt[:, k:k + 1], bis[:, 6:7], ALU.mult, ALU.add, ["bis", "Wt"], ["bis"])
        TS("dve", NBm[ib][:, 0:nk], X, bis[:, 9:10], NEG, ALU.is_lt, ALU.mult, [isk, "bis"], [("NB", ib)])

    def attend_main(i):
        ib = i % 3
        nsl = 2 * i + 2
        seq = [(g, s_) for g in range(2) for s_ in range(nsl)]
        pend = None

        def s_mm(g, s_):
            sb_ = (3, 6)[stc_[0] % 2]
            stc_[0] += 1
            MM(ps[sb_].rearrange("p (h t) -> p h t", h=4), KT[:, g, s_ * 128:(s_ + 1) * 128],
               QT[:, 4 * g:4 * g + 4, i * 128:(i + 1) * 128], True, False,
               [("KT", s_)] + [("QT", i, 4 * g + hh) for hh in range(4)], ["ps%d" % sb_])
            MM(ps[sb_], NBm[ib][:, s_ * 128:(s_ + 1) * 128], ID4, False, True, [("NB", ib), "ID4"], ["ps%d" % sb_])
            r = ptc[0] % 5
            ptc[0] += 1
            ACT(PT[r], ps[sb_], AF.Exp, ["ps%d" % sb_], ["PT%d" % r], scale=SCALE)
            return r

        def pv_mm(g, s_, r):
            MM(ps[4], Vb[:, s_, g * 128:(g + 1) * 128], PT[r], s_ == 0, s_ == nsl - 1, [("V", s_), "PT%d" % r], ["ps4"])
            if s_ % 2 == 0:
                MM(ps[5], ONES, PT[r], s_ == 0, False, ["ONES", "PT%d" % r], ["ps5"])
            elif s_ == 1:
                CP("pool", Zacc[g], PT[r], ["PT%d" % r], ["Zacc%d" % g])
            else:
                TT("pool", Zacc[g], Zacc[g], PT[r], ALU.add, ["PT%d" % r, "Zacc%d" % g], ["Zacc%d" % g])
            if s_ == nsl - 1:
                MM(ps[5], ONESF, Zacc[g], False, True, ["ONESF", "Zacc%d" % g], ["ps5"])
                ACT(otm[g], ps[4], AF.Copy, ["ps4"], ["otm%d" % g])
                ACT(rzl[g], ps[5], AF.Ln, ["ps5"], ["rzl%d" % g])
                ACT(rzz[g], rzl[g], AF.Exp, ["rzl%d" % g], ["rzz%d" % g], scale=-1.0)
                TT("pool", otm[g], otm[g], rzz[g], ALU.mult, ["otm%d" % g, "rzz%d" % g], ["otm%d" % g])
                TT("pool", MIXA[:, 4 * g:4 * g + 4, i * 128:(i + 1) * 128], otm[g].rearrange("p (h t) -> p h t", h=4),
                   SZA[:, 4 * g:4 * g + 4, i * 128:(i + 1) * 128], ALU.mult,
                   ["otm%d" % g] + [("SZA", 4 * g + hh, (i // 4) * 512) for hh in range(4)], [("MIXA", i, g)])

        for (g, s_) in seq:
            r = s_mm(g, s_)
            if pend is not None:
                pv_mm(*pend)
            pend = (g, s_, r)
            yield
        pv_mm(*pend)

    def attend_epi(i):
        pass

    def drain(g_):
        for _ in g_:
            pass

    drain(indexer(0))
    bisect(0)
    drain(indexer(1))
    drain(indexer(2))
    for i in range(8):
        if i + 1 < 8:
            bisect(i + 1)
        ga = attend_main(i)
        if i + 3 < 8:
            gi = indexer(i + 3)
            n_att = 2 * (2 * i + 2)
            n_idx = 16 * ((256 * (i + 4) + 511) // 512)
            gi_alive = True
            acc = 0
            for _ in ga:
                acc += n_idx
                while acc >= n_att:
                    acc -= n_att
                    if gi_alive:
                        try:
                            next(gi)
                        except StopIteration:
                            gi_alive = False
            if gi_alive:
                drain(gi)
        else:
            drain(ga)

    if stop < 5:
        P.wait_all_dma('sync'); P.emit(); return nc
    P.barrier()

    R4 = Region()
    I_all = R4.sb([128, 129, 4], F32)
    NBs = R4.sb([128, 129, 4], F32)
    junk_s = R4.sb([128, 132], F32)
    idx16 = R4.sb([128, 16], I32)
    ptb = R4.sb([128, 1], I32)
    Q64 = R4.sb([128, 8, 2, 4], BF16)
    Q64z = R4.sb([128, 2, 64], BF16)
    cnt_b = R4.sb([128, 4], BF16)
    idx4 = R4.sb([128, 4], I32)
    sm5 = R4.sb([128, 64], F32)
    Dw = R4.sb([128, 64], F32)
    Wbc = R4.sb([128, 64], F32)
    dg4 = R4.sb([128, 8], F32)
    ones_f = R4.sb([128, 128], F32)
    cm4 = R4.sb([128, 4], F32)
    Vn = R4.sb([128, 2, 130], BF16)
    Pn = R4.sb([128, 2, 16], BF16)
    Sn = R4.sb([128, 2, 16], F32)
    att = R4.sb([128, 128], F32)
    o_sub = R4.o
    KI = R4.sb([128, 128, 64], F32)
    KIb = R4.sb([128, 128, 64], BF16)
    kiTg = [R4.sb([128, 8, 128], BF16) for _ in range(2)]
    tmpi = R4.sb([128, 8, 64], F32)
    R5 = Region()
    R5.o = o_sub
    Kc = [R5.sb([128, 8, 256], F32) for _ in range(3)]
    Vc = [R5.sb([128, 8, 256], F32) for _ in range(3)]
    Kb = [R5.sb([128, 8, 256], BF16) for _ in range(2)]
    KTc = [R5.sb([128, 8, 2, 128], BF16) for _ in range(2)]
    Vb2 = [R5.sb([128, 8, 2, 130], BF16) for _ in range(3)]
    Sm = R5.sb([128, 8, 2, 16], F32)
    Pb = [R5.sb([128, 8, 2, 16], BF16) for _ in range(2)]

    def IDMA(out, in_, idx_ap, dsem, r, w):
        P.dma("pool", lambda e: e.indirect_dma_start(out=out, out_offset=None, in_=in_,
                                                     in_offset=bass.IndirectOffsetOnAxis(ap=idx_ap, axis=0)), dsem, r, w)

    sstage = [0]

    def gen_sample():
        DMA("sync", ptb, ptab, "s0", [], ["ptb"])
        DMA("sync", cm4[0:4, :], cm4_d, "s1", [], ["cm4"])
        cki4 = cache_ki.rearrange("p (c x) -> (p c) x", c=4)
        for q4 in range(4):
            TS("dve", idx4[:, q4:q4 + 1], ptb[:, 0:1], 4.0, float(q4), ALU.mult, ALU.add, ["ptb"], [("idx4", q4)])
            IDMA(KI[:, q4 * 32:(q4 + 1) * 32, :].rearrange("p j d -> p (j d)"), cki4, idx4[:, q4:q4 + 1], "s2_%d" % q4,
                 [("idx4", q4)], [("KI", q4)])
        for c in range(16):
            TS("dve", idx16[:, c:c + 1], ptb[:, 0:1], 16.0, float(c), ALU.mult, ALU.add, ["ptb"], ["idx16"])
        P.op("pool", lambda e: e.memset(ones_f, 1.0), [], ["ones_f"])
        P.op("pool", lambda e: e.memset(I_all[:, 128, :], -1e30), [], ["I128"])
        P.op("pool", lambda e: e.memset(Vn, 1.0), [], ["Vn"])
        CP("dve", Q64[0:64, :, 0, :], QIT_s[0:64, :, 0:4], [("QITs", j) for j in range(8)], ["Q64a"])
        DMA("sync", Q64[0:64, :, 1, :], QIT_s[64:128, :, 0:4], "s3", [("QITs", j) for j in range(8)], ["Q64b"])
        P.op("pool", lambda e: e.memset(Q64z, 0.0), [], ["Q64z"])
        CP("dve", Q64z[0:64, 0, :], Q64[0:64].rearrange("p a b t -> p (a b t)"), ["Q64a", "Q64b", "Q64z"], ["Q64z"])
        DMA("sync", Q64z[64:128, 1, :], Q64[0:64].rearrange("p a b t -> p (a b t)"), "s3b", ["Q64a", "Q64b", "Q64z"], ["Q64zb"])
        TT("dve", Dw[0:4, :].rearrange("p (h t) -> p h t", h=16), WIDX[0:4, 8, :].unsqueeze(2).to_broadcast([4, 16, 4]),
           ident_f[0:4, 0:4].unsqueeze(1).to_broadcast([4, 16, 4]), ALU.mult, [("WIDX", 8), "identf"], ["Dw"])
        MM(ps[4][:, 0:64], ones_f[0:4, :], Dw[0:4, :], True, True, ["ones_f", "Dw"], ["ps4"])
        CP("dve", Wbc, ps[4][:, 0:64], ["ps4"], ["Wbc"])
        for q4 in range(4):
            if q4 % 2 == 1:
                CP("pool", KIb[:, q4 * 32:(q4 + 1) * 32, :], KI[:, q4 * 32:(q4 + 1) * 32, :], [("KI", q4)], [("KIb", q4)])
            else:
                ACT(KIb[:, q4 * 32:(q4 + 1) * 32, :], KI[:, q4 * 32:(q4 + 1) * 32, :], AF.Copy, [("KI", q4)], [("KIb", q4)])
        wb3 = Wbc.unsqueeze(1).to_broadcast([128, 8, 64])
        for gi in range(16):
            j0 = gi * 8
            tb = gi % 2
            for jp in range(4):
                TR(psb[tb][:, jp * 128:(jp + 1) * 128], KIb[:, j0 + 2 * jp:j0 + 2 * jp + 2, :].rearrange("p a d -> p (a d)"), ident_b,
                   [("KIb", j0 // 32), "identb"], ["ps%d" % tb])
            ACT(kiTg[tb][:, 0:4, :], psb[tb][:, 0:512].rearrange("p (j t) -> p j t", j=4), AF.Copy, ["ps%d" % tb], ["kiTg%d" % tb])
            db = 2 + gi % 2
            for jj in range(8):
                MM(ps[db][:, jj * 64:(jj + 1) * 64], kiTg[tb][:, jj // 2, :], Q64z[:, jj % 2, :], True, True,
                   ["kiTg%d" % tb, "Q64z", "Q64zb"], ["ps%d" % db])
            STT(tmpi, ps[db].rearrange("p (j c) -> p j c", j=8), 0.0, wb3, ALU.max, ALU.mult, ["ps%d" % db, "Wbc"], ["tmpi"])
            RED(I_all[:, j0:j0 + 8, :], tmpi.rearrange("p j (h t) -> p j t h", h=16), ALU.add, ["tmpi"], [("Ia", gi)])
            yield
        MM(ps[4][0:4, 0:64], KIT_s[:, 0:4], Q64z[:, 0, :], True, True, ["KITs", "Q64z", "Q64zb"], ["ps4"])
        STT(tmpi[0:4, 0, :], ps[4][0:4, 0:64], 0.0, Wbc[0:4, :], ALU.max, ALU.mult, ["ps4", "Wbc"], ["tmpi"])
        RED(sm5[0:4, 0:4], tmpi[0:4, 0, :].rearrange("p (h t) -> p t h", h=16), ALU.add, ["tmpi"], ["sm5"])
        TT("dve", I_all[0:4, 128, :], sm5[0:4, 0:4], cm4[0:4, :], ALU.add, ["sm5", "cm4", "I128"], ["I128"])
        IA_KEYS = [("Ia", gi) for gi in range(16)] + ["I128"]

        def bcast4(dst, src_col4, negate=False):
            P.op("pe", lambda e: e.transpose(out=ps[5][0:4, 0:128], in_=src_col4, identity=ident_f), ["sm5", "identf"], ["ps5"])
            RED(dg4[0:4, 0:1], ps[5][0:4, 0:128], ALU.max, ["ps5"], ["dg4"])
            TS("dve", dg4[0:4, 4:8], ident_f[0:4, 0:4], dg4[0:4, 0:1], None, ALU.mult, None, ["dg4", "identf"], ["dg4b"])
            MM(ps[5][:, 256:260], ones_f[0:4, :], dg4[0:4, 4:8], True, True, ["ones_f", "dg4b"], ["ps5"])
            TS("dve", dst, ps[5][:, 256:260], -1.0 if negate else 1.0, None, ALU.mult, None, ["ps5"], ["sm5"])

        Iv = I_all.rearrange("p s t -> p t s")
        RED(sm5[:, 8:12], Iv, ALU.max, IA_KEYS, ["sm5"])
        bcast4(sm5[:, 12:16], sm5[:, 8:12])
        P.op("dve", lambda e: e.tensor_reduce(out=sm5[:, 8:12], in_=Iv[:, :, 0:128], axis=AX.X, op=ALU.min, negate=True), IA_KEYS + ["sm5"], ["sm5"])
        bcast4(sm5[:, 16:20], sm5[:, 8:12], negate=True)
        TT("dve", sm5[:, 20:24], sm5[:, 12:16], sm5[:, 16:20], ALU.subtract, ["sm5"], ["sm5"])
        STT(sm5[:, 24:28], sm5[:, 20:24], -(2.0 ** -10), sm5[:, 16:20], ALU.mult, ALU.add, ["sm5"], ["sm5"])
        TS("dve", sm5[:, 24:28], sm5[:, 24:28], -1e-6, None, ALU.add, None, ["sm5"], ["sm5"])
        TT("dve", sm5[:, 28:32], sm5[:, 12:16], sm5[:, 24:28], ALU.subtract, ["sm5"], ["sm5"])
        STT(sm5[:, 32:36], sm5[:, 28:32], 0.5, sm5[:, 24:28], ALU.mult, ALU.add, ["sm5"], ["sm5"])
        def bisect_s():
            KS = 14
            for k in range(KS):
                for t in range(4):
                    TS("dve", junk_s[:, 0:129], I_all[:, :, t], sm5[:, 32 + t:33 + t], None, ALU.is_ge, ALU.add, IA_KEYS + ["sm5"],
                       ["junk_s", "cntp"], accum=cnt_b[:, t:t + 1])
                MM(ps[4][:, 0:4], ONES, cnt_b, True, True, ["ONES", "cntp"], ["ps4"])
                TS("dve", sm5[:, 40:44], ps[4][:, 0:4], float(TOPK), 0.5, ALU.is_ge, ALU.subtract, ["ps4"], ["sm5"])
                TT("dve", sm5[:, 44:48], sm5[:, 40:44], sm5[:, 28:32], ALU.mult, ["sm5"], ["sm5"])
                yield
                if k < KS - 1:
                    STT(sm5[:, 32:36], sm5[:, 44:48], 2.0 ** -(k + 1), sm5[:, 32:36], ALU.mult, ALU.add, ["sm5"], ["sm5"])
                else:
                    STT(sm5[:, 48:52], sm5[:, 28:32], -0.5, sm5[:, 44:48], ALU.mult, ALU.add, ["sm5"], ["sm5"])
                    STT(sm5[:, 52:56], sm5[:, 48:52], 2.0 ** -(k + 1), sm5[:, 32:36], ALU.mult, ALU.add, ["sm5"], ["sm5"])
            for t in range(4):
                TS("dve", NBs[:, :, t], I_all[:, :, t], sm5[:, 52 + t:53 + t], NEG, ALU.is_lt, ALU.mult, IA_KEYS + ["sm5"], ["NBs"])


        if DUMP:
            DMA('sync', dbg_I, I_all.rearrange('p s t -> p (s t)'), 'dd0', IA_KEYS, [])
            DMA('sync', dbg_sm5, sm5, 'dd1', ['sm5'], [])
            DMA('sync', dbg_NB, NBs.rearrange('p s t -> p (s t)'), 'dd2', ['NBs'], [])
            DMA('sync', dbg_Wbc, Wbc, 'dd3', ['Wbc'], [])
        NEWK = ["Kc0", "Kc1", "Kc2", "Vc0", "Vc1", "Vc2", "Kb0", "Kb1", ("KTc", 0, 0), ("KTc", 0, 1), ("KTc", 1, 0), ("KTc", 1, 1),
                "Vb2_0", "Vb2_1", "Vb2_2", "Sm", "Pb0", "Pb1"]
        OLDK = [("KI", 0), ("KI", 1), ("KI", 2), ("KI", 3), ("KIb", 0), ("KIb", 1), ("KIb", 2), ("KIb", 3), "kiTg0", "kiTg1", "tmpi"]
        for _nk in NEWK:
            P.alias(_nk, OLDK)
        for c3 in range(3):
            P.op("pool", (lambda c3: (lambda e: e.memset(Vb2[c3], 1.0)))(c3), [], ["Vb2_%d" % c3])
        ck2 = cache_k.rearrange("p (c x) -> (p c) x", c=16)
        cv2 = cache_v.rearrange("p (c x) -> (p c) x", c=16)
        QsR = [QT_s[:, 4 * g:4 * g + 4, 0:4] for g in range(2)]

        def front(c):
            b2, b3 = c % 2, c % 3
            IDMA(Kc[b3].rearrange("p j d -> p (j d)"), ck2, idx16[:, c:c + 1], "s4_%d" % b3, ["idx16"], ["Kc%d" % b3])
            IDMA(Vc[b3].rearrange("p j d -> p (j d)"), cv2, idx16[:, c:c + 1], "s5_%d" % b3, ["idx16"], ["Vc%d" % b3])
            ACT(Kb[b2], Kc[b3], AF.Copy, ["Kc%d" % b3], ["Kb%d" % b2])
            CP("dve", Vb2[b3][:, :, :, 0:128], Vc[b3].rearrange("p j (g d) -> p j g d", g=2), ["Vc%d" % b3, "Vb2_%d" % b3], ["Vb2_%d" % b3])
            yield
            for half in range(2):
                tb = half
                for jj in range(4):
                    for g in range(2):
                        TR(psb[tb][:, (jj * 2 + g) * 128:(jj * 2 + g + 1) * 128], Kb[b2][:, half * 4 + jj, g * 128:(g + 1) * 128], ident_b,
                           ["Kb%d" % b2, "identb"], ["ps%d" % tb])
                ACT(KTc[b2][:, half * 4:half * 4 + 4, :, :], psb[tb].rearrange("p (j g t) -> p j g t", j=4, g=2), AF.Copy,
                    ["ps%d" % tb], [("KTc", b2, half)])
                yield
            sbk = 2 + b2
            for jj in range(8):
                for g in range(2):
                    MM(ps[sbk][:, (jj * 2 + g) * 16:(jj * 2 + g + 1) * 16].rearrange("p (h t) -> p h t", h=4), KTc[b2][:, jj, g, :], QsR[g], True, True,
                       [("KTc", b2, jj // 4)] + [("QTs", 4 * g + hh) for hh in range(4)], ["ps%d" % sbk])
            yield

        def back(c):
            b2, b3 = c % 2, c % 3
            sbk = 2 + b2
            TT("dve", Sm.rearrange("p j g (h t) -> p j (g h) t", h=4), ps[sbk][:, 0:256].rearrange("p (j gh t) -> p j gh t", j=8, t=4),
               NBs[:, c * 8:(c + 1) * 8, :].unsqueeze(2).to_broadcast([128, 8, 8, 4]), ALU.add, ["ps%d" % sbk, "NBs"], ["Sm"])
            ACT(Pb[b2], Sm, AF.Exp, ["Sm"], ["Pb%d" % b2], scale=SCALE)
            for jj in range(8):
                for g in range(2):
                    MM(ps[4 + g][0:16, 0:129], Pb[b2][:, jj, g, :], Vb2[b3][:, jj, g, 0:129], c == 0 and jj == 0, False,
                       ["Pb%d" % b2, "Vb2_%d" % b3], ["ps%d" % (4 + g)])
            yield

        yield from front(0)
        yield from front(1)
        yield from bisect_s()
        sstage[0] = 1
        for c in range(16):
            yield from back(c)
            if c + 2 < 16:
                yield from front(c + 2)
        for g in range(2):
            MM(ps[2][0:4, g * 16:(g + 1) * 16].rearrange("p (h t) -> p h t", h=4), KT_s[:, g, 0:4], QsR[g], True, True,
               [("KTs", g)] + [("QTs", 4 * g + hh) for hh in range(4)], ["ps2"])
            CP("dve", Vn[0:4, g, 0:128], V_s[0:4, g * 128:(g + 1) * 128], ["Vs", "Vn"], ["Vn"])
        TT("dve", Sn[0:4].rearrange("p g (h t) -> p (g h) t", h=4), ps[2][0:4, 0:32].rearrange("p (gh t) -> p gh t", t=4),
           NBs[0:4, 128, :].unsqueeze(1).to_broadcast([4, 8, 4]), ALU.add, ["ps2", "NBs"], ["Sn"])
        ACT(Pn[0:4], Sn[0:4], AF.Exp, ["Sn"], ["Pn"], scale=SCALE)
        for g in range(2):
            MM(ps[4 + g][0:16, 0:129], Pn[0:4, g, :], Vn[0:4, g, 0:129], False, True, ["Pn", "Vn"], ["ps%d" % (4 + g)])
            RCP(sm5[0:16, 60:61], ps[4 + g][0:16, 128:129], ["ps%d" % (4 + g)], ["sm5r"])
            TS("dve", att[0:16, :], ps[4 + g][0:16, 0:128], sm5[0:16, 60:61], None, ALU.mult, None, ["ps%d" % (4 + g), "sm5r"], ["att"])
            P.op("pe", lambda e: e.transpose(out=ps[0][:, 0:16], in_=att[0:16, :], identity=ident_f[0:16, 0:16]), ["att", "identf"], ["ps0"])
            TT("dve", MIX_s[:, 4 * g:4 * g + 4, 0:4], ps[0][:, 0:16].rearrange("p (h t) -> p h t", h=4), SZA_s[:, 4 * g:4 * g + 4, 0:4], ALU.mult,
               ["ps0"] + [("SZAs", 4 * g + hh) for hh in range(4)], [("MIXs", 4 * g + hh) for hh in range(4)])

        if DUMP:
            DMA('sync', dbg_att, att, 'dd4', ['att'], [])
            CP('dve', junk_s[0:16, 0:132], ps[5][0:16, 0:132], ['ps5'], ['junk_s'])
            DMA('sync', dbg_O, junk_s[0:16, 0:132], 'dd5', ['junk_s'], [])
            DMA('sync', dbg_Sm, Sm.rearrange('p j g c -> p (j g c)'), 'dd6', ['Sm'], [])

    if stop < 6:
        for _ in gen_sample():
            pass
        P.wait_all_dma('sync'); P.emit(); return nc

    R3 = Region()
    R3.o = o_QT
    WO = [R3.sb([128, 16, 512], BF16) for _ in range(3)]
    xr = [R3.sb([128, 512], F32) for _ in range(2)]
    yst = [R3.sb([128, 512], F32) for _ in range(2)]
    wo_slot = {0: 0, 1: 1, 2: 2, 3: 0}
    assert R3.o <= o_MIXA, (R3.o, o_MIXA)
    xc = [0]
    obk = [0]

    def out_step(c, i):
        s_ = wo_slot[c]
        bk = 6 + obk[0] % 2
        obk[0] += 1
        a_ = xc[0] % 2
        xc[0] += 1
        if i < 8:
            np_ = 128
            DMA("sync", xr[a_], xall[(2 * i) * 128:(2 * i + 1) * 128, c * 512:(c + 1) * 512], "xr%d" % a_, [], ["xr%d" % a_])
        else:
            np_ = 4
            DMA("sync", xr[a_][0:4, :], xall[2048:2052, c * 512:(c + 1) * 512], "xr%d" % a_, [], ["xr%d" % a_])
        for ft in range(16):
            if i < 8:
                lhs = (MIXA if ft < 8 else MIXC)[:, ft % 8, i * 128:(i + 1) * 128]
                rk = [("MIXA", i, 0), ("MIXA", i, 1)] if ft < 8 else [("MIXC", ft - 8, i // 4)]
            else:
                lhs = MIX_s[:, ft, 0:4]
                rk = [("MIXs", ft)]
            MM(ps[bk][:np_, :], lhs, WO[s_][:, ft, :], ft == 0, ft == 15, rk + ["WO%d" % s_], ["ps%d" % bk])
            if ft % 4 == 3 and ft < 15:
                yield
        TT("dve", yst[a_][:np_, :], ps[bk][:np_, :], xr[a_][:np_, :], ALU.add, ["ps%d" % bk, "xr%d" % a_], ["yst%d" % a_])
        if i < 8:
            DMA("sync", y_own[i * 128:(i + 1) * 128, c * 512:(c + 1) * 512], yst[a_], "yo%d" % a_, ["yst%d" % a_], [])
        else:
            DMA("sync", y_s[:, c * 512:(c + 1) * 512], yst[a_][0:4, :], "yo%d" % a_, ["yst%d" % a_], [])
        yield

    def load_wo(c):
        s_ = wo_slot[c]
        DMA("pool", WO[s_], w_out[:, c * 512:(c + 1) * 512].rearrange("(ft p) c -> p ft c", p=128), "wo%d" % s_, [], ["WO%d" % s_])

    def gen_out():
        load_wo(0)
        load_wo(1)
        load_wo(2)
        for c in range(4):
            for i in range(8):
                yield from out_step(c, i)
            if c == 0:
                load_wo(3)

    gs = gen_sample()
    go = gen_out()
    gs_alive, go_alive = True, True
    while gs_alive or go_alive:
        if gs_alive:
            try:
                next(gs)
            except StopIteration:
                gs_alive = False
        n_out = 3 if (sstage[0] == 0 and gs_alive) else 1
        for _ in range(n_out):
            if go_alive:
                try:
                    next(go)
                except StopIteration:
                    go_alive = False
    for _ in out_step(1, 8):
        pass
    wo_slot[0] = 1
    load_wo(0)
    for c_ in (2, 3, 0):
        for _ in out_step(c_, 8):
            pass

    P.wait_all_dma("sync")
    P.emit()
    return nc


_NC_CACHE = {}


def _rope_tables(pos, half):
    inv = (np.float32(THETA) ** (-(np.arange(half, dtype=np.float32) / np.float32(half)))).astype(np.float32)
    ang = pos.astype(np.float32)[:, None] * inv[None, :]
    return np.cos(ang).astype(np.float32), np.sin(ang).astype(np.float32)


def kernel(x_prompt, x_sample, cache_k, cache_v, cache_kidx, state_conv, page_table,
           norm_in, w_in, g_q, g_k, g_kidx, w_conv, w_out):
    x_prompt = np.asarray(x_prompt, np.float32)
    x_sample = np.asarray(x_sample, np.float32)
    if "nc" not in _NC_CACHE:
        _NC_CACHE["nc"] = build_program()
    nc = _NC_CACHE["nc"]
    ck = np.ascontiguousarray(np.asarray(cache_k, np.float32)[0].reshape(1280, PAGE * 256))
    cv = np.ascontiguousarray(np.asarray(cache_v, np.float32)[0].reshape(1280, PAGE * 256))
    cki = np.ascontiguousarray(np.asarray(cache_kidx, np.float32)[0].reshape(1280, PAGE * 64))
    w_in0 = np.ascontiguousarray(np.asarray(w_in, np.float32)[0])
    w_out0 = np.ascontiguousarray(np.asarray(w_out, np.float32)[0])
    pow2 = (2.0 ** -(np.arange(KBIS, dtype=np.float32) + 1)).astype(np.float32)
    in_maps = []
    slots_all = []
    for c in range(8):
        b, j = c // 2, c % 2
        own = T_OWN[j]
        oth = T_OWN[1 - j]
        slots = []
        for i in range(8):
            slots += [own[i], oth[i]]
        slots_all.append(slots)
        xall = np.zeros((2176, D), np.float32)
        pos = np.zeros(2176, np.int64)
        for s_, t in enumerate(slots):
            xall[s_ * 128:(s_ + 1) * 128] = x_prompt[b, t * 128:(t + 1) * 128]
            pos[s_ * 128:(s_ + 1) * 128] = np.arange(t * 128, (t + 1) * 128)
        xall[2048:2052] = x_sample[c]
        pos[2048:2052] = PAST + np.arange(4)
        for i in range(8):
            t = own[i]
            if t > 0:
                xall[2052 + 2 * i:2054 + 2 * i] = x_prompt[b, t * 128 - 2:t * 128]
        c32, s32 = _rope_tables(pos, 16)
        c16, s16 = _rope_tables(pos, 8)
        cm = np.zeros((128, 512), np.float32)
        tri = np.where(np.arange(128)[None, :] <= np.arange(128)[:, None], 0.0, -30000.0).astype(np.float32)
        for par in range(2):
            cm[:, par * 256:par * 256 + 128] = tri
            other_before = oth[par] < own[par]
            cm[:, par * 256 + 128:par * 256 + 256] = 0.0 if other_before else -30000.0
        pm = (-2.0 * cm).astype(np.float32)
        in_maps.append(dict(
            xall=xall, w_in=w_in0, w_out=w_out0,
            norm_in=np.ascontiguousarray(np.asarray(norm_in, np.float32)[0].reshape(16, 128).T),
            g_q=np.ascontiguousarray(np.asarray(g_q, np.float32)[0]),
            g_k=np.ascontiguousarray(np.asarray(g_k, np.float32)[0]),
            g_ki=np.ascontiguousarray(np.asarray(g_kidx, np.float32)[0]),
            w_conv=np.ascontiguousarray(np.asarray(w_conv, np.float32)[0].reshape(3, 8, 128).transpose(2, 0, 1)),
            cos32=c32, sin32=s32, cos16=c16, sin16=s16, cmask=cm, pmask=pm, pow2=pow2,
            stc=np.ascontiguousarray(np.asarray(state_conv, np.float32)[0, c].reshape(2, 8, 128).transpose(2, 0, 1)),
            ptab=np.ascontiguousarray(np.asarray(page_table, np.int32)[c].reshape(128, 1)),
            cache_k=ck, cache_v=cv, cache_ki=cki,
            cm4=np.where(np.arange(4)[:, None] <= np.arange(4)[None, :], 0.0, -1e30).astype(np.float32),
        ))
    res = run_bass_kernel_spmd(nc, in_maps, core_ids=list(range(8)))
    R = res.results
    y_prompt = np.zeros((NB_, S, D), np.float32)
    y_sample = np.zeros((DEC_B, DEC_T, D), np.float32)
    k_p = np.zeros((1, NB_, S, NKV, HD), np.float32)
    v_p = np.zeros((1, NB_, S, NKV, HD), np.float32)
    ki_p = np.zeros((1, NB_, S, IDXD), np.float32)
    conv_p = np.zeros((1, NB_, 2, 1024), np.float32)
    k_s = np.zeros((1, DEC_B, DEC_T, NKV, HD), np.float32)
    v_s = np.zeros((1, DEC_B, DEC_T, NKV, HD), np.float32)
    ki_s = np.zeros((1, DEC_B, DEC_T, IDXD), np.float32)
    conv_s = np.zeros((1, DEC_B, 2, 1024), np.float32)
    for c in range(8):
        b, j = c // 2, c % 2
        r = R[c]
        for i, t in enumerate(T_OWN[j]):
            sl = slice(t * 128, (t + 1) * 128)
            y_prompt[b, sl] = r["y_own"][i * 128:(i + 1) * 128]
            k_p[0, b, sl] = r["k_own"][i * 128:(i + 1) * 128].reshape(128, NKV, HD)
            v_p[0, b, sl] = r["v_own"][i * 128:(i + 1) * 128].reshape(128, NKV, HD)
            ki_p[0, b, sl] = r["ki_own"][i * 128:(i + 1) * 128]
        if j == 0:
            conv_p[0, b] = r["conv_p"].transpose(1, 2, 0).reshape(2, 1024)
        y_sample[c] = r["y_s"]
        k_s[0, c] = r["k_s"].reshape(4, NKV, HD)
        v_s[0, c] = r["v_s"].reshape(4, NKV, HD)
        ki_s[0, c] = r["ki_s"]
        conv_s[0, c] = r["conv_s"].transpose(1, 2, 0).reshape(2, 1024)
    return (y_prompt, y_sample, k_p, v_p, ki_p, conv_p, k_s, v_s, ki_s, conv_s)
```

```python
import numpy as np
import concourse.bass as bass
import concourse.mybir as mybir
from concourse.bass_utils import run_bass_kernel_spmd

F32 = mybir.dt.float32
BF16 = mybir.dt.bfloat16
I32 = mybir.dt.int32
AF = mybir.ActivationFunctionType
ALU = mybir.AluOpType
AX = mybir.AxisListType

D = 2048
S = 2048
NB_ = 4
DEC_B = 8
DEC_T = 4
PAST = 16384
PAGE = 128
NPAGES = PAST // PAGE
D_IN = 7760
HD = 128
NH = 8
NKV = 2
IDXH = 16
IDXD = 64
TOPK = 256
EPS = 1e-6
THETA = 500000.0
KBIS = 14
NEG = -30000.0
C_Q, C_K, C_V, C_ZA, C_QI, C_KI, C_WI, C_H, C_B, C_C, C_ZC = 0, 1024, 1280, 1536, 2560, 3584, 3648, 3664, 4688, 5712, 6736
T_OWN = ([0, 3, 4, 7, 8, 11, 12, 15], [1, 2, 5, 6, 9, 10, 13, 14])
ENGS = ("sync", "act", "dve", "pool", "pe")
NOSELF = ("pe",)


class Prog:
    def __init__(self, nc):
        self.nc = nc
        self.ops = {e: [] for e in ENGS}
        self.count = {e: 0 for e in ENGS}
        self.dcount = {}
        self.last_write = {}
        self.readers = {}
        self.waited = {e: {} for e in ENGS}
        self.pending_barrier = {e: None for e in ENGS}
        import os
        self.limit = int(os.environ.get('DBG_OPS', '100000000'))
        self.total = 0

    def _deps(self, eng, reads, writes):
        deps = {}

        def need(sv):
            if sv is None:
                return
            k, v = sv
            if deps.get(k, 0) < v:
                deps[k] = v

        for r in reads:
            need(self.last_write.get(r))
        for w in writes:
            need(self.last_write.get(w))
            for rd in self.readers.get(w, ()):
                need(rd)
        pb = self.pending_barrier[eng]
        if pb is not None:
            for kv in pb:
                need(kv)
            self.pending_barrier[eng] = None
        out = []
        for k, v in deps.items():
            if k == eng and eng in NOSELF:
                continue
            if self.waited[eng].get(k, 0) >= v:
                continue
            self.waited[eng][k] = v
            out.append((k, v))
        return out

    def _commit(self, me, reads, writes):
        for r in reads:
            self.readers.setdefault(r, []).append(me)
        for w in writes:
            self.last_write[w] = me
            self.readers[w] = []

    def op(self, eng, fn, reads=(), writes=()):
        self.total += 1
        if self.total > self.limit:
            return
        reads = tuple(reads)
        writes = tuple(writes)
        waits = self._deps(eng, reads, writes)
        self.count[eng] += 1
        self.ops[eng].append(("op", fn, waits))
        self._commit((eng, self.count[eng]), reads, writes)

    def dma(self, queue, fn, dsem, reads=(), writes=()):
        self.total += 1
        if self.total > self.limit:
            return
        reads = tuple(reads)
        writes = tuple(writes)
        key = ("d", dsem)
        waits = self._deps(queue, reads, writes)
        self.dcount[key] = self.dcount.get(key, 0) + 1
        self.ops[queue].append(("dma", fn, waits, key))
        self._commit((key, 16 * self.dcount[key]), reads, writes)

    def alias(self, new_key, old_keys):
        lst = self.readers.setdefault(new_key, [])
        for ok in old_keys:
            lw = self.last_write.get(ok)
            if lw is not None:
                lst.append(lw)
            lst.extend(self.readers.get(ok, ()))

    def barrier(self):
        snap = [(e, self.count[e]) for e in ENGS if self.count[e] > 0]
        snap += [(k, 16 * n) for k, n in self.dcount.items()]
        for e in ENGS:
            self.pending_barrier[e] = list(snap)

    def wait_all_dma(self, eng="sync"):
        waits = [(k, 16 * n) for k, n in self.dcount.items()]
        self.ops[eng].append(("waitonly", None, waits))

    def emit(self):
        nc = self.nc
        from contextlib import ExitStack
        with ExitStack() as es:
            sems = {}
            for e in ENGS:
                sems[e] = es.enter_context(nc.semaphore("s_" + e))
            for i, k in enumerate(self.dcount.keys()):
                sems[k] = es.enter_context(nc.semaphore("d_%d" % i))
            block = es.enter_context(nc.Block())

            def run(engname, h):
                for rec in self.ops[engname]:
                    kind, fn, waits = rec[0], rec[1], rec[2]
                    for k, v in waits:
                        h.wait_ge(sems[k], v)
                    if kind == "op":
                        fn(h).then_inc(sems[engname], 1)
                    elif kind == "dma":
                        fn(h).then_inc(sems[rec[3]], 16)

            @block.sync
            def _(h):
                run("sync", h)

            @block.scalar
            def _(h):
                run("act", h)

            @block.vector
            def _(h):
                run("dve", h)

            @block.gpsimd
            def _(h):
                run("pool", h)

            @block.tensor
            def _(h):
                run("pe", h)


def build_program(stop=99):
    nc = bass.Bass("TRN2", target_bir_lowering=False)
    P = Prog(nc)

    def din(name, shape, dt=F32):
        return nc.dram_tensor(name, list(shape), dt, kind="ExternalInput").ap()

    def dout(name, shape, dt=F32):
        return nc.dram_tensor(name, list(shape), dt, kind="ExternalOutput").ap()

    xall = din("xall", [2176, D])
    w_in = din("w_in", [D, D_IN])
    w_out = din("w_out", [D, D])
    norm_in = din("norm_in", [128, 16])
    g_q = din("g_q", [HD])
    g_k = din("g_k", [HD])
    g_ki = din("g_ki", [IDXD])
    w_conv = din("w_conv", [128, 3, 8])
    cos32 = din("cos32", [2176, 16])
    sin32 = din("sin32", [2176, 16])
    cos16 = din("cos16", [2176, 8])
    sin16 = din("sin16", [2176, 8])
    cmask_d = din("cmask", [128, 512])
    pmask_d = din("pmask", [128, 512])
    pow2_d = din("pow2", [KBIS])
    stc = din("stc", [128, 2, 8])
    ptab = din("ptab", [128, 1], I32)
    cm4_d = din("cm4", [4, 4])
    cache_k = din("cache_k", [1280, PAGE * 256])
    cache_v = din("cache_v", [1280, PAGE * 256])
    cache_ki = din("cache_ki", [1280, PAGE * 64])

    y_own = dout("y_own", [1024, D])
    y_s = dout("y_s", [4, D])
    k_own = dout("k_own", [1024, 256])
    v_own = dout("v_own", [1024, 256])
    ki_own = dout("ki_own", [1024, 64])
    conv_p = dout("conv_p", [128, 2, 8])
    k_s = dout("k_s", [4, 256])
    v_s = dout("v_s", [4, 256])
    ki_s = dout("ki_s", [4, 64])
    conv_s = dout("conv_s", [128, 2, 8])
    import os
    DUMP = os.environ.get("DBG_DUMP") == "1"
    if DUMP:
        dbg_I = dout("dbg_I", [128, 516])
        dbg_sm5 = dout("dbg_sm5", [128, 64])
        dbg_NB = dout("dbg_NB", [128, 516])
        dbg_att = dout("dbg_att", [128, 128])
        dbg_O = dout("dbg_O", [16, 132])
        dbg_Sm = dout("dbg_Sm", [128, 256])
        dbg_Wbc = dout("dbg_Wbc", [128, 64])

    BASE = 16640
    LIMIT = 229376
    cur = [BASE]
    cnt = [0]

    def alloc_at(off, shape, dt):
        cnt[0] += 1
        return nc.alloc_sbuf_tensor_at("t%d" % cnt[0], list(shape), dt, offset=off).ap()

    def nbytes(shape, dt):
        n = int(np.prod(shape[1:])) * (2 if dt == BF16 else 4)
        return (n + 63) // 64 * 64

    def sb(shape, dt):
        off = cur[0]
        cur[0] += nbytes(shape, dt)
        assert cur[0] <= LIMIT, ("SBUF overflow", cur[0])
        return alloc_at(off, shape, dt)

    ident_f = sb([128, 128], F32)
    ident_b = sb([128, 128], BF16)
    ID4 = sb([128, 512], BF16)
    ONES = sb([128, 128], BF16)
    gq_bc = sb([128, 128], F32)
    gk_bc = sb([128, 128], F32)
    gki_bc = sb([128, 64], F32)
    COS32 = sb([128, 17, 16], F32)
    SIN32 = sb([128, 17, 16], F32)
    COS16 = sb([128, 17, 8], F32)
    SIN16 = sb([128, 17, 8], F32)
    wcT = sb([128, 3, 8], F32)
    cmask = sb([128, 512], F32)
    pmask = sb([128, 512], F32)
    pow2 = sb([128, KBIS], F32)
    WIDX = sb([128, 9, 16], F32)
    ginT = sb([128, 16], F32)
    SCT = sb([128, 2, 8], F32)
    CO = sb([128, 2, 8], F32)
    COs = sb([128, 2, 8], F32)
    sml = sb([128, 80], F32)
    sml1 = sb([128, 8], F32)
    rt = sb([128, 4, 32], F32)
    bis = sb([128, 16], F32)
    Wt = sb([128, KBIS], F32)
    o_QT = cur[0]
    QT = sb([128, 8, 1024], BF16)
    KT = sb([128, 2, 2048], BF16)
    Vb = sb([128, 16, 256], BF16)
    QIT = sb([128, 8, 1024], BF16)
    KIT = sb([128, 2, 2048], BF16)
    o_MIXA = cur[0]
    MIXA = sb([128, 8, 1024], BF16)
    xstb = [alloc_at(o_MIXA + 8192 * j, [128, 2048], F32) for j in range(2)]
    QT_s = sb([128, 8, 32], BF16)
    KT_s = sb([128, 2, 32], BF16)
    V_s = sb([128, 256], BF16)
    QIT_s = sb([128, 8, 32], BF16)
    KIT_s = sb([128, 32], BF16)
    SZA_s = sb([128, 8, 32], BF16)
    MIX_s = sb([128, 16, 32], BF16)
    Us = sb([128, 8], F32)
    cvs = sb([128, 8], F32)
    XOFF = cur[0]
    xnT_oth = sb([128, 16, 1024], BF16)
    SZA = alloc_at(XOFF, [128, 8, 1024], BF16)
    MIXC = alloc_at(XOFF + 16384, [128, 8, 1024], BF16)
    ROFF = cur[0]
    RSIZE = LIMIT - ROFF
    print('SBUF: XOFF', XOFF, 'ROFF', ROFF, 'RSIZE', RSIZE)
    assert RSIZE >= 79872, RSIZE

    class Region:
        def __init__(self):
            self.o = ROFF

        def sb(self, shape, dt):
            off = self.o
            self.o += nbytes(shape, dt)
            assert self.o <= LIMIT, ("region overflow", self.o - ROFF)
            return alloc_at(off, shape, dt)

    R1 = Region()
    xnT_own = R1.sb([128, 16, 1152], BF16)
    o_x = R1.o
    xst = R1.sb([128, 2048], F32)
    o_xs = R1.o
    xs = R1.sb([128, 2048], BF16)
    Wr = [R1.sb([128, 16, 256], BF16) for _ in range(3)]
    o_tq = R1.o
    NJ = 5
    tQ = [R1.sb([128, 256], F32) for _ in range(NJ)]
    tQ2 = [R1.sb([128, 256], F32) for _ in range(NJ)]
    kbj = [R1.sb([128, 256], BF16) for _ in range(NJ)]
    rtj = [R1.sb([128, 4, 32], F32) for _ in range(NJ)]
    tA = [alloc_at(o_MIXA + 2048 * j, [128, 512], F32) for j in range(2)]
    tB = [alloc_at(o_MIXA + 4096 + 2048 * j, [128, 512], F32) for j in range(2)]
    Wr.append(alloc_at(o_MIXA + 8192, [128, 16, 256], BF16))
    U = alloc_at(o_x, [128, 8, 130], F32)
    xsb = [xs, alloc_at(o_x, [128, 2048], BF16)]
    CV = alloc_at(o_xs, [128, 8, 128], F32)

    ps = [nc.alloc_psum_tensor("ps%d" % i, [128, 512], F32).ap() for i in range(8)]
    psb = [p.bitcast(BF16) for p in ps]

    def ACT(out, in_, func, r, w, **kw):
        P.op("act", lambda e: e.activation(out=out, in_=in_, func=func, **kw), r, w)

    def TS(eng, out, in0, s1, s2, op0, op1, r, w, accum=None):
        if op1 is None:
            P.op(eng, lambda e: e.tensor_scalar(out=out, in0=in0, scalar1=s1, scalar2=None, op0=op0), r, w)
        elif accum is None:
            P.op(eng, lambda e: e.tensor_scalar(out=out, in0=in0, scalar1=s1, scalar2=s2, op0=op0, op1=op1), r, w)
        else:
            P.op(eng, lambda e: e.tensor_scalar(out=out, in0=in0, scalar1=s1, scalar2=s2, op0=op0, op1=op1,
                                                accum_out=accum), r, w)

    def TT(eng, out, in0, in1, op, r, w):
        P.op(eng, lambda e: e.tensor_tensor(out=out, in0=in0, in1=in1, op=op), r, w)

    def STT(out, in0, scalar, in1, op0, op1, r, w):
        P.op("dve", lambda e: e.scalar_tensor_tensor(out=out, in0=in0, scalar=scalar, in1=in1, op0=op0, op1=op1), r, w)

    def CP(eng, out, in_, r, w):
        P.op(eng, lambda e: e.tensor_copy(out=out, in_=in_), r, w)

    def RED(out, in_, op, r, w):
        P.op("dve", lambda e: e.tensor_reduce(out=out, in_=in_, axis=AX.X, op=op), r, w)

    def RCP(out, in_, r, w):
        P.op("dve", lambda e: e.reciprocal(out=out, in_=in_), r, w)

    def MM(out, lhsT, rhs, start, stop, r, w):
        P.op("pe", lambda e: e.matmul(out, lhsT=lhsT, rhs=rhs, start=start, stop=stop), r, w)

    def TR(out, in_, ident, r, w):
        P.op("pe", lambda e: e.transpose(out=out, in_=in_, identity=ident), r, w)

    def DMA(queue, out, in_, dsem, r, w, **kw):
        P.dma(queue, lambda e: e.dma_start(out=out, in_=in_, **kw), dsem, r, w)

    DMA("sync", ginT, norm_in, "c3", [], ["gin"])

    def load_consts():
        DMA("sync", gq_bc, g_q.partition_broadcast(128), "c0", [], ["gq"])
        DMA("sync", gk_bc, g_k.partition_broadcast(128), "c1", [], ["gk"])
        DMA("sync", gki_bc, g_ki.partition_broadcast(128), "c2", [], ["gki"])
        DMA("sync", COS32[:, 0:16, :], cos32[0:2048, :].rearrange("(t p) c -> p t c", p=128), "c4", [], ["cs0"])
        DMA("sync", SIN32[:, 0:16, :], sin32[0:2048, :].rearrange("(t p) c -> p t c", p=128), "c5", [], ["cs1"])
        DMA("sync", COS16[:, 0:16, :], cos16[0:2048, :].rearrange("(t p) c -> p t c", p=128), "c6", [], ["cs2"])
        DMA("sync", SIN16[:, 0:16, :], sin16[0:2048, :].rearrange("(t p) c -> p t c", p=128), "c7", [], ["cs3"])
        DMA("sync", COS32[:, 16, :], cos32[2048:2176, :], "c8", [], ["cs0e"])
        DMA("sync", SIN32[:, 16, :], sin32[2048:2176, :], "c9", [], ["cs1e"])
        DMA("sync", COS16[:, 16, :], cos16[2048:2176, :], "c10", [], ["cs2e"])
        DMA("sync", SIN16[:, 16, :], sin16[2048:2176, :], "c11", [], ["cs3e"])
        DMA("sync", cmask, cmask_d, "c12", [], ["cmask"])
        DMA("sync", pmask, pmask_d, "c13", [], ["pmask"])
        DMA("sync", pow2, pow2_d.partition_broadcast(128), "c14", [], ["pow2"])
        DMA("sync", wcT, w_conv, "c15", [], ["wcT"])
        DMA("sync", SCT, stc, "c16", [], ["SCT"])

    CONST_KEYS = ["gq", "gk", "gki", "gin", "cs0", "cs1", "cs2", "cs3", "cs0e", "cs1e", "cs2e", "cs3e",
                  "cmask", "pmask", "pow2", "wcT", "SCT"]
    P.op("pool", lambda e: e.memset(ident_f, 0.0), [], ["identf"])
    P.op("pool", lambda e: e.affine_select(out=ident_f, in_=ident_f, pattern=[[-1, 128]], compare_op=ALU.not_equal,
                                           fill=1.0, base=0, channel_multiplier=1), ["identf"], ["identf"])
    CP("pool", ident_b, ident_f, ["identf"], ["identb"])
    for a in range(4):
        CP("pool", ID4[:, a * 128:(a + 1) * 128], ident_f, ["identf"], ["ID4"])
    P.op("pool", lambda e: e.memset(ONES, 1.0), [], ["ONES"])

    if stop < 1:
        load_consts()
        P.wait_all_dma('sync'); P.emit(); return nc
    def tile_src(tt):
        if tt == 16:
            return xall[2048:2176, :], 128
        return xall[tt * 128:(tt + 1) * 128, :], 128

    def xn_dst(tt):
        if tt == 16:
            return xnT_own, 1024, 128
        if tt % 2 == 0:
            return xnT_own, (tt // 2) * 128, 128
        return xnT_oth, (tt // 2) * 128, 128

    p1_state = [0]

    def p1_A(tt):
        src, np_ = tile_src(tt)
        pb_ = tt % 2
        xst_, xs_ = xstb[pb_], xsb[pb_]
        kx, kxs, ksm1 = "xst%d" % pb_, "xs%d" % pb_, "sml1_%d" % pb_
        sm1 = sml1[:, 4 * pb_:4 * pb_ + 4]
        DMA("sync", xst_[:np_, :], src, "xld%d" % pb_, [], [kx])
        ACT(xs_[:np_, :], xst_[:np_, :], AF.Square, [kx], [kxs, ksm1], accum_out=sm1[:np_, 0:1])
        TS("dve", sm1[:np_, 1:2], sm1[:np_, 0:1], 1.0 / D, EPS, ALU.mult, ALU.add, [ksm1], [ksm1])
        ACT(sm1[:np_, 2:3], sm1[:np_, 1:2], AF.Sqrt, [ksm1], [ksm1])
        RCP(sm1[:np_, 3:4], sm1[:np_, 2:3], [ksm1], [ksm1])
        TS("dve", xs_[:np_, :], xst_[:np_, :], sm1[:np_, 3:4], None, ALU.mult, None, [kx, ksm1], [kxs])

    def p1_B(tt):
        src, np_ = tile_src(tt)
        pb_ = tt % 2
        xs_ = xsb[pb_]
        kxs = "xs%d" % pb_
        dst, t0, _ = xn_dst(tt)
        for half in range(2):
            bk = 2 + (2 * tt + half) % 2
            for j in range(8):
                kt = half * 8 + j
                TR(psb[bk][:, j * 128:j * 128 + np_], xs_[:np_, kt * 128:(kt + 1) * 128], ident_b[:np_, :np_],
                   [kxs, "identb"], ["ps%d" % bk])
            src_ps = psb[bk].rearrange("p (j t) -> p j t", j=8)[:, :, 0:np_]
            ACT(dst[:, half * 8:half * 8 + 8, t0:t0 + np_], src_ps, AF.Copy, ["ps%d" % bk], [("xnTraw", tt)])
        TT("pool", dst[:, :, t0:t0 + np_], dst[:, :, t0:t0 + np_], ginT.unsqueeze(2).to_broadcast([128, 16, np_]), ALU.mult,
           [("xnTraw", tt), "gin"], [("xnT", tt)])

    def p1_tile(tt):
        if tt == 0:
            p1_A(0)
        if tt + 1 <= 16:
            p1_A(tt + 1)
        p1_B(tt)

    def p1_upto(tt):
        while p1_state[0] <= min(tt, 16):
            p1_tile(p1_state[0])
            p1_state[0] += 1

    if stop < 2:
        load_consts()
        p1_upto(16)
        P.wait_all_dma('sync'); P.emit(); return nc

    wctr = [0]

    def win_cols(c0, n):
        return w_in[:, c0:c0 + n].rearrange("(kt p) c -> p kt c", p=128)

    own_tiles = [2 * i for i in range(8)] + [16]
    all_tiles = list(range(17))
    bkc = [0]
    trc = [0]
    tac = [0]

    def xn_src(tt, kt):
        dst, t0, n = xn_dst(tt)
        return dst[:, kt, t0:t0 + n], n

    def own_idx(tt):
        return tt // 2

    CK = ["cs0", "cs1", "cs2", "cs3", "cs0e", "cs1e", "cs2e", "cs3e"]

    chunks = [("k", C_K, 256, all_tiles, 2, 128, 16, gk_bc, True, 0),
              ("v", C_V, 256, all_tiles, 0, 0, 0, None, False, 0),
              ("kiw", C_KI, 80, all_tiles, 1, 64, 8, gki_bc, True, 0)]
    for c in range(4):
        chunks.append(("q", C_Q + 256 * c, 256, own_tiles, 2, 128, 16, gq_bc, True, c))
    for c in range(4):
        chunks.append(("qi", C_QI + 256 * c, 256, own_tiles, 4, 64, 8, None, False, c))

    wslot = {}

    def issue_w(ci):
        if ci >= len(chunks) or ci in wslot:
            return
        sidx = ci if ci < 3 else (ci - 3) % 2
        wslot[ci] = sidx
        DMA("pool", Wr[sidx][:, :, 0:chunks[ci][2]], win_cols(chunks[ci][1], chunks[ci][2]), "w%d" % sidx, [], ["W%d" % sidx])

    def job(ci, tt, js):
        kind, c0, ncols, _, nh, hd, rh, g_bc, do_norm, cidx = chunks[ci]
        sw = wslot[ci]
        bk = bkc[0] % 2
        bkc[0] += 1
        pk = "ps%d" % bk
        for kt in range(16):
            lhs, n = xn_src(tt, kt)
            MM(ps[bk][:, 0:ncols], lhs, Wr[sw][:, kt, 0:ncols], kt == 0, kt == 15, [("xnT", tt), "W%d" % sw], [pk])
        own = (tt == 16) or (tt % 2 == 0)
        i = own_idx(tt)
        tq, tq2, kb_, rt_ = tQ[js], tQ2[js], kbj[js], rtj[js]
        kq, kq2, kkb, krt, ksm = "tQ%d" % js, "tQ2%d" % js, "kb%d" % js, "rt%d" % js, "sml%d" % js
        sm = sml[:, 16 * js:16 * js + 16]
        if kind == "v":
            if own:
                ACT(tq[:, 0:256], ps[bk][:, 0:256], AF.Copy, [pk], [kq])
            if tt == 16:
                ACT(V_s[:, :], ps[bk][:, 0:256], AF.Copy, [pk], ["Vs"])
            else:
                ACT(Vb[:, tt, :], ps[bk][:, 0:256], AF.Copy, [pk], [("V", tt)])
            if own:
                for _ in range(6):
                    yield
                if tt == 16:
                    DMA("sync", v_s, tq[0:4, 0:256], "ko%d" % js, [kq], [])
                else:
                    DMA("sync", v_own[i * 128:(i + 1) * 128, :], tq[:, 0:256], "ko%d" % js, [kq], [])
            return
        nd = nh * hd
        ACT(tq[:, 0:ncols], ps[bk][:, 0:ncols], AF.Copy, [pk], [kq])
        if do_norm:
            for h in range(nh):
                ACT(tq2[:, h * hd:(h + 1) * hd], ps[bk][:, h * hd:(h + 1) * hd], AF.Square, [pk], [kq2, ksm],
                    accum_out=sm[:, h:h + 1])
            yield
            TS("dve", sm[:, 4:4 + nh], sm[:, 0:nh], 1.0 / hd, EPS, ALU.mult, ALU.add, [ksm], [ksm])
            yield
            ACT(sm[:, 8:8 + nh], sm[:, 4:4 + nh], AF.Sqrt, [ksm], [ksm])
            yield
            RCP(sm[:, 12:12 + nh], sm[:, 8:8 + nh], [ksm], [ksm])
            yield
            for h in range(nh):
                STT(tq2[:, h * hd:(h + 1) * hd], tq[:, h * hd:(h + 1) * hd], sm[:, 12 + h:13 + h], g_bc[:, 0:hd],
                    ALU.mult, ALU.mult, [kq, ksm, "gq", "gk", "gki"], [kq2])
                yield
            src, ksrc = tq2, kq2
        else:
            yield
            src, ksrc = tq, kq
        if kind == "kiw":
            if tt == 16:
                TS("dve", WIDX[:, 8, :], tq[:, 64:80], 1.0 / 32.0, None, ALU.mult, None, [kq], [("WIDX", 8)])
            elif own:
                TS("dve", WIDX[:, i, :], tq[:, 64:80], 1.0 / 32.0, None, ALU.mult, None, [kq], [("WIDX", i)])
        s3 = src[:, 0:nd].rearrange("p (h d) -> p h d", h=nh)
        x1 = s3[:, :, 0:rh]
        x2 = s3[:, :, rh:2 * rh]
        if rh == 16:
            cT, sT = COS32[:, tt, :], SIN32[:, tt, :]
        else:
            cT, sT = COS16[:, tt, :], SIN16[:, tt, :]
        cb = cT.unsqueeze(1).to_broadcast([128, nh, rh])
        sbb = sT.unsqueeze(1).to_broadcast([128, nh, rh])
        r4 = [rt_[:, j, 0:nh * rh].rearrange("p (h r) -> p h r", h=nh) for j in range(4)]
        TT("dve", r4[0], x1, cb, ALU.mult, [ksrc] + CK, [krt])
        TT("dve", r4[1], x2, sbb, ALU.mult, [ksrc] + CK, [krt])
        yield
        TT("dve", r4[2], x2, cb, ALU.mult, [ksrc] + CK, [krt])
        TT("dve", r4[3], x1, sbb, ALU.mult, [ksrc] + CK, [krt])
        yield
        TT("dve", x1, r4[0], r4[1], ALU.subtract, [krt], [ksrc])
        TT("dve", x2, r4[2], r4[3], ALU.add, [krt], [ksrc])
        yield
        if kind == "kiw":
            P.op("pool", lambda e: e.memset(kb_[:, 64:192], 0.0), [], [kkb])
            CP("dve", kb_[:, 0:64], src[:, 0:64], [ksrc], [kkb])
            ACT(kb_[:, 192:256], src[:, 0:64], AF.Copy, [ksrc], [kkb])
            nblk = 2
        else:
            CP("dve", kb_[:, 0:256], src[:, 0:256], [ksrc], [kkb])
            nblk = 2
        yield
        tb = 2 + trc[0] % 2
        trc[0] += 1
        for j in range(nblk):
            TR(psb[tb][:, j * 128:(j + 1) * 128], kb_[:, j * 128:(j + 1) * 128], ident_b, [kkb, "identb"], ["ps%d" % tb])
        for j in range(nblk):
            if kind == "k":
                dst, wk = (KT_s[:, j, 0:32], ("KTs", j)) if tt == 16 else (KT[:, j, tt * 128:(tt + 1) * 128], ("KT", tt))
            elif kind == "kiw":
                if tt == 16:
                    if j == 1:
                        continue
                    dst, wk = KIT_s[:, 0:32], "KITs"
                else:
                    dst, wk = KIT[:, j, tt * 128:(tt + 1) * 128], ("KIT", tt, j)
            elif kind == "q":
                hh = 2 * cidx + j
                dst, wk = (QT_s[:, hh, 0:32], ("QTs", hh)) if tt == 16 else (QT[:, hh, i * 128:(i + 1) * 128], ("QT", i, hh))
            else:
                hh = 2 * cidx + j
                dst, wk = (QIT_s[:, hh, 0:32], ("QITs", hh)) if tt == 16 else (QIT[:, hh, i * 128:(i + 1) * 128], ("QIT", i, hh))
            ncp = dst.shape[-1]
            ACT(dst, psb[tb][:, j * 128:j * 128 + ncp], AF.Copy, ["ps%d" % tb], [wk])
        if own and kind in ("k", "kiw"):
            for _ in range(6):
                yield
            if kind == "k":
                if tt == 16:
                    DMA("sync", k_s, src[0:4, 0:256], "ko%d" % js, [ksrc], [])
                else:
                    DMA("sync", k_own[i * 128:(i + 1) * 128, :], src[:, 0:256], "ko%d" % js, [ksrc], [])
            else:
                if tt == 16:
                    DMA("sync", ki_s, src[0:4, 0:64], "ko%d" % js, [ksrc], [])
                else:
                    DMA("sync", ki_own[i * 128:(i + 1) * 128, :], src[:, 0:64], "ko%d" % js, [ksrc], [])

    def all_jobs():
        for tt in all_tiles:
            for ci in range(3):
                yield (ci, tt)
        for ci in range(3, len(chunks)):
            for tt in chunks[ci][3]:
                yield (ci, tt)

    bkB = [0]
    wB = [0]

    def gen_B():
        oth_keys = [("xnT", tt) for tt in range(1, 16, 2)]
        tok_chunks = [(0, 512), (512, 512), (1024, 32)]

        def fproj(s, sub, c0, n):
            bk = 4 + bkB[0] % 4
            bkB[0] += 1
            tts = [("xnT", 2 * i) for i in range(8)] if c0 < 1024 else [("xnT", 16)]
            for kt in range(16):
                MM(ps[bk][:, 0:n], Wr[s][:, kt, sub * 128:(sub + 1) * 128], xnT_own[:, kt, c0:c0 + n], kt == 0, kt == 15,
                   tts + ["W%d" % s], ["ps%d" % bk])
            return bk

        for f2 in range(4):
            s = 2 + wB[0] % 2
            wB[0] += 1
            DMA("pool", Wr[s][:, :, 0:256], win_cols(C_ZA + 256 * f2, 256), "w%d" % s, [], ["W%d" % s])
            for sub in range(2):
                f = 2 * f2 + sub
                for (c0, n) in tok_chunks:
                    bk = fproj(s, sub, c0, n)
                    if c0 < 1024:
                        ACT(SZA[:, f, c0:c0 + n], ps[bk][:, 0:n], AF.Silu, ["ps%d" % bk] + oth_keys, [("SZA", f, c0)])
                        yield
                    else:
                        ACT(SZA_s[:, f, 0:32], ps[bk][:, 0:32], AF.Silu, ["ps%d" % bk], [("SZAs", f)])
                    yield

        sA, sB = 2, 3

        def load_A(f_):
            DMA("pool", Wr[sA][:, :, 0:128], win_cols(C_H + 128 * f_, 128), "w%d" % sA, [], ["W%d" % sA])
            DMA("pool", Wr[sA][:, :, 128:256], win_cols(C_C + 128 * f_, 128), "w%d" % sA, [], ["W%d" % sA])

        def load_B(f_):
            DMA("pool", Wr[sB][:, :, 0:128], win_cols(C_B + 128 * f_, 128), "w%d" % sB, [], ["W%d" % sB])
            DMA("pool", Wr[sB][:, :, 128:256], win_cols(C_ZC + 128 * f_, 128), "w%d" % sB, [], ["W%d" % sB])

        load_A(0)
        load_B(0)
        for f in range(8):
            for ci, (c0, n) in enumerate(tok_chunks):
                bh = fproj(sA, 0, c0, n)
                yield
                bc = fproj(sA, 1, c0, n)
                yield
                a = tac[0] % 2
                tac[0] += 1
                ACT(tA[a][:, 0:n], ps[bh][:, 0:n], AF.Copy, ["ps%d" % bh], ["tA%d" % a])
                if c0 < 1024:
                    TT("dve", U[:, 4 * ci:4 * ci + 4, 2:130], tA[a][:, 0:512].rearrange("p (i t) -> p i t", i=4),
                       ps[bc][:, 0:512].rearrange("p (i t) -> p i t", i=4), ALU.mult, ["tA%d" % a, "ps%d" % bc], [("U", ci)])
                else:
                    TT("dve", tB[a][:, 0:32], tA[a][:, 0:32], ps[bc][:, 0:32], ALU.mult, ["tA%d" % a, "ps%d" % bc], ["tB%d" % a])
                    CP("dve", U[:, :, 0:2], tB[a][:, 4:20].rearrange("p (i two) -> p i two", two=2), ["tB%d" % a], [("U", "h")])
                    CP("dve", Us[:, 2:6], tB[a][:, 0:4], ["tB%d" % a], ["Us"])
                    CP("dve", Us[:, 0:2], SCT[:, :, f], ["SCT"], ["Us"])
            if f + 1 < 8:
                load_A(f + 1)
            ukeys = [("U", 0), ("U", 1), ("U", "h")]
            TS("dve", CV, U[:, :, 0:128], wcT[:, 0, f:f + 1], None, ALU.mult, None, ukeys + ["wcT"], ["CV"])
            STT(CV, U[:, :, 1:129], wcT[:, 1, f:f + 1], CV, ALU.mult, ALU.add, ukeys + ["wcT", "CV"], ["CV"])
            STT(CV, U[:, :, 2:130], wcT[:, 2, f:f + 1], CV, ALU.mult, ALU.add, ukeys + ["wcT", "CV"], ["CV"])
            TS("dve", cvs[:, 0:4], Us[:, 0:4], wcT[:, 0, f:f + 1], None, ALU.mult, None, ["Us", "wcT"], ["cvs"])
            STT(cvs[:, 0:4], Us[:, 1:5], wcT[:, 1, f:f + 1], cvs[:, 0:4], ALU.mult, ALU.add, ["Us", "wcT", "cvs"], ["cvs"])
            STT(cvs[:, 0:4], Us[:, 2:6], wcT[:, 2, f:f + 1], cvs[:, 0:4], ALU.mult, ALU.add, ["Us", "wcT", "cvs"], ["cvs"])
            CP("dve", CO[:, :, f], U[:, 7, 128:130], ukeys, [("CO", f)])
            CP("dve", COs[:, :, f], Us[:, 4:6], ["Us"], [("COs", f)])
            for ci, (c0, n) in enumerate(tok_chunks):
                bb = fproj(sB, 0, c0, n)
                yield
                bz = fproj(sB, 1, c0, n)
                yield
                a = tac[0] % 2
                tac[0] += 1
                ACT(tA[a][:, 0:n], ps[bz][:, 0:n], AF.Silu, ["ps%d" % bz], ["tA%d" % a])
                if c0 < 1024:
                    TT("dve", tB[a][:, 0:512], CV[:, 4 * ci:4 * ci + 4, :].rearrange("p i t -> p (i t)"), ps[bb][:, 0:512], ALU.mult,
                       ["CV", "ps%d" % bb], ["tB%d" % a])
                    TT("dve", MIXC[:, f, c0:c0 + 512], tB[a][:, 0:512], tA[a][:, 0:512], ALU.mult,
                       ["tA%d" % a, "tB%d" % a] + oth_keys, [("MIXC", f, ci)])
                else:
                    TT("dve", tB[a][:, 0:4], cvs[:, 0:4], ps[bb][:, 0:4], ALU.mult, ["cvs", "ps%d" % bb], ["tB%d" % a])
                    TT("dve", MIX_s[:, 8 + f, 0:4], tB[a][:, 0:4], tA[a][:, 0:4], ALU.mult, ["tA%d" % a, "tB%d" % a], [("MIXs", 8 + f)])
            if f + 1 < 8:
                load_B(f + 1)
        DMA("sync", conv_p, CO, "cvo", [("CO", f) for f in range(8)], [])
        DMA("sync", conv_s, COs, "cvo2", [("COs", f) for f in range(8)], [])


    issue_w(0)
    issue_w(1)
    issue_w(2)
    p1_upto(2)
    load_consts()
    free_slots = list(range(NJ))
    active = []
    p1_alias_done = [False]
    gB = [None]
    gB_done = [False]
    want_B = [False]

    def make_B():
        okeys = [("xnT", t_) for t_ in range(1, 16, 2)]
        for f_ in range(8):
            for c_ in (0, 512):
                P.alias(("SZA", f_, c_), okeys)
            for c_ in range(2):
                P.alias(("MIXC", f_, c_), okeys)
        gB[0] = gen_B()

    jit = all_jobs()
    pending = next(jit, None)
    while pending is not None or active:
        while pending is not None and free_slots:
            ci, tt = pending
            p1_upto(tt + 1)
            if p1_state[0] > 16 and not p1_alias_done[0]:
                p1_alias_done[0] = True
                for _k in ("tA0", "tA1", "tB0", "tB1"):
                    P.alias(_k, ["xst0"])
                P.alias("W3", ["xst1"])
                for _k in (("U", 0), ("U", 1), ("U", "h")):
                    P.alias(_k, ["xs1"])
                P.alias("CV", ["xs0"])
            if ci >= 3:
                want_B[0] = True
            if ci >= 3:
                issue_w(ci)
                issue_w(ci + 1)
            js = free_slots.pop(0)
            g_ = job(ci, tt, js)
            try:
                next(g_)
                active.append((g_, js))
            except StopIteration:
                free_slots.append(js)
            pending = next(jit, None)
        for ent in list(active):
            g, js = ent
            try:
                next(g)
            except StopIteration:
                active.remove(ent)
                free_slots.append(js)
        if want_B[0] and gB[0] is None:
            make_B()
        if gB[0] is not None and not gB_done[0]:
            try:
                next(gB[0])
            except StopIteration:
                gB_done[0] = True
    if gB[0] is None:
        make_B()
    if not gB_done[0]:
        for _ in gB[0]:
            pass

    if stop < 4:
        P.wait_all_dma('sync'); P.emit(); return nc
    P.barrier()

    R2 = Region()
    IS = [R2.sb([128, 2048], F32) for _ in range(3)]
    NBm = [R2.sb([128, 2048], BF16) for _ in range(3)]
    junk = R2.sb([128, 2048], BF16)
    Rh = [R2.sb([128, 512], BF16) for _ in range(6)]
    Dg = [R2.sb([128, 16, 128], BF16) for _ in range(2)]
    PT = [R2.sb([128, 512], BF16) for _ in range(5)]
    rz = R2.sb([128, 512], F32)
    otmp = R2.sb([128, 512], F32)
    t256 = R2.sb([128, 256], F32)
    cmaskb = R2.sb([128, 512], BF16)
    Zacc = [R2.sb([128, 512], F32) for _ in range(2)]
    ONESF = R2.sb([128, 128], F32)
    otm = [R2.sb([128, 512], F32) for _ in range(2)]
    rzl = [R2.sb([128, 512], F32) for _ in range(2)]
    rzz = [R2.sb([128, 512], F32) for _ in range(2)]
    rhc = [0]
    ptc = [0]
    dbk = [0]
    stc_ = [0]
    SCALE = float(HD) ** -0.5
    CP("pool", cmaskb, cmask, ["cmask"], ["cmaskb"])
    P.op("pool", lambda e: e.memset(ONESF, 1.0), [], ["ONESF"])

    def indexer(i):
        ib = i % 2
        isb = i % 3
        nk = 256 * (i + 1)
        par = i % 2
        for h in range(16):
            TS("pool", Dg[ib][:, h, :], ident_b, WIDX[:, i, h:h + 1], 0.0, ALU.mult, ALU.add, ["identb", ("WIDX", i)], [("Dg", ib)])
        nch = (nk + 511) // 512
        for c in range(nch):
            k0 = 512 * c
            ncol = min(512, nk - k0)
            m0 = nk - 256
            has_mask = (k0 + ncol > m0)
            prev = None
            for h in range(16):
                hp = h % 2
                bk = (0, 1, 7)[dbk[0] % 3]
                dbk[0] += 1
                MM(ps[bk][:, 0:ncol], QIT[:, h // 2, i * 128:(i + 1) * 128],
                   KIT[:, hp, k0:k0 + ncol], True, True,
                   [("QIT", i, h // 2)] + [("KIT", tt, hp) for tt in range(k0 // 128, (k0 + ncol) // 128)], ["ps%d" % bk])
                r = rhc[0] % 6
                rhc[0] += 1
                ACT(Rh[r][:, 0:ncol], ps[bk][:, 0:ncol], AF.Relu, ["ps%d" % bk], ["Rh%d" % r])
                if prev is not None:
                    ph, pr = prev
                    MM(ps[2][:, 0:ncol], Dg[ib][:, ph, :], Rh[pr][:, 0:ncol], ph == 0, False, [("Dg", ib), "Rh%d" % pr], ["ps2"])
                prev = (h, r)
                yield
            ph, pr = prev
            MM(ps[2][:, 0:ncol], Dg[ib][:, ph, :], Rh[pr][:, 0:ncol], False, not has_mask, [("Dg", ib), "Rh%d" % pr], ["ps2"])
            if has_mask:
                lo = m0 - k0
                MM(ps[2][:, lo:lo + 256], ident_b, cmaskb[:, par * 256:(par + 1) * 256], False, True, ["identb", "cmaskb"], ["ps2"])
            ACT(IS[isb][:, k0:k0 + ncol], ps[2][:, 0:ncol], AF.Copy, ["ps2"], [("IS", isb)])

    def bisect(i):
        ib = i % 3
        nk = 256 * (i + 1)
        par = i % 2
        isk = ("IS", ib)
        X = IS[ib][:, 0:nk]
        RED(bis[:, 0:1], X, ALU.max, [isk], ["bis"])
        TT("dve", t256, IS[ib][:, nk - 256:nk], pmask[:, par * 256:(par + 1) * 256], ALU.add, [isk, "pmask"], ["t256"])
        RED(bis[:, 1:2], t256, ALU.min, ["t256"], ["bis"])
        if nk > 256:
            RED(bis[:, 2:3], IS[ib][:, 0:nk - 256], ALU.min, [isk], ["bis"])
            TT("dve", bis[:, 1:2], bis[:, 1:2], bis[:, 2:3], ALU.min, ["bis"], ["bis"])
        TT("dve", bis[:, 3:4], bis[:, 0:1], bis[:, 1:2], ALU.subtract, ["bis"], ["bis"])
        STT(bis[:, 4:5], bis[:, 3:4], -(2.0 ** -10), bis[:, 1:2], ALU.mult, ALU.add, ["bis"], ["bis"])
        TS("dve", bis[:, 4:5], bis[:, 4:5], -1e-6, None, ALU.add, None, ["bis"], ["bis"])
        TT("dve", bis[:, 5:6], bis[:, 0:1], bis[:, 4:5], ALU.subtract, ["bis"], ["bis"])
        TS("dve", Wt, pow2, bis[:, 5:6], None, ALU.mult, None, ["pow2", "bis"], ["Wt"])
        TT("dve", bis[:, 6:7], bis[:, 4:5], Wt[:, 0:1], ALU.add, ["bis", "Wt"], ["bis"])
        for k in range(KBIS):
            TS("dve", junk[:, 0:nk], X, bis[:, 6:7], None, ALU.is_ge, ALU.add, [isk, "bis"], ["junk", "bis"], accum=bis[:, 7:8])
            TS("dve", bis[:, 8:9], bis[:, 7:8], float(TOPK), 0.5, ALU.is_ge, ALU.subtract, ["bis"], ["bis"])
            if k < KBIS - 1:
                STT(bis[:, 6:7], bis[:, 8:9], Wt[:, k:k + 1], bis[:, 6:7], ALU.mult, ALU.add, ["bis", "Wt"], ["bis"])
            else:
                TS("dve", bis[:, 8:9], bis[:, 8:9], -0.5, None, ALU.add, None, ["bis"], ["bis"])
                STT(bis[:, 9:10], bis[:, 8:9], Wt[:, k:k + 1], bis[:, 6:7], ALU.mult, ALU.add, ["bis", "Wt"], ["bis"])
        TS("dve", NBm[ib][:, 0:nk], X, bis[:, 9:10], NEG, ALU.is_lt, ALU.mult, [isk, "bis"], [("NB", ib)])

    def attend_main(i):
        ib = i % 3
        nsl = 2 * i + 2
        seq = [(g, s_) for g in range(2) for s_ in range(nsl)]
        pend = None

        def s_mm(g, s_):
            sb_ = (3, 6)[stc_[0] % 2]
            stc_[0] += 1
            MM(ps[sb_].rearrange("p (h t) -> p h t", h=4), KT[:, g, s_ * 128:(s_ + 1) * 128],
               QT[:, 4 * g:4 * g + 4, i * 128:(i + 1) * 128], True, False,
               [("KT", s_)] + [("QT", i, 4 * g + hh) for hh in range(4)], ["ps%d" % sb_])
            MM(ps[sb_], NBm[ib][:, s_ * 128:(s_ + 1) * 128], ID4, False, True, [("NB", ib), "ID4"], ["ps%d" % sb_])
            r = ptc[0] % 5
            ptc[0] += 1
            ACT(PT[r], ps[sb_], AF.Exp, ["ps%d" % sb_], ["PT%d" % r], scale=SCALE)
            return r

        def pv_mm(g, s_, r):
            MM(ps[4], Vb[:, s_, g * 128:(g + 1) * 128], PT[r], s_ == 0, s_ == nsl - 1, [("V", s_), "PT%d" % r], ["ps4"])
            if s_ % 2 == 0:
                MM(ps[5], ONES, PT[r], s_ == 0, False, ["ONES", "PT%d" % r], ["ps5"])
            elif s_ == 1:
                CP("pool", Zacc[g], PT[r], ["PT%d" % r], ["Zacc%d" % g])
            else:
                TT("pool", Zacc[g], Zacc[g], PT[r], ALU.add, ["PT%d" % r, "Zacc%d" % g], ["Zacc%d" % g])
            if s_ == nsl - 1:
                MM(ps[5], ONESF, Zacc[g], False, True, ["ONESF", "Zacc%d" % g], ["ps5"])
                ACT(otm[g], ps[4], AF.Copy, ["ps4"], ["otm%d" % g])
                ACT(rzl[g], ps[5], AF.Ln, ["ps5"], ["rzl%d" % g])
                ACT(rzz[g], rzl[g], AF.Exp, ["rzl%d" % g], ["rzz%d" % g], scale=-1.0)
                TT("pool", otm[g], otm[g], rzz[g], ALU.mult, ["otm%d" % g, "rzz%d" % g], ["otm%d" % g])
                TT("pool", MIXA[:, 4 * g:4 * g + 4, i * 128:(i + 1) * 128], otm[g].rearrange("p (h t) -> p h t", h=4),
                   SZA[:, 4 * g:4 * g + 4, i * 128:(i + 1) * 128], ALU.mult,
                   ["otm%d" % g] + [("SZA", 4 * g + hh, (i // 4) * 512) for hh in range(4)], [("MIXA", i, g)])

        for (g, s_) in seq:
            r = s_mm(g, s_)
            if pend is not None:
                pv_mm(*pend)
            pend = (g, s_, r)
            yield
        pv_mm(*pend)

    def attend_epi(i):
        pass

    def drain(g_):
        for _ in g_:
            pass

    drain(indexer(0))
    bisect(0)
    drain(indexer(1))
    drain(indexer(2))
    for i in range(8):
        if i + 1 < 8:
            bisect(i + 1)
        ga = attend_main(i)
        if i + 3 < 8:
            gi = indexer(i + 3)
            n_att = 2 * (2 * i + 2)
            n_idx = 16 * ((256 * (i + 4) + 511) // 512)
            gi_alive = True
            acc = 0
            for _ in ga:
                acc += n_idx
                while acc >= n_att:
                    acc -= n_att
                    if gi_alive:
                        try:
                            next(gi)
                        except StopIteration:
                            gi_alive = False
            if gi_alive:
                drain(gi)
        else:
            drain(ga)

    if stop < 5:
        P.wait_all_dma('sync'); P.emit(); return nc
    P.barrier()

    R4 = Region()
    I_all = R4.sb([128, 129, 4], F32)
    NBs = R4.sb([128, 129, 4], F32)
    junk_s = R4.sb([128, 132], F32)
    idx16 = R4.sb([128, 16], I32)
    ptb = R4.sb([128, 1], I32)
    Q64 = R4.sb([128, 8, 2, 4], BF16)
    Q64z = R4.sb([128, 2, 64], BF16)
    cnt_b = R4.sb([128, 4], BF16)
    idx4 = R4.sb([128, 4], I32)
    sm5 = R4.sb([128, 64], F32)
    Dw = R4.sb([128, 64], F32)
    Wbc = R4.sb([128, 64], F32)
    dg4 = R4.sb([128, 8], F32)
    ones_f = R4.sb([128, 128], F32)
    cm4 = R4.sb([128, 4], F32)
    Vn = R4.sb([128, 2, 130], BF16)
    Pn = R4.sb([128, 2, 16], BF16)
    Sn = R4.sb([128, 2, 16], F32)
    att = R4.sb([128, 128], F32)
    o_sub = R4.o
    KI = R4.sb([128, 128, 64], F32)
    KIb = R4.sb([128, 128, 64], BF16)
    kiTg = [R4.sb([128, 8, 128], BF16) for _ in range(2)]
    tmpi = R4.sb([128, 8, 64], F32)
    R5 = Region()
    R5.o = o_sub
    Kc = []
    Vc = [R5.sb([128, 8, 256], F32) for _ in range(3)]
    Kb = [R5.sb([128, 8, 256], BF16) for _ in range(3)]
    KTc = [R5.sb([128, 8, 2, 128], BF16) for _ in range(2)]
    Vb2 = [R5.sb([128, 8, 2, 130], BF16) for _ in range(3)]
    Sm = R5.sb([128, 8, 2, 16], F32)
    Pb = [R5.sb([128, 8, 2, 16], BF16) for _ in range(2)]

    def IDMA(out, in_, idx_ap, dsem, r, w):
        P.dma("pool", lambda e: e.indirect_dma_start(out=out, out_offset=None, in_=in_,
                                                     in_offset=bass.IndirectOffsetOnAxis(ap=idx_ap, axis=0)), dsem, r, w)

    sstage = [0]

    def gen_sample():
        DMA("sync", ptb, ptab, "s0", [], ["ptb"])
        DMA("sync", cm4[0:4, :], cm4_d, "s1", [], ["cm4"])
        cki4 = cache_ki.rearrange("p (c x) -> (p c) x", c=4)
        for q4 in range(4):
            TS("dve", idx4[:, q4:q4 + 1], ptb[:, 0:1], 4.0, float(q4), ALU.mult, ALU.add, ["ptb"], [("idx4", q4)])
            IDMA(KI[:, q4 * 32:(q4 + 1) * 32, :].rearrange("p j d -> p (j d)"), cki4, idx4[:, q4:q4 + 1], "s2_%d" % q4,
                 [("idx4", q4)], [("KI", q4)])
        for c in range(16):
            TS("dve", idx16[:, c:c + 1], ptb[:, 0:1], 16.0, float(c), ALU.mult, ALU.add, ["ptb"], ["idx16"])
        P.op("pool", lambda e: e.memset(ones_f, 1.0), [], ["ones_f"])
        P.op("pool", lambda e: e.memset(I_all[:, 128, :], -1e30), [], ["I128"])
        P.op("pool", lambda e: e.memset(Vn, 1.0), [], ["Vn"])
        CP("dve", Q64[0:64, :, 0, :], QIT_s[0:64, :, 0:4], [("QITs", j) for j in range(8)], ["Q64a"])
        DMA("sync", Q64[0:64, :, 1, :], QIT_s[64:128, :, 0:4], "s3", [("QITs", j) for j in range(8)], ["Q64b"])
        P.op("pool", lambda e: e.memset(Q64z, 0.0), [], ["Q64z"])
        CP("dve", Q64z[0:64, 0, :], Q64[0:64].rearrange("p a b t -> p (a b t)"), ["Q64a", "Q64b", "Q64z"], ["Q64z"])
        DMA("sync", Q64z[64:128, 1, :], Q64[0:64].rearrange("p a b t -> p (a b t)"), "s3b", ["Q64a", "Q64b", "Q64z"], ["Q64zb"])
        TT("dve", Dw[0:4, :].rearrange("p (h t) -> p h t", h=16), WIDX[0:4, 8, :].unsqueeze(2).to_broadcast([4, 16, 4]),
           ident_f[0:4, 0:4].unsqueeze(1).to_broadcast([4, 16, 4]), ALU.mult, [("WIDX", 8), "identf"], ["Dw"])
        MM(ps[4][:, 0:64], ones_f[0:4, :], Dw[0:4, :], True, True, ["ones_f", "Dw"], ["ps4"])
        CP("dve", Wbc, ps[4][:, 0:64], ["ps4"], ["Wbc"])
        for q4 in range(4):
            if q4 % 2 == 1:
                CP("pool", KIb[:, q4 * 32:(q4 + 1) * 32, :], KI[:, q4 * 32:(q4 + 1) * 32, :], [("KI", q4)], [("KIb", q4)])
            else:
                ACT(KIb[:, q4 * 32:(q4 + 1) * 32, :], KI[:, q4 * 32:(q4 + 1) * 32, :], AF.Copy, [("KI", q4)], [("KIb", q4)])
        wb3 = Wbc.unsqueeze(1).to_broadcast([128, 8, 64])
        for gi in range(16):
            j0 = gi * 8
            tb = gi % 2
            for jp in range(4):
                TR(psb[tb][:, jp * 128:(jp + 1) * 128], KIb[:, j0 + 2 * jp:j0 + 2 * jp + 2, :].rearrange("p a d -> p (a d)"), ident_b,
                   [("KIb", j0 // 32), "identb"], ["ps%d" % tb])
            ACT(kiTg[tb][:, 0:4, :], psb[tb][:, 0:512].rearrange("p (j t) -> p j t", j=4), AF.Copy, ["ps%d" % tb], ["kiTg%d" % tb])
            db = 2 + gi % 2
            for jj in range(8):
                MM(ps[db][:, jj * 64:(jj + 1) * 64], kiTg[tb][:, jj // 2, :], Q64z[:, jj % 2, :], True, True,
                   ["kiTg%d" % tb, "Q64z", "Q64zb"], ["ps%d" % db])
            STT(tmpi, ps[db].rearrange("p (j c) -> p j c", j=8), 0.0, wb3, ALU.max, ALU.mult, ["ps%d" % db, "Wbc"], ["tmpi"])
            RED(I_all[:, j0:j0 + 8, :], tmpi.rearrange("p j (h t) -> p j t h", h=16), ALU.add, ["tmpi"], [("Ia", gi)])
            yield
        MM(ps[4][0:4, 0:64], KIT_s[:, 0:4], Q64z[:, 0, :], True, True, ["KITs", "Q64z", "Q64zb"], ["ps4"])
        STT(tmpi[0:4, 0, :], ps[4][0:4, 0:64], 0.0, Wbc[0:4, :], ALU.max, ALU.mult, ["ps4", "Wbc"], ["tmpi"])
        RED(sm5[0:4, 0:4], tmpi[0:4, 0, :].rearrange("p (h t) -> p t h", h=16), ALU.add, ["tmpi"], ["sm5"])
        TT("dve", I_all[0:4, 128, :], sm5[0:4, 0:4], cm4[0:4, :], ALU.add, ["sm5", "cm4", "I128"], ["I128"])
        IA_KEYS = [("Ia", gi) for gi in range(16)] + ["I128"]

        def bcast4(dst, src_col4, negate=False):
            P.op("pe", lambda e: e.transpose(out=ps[5][0:4, 0:128], in_=src_col4, identity=ident_f), ["sm5", "identf"], ["ps5"])
            RED(dg4[0:4, 0:1], ps[5][0:4, 0:128], ALU.max, ["ps5"], ["dg4"])
            TS("dve", dg4[0:4, 4:8], ident_f[0:4, 0:4], dg4[0:4, 0:1], None, ALU.mult, None, ["dg4", "identf"], ["dg4b"])
            MM(ps[5][:, 256:260], ones_f[0:4, :], dg4[0:4, 4:8], True, True, ["ones_f", "dg4b"], ["ps5"])
            TS("dve", dst, ps[5][:, 256:260], -1.0 if negate else 1.0, None, ALU.mult, None, ["ps5"], ["sm5"])

        Iv = I_all.rearrange("p s t -> p t s")
        RED(sm5[:, 8:12], Iv, ALU.max, IA_KEYS, ["sm5"])
        bcast4(sm5[:, 12:16], sm5[:, 8:12])
        P.op("dve", lambda e: e.tensor_reduce(out=sm5[:, 8:12], in_=Iv[:, :, 0:128], axis=AX.X, op=ALU.min, negate=True), IA_KEYS + ["sm5"], ["sm5"])
        bcast4(sm5[:, 16:20], sm5[:, 8:12], negate=True)
        TT("dve", sm5[:, 20:24], sm5[:, 12:16], sm5[:, 16:20], ALU.subtract, ["sm5"], ["sm5"])
        STT(sm5[:, 24:28], sm5[:, 20:24], -(2.0 ** -10), sm5[:, 16:20], ALU.mult, ALU.add, ["sm5"], ["sm5"])
        TS("dve", sm5[:, 24:28], sm5[:, 24:28], -1e-6, None, ALU.add, None, ["sm5"], ["sm5"])
        TT("dve", sm5[:, 28:32], sm5[:, 12:16], sm5[:, 24:28], ALU.subtract, ["sm5"], ["sm5"])
        STT(sm5[:, 32:36], sm5[:, 28:32], 0.5, sm5[:, 24:28], ALU.mult, ALU.add, ["sm5"], ["sm5"])
        def bisect_s():
            KS = 14
            for k in range(KS):
                for t in range(4):
                    TS("dve", junk_s[:, 0:129], I_all[:, :, t], sm5[:, 32 + t:33 + t], None, ALU.is_ge, ALU.add, IA_KEYS + ["sm5"],
                       ["junk_s", "cntp"], accum=cnt_b[:, t:t + 1])
                MM(ps[4][:, 0:4], ONES, cnt_b, True, True, ["ONES", "cntp"], ["ps4"])
                TS("dve", sm5[:, 40:44], ps[4][:, 0:4], float(TOPK), 0.5, ALU.is_ge, ALU.subtract, ["ps4"], ["sm5"])
                TT("dve", sm5[:, 44:48], sm5[:, 40:44], sm5[:, 28:32], ALU.mult, ["sm5"], ["sm5"])
                yield
                if k < KS - 1:
                    STT(sm5[:, 32:36], sm5[:, 44:48], 2.0 ** -(k + 1), sm5[:, 32:36], ALU.mult, ALU.add, ["sm5"], ["sm5"])
                else:
                    STT(sm5[:, 48:52], sm5[:, 28:32], -0.5, sm5[:, 44:48], ALU.mult, ALU.add, ["sm5"], ["sm5"])
                    STT(sm5[:, 52:56], sm5[:, 48:52], 2.0 ** -(k + 1), sm5[:, 32:36], ALU.mult, ALU.add, ["sm5"], ["sm5"])
            for t in range(4):
                TS("dve", NBs[:, :, t], I_all[:, :, t], sm5[:, 52 + t:53 + t], NEG, ALU.is_lt, ALU.mult, IA_KEYS + ["sm5"], ["NBs"])


        if DUMP:
            DMA('sync', dbg_I, I_all.rearrange('p s t -> p (s t)'), 'dd0', IA_KEYS, [])
            DMA('sync', dbg_sm5, sm5, 'dd1', ['sm5'], [])
            DMA('sync', dbg_NB, NBs.rearrange('p s t -> p (s t)'), 'dd2', ['NBs'], [])
            DMA('sync', dbg_Wbc, Wbc, 'dd3', ['Wbc'], [])
        NEWK = ["Kc0", "Kc1", "Kc2", "Vc0", "Vc1", "Vc2", "Kb0", "Kb1", "Kb2", ("KTc", 0, 0), ("KTc", 0, 1), ("KTc", 1, 0), ("KTc", 1, 1),
                "Vb2_0", "Vb2_1", "Vb2_2", "Sm", "Pb0", "Pb1"]
        OLDK = [("KI", 0), ("KI", 1), ("KI", 2), ("KI", 3), ("KIb", 0), ("KIb", 1), ("KIb", 2), ("KIb", 3), "kiTg0", "kiTg1", "tmpi"]
        for _nk in NEWK:
            P.alias(_nk, OLDK)
        for c3 in range(3):
            P.op("pool", (lambda c3: (lambda e: e.memset(Vb2[c3], 1.0)))(c3), [], ["Vb2_%d" % c3])
        ck2 = cache_k.rearrange("p (c x) -> (p c) x", c=16)
        cv2 = cache_v.rearrange("p (c x) -> (p c) x", c=16)
        QsR = [QT_s[:, 4 * g:4 * g + 4, 0:4] for g in range(2)]

        def front(c):
            b2, b3 = c % 2, c % 3
            IDMA(Kb[b3].rearrange("p j d -> p (j d)"), ck2, idx16[:, c:c + 1], "s4_%d" % b3, ["idx16"], ["Kb%d" % b3])
            IDMA(Vc[b3].rearrange("p j d -> p (j d)"), cv2, idx16[:, c:c + 1], "s5_%d" % b3, ["idx16"], ["Vc%d" % b3])
            CP("dve", Vb2[b3][:, :, :, 0:128], Vc[b3].rearrange("p j (g d) -> p j g d", g=2), ["Vc%d" % b3, "Vb2_%d" % b3], ["Vb2_%d" % b3])
            yield
            for half in range(2):
                tb = half
                for jj in range(4):
                    for g in range(2):
                        TR(psb[tb][:, (jj * 2 + g) * 128:(jj * 2 + g + 1) * 128], Kb[b3][:, half * 4 + jj, g * 128:(g + 1) * 128], ident_b,
                           ["Kb%d" % b3, "identb"], ["ps%d" % tb])
                ACT(KTc[b2][:, half * 4:half * 4 + 4, :, :], psb[tb].rearrange("p (j g t) -> p j g t", j=4, g=2), AF.Copy,
                    ["ps%d" % tb], [("KTc", b2, half)])
                yield
            sbk = 2 + b2
            for jj in range(8):
                for g in range(2):
                    MM(ps[sbk][:, (jj * 2 + g) * 16:(jj * 2 + g + 1) * 16].rearrange("p (h t) -> p h t", h=4), KTc[b2][:, jj, g, :], QsR[g], True, True,
                       [("KTc", b2, jj // 4)] + [("QTs", 4 * g + hh) for hh in range(4)], ["ps%d" % sbk])
            yield

        def back(c):
            b2, b3 = c % 2, c % 3
            sbk = 2 + b2
            TT("dve", Sm.rearrange("p j g (h t) -> p j (g h) t", h=4), ps[sbk][:, 0:256].rearrange("p (j gh t) -> p j gh t", j=8, t=4),
               NBs[:, c * 8:(c + 1) * 8, :].unsqueeze(2).to_broadcast([128, 8, 8, 4]), ALU.add, ["ps%d" % sbk, "NBs"], ["Sm"])
            ACT(Pb[b2], Sm, AF.Exp, ["Sm"], ["Pb%d" % b2], scale=SCALE)
            for jj in range(8):
                for g in range(2):
                    MM(ps[4 + g][0:16, 0:129], Pb[b2][:, jj, g, :], Vb2[b3][:, jj, g, 0:129], c == 0 and jj == 0, False,
                       ["Pb%d" % b2, "Vb2_%d" % b3], ["ps%d" % (4 + g)])
            yield

        yield from front(0)
        yield from front(1)
        yield from bisect_s()
        sstage[0] = 1
        for c in range(16):
            yield from back(c)
            if c + 2 < 16:
                yield from front(c + 2)
        for g in range(2):
            MM(ps[2][0:4, g * 16:(g + 1) * 16].rearrange("p (h t) -> p h t", h=4), KT_s[:, g, 0:4], QsR[g], True, True,
               [("KTs", g)] + [("QTs", 4 * g + hh) for hh in range(4)], ["ps2"])
            CP("dve", Vn[0:4, g, 0:128], V_s[0:4, g * 128:(g + 1) * 128], ["Vs", "Vn"], ["Vn"])
        TT("dve", Sn[0:4].rearrange("p g (h t) -> p (g h) t", h=4), ps[2][0:4, 0:32].rearrange("p (gh t) -> p gh t", t=4),
           NBs[0:4, 128, :].unsqueeze(1).to_broadcast([4, 8, 4]), ALU.add, ["ps2", "NBs"], ["Sn"])
        ACT(Pn[0:4], Sn[0:4], AF.Exp, ["Sn"], ["Pn"], scale=SCALE)
        for g in range(2):
            MM(ps[4 + g][0:16, 0:129], Pn[0:4, g, :], Vn[0:4, g, 0:129], False, True, ["Pn", "Vn"], ["ps%d" % (4 + g)])
            RCP(sm5[0:16, 60:61], ps[4 + g][0:16, 128:129], ["ps%d" % (4 + g)], ["sm5r"])
            TS("dve", att[0:16, :], ps[4 + g][0:16, 0:128], sm5[0:16, 60:61], None, ALU.mult, None, ["ps%d" % (4 + g), "sm5r"], ["att"])
            P.op("pe", lambda e: e.transpose(out=ps[0][:, 0:16], in_=att[0:16, :], identity=ident_f[0:16, 0:16]), ["att", "identf"], ["ps0"])
            TT("dve", MIX_s[:, 4 * g:4 * g + 4, 0:4], ps[0][:, 0:16].rearrange("p (h t) -> p h t", h=4), SZA_s[:, 4 * g:4 * g + 4, 0:4], ALU.mult,
               ["ps0"] + [("SZAs", 4 * g + hh) for hh in range(4)], [("MIXs", 4 * g + hh) for hh in range(4)])

        if DUMP:
            DMA('sync', dbg_att, att, 'dd4', ['att'], [])
            CP('dve', junk_s[0:16, 0:132], ps[5][0:16, 0:132], ['ps5'], ['junk_s'])
            DMA('sync', dbg_O, junk_s[0:16, 0:132], 'dd5', ['junk_s'], [])
            DMA('sync', dbg_Sm, Sm.rearrange('p j g c -> p (j g c)'), 'dd6', ['Sm'], [])

    if stop < 6:
        for _ in gen_sample():
            pass
        P.wait_all_dma('sync'); P.emit(); return nc

    R3 = Region()
    R3.o = o_QT
    WO = [R3.sb([128, 16, 512], BF16) for _ in range(3)]
    xr = [R3.sb([128, 512], F32) for _ in range(2)]
    yst = [R3.sb([128, 512], F32) for _ in range(2)]
    wo_slot = {0: 0, 1: 1, 2: 2, 3: 0}
    assert R3.o <= o_MIXA, (R3.o, o_MIXA)
    xc = [0]
    obk = [0]

    def out_step(c, i):
        s_ = wo_slot[c]
        bk = 6 + obk[0] % 2
        obk[0] += 1
        a_ = xc[0] % 2
        xc[0] += 1
        if i < 8:
            np_ = 128
            DMA("sync", xr[a_], xall[(2 * i) * 128:(2 * i + 1) * 128, c * 512:(c + 1) * 512], "xr%d" % a_, [], ["xr%d" % a_])
        else:
            np_ = 4
            DMA("sync", xr[a_][0:4, :], xall[2048:2052, c * 512:(c + 1) * 512], "xr%d" % a_, [], ["xr%d" % a_])
        for ft in range(16):
            if i < 8:
                lhs = (MIXA if ft < 8 else MIXC)[:, ft % 8, i * 128:(i + 1) * 128]
                rk = [("MIXA", i, 0), ("MIXA", i, 1)] if ft < 8 else [("MIXC", ft - 8, i // 4)]
            else:
                lhs = MIX_s[:, ft, 0:4]
                rk = [("MIXs", ft)]
            MM(ps[bk][:np_, :], lhs, WO[s_][:, ft, :], ft == 0, ft == 15, rk + ["WO%d" % s_], ["ps%d" % bk])
            if ft % 4 == 3 and ft < 15:
                yield
        TT("dve", yst[a_][:np_, :], ps[bk][:np_, :], xr[a_][:np_, :], ALU.add, ["ps%d" % bk, "xr%d" % a_], ["yst%d" % a_])
        if i < 8:
            DMA("sync", y_own[i * 128:(i + 1) * 128, c * 512:(c + 1) * 512], yst[a_], "yo%d" % a_, ["yst%d" % a_], [])
        else:
            DMA("sync", y_s[:, c * 512:(c + 1) * 512], yst[a_][0:4, :], "yo%d" % a_, ["yst%d" % a_], [])
        yield

    def load_wo(c):
        s_ = wo_slot[c]
        DMA("pool", WO[s_], w_out[:, c * 512:(c + 1) * 512].rearrange("(ft p) c -> p ft c", p=128), "wo%d" % s_, [], ["WO%d" % s_])

    def gen_out():
        load_wo(0)
        load_wo(1)
        load_wo(2)
        for c in range(4):
            for i in range(8):
                yield from out_step(c, i)
            if c == 0:
                load_wo(3)

    gs = gen_sample()
    go = gen_out()
    gs_alive, go_alive = True, True
    while gs_alive or go_alive:
        if gs_alive:
            try:
                next(gs)
            except StopIteration:
                gs_alive = False
        n_out = 3 if (sstage[0] == 0 and gs_alive) else 1
        for _ in range(n_out):
            if go_alive:
                try:
                    next(go)
                except StopIteration:
                    go_alive = False
    for _ in out_step(1, 8):
        pass
    wo_slot[0] = 1
    load_wo(0)
    for c_ in (2, 3, 0):
        for _ in out_step(c_, 8):
            pass

    P.wait_all_dma("sync")
    P.emit()
    return nc


_NC_CACHE = {}


def _rope_tables(pos, half):
    inv = (np.float32(THETA) ** (-(np.arange(half, dtype=np.float32) / np.float32(half)))).astype(np.float32)
    ang = pos.astype(np.float32)[:, None] * inv[None, :]
    return np.cos(ang).astype(np.float32), np.sin(ang).astype(np.float32)


def kernel(x_prompt, x_sample, cache_k, cache_v, cache_kidx, state_conv, page_table,
           norm_in, w_in, g_q, g_k, g_kidx, w_conv, w_out):
    x_prompt = np.asarray(x_prompt, np.float32)
    x_sample = np.asarray(x_sample, np.float32)
    if "nc" not in _NC_CACHE:
        _NC_CACHE["nc"] = build_program()
    nc = _NC_CACHE["nc"]
    ck = np.ascontiguousarray(np.asarray(cache_k, np.float32)[0].reshape(1280, PAGE * 256))
    cv = np.ascontiguousarray(np.asarray(cache_v, np.float32)[0].reshape(1280, PAGE * 256))
    cki = np.ascontiguousarray(np.asarray(cache_kidx, np.float32)[0].reshape(1280, PAGE * 64))
    w_in0 = np.ascontiguousarray(np.asarray(w_in, np.float32)[0])
    w_out0 = np.ascontiguousarray(np.asarray(w_out, np.float32)[0])
    pow2 = (2.0 ** -(np.arange(KBIS, dtype=np.float32) + 1)).astype(np.float32)
    in_maps = []
    slots_all = []
    for c in range(8):
        b, j = c // 2, c % 2
        own = T_OWN[j]
        oth = T_OWN[1 - j]
        slots = []
        for i in range(8):
            slots += [own[i], oth[i]]
        slots_all.append(slots)
        xall = np.zeros((2176, D), np.float32)
        pos = np.zeros(2176, np.int64)
        for s_, t in enumerate(slots):
            xall[s_ * 128:(s_ + 1) * 128] = x_prompt[b, t * 128:(t + 1) * 128]
            pos[s_ * 128:(s_ + 1) * 128] = np.arange(t * 128, (t + 1) * 128)
        xall[2048:2052] = x_sample[c]
        pos[2048:2052] = PAST + np.arange(4)
        for i in range(8):
            t = own[i]
            if t > 0:
                xall[2052 + 2 * i:2054 + 2 * i] = x_prompt[b, t * 128 - 2:t * 128]
        c32, s32 = _rope_tables(pos, 16)
        c16, s16 = _rope_tables(pos, 8)
        cm = np.zeros((128, 512), np.float32)
        tri = np.where(np.arange(128)[None, :] <= np.arange(128)[:, None], 0.0, -30000.0).astype(np.float32)
        for par in range(2):
            cm[:, par * 256:par * 256 + 128] = tri
            other_before = oth[par] < own[par]
            cm[:, par * 256 + 128:par * 256 + 256] = 0.0 if other_before else -30000.0
        pm = (-2.0 * cm).astype(np.float32)
        in_maps.append(dict(
            xall=xall, w_in=w_in0, w_out=w_out0,
            norm_in=np.ascontiguousarray(np.asarray(norm_in, np.float32)[0].reshape(16, 128).T),
            g_q=np.ascontiguousarray(np.asarray(g_q, np.float32)[0]),
            g_k=np.ascontiguousarray(np.asarray(g_k, np.float32)[0]),
            g_ki=np.ascontiguousarray(np.asarray(g_kidx, np.float32)[0]),
            w_conv=np.ascontiguousarray(np.asarray(w_conv, np.float32)[0].reshape(3, 8, 128).transpose(2, 0, 1)),
            cos32=c32, sin32=s32, cos16=c16, sin16=s16, cmask=cm, pmask=pm, pow2=pow2,
            stc=np.ascontiguousarray(np.asarray(state_conv, np.float32)[0, c].reshape(2, 8, 128).transpose(2, 0, 1)),
            ptab=np.ascontiguousarray(np.asarray(page_table, np.int32)[c].reshape(128, 1)),
            cache_k=ck, cache_v=cv, cache_ki=cki,
            cm4=np.where(np.arange(4)[:, None] <= np.arange(4)[None, :], 0.0, -1e30).astype(np.float32),
        ))
    res = run_bass_kernel_spmd(nc, in_maps, core_ids=list(range(8)))
    R = res.results
    y_prompt = np.zeros((NB_, S, D), np.float32)
    y_sample = np.zeros((DEC_B, DEC_T, D), np.float32)
    k_p = np.zeros((1, NB_, S, NKV, HD), np.float32)
    v_p = np.zeros((1, NB_, S, NKV, HD), np.float32)
    ki_p = np.zeros((1, NB_, S, IDXD), np.float32)
    conv_p = np.zeros((1, NB_, 2, 1024), np.float32)
    k_s = np.zeros((1, DEC_B, DEC_T, NKV, HD), np.float32)
    v_s = np.zeros((1, DEC_B, DEC_T, NKV, HD), np.float32)
    ki_s = np.zeros((1, DEC_B, DEC_T, IDXD), np.float32)
    conv_s = np.zeros((1, DEC_B, 2, 1024), np.float32)
    for c in range(8):
        b, j = c // 2, c % 2
        r = R[c]
        for i, t in enumerate(T_OWN[j]):
            sl = slice(t * 128, (t + 1) * 128)
            y_prompt[b, sl] = r["y_own"][i * 128:(i + 1) * 128]
            k_p[0, b, sl] = r["k_own"][i * 128:(i + 1) * 128].reshape(128, NKV, HD)
            v_p[0, b, sl] = r["v_own"][i * 128:(i + 1) * 128].reshape(128, NKV, HD)
            ki_p[0, b, sl] = r["ki_own"][i * 128:(i + 1) * 128]
        if j == 0:
            conv_p[0, b] = r["conv_p"].transpose(1, 2, 0).reshape(2, 1024)
        y_sample[c] = r["y_s"]
        k_s[0, c] = r["k_s"].reshape(4, NKV, HD)
        v_s[0, c] = r["v_s"].reshape(4, NKV, HD)
        ki_s[0, c] = r["ki_s"]
        conv_s[0, c] = r["conv_s"].transpose(1, 2, 0).reshape(2, 1024)
    return (y_prompt, y_sample, k_p, v_p, ki_p, conv_p, k_s, v_s, ki_s, conv_s)
```

```python
import numpy as np
import concourse.bass as bass
import concourse.mybir as mybir
from concourse.bass_utils import run_bass_kernel_spmd

F32 = mybir.dt.float32
BF16 = mybir.dt.bfloat16
I32 = mybir.dt.int32
AF = mybir.ActivationFunctionType
ALU = mybir.AluOpType
AX = mybir.AxisListType

D = 2048
S = 2048
NB_ = 4
DEC_B = 8
DEC_T = 4
PAST = 16384
PAGE = 128
NPAGES = PAST // PAGE
D_IN = 7760
HD = 128
NH = 8
NKV = 2
IDXH = 16
IDXD = 64
TOPK = 256
EPS = 1e-6
THETA = 500000.0
KBIS = 14
NEG = -30000.0
C_Q, C_K, C_V, C_ZA, C_QI, C_KI, C_WI, C_H, C_B, C_C, C_ZC = 0, 1024, 1280, 1536, 2560, 3584, 3648, 3664, 4688, 5712, 6736
T_OWN = ([0, 3, 4, 7, 8, 11, 12, 15], [1, 2, 5, 6, 9, 10, 13, 14])
ENGS = ("sync", "act", "dve", "pool", "pe")
NOSELF = ("pe",)


class Prog:
    def __init__(self, nc):
        self.nc = nc
        self.ops = {e: [] for e in ENGS}
        self.count = {e: 0 for e in ENGS}
        self.dcount = {}
        self.last_write = {}
        self.readers = {}
        self.waited = {e: {} for e in ENGS}
        self.pending_barrier = {e: None for e in ENGS}
        import os
        self.limit = int(os.environ.get('DBG_OPS', '100000000'))
        self.total = 0

    def _deps(self, eng, reads, writes):
        deps = {}

        def need(sv):
            if sv is None:
                return
            k, v = sv
            if deps.get(k, 0) < v:
                deps[k] = v

        for r in reads:
            need(self.last_write.get(r))
        for w in writes:
            need(self.last_write.get(w))
            for rd in self.readers.get(w, ()):
                need(rd)
        pb = self.pending_barrier[eng]
        if pb is not None:
            for kv in pb:
                need(kv)
            self.pending_barrier[eng] = None
        out = []
        for k, v in deps.items():
            if k == eng and eng in NOSELF:
                continue
            if self.waited[eng].get(k, 0) >= v:
                continue
            self.waited[eng][k] = v
            out.append((k, v))
        return out

    def _commit(self, me, reads, writes):
        for r in reads:
            self.readers.setdefault(r, []).append(me)
        for w in writes:
            self.last_write[w] = me
            self.readers[w] = []

    def op(self, eng, fn, reads=(), writes=()):
        self.total += 1
        if self.total > self.limit:
            return
        reads = tuple(reads)
        writes = tuple(writes)
        waits = self._deps(eng, reads, writes)
        self.count[eng] += 1
        self.ops[eng].append(("op", fn, waits))
        self._commit((eng, self.count[eng]), reads, writes)

    def dma(self, queue, fn, dsem, reads=(), writes=()):
        self.total += 1
        if self.total > self.limit:
            return
        reads = tuple(reads)
        writes = tuple(writes)
        key = ("d", dsem)
        waits = self._deps(queue, reads, writes)
        self.dcount[key] = self.dcount.get(key, 0) + 1
        self.ops[queue].append(("dma", fn, waits, key))
        self._commit((key, 16 * self.dcount[key]), reads, writes)

    def alias(self, new_key, old_keys):
        lst = self.readers.setdefault(new_key, [])
        for ok in old_keys:
            lw = self.last_write.get(ok)
            if lw is not None:
                lst.append(lw)
            lst.extend(self.readers.get(ok, ()))

    def barrier(self):
        snap = [(e, self.count[e]) for e in ENGS if self.count[e] > 0]
        snap += [(k, 16 * n) for k, n in self.dcount.items()]
        for e in ENGS:
            self.pending_barrier[e] = list(snap)

    def wait_all_dma(self, eng="sync"):
        waits = [(k, 16 * n) for k, n in self.dcount.items()]
        self.ops[eng].append(("waitonly", None, waits))

    def emit(self):
        nc = self.nc
        from contextlib import ExitStack
        with ExitStack() as es:
            sems = {}
            for e in ENGS:
                sems[e] = es.enter_context(nc.semaphore("s_" + e))
            for i, k in enumerate(self.dcount.keys()):
                sems[k] = es.enter_context(nc.semaphore("d_%d" % i))
            block = es.enter_context(nc.Block())

            def run(engname, h):
                for rec in self.ops[engname]:
                    kind, fn, waits = rec[0], rec[1], rec[2]
                    for k, v in waits:
                        h.wait_ge(sems[k], v)
                    if kind == "op":
                        fn(h).then_inc(sems[engname], 1)
                    elif kind == "dma":
                        fn(h).then_inc(sems[rec[3]], 16)

            @block.sync
            def _(h):
                run("sync", h)

            @block.scalar
            def _(h):
                run("act", h)

            @block.vector
            def _(h):
                run("dve", h)

            @block.gpsimd
            def _(h):
                run("pool", h)

            @block.tensor
            def _(h):
                run("pe", h)


def build_program(stop=99):
    nc = bass.Bass("TRN2", target_bir_lowering=False)
    P = Prog(nc)

    def din(name, shape, dt=F32):
        return nc.dram_tensor(name, list(shape), dt, kind="ExternalInput").ap()

    def dout(name, shape, dt=F32):
        return nc.dram_tensor(name, list(shape), dt, kind="ExternalOutput").ap()

    xall = din("xall", [2176, D])
    w_in = din("w_in", [D, D_IN])
    w_out = din("w_out", [D, D])
    norm_in = din("norm_in", [128, 16])
    g_q = din("g_q", [HD])
    g_k = din("g_k", [HD])
    g_ki = din("g_ki", [IDXD])
    w_conv = din("w_conv", [128, 3, 8])
    cos32 = din("cos32", [2176, 16])
    sin32 = din("sin32", [2176, 16])
    cos16 = din("cos16", [2176, 8])
    sin16 = din("sin16", [2176, 8])
    cmask_d = din("cmask", [128, 512])
    pmask_d = din("pmask", [128, 512])
    pow2_d = din("pow2", [KBIS])
    stc = din("stc", [128, 2, 8])
    ptab = din("ptab", [128, 1], I32)
    cm4_d = din("cm4", [4, 4])
    cache_k = din("cache_k", [1280, PAGE * 256])
    cache_v = din("cache_v", [1280, PAGE * 256])
    cache_ki = din("cache_ki", [1280, PAGE * 64])

    y_own = dout("y_own", [1024, D])
    y_s = dout("y_s", [4, D])
    k_own = dout("k_own", [1024, 256])
    v_own = dout("v_own", [1024, 256])
    ki_own = dout("ki_own", [1024, 64])
    conv_p = dout("conv_p", [128, 2, 8])
    k_s = dout("k_s", [4, 256])
    v_s = dout("v_s", [4, 256])
    ki_s = dout("ki_s", [4, 64])
    conv_s = dout("conv_s", [128, 2, 8])
    import os
    DUMP = os.environ.get("DBG_DUMP") == "1"
    if DUMP:
        dbg_I = dout("dbg_I", [128, 516])
        dbg_sm5 = dout("dbg_sm5", [128, 64])
        dbg_NB = dout("dbg_NB", [128, 516])
        dbg_att = dout("dbg_att", [128, 128])
        dbg_O = dout("dbg_O", [16, 132])
        dbg_Sm = dout("dbg_Sm", [128, 256])
        dbg_Wbc = dout("dbg_Wbc", [128, 64])

    BASE = 16640
    LIMIT = 229376
    cur = [BASE]
    cnt = [0]

    def alloc_at(off, shape, dt):
        cnt[0] += 1
        return nc.alloc_sbuf_tensor_at("t%d" % cnt[0], list(shape), dt, offset=off).ap()

    def nbytes(shape, dt):
        n = int(np.prod(shape[1:])) * (2 if dt == BF16 else 4)
        return (n + 63) // 64 * 64

    def sb(shape, dt):
        off = cur[0]
        cur[0] += nbytes(shape, dt)
        assert cur[0] <= LIMIT, ("SBUF overflow", cur[0])
        return alloc_at(off, shape, dt)

    ident_f = sb([128, 128], F32)
    ident_b = sb([128, 128], BF16)
    ID4 = sb([128, 512], BF16)
    ONES = sb([128, 128], BF16)
    gq_bc = sb([128, 128], F32)
    gk_bc = sb([128, 128], F32)
    gki_bc = sb([128, 64], F32)
    COS32 = sb([128, 17, 16], F32)
    SIN32 = sb([128, 17, 16], F32)
    COS16 = sb([128, 17, 8], F32)
    SIN16 = sb([128, 17, 8], F32)
    wcT = sb([128, 3, 8], F32)
    cmask = sb([128, 512], F32)
    pmask = sb([128, 512], F32)
    pow2 = sb([128, KBIS], F32)
    WIDX = sb([128, 9, 16], F32)
    ginT = sb([128, 16], F32)
    SCT = sb([128, 2, 8], F32)
    CO = sb([128, 2, 8], F32)
    COs = sb([128, 2, 8], F32)
    sml = sb([128, 80], F32)
    sml1 = sb([128, 8], F32)
    rt = sb([128, 4, 32], F32)
    bis = sb([128, 16], F32)
    Wt = sb([128, KBIS], F32)
    o_QT = cur[0]
    QT = sb([128, 8, 1024], BF16)
    KT = sb([128, 2, 2048], BF16)
    Vb = sb([128, 16, 256], BF16)
    QIT = sb([128, 8, 1024], BF16)
    KIT = sb([128, 2, 2048], BF16)
    o_MIXA = cur[0]
    MIXA = sb([128, 8, 1024], BF16)
    xstb = [alloc_at(o_MIXA + 8192 * j, [128, 2048], F32) for j in range(2)]
    QT_s = sb([128, 8, 32], BF16)
    KT_s = sb([128, 2, 32], BF16)
    V_s = sb([128, 256], BF16)
    QIT_s = sb([128, 8, 32], BF16)
    KIT_s = sb([128, 32], BF16)
    SZA_s = sb([128, 8, 32], BF16)
    MIX_s = sb([128, 16, 32], BF16)
    Us = sb([128, 8], F32)
    cvs = sb([128, 8], F32)
    XOFF = cur[0]
    xnT_oth = sb([128, 16, 1024], BF16)
    SZA = alloc_at(XOFF, [128, 8, 1024], BF16)
    MIXC = alloc_at(XOFF + 16384, [128, 8, 1024], BF16)
    ROFF = cur[0]
    RSIZE = LIMIT - ROFF
    print('SBUF: XOFF', XOFF, 'ROFF', ROFF, 'RSIZE', RSIZE)
    assert RSIZE >= 79872, RSIZE

    class Region:
        def __init__(self):
            self.o = ROFF

        def sb(self, shape, dt):
            off = self.o
            self.o += nbytes(shape, dt)
            assert self.o <= LIMIT, ("region overflow", self.o - ROFF)
            return alloc_at(off, shape, dt)

    R1 = Region()
    xnT_own = R1.sb([128, 16, 1152], BF16)
    o_x = R1.o
    xst = R1.sb([128, 2048], F32)
    o_xs = R1.o
    xs = R1.sb([128, 2048], BF16)
    Wr = [R1.sb([128, 16, 256], BF16) for _ in range(3)]
    o_tq = R1.o
    NJ = 5
    tQ = [R1.sb([128, 256], F32) for _ in range(NJ)]
    tQ2 = [R1.sb([128, 256], F32) for _ in range(NJ)]
    kbj = [R1.sb([128, 256], BF16) for _ in range(NJ)]
    rtj = [R1.sb([128, 4, 32], F32) for _ in range(NJ)]
    tA = [alloc_at(o_MIXA + 2048 * j, [128, 512], F32) for j in range(2)]
    tB = [alloc_at(o_MIXA + 4096 + 2048 * j, [128, 512], F32) for j in range(2)]
    Wr.append(alloc_at(o_MIXA + 8192, [128, 16, 256], BF16))
    U = alloc_at(o_x, [128, 8, 130], F32)
    xsb = [xs, alloc_at(o_x, [128, 2048], BF16)]
    CV = alloc_at(o_xs, [128, 8, 128], F32)

    ps = [nc.alloc_psum_tensor("ps%d" % i, [128, 512], F32).ap() for i in range(8)]
    psb = [p.bitcast(BF16) for p in ps]

    def ACT(out, in_, func, r, w, **kw):
        P.op("act", lambda e: e.activation(out=out, in_=in_, func=func, **kw), r, w)

    def TS(eng, out, in0, s1, s2, op0, op1, r, w, accum=None):
        if op1 is None:
            P.op(eng, lambda e: e.tensor_scalar(out=out, in0=in0, scalar1=s1, scalar2=None, op0=op0), r, w)
        elif accum is None:
            P.op(eng, lambda e: e.tensor_scalar(out=out, in0=in0, scalar1=s1, scalar2=s2, op0=op0, op1=op1), r, w)
        else:
            P.op(eng, lambda e: e.tensor_scalar(out=out, in0=in0, scalar1=s1, scalar2=s2, op0=op0, op1=op1,
                                                accum_out=accum), r, w)

    def TT(eng, out, in0, in1, op, r, w):
        P.op(eng, lambda e: e.tensor_tensor(out=out, in0=in0, in1=in1, op=op), r, w)

    def STT(out, in0, scalar, in1, op0, op1, r, w):
        P.op("dve", lambda e: e.scalar_tensor_tensor(out=out, in0=in0, scalar=scalar, in1=in1, op0=op0, op1=op1), r, w)

    def CP(eng, out, in_, r, w):
        P.op(eng, lambda e: e.tensor_copy(out=out, in_=in_), r, w)

    def RED(out, in_, op, r, w):
        P.op("dve", lambda e: e.tensor_reduce(out=out, in_=in_, axis=AX.X, op=op), r, w)

    def RCP(out, in_, r, w):
        P.op("dve", lambda e: e.reciprocal(out=out, in_=in_), r, w)

    def MM(out, lhsT, rhs, start, stop, r, w):
        P.op("pe", lambda e: e.matmul(out, lhsT=lhsT, rhs=rhs, start=start, stop=stop), r, w)

    def TR(out, in_, ident, r, w):
        P.op("pe", lambda e: e.transpose(out=out, in_=in_, identity=ident), r, w)

    def DMA(queue, out, in_, dsem, r, w, **kw):
        P.dma(queue, lambda e: e.dma_start(out=out, in_=in_, **kw), dsem, r, w)

    DMA("sync", ginT, norm_in, "c3", [], ["gin"])

    def load_consts():
        DMA("sync", gq_bc, g_q.partition_broadcast(128), "c0", [], ["gq"])
        DMA("sync", gk_bc, g_k.partition_broadcast(128), "c1", [], ["gk"])
        DMA("sync", gki_bc, g_ki.partition_broadcast(128), "c2", [], ["gki"])
        DMA("sync", COS32[:, 0:16, :], cos32[0:2048, :].rearrange("(t p) c -> p t c", p=128), "c4", [], ["cs0"])
        DMA("sync", SIN32[:, 0:16, :], sin32[0:2048, :].rearrange("(t p) c -> p t c", p=128), "c5", [], ["cs1"])
        DMA("sync", COS16[:, 0:16, :], cos16[0:2048, :].rearrange("(t p) c -> p t c", p=128), "c6", [], ["cs2"])
        DMA("sync", SIN16[:, 0:16, :], sin16[0:2048, :].rearrange("(t p) c -> p t c", p=128), "c7", [], ["cs3"])
        DMA("sync", COS32[:, 16, :], cos32[2048:2176, :], "c8", [], ["cs0e"])
        DMA("sync", SIN32[:, 16, :], sin32[2048:2176, :], "c9", [], ["cs1e"])
        DMA("sync", COS16[:, 16, :], cos16[2048:2176, :], "c10", [], ["cs2e"])
        DMA("sync", SIN16[:, 16, :], sin16[2048:2176, :], "c11", [], ["cs3e"])
        DMA("sync", cmask, cmask_d, "c12", [], ["cmask"])
        DMA("sync", pmask, pmask_d, "c13", [], ["pmask"])
        DMA("sync", pow2, pow2_d.partition_broadcast(128), "c14", [], ["pow2"])
        DMA("sync", wcT, w_conv, "c15", [], ["wcT"])
        DMA("sync", SCT, stc, "c16", [], ["SCT"])

    CONST_KEYS = ["gq", "gk", "gki", "gin", "cs0", "cs1", "cs2", "cs3", "cs0e", "cs1e", "cs2e", "cs3e",
                  "cmask", "pmask", "pow2", "wcT", "SCT"]
    P.op("pool", lambda e: e.memset(ident_f, 0.0), [], ["identf"])
    P.op("pool", lambda e: e.affine_select(out=ident_f, in_=ident_f, pattern=[[-1, 128]], compare_op=ALU.not_equal,
                                           fill=1.0, base=0, channel_multiplier=1), ["identf"], ["identf"])
    CP("pool", ident_b, ident_f, ["identf"], ["identb"])
    for a in range(4):
        CP("pool", ID4[:, a * 128:(a + 1) * 128], ident_f, ["identf"], ["ID4"])
    P.op("pool", lambda e: e.memset(ONES, 1.0), [], ["ONES"])

    if stop < 1:
        load_consts()
        P.wait_all_dma('sync'); P.emit(); return nc
    def tile_src(tt):
        if tt == 16:
            return xall[2048:2176, :], 128
        return xall[tt * 128:(tt + 1) * 128, :], 128

    def xn_dst(tt):
        if tt == 16:
            return xnT_own, 1024, 128
        if tt % 2 == 0:
            return xnT_own, (tt // 2) * 128, 128
        return xnT_oth, (tt // 2) * 128, 128

    p1_state = [0]

    def p1_A(tt):
        src, np_ = tile_src(tt)
        pb_ = tt % 2
        xst_, xs_ = xstb[pb_], xsb[pb_]
        kx, kxs, ksm1 = "xst%d" % pb_, "xs%d" % pb_, "sml1_%d" % pb_
        sm1 = sml1[:, 4 * pb_:4 * pb_ + 4]
        DMA("sync", xst_[:np_, :], src, "xld%d" % pb_, [], [kx])
        ACT(xs_[:np_, :], xst_[:np_, :], AF.Square, [kx], [kxs, ksm1], accum_out=sm1[:np_, 0:1])
        TS("dve", sm1[:np_, 1:2], sm1[:np_, 0:1], 1.0 / D, EPS, ALU.mult, ALU.add, [ksm1], [ksm1])
        ACT(sm1[:np_, 2:3], sm1[:np_, 1:2], AF.Sqrt, [ksm1], [ksm1])
        RCP(sm1[:np_, 3:4], sm1[:np_, 2:3], [ksm1], [ksm1])
        TS("dve", xs_[:np_, :], xst_[:np_, :], sm1[:np_, 3:4], None, ALU.mult, None, [kx, ksm1], [kxs])

    def p1_B(tt):
        src, np_ = tile_src(tt)
        pb_ = tt % 2
        xs_ = xsb[pb_]
        kxs = "xs%d" % pb_
        dst, t0, _ = xn_dst(tt)
        for half in range(2):
            bk = 2 + (2 * tt + half) % 2
            for j in range(8):
                kt = half * 8 + j
                TR(psb[bk][:, j * 128:j * 128 + np_], xs_[:np_, kt * 128:(kt + 1) * 128], ident_b[:np_, :np_],
                   [kxs, "identb"], ["ps%d" % bk])
            src_ps = psb[bk].rearrange("p (j t) -> p j t", j=8)[:, :, 0:np_]
            ACT(dst[:, half * 8:half * 8 + 8, t0:t0 + np_], src_ps, AF.Copy, ["ps%d" % bk], [("xnTraw", tt)])
        TT("pool", dst[:, :, t0:t0 + np_], dst[:, :, t0:t0 + np_], ginT.unsqueeze(2).to_broadcast([128, 16, np_]), ALU.mult,
           [("xnTraw", tt), "gin"], [("xnT", tt)])

    def p1_tile(tt):
        if tt == 0:
            p1_A(0)
        if tt + 1 <= 16:
            p1_A(tt + 1)
        p1_B(tt)

    def p1_upto(tt):
        while p1_state[0] <= min(tt, 16):
            p1_tile(p1_state[0])
            p1_state[0] += 1

    if stop < 2:
        load_consts()
        p1_upto(16)
        P.wait_all_dma('sync'); P.emit(); return nc

    wctr = [0]

    def win_cols(c0, n):
        return w_in[:, c0:c0 + n].rearrange("(kt p) c -> p kt c", p=128)

    own_tiles = [2 * i for i in range(8)] + [16]
    all_tiles = list(range(17))
    bkc = [0]
    trc = [0]
    tac = [0]

    def xn_src(tt, kt):
        dst, t0, n = xn_dst(tt)
        return dst[:, kt, t0:t0 + n], n

    def own_idx(tt):
        return tt // 2

    CK = ["cs0", "cs1", "cs2", "cs3", "cs0e", "cs1e", "cs2e", "cs3e"]

    chunks = [("k", C_K, 256, all_tiles, 2, 128, 16, gk_bc, True, 0),
              ("v", C_V, 256, all_tiles, 0, 0, 0, None, False, 0),
              ("kiw", C_KI, 80, all_tiles, 1, 64, 8, gki_bc, True, 0)]
    for c in range(4):
        chunks.append(("q", C_Q + 256 * c, 256, own_tiles, 2, 128, 16, gq_bc, True, c))
    for c in range(4):
        chunks.append(("qi", C_QI + 256 * c, 256, own_tiles, 4, 64, 8, None, False, c))

    wslot = {}

    def issue_w(ci):
        if ci >= len(chunks) or ci in wslot:
            return
        sidx = ci if ci < 3 else (ci - 3) % 2
        wslot[ci] = sidx
        DMA("pool", Wr[sidx][:, :, 0:chunks[ci][2]], win_cols(chunks[ci][1], chunks[ci][2]), "w%d" % sidx, [], ["W%d" % sidx])

    def job(ci, tt, js):
        kind, c0, ncols, _, nh, hd, rh, g_bc, do_norm, cidx = chunks[ci]
        sw = wslot[ci]
        bk = bkc[0] % 2
        bkc[0] += 1
        pk = "ps%d" % bk
        for kt in range(16):
            lhs, n = xn_src(tt, kt)
            MM(ps[bk][:, 0:ncols], lhs, Wr[sw][:, kt, 0:ncols], kt == 0, kt == 15, [("xnT", tt), "W%d" % sw], [pk])
        own = (tt == 16) or (tt % 2 == 0)
        i = own_idx(tt)
        tq, tq2, kb_, rt_ = tQ[js], tQ2[js], kbj[js], rtj[js]
        kq, kq2, kkb, krt, ksm = "tQ%d" % js, "tQ2%d" % js, "kb%d" % js, "rt%d" % js, "sml%d" % js
        sm = sml[:, 16 * js:16 * js + 16]
        if kind == "v":
            if own:
                ACT(tq[:, 0:256], ps[bk][:, 0:256], AF.Copy, [pk], [kq])
            if tt == 16:
                ACT(V_s[:, :], ps[bk][:, 0:256], AF.Copy, [pk], ["Vs"])
            else:
                ACT(Vb[:, tt, :], ps[bk][:, 0:256], AF.Copy, [pk], [("V", tt)])
            if own:
                for _ in range(6):
                    yield
                if tt == 16:
                    DMA("sync", v_s, tq[0:4, 0:256], "ko%d" % js, [kq], [])
                else:
                    DMA("sync", v_own[i * 128:(i + 1) * 128, :], tq[:, 0:256], "ko%d" % js, [kq], [])
            return
        nd = nh * hd
        ACT(tq[:, 0:ncols], ps[bk][:, 0:ncols], AF.Copy, [pk], [kq])
        if do_norm:
            for h in range(nh):
                ACT(tq2[:, h * hd:(h + 1) * hd], ps[bk][:, h * hd:(h + 1) * hd], AF.Square, [pk], [kq2, ksm],
                    accum_out=sm[:, h:h + 1])
            yield
            TS("dve", sm[:, 4:4 + nh], sm[:, 0:nh], 1.0 / hd, EPS, ALU.mult, ALU.add, [ksm], [ksm])
            yield
            ACT(sm[:, 8:8 + nh], sm[:, 4:4 + nh], AF.Sqrt, [ksm], [ksm])
            yield
            RCP(sm[:, 12:12 + nh], sm[:, 8:8 + nh], [ksm], [ksm])
            yield
            for h in range(nh):
                STT(tq2[:, h * hd:(h + 1) * hd], tq[:, h * hd:(h + 1) * hd], sm[:, 12 + h:13 + h], g_bc[:, 0:hd],
                    ALU.mult, ALU.mult, [kq, ksm, "gq", "gk", "gki"], [kq2])
                yield
            src, ksrc = tq2, kq2
        else:
            yield
            src, ksrc = tq, kq
        if kind == "kiw":
            if tt == 16:
                TS("dve", WIDX[:, 8, :], tq[:, 64:80], 1.0 / 32.0, None, ALU.mult, None, [kq], [("WIDX", 8)])
            elif own:
                TS("dve", WIDX[:, i, :], tq[:, 64:80], 1.0 / 32.0, None, ALU.mult, None, [kq], [("WIDX", i)])
        s3 = src[:, 0:nd].rearrange("p (h d) -> p h d", h=nh)
        x1 = s3[:, :, 0:rh]
        x2 = s3[:, :, rh:2 * rh]
        if rh == 16:
            cT, sT = COS32[:, tt, :], SIN32[:, tt, :]
        else:
            cT, sT = COS16[:, tt, :], SIN16[:, tt, :]
        cb = cT.unsqueeze(1).to_broadcast([128, nh, rh])
        sbb = sT.unsqueeze(1).to_broadcast([128, nh, rh])
        r4 = [rt_[:, j, 0:nh * rh].rearrange("p (h r) -> p h r", h=nh) for j in range(4)]
        TT("dve", r4[0], x1, cb, ALU.mult, [ksrc] + CK, [krt])
        TT("dve", r4[1], x2, sbb, ALU.mult, [ksrc] + CK, [krt])
        yield
        TT("dve", r4[2], x2, cb, ALU.mult, [ksrc] + CK, [krt])
        TT("dve", r4[3], x1, sbb, ALU.mult, [ksrc] + CK, [krt])
        yield
        TT("dve", x1, r4[0], r4[1], ALU.subtract, [krt], [ksrc])
        TT("dve", x2, r4[2], r4[3], ALU.add, [krt], [ksrc])
        yield
        if kind == "kiw":
            P.op("pool", lambda e: e.memset(kb_[:, 64:192], 0.0), [], [kkb])
            CP("dve", kb_[:, 0:64], src[:, 0:64], [ksrc], [kkb])
            ACT(kb_[:, 192:256], src[:, 0:64], AF.Copy, [ksrc], [kkb])
            nblk = 2
        else:
            CP("dve", kb_[:, 0:256], src[:, 0:256], [ksrc], [kkb])
            nblk = 2
        yield
        tb = 2 + trc[0] % 2
        trc[0] += 1
        for j in range(nblk):
            TR(psb[tb][:, j * 128:(j + 1) * 128], kb_[:, j * 128:(j + 1) * 128], ident_b, [kkb, "identb"], ["ps%d" % tb])
        for j in range(nblk):
            if kind == "k":
                dst, wk = (KT_s[:, j, 0:32], ("KTs", j)) if tt == 16 else (KT[:, j, tt * 128:(tt + 1) * 128], ("KT", tt))
            elif kind == "kiw":
                if tt == 16:
                    if j == 1:
                        continue
                    dst, wk = KIT_s[:, 0:32], "KITs"
                else:
                    dst, wk = KIT[:, j, tt * 128:(tt + 1) * 128], ("KIT", tt, j)
            elif kind == "q":
                hh = 2 * cidx + j
                dst, wk = (QT_s[:, hh, 0:32], ("QTs", hh)) if tt == 16 else (QT[:, hh, i * 128:(i + 1) * 128], ("QT", i, hh))
            else:
                hh = 2 * cidx + j
                dst, wk = (QIT_s[:, hh, 0:32], ("QITs", hh)) if tt == 16 else (QIT[:, hh, i * 128:(i + 1) * 128], ("QIT", i, hh))
            ncp = dst.shape[-1]
            ACT(dst, psb[tb][:, j * 128:j * 128 + ncp], AF.Copy, ["ps%d" % tb], [wk])
        if own and kind in ("k", "kiw"):
            for _ in range(6):
                yield
            if kind == "k":
                if tt == 16:
                    DMA("sync", k_s, src[0:4, 0:256], "ko%d" % js, [ksrc], [])
                else:
                    DMA("sync", k_own[i * 128:(i + 1) * 128, :], src[:, 0:256], "ko%d" % js, [ksrc], [])
            else:
                if tt == 16:
                    DMA("sync", ki_s, src[0:4, 0:64], "ko%d" % js, [ksrc], [])
                else:
                    DMA("sync", ki_own[i * 128:(i + 1) * 128, :], src[:, 0:64], "ko%d" % js, [ksrc], [])

    def all_jobs():
        for tt in all_tiles:
            for ci in range(3):
                yield (ci, tt)
        for ci in range(3, len(chunks)):
            for tt in chunks[ci][3]:
                yield (ci, tt)

    bkB = [0]
    wB = [0]

    def gen_B():
        oth_keys = [("xnT", tt) for tt in range(1, 16, 2)]
        tok_chunks = [(0, 512), (512, 512), (1024, 32)]

        def fproj(s, sub, c0, n):
            bk = 4 + bkB[0] % 4
            bkB[0] += 1
            tts = [("xnT", 2 * i) for i in range(8)] if c0 < 1024 else [("xnT", 16)]
            for kt in range(16):
                MM(ps[bk][:, 0:n], Wr[s][:, kt, sub * 128:(sub + 1) * 128], xnT_own[:, kt, c0:c0 + n], kt == 0, kt == 15,
                   tts + ["W%d" % s], ["ps%d" % bk])
            return bk

        for f2 in range(4):
            s = 2 + wB[0] % 2
            wB[0] += 1
            DMA("pool", Wr[s][:, :, 0:256], win_cols(C_ZA + 256 * f2, 256), "w%d" % s, [], ["W%d" % s])
            for sub in range(2):
                f = 2 * f2 + sub
                for (c0, n) in tok_chunks:
                    bk = fproj(s, sub, c0, n)
                    if c0 < 1024:
                        ACT(SZA[:, f, c0:c0 + n], ps[bk][:, 0:n], AF.Silu, ["ps%d" % bk] + oth_keys, [("SZA", f, c0)])
                        yield
                    else:
                        ACT(SZA_s[:, f, 0:32], ps[bk][:, 0:32], AF.Silu, ["ps%d" % bk], [("SZAs", f)])
                    yield

        sA, sB = 2, 3

        def load_A(f_):
            DMA("pool", Wr[sA][:, :, 0:128], win_cols(C_H + 128 * f_, 128), "w%d" % sA, [], ["W%d" % sA])
            DMA("pool", Wr[sA][:, :, 128:256], win_cols(C_C + 128 * f_, 128), "w%d" % sA, [], ["W%d" % sA])

        def load_B(f_):
            DMA("pool", Wr[sB][:, :, 0:128], win_cols(C_B + 128 * f_, 128), "w%d" % sB, [], ["W%d" % sB])
            DMA("pool", Wr[sB][:, :, 128:256], win_cols(C_ZC + 128 * f_, 128), "w%d" % sB, [], ["W%d" % sB])

        load_A(0)
        load_B(0)
        for f in range(8):
            for ci, (c0, n) in enumerate(tok_chunks):
                bh = fproj(sA, 0, c0, n)
                yield
                bc = fproj(sA, 1, c0, n)
                yield
                a = tac[0] % 2
                tac[0] += 1
                ACT(tA[a][:, 0:n], ps[bh][:, 0:n], AF.Copy, ["ps%d" % bh], ["tA%d" % a])
                if c0 < 1024:
                    TT("dve", U[:, 4 * ci:4 * ci + 4, 2:130], tA[a][:, 0:512].rearrange("p (i t) -> p i t", i=4),
                       ps[bc][:, 0:512].rearrange("p (i t) -> p i t", i=4), ALU.mult, ["tA%d" % a, "ps%d" % bc], [("U", ci)])
                else:
                    TT("dve", tB[a][:, 0:32], tA[a][:, 0:32], ps[bc][:, 0:32], ALU.mult, ["tA%d" % a, "ps%d" % bc], ["tB%d" % a])
                    CP("dve", U[:, :, 0:2], tB[a][:, 4:20].rearrange("p (i two) -> p i two", two=2), ["tB%d" % a], [("U", "h")])
                    CP("dve", Us[:, 2:6], tB[a][:, 0:4], ["tB%d" % a], ["Us"])
                    CP("dve", Us[:, 0:2], SCT[:, :, f], ["SCT"], ["Us"])
            if f + 1 < 8:
                load_A(f + 1)
            ukeys = [("U", 0), ("U", 1), ("U", "h")]
            TS("dve", CV, U[:, :, 0:128], wcT[:, 0, f:f + 1], None, ALU.mult, None, ukeys + ["wcT"], ["CV"])
            STT(CV, U[:, :, 1:129], wcT[:, 1, f:f + 1], CV, ALU.mult, ALU.add, ukeys + ["wcT", "CV"], ["CV"])
            STT(CV, U[:, :, 2:130], wcT[:, 2, f:f + 1], CV, ALU.mult, ALU.add, ukeys + ["wcT", "CV"], ["CV"])
            TS("dve", cvs[:, 0:4], Us[:, 0:4], wcT[:, 0, f:f + 1], None, ALU.mult, None, ["Us", "wcT"], ["cvs"])
            STT(cvs[:, 0:4], Us[:, 1:5], wcT[:, 1, f:f + 1], cvs[:, 0:4], ALU.mult, ALU.add, ["Us", "wcT", "cvs"], ["cvs"])
            STT(cvs[:, 0:4], Us[:, 2:6], wcT[:, 2, f:f + 1], cvs[:, 0:4], ALU.mult, ALU.add, ["Us", "wcT", "cvs"], ["cvs"])
            CP("dve", CO[:, :, f], U[:, 7, 128:130], ukeys, [("CO", f)])
            CP("dve", COs[:, :, f], Us[:, 4:6], ["Us"], [("COs", f)])
            for ci, (c0, n) in enumerate(tok_chunks):
                bb = fproj(sB, 0, c0, n)
                yield
                bz = fproj(sB, 1, c0, n)
                yield
                a = tac[0] % 2
                tac[0] += 1
                ACT(tA[a][:, 0:n], ps[bz][:, 0:n], AF.Silu, ["ps%d" % bz], ["tA%d" % a])
                if c0 < 1024:
                    TT("dve", tB[a][:, 0:512], CV[:, 4 * ci:4 * ci + 4, :].rearrange("p i t -> p (i t)"), ps[bb][:, 0:512], ALU.mult,
                       ["CV", "ps%d" % bb], ["tB%d" % a])
                    TT("dve", MIXC[:, f, c0:c0 + 512], tB[a][:, 0:512], tA[a][:, 0:512], ALU.mult,
                       ["tA%d" % a, "tB%d" % a] + oth_keys, [("MIXC", f, ci)])
                else:
                    TT("dve", tB[a][:, 0:4], cvs[:, 0:4], ps[bb][:, 0:4], ALU.mult, ["cvs", "ps%d" % bb], ["tB%d" % a])
                    TT("dve", MIX_s[:, 8 + f, 0:4], tB[a][:, 0:4], tA[a][:, 0:4], ALU.mult, ["tA%d" % a, "tB%d" % a], [("MIXs", 8 + f)])
            if f + 1 < 8:
                load_B(f + 1)
        DMA("sync", conv_p, CO, "cvo", [("CO", f) for f in range(8)], [])
        DMA("sync", conv_s, COs, "cvo2", [("COs", f) for f in range(8)], [])


    issue_w(0)
    issue_w(1)
    issue_w(2)
    p1_upto(2)
    load_consts()
    free_slots = list(range(NJ))
    active = []
    p1_alias_done = [False]
    gB = [None]
    gB_done = [False]
    want_B = [False]

    def make_B():
        okeys = [("xnT", t_) for t_ in range(1, 16, 2)]
        for f_ in range(8):
            for c_ in (0, 512):
                P.alias(("SZA", f_, c_), okeys)
            for c_ in range(2):
                P.alias(("MIXC", f_, c_), okeys)
        gB[0] = gen_B()

    jit = all_jobs()
    pending = next(jit, None)
    while pending is not None or active:
        while pending is not None and free_slots:
            ci, tt = pending
            p1_upto(tt + 1)
            if p1_state[0] > 16 and not p1_alias_done[0]:
                p1_alias_done[0] = True
                for _k in ("tA0", "tA1", "tB0", "tB1"):
                    P.alias(_k, ["xst0"])
                P.alias("W3", ["xst1"])
                for _k in (("U", 0), ("U", 1), ("U", "h")):
                    P.alias(_k, ["xs1"])
                P.alias("CV", ["xs0"])
            if ci >= 3:
                want_B[0] = True
            if ci >= 3:
                issue_w(ci)
                issue_w(ci + 1)
            js = free_slots.pop(0)
            g_ = job(ci, tt, js)
            try:
                next(g_)
                active.append((g_, js))
            except StopIteration:
                free_slots.append(js)
            pending = next(jit, None)
        for ent in list(active):
            g, js = ent
            try:
                next(g)
            except StopIteration:
                active.remove(ent)
                free_slots.append(js)
        if want_B[0] and gB[0] is None:
            make_B()
        if gB[0] is not None and not gB_done[0]:
            try:
                next(gB[0])
            except StopIteration:
                gB_done[0] = True
    if gB[0] is None:
        make_B()
    if not gB_done[0]:
        for _ in gB[0]:
            pass

    if stop < 4:
        P.wait_all_dma('sync'); P.emit(); return nc
    P.barrier()

    R2 = Region()
    IS = [R2.sb([128, 2048], F32) for _ in range(3)]
    NBm = [R2.sb([128, 2048], BF16) for _ in range(3)]
    junk = R2.sb([128, 2048], BF16)
    Rh = [R2.sb([128, 512], BF16) for _ in range(6)]
    Dg = [R2.sb([128, 16, 128], BF16) for _ in range(2)]
    PT = [R2.sb([128, 512], BF16) for _ in range(5)]
    rz = R2.sb([128, 512], F32)
    otmp = R2.sb([128, 512], F32)
    t256 = R2.sb([128, 256], F32)
    cmaskb = R2.sb([128, 512], BF16)
    Zacc = [R2.sb([128, 512], F32) for _ in range(2)]
    ONESF = R2.sb([128, 128], F32)
    otm = [R2.sb([128, 512], F32) for _ in range(2)]
    rzl = [R2.sb([128, 512], F32) for _ in range(2)]
    rzz = [R2.sb([128, 512], F32) for _ in range(2)]
    rhc = [0]
    ptc = [0]
    dbk = [0]
    stc_ = [0]
    SCALE = float(HD) ** -0.5
    CP("pool", cmaskb, cmask, ["cmask"], ["cmaskb"])
    P.op("pool", lambda e: e.memset(ONESF, 1.0), [], ["ONESF"])

    def indexer(i):
        ib = i % 2
        isb = i % 3
        nk = 256 * (i + 1)
        par = i % 2
        for h in range(16):
            TS("pool", Dg[ib][:, h, :], ident_b, WIDX[:, i, h:h + 1], 0.0, ALU.mult, ALU.add, ["identb", ("WIDX", i)], [("Dg", ib)])
        nch = (nk + 511) // 512
        for c in range(nch):
            k0 = 512 * c
            ncol = min(512, nk - k0)
            m0 = nk - 256
            has_mask = (k0 + ncol > m0)
            prev = None
            for h in range(16):
                hp = h % 2
                bk = (0, 1, 7)[dbk[0] % 3]
                dbk[0] += 1
                MM(ps[bk][:, 0:ncol], QIT[:, h // 2, i * 128:(i + 1) * 128],
                   KIT[:, hp, k0:k0 + ncol], True, True,
                   [("QIT", i, h // 2)] + [("KIT", tt, hp) for tt in range(k0 // 128, (k0 + ncol) // 128)], ["ps%d" % bk])
                r = rhc[0] % 6
                rhc[0] += 1
                ACT(Rh[r][:, 0:ncol], ps[bk][:, 0:ncol], AF.Relu, ["ps%d" % bk], ["Rh%d" % r])
                if prev is not None:
                    ph, pr = prev
                    MM(ps[2][:, 0:ncol], Dg[ib][:, ph, :], Rh[pr][:, 0:ncol], ph == 0, False, [("Dg", ib), "Rh%d" % pr], ["ps2"])
                prev = (h, r)
                yield
            ph, pr = prev
            MM(ps[2][:, 0:ncol], Dg[ib][:, ph, :], Rh[pr][:, 0:ncol], False, not has_mask, [("Dg", ib), "Rh%d" % pr], ["ps2"])
            if has_mask:
                lo = m0 - k0
                MM(ps[2][:, lo:lo + 256], ident_b, cmaskb[:, par * 256:(par + 1) * 256], False, True, ["identb", "cmaskb"], ["ps2"])
            ACT(IS[isb][:, k0:k0 + ncol], ps[2][:, 0:ncol], AF.Copy, ["ps2"], [("IS", isb)])

    def bisect(i):
        ib = i % 3
        nk = 256 * (i + 1)
        par = i % 2
        isk = ("IS", ib)
        X = IS[ib][:, 0:nk]
        RED(bis[:, 0:1], X, ALU.max, [isk], ["bis"])
        TT("dve", t256, IS[ib][:, nk - 256:nk], pmask[:, par * 256:(par + 1) * 256], ALU.add, [isk, "pmask"], ["t256"])
        RED(bis[:, 1:2], t256, ALU.min, ["t256"], ["bis"])
        if nk > 256:
            RED(bis[:, 2:3], IS[ib][:, 0:nk - 256], ALU.min, [isk], ["bis"])
            TT("dve", bis[:, 1:2], bis[:, 1:2], bis[:, 2:3], ALU.min, ["bis"], ["bis"])
        TT("dve", bis[:, 3:4], bis[:, 0:1], bis[:, 1:2], ALU.subtract, ["bis"], ["bis"])
        STT(bis[:, 4:5], bis[:, 3:4], -(2.0 ** -10), bis[:, 1:2], ALU.mult, ALU.add, ["bis"], ["bis"])
        TS("dve", bis[:, 4:5], bis[:, 4:5], -1e-6, None, ALU.add, None, ["bis"], ["bis"])
        TT("dve", bis[:, 5:6], bis[:, 0:1], bis[:, 4:5], ALU.subtract, ["bis"], ["bis"])
        TS("dve", Wt, pow2, bis[:, 5:6], None, ALU.mult, None, ["pow2", "bis"], ["Wt"])
        TT("dve", bis[:, 6:7], bis[:, 4:5], Wt[:, 0:1], ALU.add, ["bis", "Wt"], ["bis"])
        for k in range(KBIS):
            TS("dve", junk[:, 0:nk], X, bis[:, 6:7], None, ALU.is_ge, ALU.add, [isk, "bis"], ["junk", "bis"], accum=bis[:, 7:8])
            TS("dve", bis[:, 8:9], bis[:, 7:8], float(TOPK), 0.5, ALU.is_ge, ALU.subtract, ["bis"], ["bis"])
            if k < KBIS - 1:
                STT(bis[:, 6:7], bis[:, 8:9], Wt[:, k:k + 1], bis[:, 6:7], ALU.mult, ALU.add, ["bis", "Wt"], ["bis"])
            else:
                TS("dve", bis[:, 8:9], bis[:, 8:9], -0.5, None, ALU.add, None, ["bis"], ["bis"])
                STT(bis[:, 9:10], bis[:, 8:9], Wt[:, k:k + 1], bis[:, 6:7], ALU.mult, ALU.add, ["bis", "Wt"], ["bis"])
        TS("dve", NBm[ib][:, 0:nk], X, bis[:, 9:10], NEG, ALU.is_lt, ALU.mult, [isk, "bis"], [("NB", ib)])

    def attend_main(i):
        ib = i % 3
        nsl = 2 * i + 2
        seq = [(g, s_) for g in range(2) for s_ in range(nsl)]
        pend = None

        def s_mm(g, s_):
            sb_ = (3, 6)[stc_[0] % 2]
            stc_[0] += 1
            MM(ps[sb_].rearrange("p (h t) -> p h t", h=4), KT[:, g, s_ * 128:(s_ + 1) * 128],
               QT[:, 4 * g:4 * g + 4, i * 128:(i + 1) * 128], True, False,
               [("KT", s_)] + [("QT", i, 4 * g + hh) for hh in range(4)], ["ps%d" % sb_])
            MM(ps[sb_], NBm[ib][:, s_ * 128:(s_ + 1) * 128], ID4, False, True, [("NB", ib), "ID4"], ["ps%d" % sb_])
            r = ptc[0] % 5
            ptc[0] += 1
            ACT(PT[r], ps[sb_], AF.Exp, ["ps%d" % sb_], ["PT%d" % r], scale=SCALE)
            return r

        def pv_mm(g, s_, r):
            MM(ps[4], Vb[:, s_, g * 128:(g + 1) * 128], PT[r], s_ == 0, s_ == nsl - 1, [("V", s_), "PT%d" % r], ["ps4"])
            if s_ % 2 == 0:
                MM(ps[5], ONES, PT[r], s_ == 0, False, ["ONES", "PT%d" % r], ["ps5"])
            elif s_ == 1:
                CP("pool", Zacc[g], PT[r], ["PT%d" % r], ["Zacc%d" % g])
            else:
                TT("pool", Zacc[g], Zacc[g], PT[r], ALU.add, ["PT%d" % r, "Zacc%d" % g], ["Zacc%d" % g])
            if s_ == nsl - 1:
                MM(ps[5], ONESF, Zacc[g], False, True, ["ONESF", "Zacc%d" % g], ["ps5"])
                ACT(otm[g], ps[4], AF.Copy, ["ps4"], ["otm%d" % g])
                ACT(rzl[g], ps[5], AF.Ln, ["ps5"], ["rzl%d" % g])
                ACT(rzz[g], rzl[g], AF.Exp, ["rzl%d" % g], ["rzz%d" % g], scale=-1.0)
                TT("pool", otm[g], otm[g], rzz[g], ALU.mult, ["otm%d" % g, "rzz%d" % g], ["otm%d" % g])
                TT("pool", MIXA[:, 4 * g:4 * g + 4, i * 128:(i + 1) * 128], otm[g].rearrange("p (h t) -> p h t", h=4),
                   SZA[:, 4 * g:4 * g + 4, i * 128:(i + 1) * 128], ALU.mult,
                   ["otm%d" % g] + [("SZA", 4 * g + hh, (i // 4) * 512) for hh in range(4)], [("MIXA", i, g)])

        for (g, s_) in seq:
            r = s_mm(g, s_)
            if pend is not None:
                pv_mm(*pend)
            pend = (g, s_, r)
            yield
        pv_mm(*pend)

    def attend_epi(i):
        pass

    def drain(g_):
        for _ in g_:
            pass

    drain(indexer(0))
    bisect(0)
    drain(indexer(1))
    drain(indexer(2))
    for i in range(8):
        if i + 1 < 8:
            bisect(i + 1)
        ga = attend_main(i)
        if i + 3 < 8:
            gi = indexer(i + 3)
            n_att = 2 * (2 * i + 2)
            n_idx = 16 * ((256 * (i + 4) + 511) // 512)
            gi_alive = True
            acc = 0
            for _ in ga:
                acc += n_idx
                while acc >= n_att:
                    acc -= n_att
                    if gi_alive:
                        try:
                            next(gi)
                        except StopIteration:
                            gi_alive = False
            if gi_alive:
                drain(gi)
        else:
            drain(ga)

    if stop < 5:
        P.wait_all_dma('sync'); P.emit(); return nc
    P.barrier()

    R4 = Region()
    I_all = R4.sb([128, 129, 4], F32)
    NBs = R4.sb([128, 129, 4], F32)
    junk_s = R4.sb([128, 132], F32)
    idx16 = R4.sb([128, 16], I32)
    ptb = R4.sb([128, 1], I32)
    Q64 = R4.sb([128, 8, 2, 4], BF16)
    Q64z = R4.sb([128, 2, 64], BF16)
    cnt_b = R4.sb([128, 4], BF16)
    idx4 = R4.sb([128, 4], I32)
    sm5 = R4.sb([128, 64], F32)
    Dw = R4.sb([128, 64], F32)
    Wbc = R4.sb([128, 64], F32)
    dg4 = R4.sb([128, 8], F32)
    ones_f = R4.sb([128, 128], F32)
    cm4 = R4.sb([128, 4], F32)
    Vn = R4.sb([128, 2, 130], BF16)
    Pn = R4.sb([128, 2, 16], BF16)
    Sn = R4.sb([128, 2, 16], F32)
    att = R4.sb([128, 128], F32)
    o_sub = R4.o
    KI = R4.sb([128, 128, 64], F32)
    KIb = R4.sb([128, 128, 64], BF16)
    kiTg = [R4.sb([128, 8, 128], BF16) for _ in range(2)]
    tmpi = R4.sb([128, 8, 64], F32)
    R5 = Region()
    R5.o = o_sub
    Kc = [R5.sb([128, 8, 256], F32) for _ in range(3)]
    Vc = [R5.sb([128, 8, 256], F32) for _ in range(3)]
    Kb = [R5.sb([128, 8, 256], BF16) for _ in range(2)]
    KTc = [R5.sb([128, 8, 2, 128], BF16) for _ in range(2)]
    Vb2 = [R5.sb([128, 8, 2, 130], BF16) for _ in range(3)]
    Sm = R5.sb([128, 8, 2, 16], F32)
    Pb = [R5.sb([128, 8, 2, 16], BF16) for _ in range(2)]

    def IDMA(out, in_, idx_ap, dsem, r, w):
        P.dma("pool", lambda e: e.indirect_dma_start(out=out, out_offset=None, in_=in_,
                                                     in_offset=bass.IndirectOffsetOnAxis(ap=idx_ap, axis=0)), dsem, r, w)

    sstage = [0]

    def gen_sample():
        DMA("sync", ptb, ptab, "s0", [], ["ptb"])
        DMA("sync", cm4[0:4, :], cm4_d, "s1", [], ["cm4"])
        cki4 = cache_ki.rearrange("p (c x) -> (p c) x", c=4)
        for q4 in range(4):
            TS("dve", idx4[:, q4:q4 + 1], ptb[:, 0:1], 4.0, float(q4), ALU.mult, ALU.add, ["ptb"], [("idx4", q4)])
            IDMA(KI[:, q4 * 32:(q4 + 1) * 32, :].rearrange("p j d -> p (j d)"), cki4, idx4[:, q4:q4 + 1], "s2_%d" % q4,
                 [("idx4", q4)], [("KI", q4)])
        for c in range(16):
            TS("dve", idx16[:, c:c + 1], ptb[:, 0:1], 16.0, float(c), ALU.mult, ALU.add, ["ptb"], ["idx16"])
        P.op("pool", lambda e: e.memset(ones_f, 1.0), [], ["ones_f"])
        P.op("pool", lambda e: e.memset(I_all[:, 128, :], -1e30), [], ["I128"])
        P.op("pool", lambda e: e.memset(Vn, 1.0), [], ["Vn"])
        CP("dve", Q64[0:64, :, 0, :], QIT_s[0:64, :, 0:4], [("QITs", j) for j in range(8)], ["Q64a"])
        DMA("sync", Q64[0:64, :, 1, :], QIT_s[64:128, :, 0:4], "s3", [("QITs", j) for j in range(8)], ["Q64b"])
        P.op("pool", lambda e: e.memset(Q64z, 0.0), [], ["Q64z"])
        CP("dve", Q64z[0:64, 0, :], Q64[0:64].rearrange("p a b t -> p (a b t)"), ["Q64a", "Q64b", "Q64z"], ["Q64z"])
        DMA("sync", Q64z[64:128, 1, :], Q64[0:64].rearrange("p a b t -> p (a b t)"), "s3b", ["Q64a", "Q64b", "Q64z"], ["Q64zb"])
        TT("dve", Dw[0:4, :].rearrange("p (h t) -> p h t", h=16), WIDX[0:4, 8, :].unsqueeze(2).to_broadcast([4, 16, 4]),
           ident_f[0:4, 0:4].unsqueeze(1).to_broadcast([4, 16, 4]), ALU.mult, [("WIDX", 8), "identf"], ["Dw"])
        MM(ps[4][:, 0:64], ones_f[0:4, :], Dw[0:4, :], True, True, ["ones_f", "Dw"], ["ps4"])
        CP("dve", Wbc, ps[4][:, 0:64], ["ps4"], ["Wbc"])
        for q4 in range(4):
            if q4 % 2 == 1:
                CP("pool", KIb[:, q4 * 32:(q4 + 1) * 32, :], KI[:, q4 * 32:(q4 + 1) * 32, :], [("KI", q4)], [("KIb", q4)])
            else:
                ACT(KIb[:, q4 * 32:(q4 + 1) * 32, :], KI[:, q4 * 32:(q4 + 1) * 32, :], AF.Copy, [("KI", q4)], [("KIb", q4)])
        wb3 = Wbc.unsqueeze(1).to_broadcast([128, 8, 64])
        for gi in range(16):
            j0 = gi * 8
            tb = gi % 2
            for jp in range(4):
                TR(psb[tb][:, jp * 128:(jp + 1) * 128], KIb[:, j0 + 2 * jp:j0 + 2 * jp + 2, :].rearrange("p a d -> p (a d)"), ident_b,
                   [("KIb", j0 // 32), "identb"], ["ps%d" % tb])
            ACT(kiTg[tb][:, 0:4, :], psb[tb][:, 0:512].rearrange("p (j t) -> p j t", j=4), AF.Copy, ["ps%d" % tb], ["kiTg%d" % tb])
            db = 2 + gi % 2
            for jj in range(8):
                MM(ps[db][:, jj * 64:(jj + 1) * 64], kiTg[tb][:, jj // 2, :], Q64z[:, jj % 2, :], True, True,
                   ["kiTg%d" % tb, "Q64z", "Q64zb"], ["ps%d" % db])
            STT(tmpi, ps[db].rearrange("p (j c) -> p j c", j=8), 0.0, wb3, ALU.max, ALU.mult, ["ps%d" % db, "Wbc"], ["tmpi"])
            RED(I_all[:, j0:j0 + 8, :], tmpi.rearrange("p j (h t) -> p j t h", h=16), ALU.add, ["tmpi"], [("Ia", gi)])
            yield
        MM(ps[4][0:4, 0:64], KIT_s[:, 0:4], Q64z[:, 0, :], True, True, ["KITs", "Q64z", "Q64zb"], ["ps4"])
        STT(tmpi[0:4, 0, :], ps[4][0:4, 0:64], 0.0, Wbc[0:4, :], ALU.max, ALU.mult, ["ps4", "Wbc"], ["tmpi"])
        RED(sm5[0:4, 0:4], tmpi[0:4, 0, :].rearrange("p (h t) -> p t h", h=16), ALU.add, ["tmpi"], ["sm5"])
        TT("dve", I_all[0:4, 128, :], sm5[0:4, 0:4], cm4[0:4, :], ALU.add, ["sm5", "cm4", "I128"], ["I128"])
        IA_KEYS = [("Ia", gi) for gi in range(16)] + ["I128"]

        def bcast4(dst, src_col4, negate=False):
            P.op("pe", lambda e: e.transpose(out=ps[5][0:4, 0:128], in_=src_col4, identity=ident_f), ["sm5", "identf"], ["ps5"])
            RED(dg4[0:4, 0:1], ps[5][0:4, 0:128], ALU.max, ["ps5"], ["dg4"])
            TS("dve", dg4[0:4, 4:8], ident_f[0:4, 0:4], dg4[0:4, 0:1], None, ALU.mult, None, ["dg4", "identf"], ["dg4b"])
            MM(ps[5][:, 256:260], ones_f[0:4, :], dg4[0:4, 4:8], True, True, ["ones_f", "dg4b"], ["ps5"])
            TS("dve", dst, ps[5][:, 256:260], -1.0 if negate else 1.0, None, ALU.mult, None, ["ps5"], ["sm5"])

        Iv = I_all.rearrange("p s t -> p t s")
        RED(sm5[:, 8:12], Iv, ALU.max, IA_KEYS, ["sm5"])
        bcast4(sm5[:, 12:16], sm5[:, 8:12])
        P.op("dve", lambda e: e.tensor_reduce(out=sm5[:, 8:12], in_=Iv[:, :, 0:128], axis=AX.X, op=ALU.min, negate=True), IA_KEYS + ["sm5"], ["sm5"])
        bcast4(sm5[:, 16:20], sm5[:, 8:12], negate=True)
        TT("dve", sm5[:, 20:24], sm5[:, 12:16], sm5[:, 16:20], ALU.subtract, ["sm5"], ["sm5"])
        STT(sm5[:, 24:28], sm5[:, 20:24], -(2.0 ** -10), sm5[:, 16:20], ALU.mult, ALU.add, ["sm5"], ["sm5"])
        TS("dve", sm5[:, 24:28], sm5[:, 24:28], -1e-6, None, ALU.add, None, ["sm5"], ["sm5"])
        TT("dve", sm5[:, 28:32], sm5[:, 12:16], sm5[:, 24:28], ALU.subtract, ["sm5"], ["sm5"])
        STT(sm5[:, 32:36], sm5[:, 28:32], 0.5, sm5[:, 24:28], ALU.mult, ALU.add, ["sm5"], ["sm5"])
        def bisect_s():
            KS = 14
            for k in range(KS):
                for t in range(4):
                    TS("dve", junk_s[:, 0:129], I_all[:, :, t], sm5[:, 32 + t:33 + t], None, ALU.is_ge, ALU.add, IA_KEYS + ["sm5"],
                       ["junk_s", "cntp"], accum=cnt_b[:, t:t + 1])
                MM(ps[4][:, 0:4], ONES, cnt_b, True, True, ["ONES", "cntp"], ["ps4"])
                TS("dve", sm5[:, 40:44], ps[4][:, 0:4], float(TOPK), 0.5, ALU.is_ge, ALU.subtract, ["ps4"], ["sm5"])
                TT("dve", sm5[:, 44:48], sm5[:, 40:44], sm5[:, 28:32], ALU.mult, ["sm5"], ["sm5"])
                yield
                if k < KS - 1:
                    STT(sm5[:, 32:36], sm5[:, 44:48], 2.0 ** -(k + 1), sm5[:, 32:36], ALU.mult, ALU.add, ["sm5"], ["sm5"])
                else:
                    STT(sm5[:, 48:52], sm5[:, 28:32], -0.5, sm5[:, 44:48], ALU.mult, ALU.add, ["sm5"], ["sm5"])
                    STT(sm5[:, 52:56], sm5[:, 48:52], 2.0 ** -(k + 1), sm5[:, 32:36], ALU.mult, ALU.add, ["sm5"], ["sm5"])
            for t in range(4):
                TS("dve", NBs[:, :, t], I_all[:, :, t], sm5[:, 52 + t:53 + t], NEG, ALU.is_lt, ALU.mult, IA_KEYS + ["sm5"], ["NBs"])


        if DUMP:
            DMA('sync', dbg_I, I_all.rearrange('p s t -> p (s t)'), 'dd0', IA_KEYS, [])
            DMA('sync', dbg_sm5, sm5, 'dd1', ['sm5'], [])
            DMA('sync', dbg_NB, NBs.rearrange('p s t -> p (s t)'), 'dd2', ['NBs'], [])
            DMA('sync', dbg_Wbc, Wbc, 'dd3', ['Wbc'], [])
        NEWK = ["Kc0", "Kc1", "Kc2", "Vc0", "Vc1", "Vc2", "Kb0", "Kb1", ("KTc", 0, 0), ("KTc", 0, 1), ("KTc", 1, 0), ("KTc", 1, 1),
                "Vb2_0", "Vb2_1", "Vb2_2", "Sm", "Pb0", "Pb1"]
        OLDK = [("KI", 0), ("KI", 1), ("KI", 2), ("KI", 3), ("KIb", 0), ("KIb", 1), ("KIb", 2), ("KIb", 3), "kiTg0", "kiTg1", "tmpi"]
        for _nk in NEWK:
            P.alias(_nk, OLDK)
        for c3 in range(3):
            P.op("pool", (lambda c3: (lambda e: e.memset(Vb2[c3], 1.0)))(c3), [], ["Vb2_%d" % c3])
        ck2 = cache_k.rearrange("p (c x) -> (p c) x", c=16)
        cv2 = cache_v.rearrange("p (c x) -> (p c) x", c=16)
        QsR = [QT_s[:, 4 * g:4 * g + 4, 0:4] for g in range(2)]

        def front(c):
            b2, b3 = c % 2, c % 3
            IDMA(Kc[b3].rearrange("p j d -> p (j d)"), ck2, idx16[:, c:c + 1], "s4_%d" % b3, ["idx16"], ["Kc%d" % b3])
            IDMA(Vc[b3].rearrange("p j d -> p (j d)"), cv2, idx16[:, c:c + 1], "s5_%d" % b3, ["idx16"], ["Vc%d" % b3])
            ACT(Kb[b2], Kc[b3], AF.Copy, ["Kc%d" % b3], ["Kb%d" % b2])
            CP("dve", Vb2[b3][:, :, :, 0:128], Vc[b3].rearrange("p j (g d) -> p j g d", g=2), ["Vc%d" % b3, "Vb2_%d" % b3], ["Vb2_%d" % b3])
            yield
            for half in range(2):
                tb = half
                for jj in range(4):
                    for g in range(2):
                        TR(psb[tb][:, (jj * 2 + g) * 128:(jj * 2 + g + 1) * 128], Kb[b2][:, half * 4 + jj, g * 128:(g + 1) * 128], ident_b,
                           ["Kb%d" % b2, "identb"], ["ps%d" % tb])
                ACT(KTc[b2][:, half * 4:half * 4 + 4, :, :], psb[tb].rearrange("p (j g t) -> p j g t", j=4, g=2), AF.Copy,
                    ["ps%d" % tb], [("KTc", b2, half)])
                yield
            sbk = 2 + b2
            for jj in range(8):
                for g in range(2):
                    MM(ps[sbk][:, (jj * 2 + g) * 16:(jj * 2 + g + 1) * 16].rearrange("p (h t) -> p h t", h=4), KTc[b2][:, jj, g, :], QsR[g], True, True,
                       [("KTc", b2, jj // 4)] + [("QTs", 4 * g + hh) for hh in range(4)], ["ps%d" % sbk])
            yield

        def back(c):
            b2, b3 = c % 2, c % 3
            sbk = 2 + b2
            TT("dve", Sm.rearrange("p j g (h t) -> p j (g h) t", h=4), ps[sbk][:, 0:256].rearrange("p (j gh t) -> p j gh t", j=8, t=4),
               NBs[:, c * 8:(c + 1) * 8, :].unsqueeze(2).to_broadcast([128, 8, 8, 4]), ALU.add, ["ps%d" % sbk, "NBs"], ["Sm"])
            ACT(Pb[b2], Sm, AF.Exp, ["Sm"], ["Pb%d" % b2], scale=SCALE)
            for jj in range(8):
                for g in range(2):
                    MM(ps[4 + g][0:16, 0:129], Pb[b2][:, jj, g, :], Vb2[b3][:, jj, g, 0:129], c == 0 and jj == 0, False,
                       ["Pb%d" % b2, "Vb2_%d" % b3], ["ps%d" % (4 + g)])
            yield

        yield from front(0)
        yield from front(1)
        yield from bisect_s()
        sstage[0] = 1
        for c in range(16):
            yield from back(c)
            if c + 2 < 16:
                yield from front(c + 2)
        for g in range(2):
            MM(ps[2][0:4, g * 16:(g + 1) * 16].rearrange("p (h t) -> p h t", h=4), KT_s[:, g, 0:4], QsR[g], True, True,
               [("KTs", g)] + [("QTs", 4 * g + hh) for hh in range(4)], ["ps2"])
            CP("dve", Vn[0:4, g, 0:128], V_s[0:4, g * 128:(g + 1) * 128], ["Vs", "Vn"], ["Vn"])
        TT("dve", Sn[0:4].rearrange("p g (h t) -> p (g h) t", h=4), ps[2][0:4, 0:32].rearrange("p (gh t) -> p gh t", t=4),
           NBs[0:4, 128, :].unsqueeze(1).to_broadcast([4, 8, 4]), ALU.add, ["ps2", "NBs"], ["Sn"])
        ACT(Pn[0:4], Sn[0:4], AF.Exp, ["Sn"], ["Pn"], scale=SCALE)
        for g in range(2):
            MM(ps[4 + g][0:16, 0:129], Pn[0:4, g, :], Vn[0:4, g, 0:129], False, True, ["Pn", "Vn"], ["ps%d" % (4 + g)])
            RCP(sm5[0:16, 60:61], ps[4 + g][0:16, 128:129], ["ps%d" % (4 + g)], ["sm5r"])
            TS("dve", att[0:16, :], ps[4 + g][0:16, 0:128], sm5[0:16, 60:61], None, ALU.mult, None, ["ps%d" % (4 + g), "sm5r"], ["att"])
            P.op("pe", lambda e: e.transpose(out=ps[0][:, 0:16], in_=att[0:16, :], identity=ident_f[0:16, 0:16]), ["att", "identf"], ["ps0"])
            TT("dve", MIX_s[:, 4 * g:4 * g + 4, 0:4], ps[0][:, 0:16].rearrange("p (h t) -> p h t", h=4), SZA_s[:, 4 * g:4 * g + 4, 0:4], ALU.mult,
               ["ps0"] + [("SZAs", 4 * g + hh) for hh in range(4)], [("MIXs", 4 * g + hh) for hh in range(4)])

        if DUMP:
            DMA('sync', dbg_att, att, 'dd4', ['att'], [])
            CP('dve', junk_s[0:16, 0:132], ps[5][0:16, 0:132], ['ps5'], ['junk_s'])
            DMA('sync', dbg_O, junk_s[0:16, 0:132], 'dd5', ['junk_s'], [])
            DMA('sync', dbg_Sm, Sm.rearrange('p j g c -> p (j g c)'), 'dd6', ['Sm'], [])

    if stop < 6:
        for _ in gen_sample():
            pass
        P.wait_all_dma('sync'); P.emit(); return nc

    R3 = Region()
    R3.o = o_QT
    WO = [R3.sb([128, 16, 512], BF16) for _ in range(3)]
    xr = [R3.sb([128, 512], F32) for _ in range(2)]
    yst = [R3.sb([128, 512], F32) for _ in range(2)]
    wo_slot = {0: 0, 1: 1, 2: 2, 3: 0}
    assert R3.o <= o_MIXA, (R3.o, o_MIXA)
    xc = [0]
    obk = [0]

    step_list = [(c_, i_) for c_ in range(4) for i_ in range(8)] + [(1, 8), (2, 8), (3, 8), (0, 8)]
    xr_buf = {}

    def issue_xr(c, i):
        if (c, i) in xr_buf:
            return
        a_ = xc[0] % 2
        xc[0] += 1
        xr_buf[(c, i)] = a_
        if i < 8:
            DMA("sync", xr[a_], xall[(2 * i) * 128:(2 * i + 1) * 128, c * 512:(c + 1) * 512], "xr%d" % a_, [], ["xr%d" % a_])
        else:
            DMA("sync", xr[a_][0:4, :], xall[2048:2052, c * 512:(c + 1) * 512], "xr%d" % a_, [], ["xr%d" % a_])

    def out_step(c, i):
        s_ = wo_slot[c]
        bk = 6 + obk[0] % 2
        obk[0] += 1
        issue_xr(c, i)
        a_ = xr_buf[(c, i)]
        k_ = step_list.index((c, i))
        if k_ + 1 < len(step_list):
            issue_xr(*step_list[k_ + 1])
        np_ = 128 if i < 8 else 4
        for ft in range(16):
            if i < 8:
                lhs = (MIXA if ft < 8 else MIXC)[:, ft % 8, i * 128:(i + 1) * 128]
                rk = [("MIXA", i, 0), ("MIXA", i, 1)] if ft < 8 else [("MIXC", ft - 8, i // 4)]
            else:
                lhs = MIX_s[:, ft, 0:4]
                rk = [("MIXs", ft)]
            MM(ps[bk][:np_, :], lhs, WO[s_][:, ft, :], ft == 0, ft == 15, rk + ["WO%d" % s_], ["ps%d" % bk])
            if ft % 4 == 3 and ft < 15:
                yield
        TT("dve", yst[a_][:np_, :], ps[bk][:np_, :], xr[a_][:np_, :], ALU.add, ["ps%d" % bk, "xr%d" % a_], ["yst%d" % a_])
        if i < 8:
            DMA("sync", y_own[i * 128:(i + 1) * 128, c * 512:(c + 1) * 512], yst[a_], "yo%d" % a_, ["yst%d" % a_], [])
        else:
            DMA("sync", y_s[:, c * 512:(c + 1) * 512], yst[a_][0:4, :], "yo%d" % a_, ["yst%d" % a_], [])
        yield

    def load_wo(c):
        s_ = wo_slot[c]
        DMA("pool", WO[s_], w_out[:, c * 512:(c + 1) * 512].rearrange("(ft p) c -> p ft c", p=128), "wo%d" % s_, [], ["WO%d" % s_])

    def gen_out():
        load_wo(0)
        load_wo(1)
        load_wo(2)
        for c in range(4):
            for i in range(8):
                yield from out_step(c, i)
            if c == 0:
                load_wo(3)

    gs = gen_sample()
    go = gen_out()
    gs_alive, go_alive = True, True
    while gs_alive or go_alive:
        if gs_alive:
            try:
                next(gs)
            except StopIteration:
                gs_alive = False
        n_out = 3 if (sstage[0] == 0 and gs_alive) else 1
        for _ in range(n_out):
            if go_alive:
                try:
                    next(go)
                except StopIteration:
                    go_alive = False
    for _ in out_step(1, 8):
        pass
    wo_slot[0] = 1
    load_wo(0)
    for c_ in (2, 3, 0):
        for _ in out_step(c_, 8):
            pass

    P.wait_all_dma("sync")
    P.emit()
    return nc


_NC_CACHE = {}


def _rope_tables(pos, half):
    inv = (np.float32(THETA) ** (-(np.arange(half, dtype=np.float32) / np.float32(half)))).astype(np.float32)
    ang = pos.astype(np.float32)[:, None] * inv[None, :]
    return np.cos(ang).astype(np.float32), np.sin(ang).astype(np.float32)


def kernel(x_prompt, x_sample, cache_k, cache_v, cache_kidx, state_conv, page_table,
           norm_in, w_in, g_q, g_k, g_kidx, w_conv, w_out):
    x_prompt = np.asarray(x_prompt, np.float32)
    x_sample = np.asarray(x_sample, np.float32)
    if "nc" not in _NC_CACHE:
        _NC_CACHE["nc"] = build_program()
    nc = _NC_CACHE["nc"]
    ck = np.ascontiguousarray(np.asarray(cache_k, np.float32)[0].reshape(1280, PAGE * 256))
    cv = np.ascontiguousarray(np.asarray(cache_v, np.float32)[0].reshape(1280, PAGE * 256))
    cki = np.ascontiguousarray(np.asarray(cache_kidx, np.float32)[0].reshape(1280, PAGE * 64))
    w_in0 = np.ascontiguousarray(np.asarray(w_in, np.float32)[0])
    w_out0 = np.ascontiguousarray(np.asarray(w_out, np.float32)[0])
    pow2 = (2.0 ** -(np.arange(KBIS, dtype=np.float32) + 1)).astype(np.float32)
    in_maps = []
    slots_all = []
    for c in range(8):
        b, j = c // 2, c % 2
        own = T_OWN[j]
        oth = T_OWN[1 - j]
        slots = []
        for i in range(8):
            slots += [own[i], oth[i]]
        slots_all.append(slots)
        xall = np.zeros((2176, D), np.float32)
        pos = np.zeros(2176, np.int64)
        for s_, t in enumerate(slots):
            xall[s_ * 128:(s_ + 1) * 128] = x_prompt[b, t * 128:(t + 1) * 128]
            pos[s_ * 128:(s_ + 1) * 128] = np.arange(t * 128, (t + 1) * 128)
        xall[2048:2052] = x_sample[c]
        pos[2048:2052] = PAST + np.arange(4)
        for i in range(8):
            t = own[i]
            if t > 0:
                xall[2052 + 2 * i:2054 + 2 * i] = x_prompt[b, t * 128 - 2:t * 128]
        c32, s32 = _rope_tables(pos, 16)
        c16, s16 = _rope_tables(pos, 8)
        cm = np.zeros((128, 512), np.float32)
        tri = np.where(np.arange(128)[None, :] <= np.arange(128)[:, None], 0.0, -30000.0).astype(np.float32)
        for par in range(2):
            cm[:, par * 256:par * 256 + 128] = tri
            other_before = oth[par] < own[par]
            cm[:, par * 256 + 128:par * 256 + 256] = 0.0 if other_before else -30000.0
        pm = (-2.0 * cm).astype(np.float32)
        in_maps.append(dict(
            xall=xall, w_in=w_in0, w_out=w_out0,
            norm_in=np.ascontiguousarray(np.asarray(norm_in, np.float32)[0].reshape(16, 128).T),
            g_q=np.ascontiguousarray(np.asarray(g_q, np.float32)[0]),
            g_k=np.ascontiguousarray(np.asarray(g_k, np.float32)[0]),
            g_ki=np.ascontiguousarray(np.asarray(g_kidx, np.float32)[0]),
            w_conv=np.ascontiguousarray(np.asarray(w_conv, np.float32)[0].reshape(3, 8, 128).transpose(2, 0, 1)),
            cos32=c32, sin32=s32, cos16=c16, sin16=s16, cmask=cm, pmask=pm, pow2=pow2,
            stc=np.ascontiguousarray(np.asarray(state_conv, np.float32)[0, c].reshape(2, 8, 128).transpose(2, 0, 1)),
            ptab=np.ascontiguousarray(np.asarray(page_table, np.int32)[c].reshape(128, 1)),
            cache_k=ck, cache_v=cv, cache_ki=cki,
            cm4=np.where(np.arange(4)[:, None] <= np.arange(4)[None, :], 0.0, -1e30).astype(np.float32),
        ))
    res = run_bass_kernel_spmd(nc, in_maps, core_ids=list(range(8)))
    R = res.results
    y_prompt = np.zeros((NB_, S, D), np.float32)
    y_sample = np.zeros((DEC_B, DEC_T, D), np.float32)
    k_p = np.zeros((1, NB_, S, NKV, HD), np.float32)
    v_p = np.zeros((1, NB_, S, NKV, HD), np.float32)
    ki_p = np.zeros((1, NB_, S, IDXD), np.float32)
    conv_p = np.zeros((1, NB_, 2, 1024), np.float32)
    k_s = np.zeros((1, DEC_B, DEC_T, NKV, HD), np.float32)
    v_s = np.zeros((1, DEC_B, DEC_T, NKV, HD), np.float32)
    ki_s = np.zeros((1, DEC_B, DEC_T, IDXD), np.float32)
    conv_s = np.zeros((1, DEC_B, 2, 1024), np.float32)
    for c in range(8):
        b, j = c // 2, c % 2
        r = R[c]
        for i, t in enumerate(T_OWN[j]):
            sl = slice(t * 128, (t + 1) * 128)
            y_prompt[b, sl] = r["y_own"][i * 128:(i + 1) * 128]
            k_p[0, b, sl] = r["k_own"][i * 128:(i + 1) * 128].reshape(128, NKV, HD)
            v_p[0, b, sl] = r["v_own"][i * 128:(i + 1) * 128].reshape(128, NKV, HD)
            ki_p[0, b, sl] = r["ki_own"][i * 128:(i + 1) * 128]
        if j == 0:
            conv_p[0, b] = r["conv_p"].transpose(1, 2, 0).reshape(2, 1024)
        y_sample[c] = r["y_s"]
        k_s[0, c] = r["k_s"].reshape(4, NKV, HD)
        v_s[0, c] = r["v_s"].reshape(4, NKV, HD)
        ki_s[0, c] = r["ki_s"]
        conv_s[0, c] = r["conv_s"].transpose(1, 2, 0).reshape(2, 1024)
    return (y_prompt, y_sample, k_p, v_p, ki_p, conv_p, k_s, v_s, ki_s, conv_s)
```

```python
import numpy as np
import concourse.bass as bass
import concourse.mybir as mybir
from concourse.bass_utils import run_bass_kernel_spmd

F32 = mybir.dt.float32
BF16 = mybir.dt.bfloat16
I32 = mybir.dt.int32
AF = mybir.ActivationFunctionType
ALU = mybir.AluOpType
AX = mybir.AxisListType

D = 2048
S = 2048
NB_ = 4
DEC_B = 8
DEC_T = 4
PAST = 16384
PAGE = 128
NPAGES = PAST // PAGE
D_IN = 7760
HD = 128
NH = 8
NKV = 2
IDXH = 16
IDXD = 64
TOPK = 256
EPS = 1e-6
THETA = 500000.0
KBIS = 14
NEG = -30000.0
C_Q, C_K, C_V, C_ZA, C_QI, C_KI, C_WI, C_H, C_B, C_C, C_ZC = 0, 1024, 1280, 1536, 2560, 3584, 3648, 3664, 4688, 5712, 6736
T_OWN = ([0, 3, 4, 7, 8, 11, 12, 15], [1, 2, 5, 6, 9, 10, 13, 14])
ENGS = ("sync", "act", "dve", "pool", "pe")
NOSELF = ("pe",)


class Prog:
    def __init__(self, nc):
        self.nc = nc
        self.ops = {e: [] for e in ENGS}
        self.count = {e: 0 for e in ENGS}
        self.dcount = {}
        self.last_write = {}
        self.readers = {}
        self.waited = {e: {} for e in ENGS}
        self.pending_barrier = {e: None for e in ENGS}
        import os
        self.limit = int(os.environ.get('DBG_OPS', '100000000'))
        self.total = 0

    def _deps(self, eng, reads, writes):
        deps = {}

        def need(sv):
            if sv is None:
                return
            k, v = sv
            if deps.get(k, 0) < v:
                deps[k] = v

        for r in reads:
            need(self.last_write.get(r))
        for w in writes:
            need(self.last_write.get(w))
            for rd in self.readers.get(w, ()):
                need(rd)
        pb = self.pending_barrier[eng]
        if pb is not None:
            for kv in pb:
                need(kv)
            self.pending_barrier[eng] = None
        out = []
        for k, v in deps.items():
            if k == eng and eng in NOSELF:
                continue
            if self.waited[eng].get(k, 0) >= v:
                continue
            self.waited[eng][k] = v
            out.append((k, v))
        return out

    def _commit(self, me, reads, writes):
        for r in reads:
            self.readers.setdefault(r, []).append(me)
        for w in writes:
            self.last_write[w] = me
            self.readers[w] = []

    def op(self, eng, fn, reads=(), writes=()):
        self.total += 1
        if self.total > self.limit:
            return
        reads = tuple(reads)
        writes = tuple(writes)
        waits = self._deps(eng, reads, writes)
        self.count[eng] += 1
        self.ops[eng].append(("op", fn, waits))
        self._commit((eng, self.count[eng]), reads, writes)

    def dma(self, queue, fn, dsem, reads=(), writes=()):
        self.total += 1
        if self.total > self.limit:
            return
        reads = tuple(reads)
        writes = tuple(writes)
        key = ("d", dsem)
        waits = self._deps(queue, reads, writes)
        self.dcount[key] = self.dcount.get(key, 0) + 1
        self.ops[queue].append(("dma", fn, waits, key))
        self._commit((key, 16 * self.dcount[key]), reads, writes)

    def alias(self, new_key, old_keys):
        lst = self.readers.setdefault(new_key, [])
        for ok in old_keys:
            lw = self.last_write.get(ok)
            if lw is not None:
                lst.append(lw)
            lst.extend(self.readers.get(ok, ()))

    def barrier(self):
        snap = [(e, self.count[e]) for e in ENGS if self.count[e] > 0]
        snap += [(k, 16 * n) for k, n in self.dcount.items()]
        for e in ENGS:
            self.pending_barrier[e] = list(snap)

    def wait_all_dma(self, eng="sync"):
        waits = [(k, 16 * n) for k, n in self.dcount.items()]
        self.ops[eng].append(("waitonly", None, waits))

    def emit(self):
        nc = self.nc
        from contextlib import ExitStack
        with ExitStack() as es:
            sems = {}
            for e in ENGS:
                sems[e] = es.enter_context(nc.semaphore("s_" + e))
            for i, k in enumerate(self.dcount.keys()):
                sems[k] = es.enter_context(nc.semaphore("d_%d" % i))
            block = es.enter_context(nc.Block())

            def run(engname, h):
                for rec in self.ops[engname]:
                    kind, fn, waits = rec[0], rec[1], rec[2]
                    for k, v in waits:
                        h.wait_ge(sems[k], v)
                    if kind == "op":
                        fn(h).then_inc(sems[engname], 1)
                    elif kind == "dma":
                        fn(h).then_inc(sems[rec[3]], 16)

            @block.sync
            def _(h):
                run("sync", h)

            @block.scalar
            def _(h):
                run("act", h)

            @block.vector
            def _(h):
                run("dve", h)

            @block.gpsimd
            def _(h):
                run("pool", h)

            @block.tensor
            def _(h):
                run("pe", h)


def build_program(stop=99):
    nc = bass.Bass("TRN2", target_bir_lowering=False)
    P = Prog(nc)

    def din(name, shape, dt=F32):
        return nc.dram_tensor(name, list(shape), dt, kind="ExternalInput").ap()

    def dout(name, shape, dt=F32):
        return nc.dram_tensor(name, list(shape), dt, kind="ExternalOutput").ap()

    xall = din("xall", [2176, D])
    w_in = din("w_in", [D, D_IN])
    w_out = din("w_out", [D, D])
    norm_in = din("norm_in", [128, 16])
    g_q = din("g_q", [HD])
    g_k = din("g_k", [HD])
    g_ki = din("g_ki", [IDXD])
    w_conv = din("w_conv", [128, 3, 8])
    cos32 = din("cos32", [2176, 16])
    sin32 = din("sin32", [2176, 16])
    cos16 = din("cos16", [2176, 8])
    sin16 = din("sin16", [2176, 8])
    cmask_d = din("cmask", [128, 512])
    pmask_d = din("pmask", [128, 512])
    pow2_d = din("pow2", [KBIS])
    stc = din("stc", [128, 2, 8])
    ptab = din("ptab", [128, 1], I32)
    cm4_d = din("cm4", [4, 4])
    cache_k = din("cache_k", [1280, PAGE * 256])
    cache_v = din("cache_v", [1280, PAGE * 256])
    cache_ki = din("cache_ki", [1280, PAGE * 64])

    y_own = dout("y_own", [1024, D])
    y_s = dout("y_s", [4, D])
    k_own = dout("k_own", [1024, 256])
    v_own = dout("v_own", [1024, 256])
    ki_own = dout("ki_own", [1024, 64])
    conv_p = dout("conv_p", [128, 2, 8])
    k_s = dout("k_s", [4, 256])
    v_s = dout("v_s", [4, 256])
    ki_s = dout("ki_s", [4, 64])
    conv_s = dout("conv_s", [128, 2, 8])
    import os
    DUMP = os.environ.get("DBG_DUMP") == "1"
    if DUMP:
        dbg_I = dout("dbg_I", [128, 516])
        dbg_sm5 = dout("dbg_sm5", [128, 64])
        dbg_NB = dout("dbg_NB", [128, 516])
        dbg_att = dout("dbg_att", [128, 128])
        dbg_O = dout("dbg_O", [16, 132])
        dbg_Sm = dout("dbg_Sm", [128, 256])
        dbg_Wbc = dout("dbg_Wbc", [128, 64])

    BASE = 16640
    LIMIT = 229376
    cur = [BASE]
    cnt = [0]

    def alloc_at(off, shape, dt):
        cnt[0] += 1
        return nc.alloc_sbuf_tensor_at("t%d" % cnt[0], list(shape), dt, offset=off).ap()

    def nbytes(shape, dt):
        n = int(np.prod(shape[1:])) * (2 if dt == BF16 else 4)
        return (n + 63) // 64 * 64

    def sb(shape, dt):
        off = cur[0]
        cur[0] += nbytes(shape, dt)
        assert cur[0] <= LIMIT, ("SBUF overflow", cur[0])
        return alloc_at(off, shape, dt)

    ident_f = sb([128, 128], F32)
    ident_b = sb([128, 128], BF16)
    ID4 = sb([128, 512], BF16)
    ONES = sb([128, 128], BF16)
    gq_bc = sb([128, 128], F32)
    gk_bc = sb([128, 128], F32)
    gki_bc = sb([128, 64], F32)
    COS32 = sb([128, 17, 16], F32)
    SIN32 = sb([128, 17, 16], F32)
    COS16 = sb([128, 17, 8], F32)
    SIN16 = sb([128, 17, 8], F32)
    wcT = sb([128, 3, 8], F32)
    cmask = sb([128, 512], F32)
    pmask = sb([128, 512], F32)
    pow2 = sb([128, KBIS], F32)
    WIDX = sb([128, 9, 16], F32)
    ginT = sb([128, 16], F32)
    SCT = sb([128, 2, 8], F32)
    CO = sb([128, 2, 8], F32)
    COs = sb([128, 2, 8], F32)
    sml = sb([128, 80], F32)
    sml1 = sb([128, 8], F32)
    rt = sb([128, 4, 32], F32)
    bis = sb([128, 16], F32)
    Wt = sb([128, KBIS], F32)
    o_QT = cur[0]
    QT = sb([128, 8, 1024], BF16)
    KT = sb([128, 2, 2048], BF16)
    Vb = sb([128, 16, 256], BF16)
    QIT = sb([128, 8, 1024], BF16)
    KIT = sb([128, 2, 2048], BF16)
    o_MIXA = cur[0]
    MIXA = sb([128, 8, 1024], BF16)
    xstb = [alloc_at(o_MIXA + 8192 * j, [128, 2048], F32) for j in range(2)]
    QT_s = sb([128, 8, 32], BF16)
    KT_s = sb([128, 2, 32], BF16)
    V_s = sb([128, 256], BF16)
    QIT_s = sb([128, 8, 32], BF16)
    KIT_s = sb([128, 32], BF16)
    SZA_s = sb([128, 8, 32], BF16)
    MIX_s = sb([128, 16, 32], BF16)
    Us = sb([128, 8], F32)
    cvs = sb([128, 8], F32)
    XOFF = cur[0]
    xnT_oth = sb([128, 16, 1024], BF16)
    SZA = alloc_at(XOFF, [128, 8, 1024], BF16)
    MIXC = alloc_at(XOFF + 16384, [128, 8, 1024], BF16)
    ROFF = cur[0]
    RSIZE = LIMIT - ROFF
    print('SBUF: XOFF', XOFF, 'ROFF', ROFF, 'RSIZE', RSIZE)
    assert RSIZE >= 79872, RSIZE

    class Region:
        def __init__(self):
            self.o = ROFF

        def sb(self, shape, dt):
            off = self.o
            self.o += nbytes(shape, dt)
            assert self.o <= LIMIT, ("region overflow", self.o - ROFF)
            return alloc_at(off, shape, dt)

    R1 = Region()
    xnT_own = R1.sb([128, 16, 1152], BF16)
    o_x = R1.o
    xst = R1.sb([128, 2048], F32)
    o_xs = R1.o
    xs = R1.sb([128, 2048], BF16)
    Wr = [R1.sb([128, 16, 256], BF16) for _ in range(3)]
    o_tq = R1.o
    NJ = 5
    tQ = [R1.sb([128, 256], F32) for _ in range(NJ)]
    tQ2 = [R1.sb([128, 256], F32) for _ in range(NJ)]
    kbj = [R1.sb([128, 256], BF16) for _ in range(NJ)]
    rtj = [R1.sb([128, 4, 32], F32) for _ in range(NJ)]
    tA = [alloc_at(o_MIXA + 2048 * j, [128, 512], F32) for j in range(2)]
    tB = [alloc_at(o_MIXA + 4096 + 2048 * j, [128, 512], F32) for j in range(2)]
    Wr.append(alloc_at(o_MIXA + 8192, [128, 16, 256], BF16))
    U = alloc_at(o_x, [128, 8, 130], F32)
    xsb = [xs, alloc_at(o_x, [128, 2048], BF16)]
    CV = alloc_at(o_xs, [128, 8, 128], F32)

    ps = [nc.alloc_psum_tensor("ps%d" % i, [128, 512], F32).ap() for i in range(8)]
    psb = [p.bitcast(BF16) for p in ps]

    def ACT(out, in_, func, r, w, **kw):
        P.op("act", lambda e: e.activation(out=out, in_=in_, func=func, **kw), r, w)

    def TS(eng, out, in0, s1, s2, op0, op1, r, w, accum=None):
        if op1 is None:
            P.op(eng, lambda e: e.tensor_scalar(out=out, in0=in0, scalar1=s1, scalar2=None, op0=op0), r, w)
        elif accum is None:
            P.op(eng, lambda e: e.tensor_scalar(out=out, in0=in0, scalar1=s1, scalar2=s2, op0=op0, op1=op1), r, w)
        else:
            P.op(eng, lambda e: e.tensor_scalar(out=out, in0=in0, scalar1=s1, scalar2=s2, op0=op0, op1=op1,
                                                accum_out=accum), r, w)

    def TT(eng, out, in0, in1, op, r, w):
        P.op(eng, lambda e: e.tensor_tensor(out=out, in0=in0, in1=in1, op=op), r, w)

    def STT(out, in0, scalar, in1, op0, op1, r, w):
        P.op("dve", lambda e: e.scalar_tensor_tensor(out=out, in0=in0, scalar=scalar, in1=in1, op0=op0, op1=op1), r, w)

    def CP(eng, out, in_, r, w):
        P.op(eng, lambda e: e.tensor_copy(out=out, in_=in_), r, w)

    def RED(out, in_, op, r, w):
        P.op("dve", lambda e: e.tensor_reduce(out=out, in_=in_, axis=AX.X, op=op), r, w)

    def RCP(out, in_, r, w):
        P.op("dve", lambda e: e.reciprocal(out=out, in_=in_), r, w)

    def MM(out, lhsT, rhs, start, stop, r, w):
        P.op("pe", lambda e: e.matmul(out, lhsT=lhsT, rhs=rhs, start=start, stop=stop), r, w)

    def TR(out, in_, ident, r, w):
        P.op("pe", lambda e: e.transpose(out=out, in_=in_, identity=ident), r, w)

    def DMA(queue, out, in_, dsem, r, w, **kw):
        P.dma(queue, lambda e: e.dma_start(out=out, in_=in_, **kw), dsem, r, w)

    DMA("sync", ginT, norm_in, "c3", [], ["gin"])

    def load_consts():
        DMA("sync", gq_bc, g_q.partition_broadcast(128), "c0", [], ["gq"])
        DMA("sync", gk_bc, g_k.partition_broadcast(128), "c1", [], ["gk"])
        DMA("sync", gki_bc, g_ki.partition_broadcast(128), "c2", [], ["gki"])
        DMA("sync", COS32[:, 0:16, :], cos32[0:2048, :].rearrange("(t p) c -> p t c", p=128), "c4", [], ["cs0"])
        DMA("sync", SIN32[:, 0:16, :], sin32[0:2048, :].rearrange("(t p) c -> p t c", p=128), "c5", [], ["cs1"])
        DMA("sync", COS16[:, 0:16, :], cos16[0:2048, :].rearrange("(t p) c -> p t c", p=128), "c6", [], ["cs2"])
        DMA("sync", SIN16[:, 0:16, :], sin16[0:2048, :].rearrange("(t p) c -> p t c", p=128), "c7", [], ["cs3"])
        DMA("sync", COS32[:, 16, :], cos32[2048:2176, :], "c8", [], ["cs0e"])
        DMA("sync", SIN32[:, 16, :], sin32[2048:2176, :], "c9", [], ["cs1e"])
        DMA("sync", COS16[:, 16, :], cos16[2048:2176, :], "c10", [], ["cs2e"])
        DMA("sync", SIN16[:, 16, :], sin16[2048:2176, :], "c11", [], ["cs3e"])
        DMA("sync", cmask, cmask_d, "c12", [], ["cmask"])
        DMA("sync", pmask, pmask_d, "c13", [], ["pmask"])
        DMA("sync", pow2, pow2_d.partition_broadcast(128), "c14", [], ["pow2"])
        DMA("sync", wcT, w_conv, "c15", [], ["wcT"])
        DMA("sync", SCT, stc, "c16", [], ["SCT"])

    CONST_KEYS = ["gq", "gk", "gki", "gin", "cs0", "cs1", "cs2", "cs3", "cs0e", "cs1e", "cs2e", "cs3e",
                  "cmask", "pmask", "pow2", "wcT", "SCT"]
    P.op("pool", lambda e: e.memset(ident_f, 0.0), [], ["identf"])
    P.op("pool", lambda e: e.affine_select(out=ident_f, in_=ident_f, pattern=[[-1, 128]], compare_op=ALU.not_equal,
                                           fill=1.0, base=0, channel_multiplier=1), ["identf"], ["identf"])
    CP("pool", ident_b, ident_f, ["identf"], ["identb"])
    for a in range(4):
        CP("pool", ID4[:, a * 128:(a + 1) * 128], ident_f, ["identf"], ["ID4"])
    P.op("pool", lambda e: e.memset(ONES, 1.0), [], ["ONES"])

    if stop < 1:
        load_consts()
        P.wait_all_dma('sync'); P.emit(); return nc
    def tile_src(tt):
        if tt == 16:
            return xall[2048:2176, :], 128
        return xall[tt * 128:(tt + 1) * 128, :], 128

    def xn_dst(tt):
        if tt == 16:
            return xnT_own, 1024, 128
        if tt % 2 == 0:
            return xnT_own, (tt // 2) * 128, 128
        return xnT_oth, (tt // 2) * 128, 128

    p1_state = [0]

    def p1_A(tt):
        src, np_ = tile_src(tt)
        pb_ = tt % 2
        xst_, xs_ = xstb[pb_], xsb[pb_]
        kx, kxs, ksm1 = "xst%d" % pb_, "xs%d" % pb_, "sml1_%d" % pb_
        sm1 = sml1[:, 4 * pb_:4 * pb_ + 4]
        DMA("sync", xst_[:np_, :], src, "xld%d" % pb_, [], [kx])
        ACT(xs_[:np_, :], xst_[:np_, :], AF.Square, [kx], [kxs, ksm1], accum_out=sm1[:np_, 0:1])
        TS("dve", sm1[:np_, 1:2], sm1[:np_, 0:1], 1.0 / D, EPS, ALU.mult, ALU.add, [ksm1], [ksm1])
        ACT(sm1[:np_, 2:3], sm1[:np_, 1:2], AF.Sqrt, [ksm1], [ksm1])
        RCP(sm1[:np_, 3:4], sm1[:np_, 2:3], [ksm1], [ksm1])
        TS("dve", xs_[:np_, :], xst_[:np_, :], sm1[:np_, 3:4], None, ALU.mult, None, [kx, ksm1], [kxs])

    def p1_B(tt):
        src, np_ = tile_src(tt)
        pb_ = tt % 2
        xs_ = xsb[pb_]
        kxs = "xs%d" % pb_
        dst, t0, _ = xn_dst(tt)
        for half in range(2):
            bk = 2 + (2 * tt + half) % 2
            for j in range(8):
                kt = half * 8 + j
                TR(psb[bk][:, j * 128:j * 128 + np_], xs_[:np_, kt * 128:(kt + 1) * 128], ident_b[:np_, :np_],
                   [kxs, "identb"], ["ps%d" % bk])
            src_ps = psb[bk].rearrange("p (j t) -> p j t", j=8)[:, :, 0:np_]
            ACT(dst[:, half * 8:half * 8 + 8, t0:t0 + np_], src_ps, AF.Copy, ["ps%d" % bk], [("xnTraw", tt)])
        TT("pool", dst[:, :, t0:t0 + np_], dst[:, :, t0:t0 + np_], ginT.unsqueeze(2).to_broadcast([128, 16, np_]), ALU.mult,
           [("xnTraw", tt), "gin"], [("xnT", tt)])

    def p1_tile(tt):
        if tt == 0:
            p1_A(0)
        if tt + 1 <= 16:
            p1_A(tt + 1)
        p1_B(tt)

    def p1_upto(tt):
        while p1_state[0] <= min(tt, 16):
            p1_tile(p1_state[0])
            p1_state[0] += 1

    if stop < 2:
        load_consts()
        p1_upto(16)
        P.wait_all_dma('sync'); P.emit(); return nc

    wctr = [0]

    def win_cols(c0, n):
        return w_in[:, c0:c0 + n].rearrange("(kt p) c -> p kt c", p=128)

    own_tiles = [2 * i for i in range(8)] + [16]
    all_tiles = list(range(17))
    bkc = [0]
    trc = [0]
    tac = [0]

    def xn_src(tt, kt):
        dst, t0, n = xn_dst(tt)
        return dst[:, kt, t0:t0 + n], n

    def own_idx(tt):
        return tt // 2

    CK = ["cs0", "cs1", "cs2", "cs3", "cs0e", "cs1e", "cs2e", "cs3e"]

    chunks = [("k", C_K, 256, all_tiles, 2, 128, 16, gk_bc, True, 0),
              ("v", C_V, 256, all_tiles, 0, 0, 0, None, False, 0),
              ("kiw", C_KI, 80, all_tiles, 1, 64, 8, gki_bc, True, 0)]
    for c in range(4):
        chunks.append(("q", C_Q + 256 * c, 256, own_tiles, 2, 128, 16, gq_bc, True, c))
    for c in range(4):
        chunks.append(("qi", C_QI + 256 * c, 256, own_tiles, 4, 64, 8, None, False, c))

    wslot = {}

    def issue_w(ci):
        if ci >= len(chunks) or ci in wslot:
            return
        sidx = ci if ci < 3 else (ci - 3) % 2
        wslot[ci] = sidx
        DMA("pool", Wr[sidx][:, :, 0:chunks[ci][2]], win_cols(chunks[ci][1], chunks[ci][2]), "w%d" % sidx, [], ["W%d" % sidx])

    def job(ci, tt, js):
        kind, c0, ncols, _, nh, hd, rh, g_bc, do_norm, cidx = chunks[ci]
        sw = wslot[ci]
        bk = bkc[0] % 2
        bkc[0] += 1
        pk = "ps%d" % bk
        for kt in range(16):
            lhs, n = xn_src(tt, kt)
            MM(ps[bk][:, 0:ncols], lhs, Wr[sw][:, kt, 0:ncols], kt == 0, kt == 15, [("xnT", tt), "W%d" % sw], [pk])
        own = (tt == 16) or (tt % 2 == 0)
        i = own_idx(tt)
        tq, tq2, kb_, rt_ = tQ[js], tQ2[js], kbj[js], rtj[js]
        kq, kq2, kkb, krt, ksm = "tQ%d" % js, "tQ2%d" % js, "kb%d" % js, "rt%d" % js, "sml%d" % js
        sm = sml[:, 16 * js:16 * js + 16]
        if kind == "v":
            if own:
                ACT(tq[:, 0:256], ps[bk][:, 0:256], AF.Copy, [pk], [kq])
            if tt == 16:
                ACT(V_s[:, :], ps[bk][:, 0:256], AF.Copy, [pk], ["Vs"])
            else:
                ACT(Vb[:, tt, :], ps[bk][:, 0:256], AF.Copy, [pk], [("V", tt)])
            if own:
                for _ in range(6):
                    yield
                if tt == 16:
                    DMA("sync", v_s, tq[0:4, 0:256], "ko%d" % js, [kq], [])
                else:
                    DMA("sync", v_own[i * 128:(i + 1) * 128, :], tq[:, 0:256], "ko%d" % js, [kq], [])
            return
        nd = nh * hd
        ACT(tq[:, 0:ncols], ps[bk][:, 0:ncols], AF.Copy, [pk], [kq])
        if do_norm:
            for h in range(nh):
                ACT(tq2[:, h * hd:(h + 1) * hd], ps[bk][:, h * hd:(h + 1) * hd], AF.Square, [pk], [kq2, ksm],
                    accum_out=sm[:, h:h + 1])
            yield
            TS("dve", sm[:, 4:4 + nh], sm[:, 0:nh], 1.0 / hd, EPS, ALU.mult, ALU.add, [ksm], [ksm])
            yield
            ACT(sm[:, 8:8 + nh], sm[:, 4:4 + nh], AF.Sqrt, [ksm], [ksm])
            yield
            RCP(sm[:, 12:12 + nh], sm[:, 8:8 + nh], [ksm], [ksm])
            yield
            for h in range(nh):
                STT(tq2[:, h * hd:(h + 1) * hd], tq[:, h * hd:(h + 1) * hd], sm[:, 12 + h:13 + h], g_bc[:, 0:hd],
                    ALU.mult, ALU.mult, [kq, ksm, "gq", "gk", "gki"], [kq2])
                yield
            src, ksrc = tq2, kq2
        else:
            yield
            src, ksrc = tq, kq
        if kind == "kiw":
            if tt == 16:
                TS("dve", WIDX[:, 8, :], tq[:, 64:80], 1.0 / 32.0, None, ALU.mult, None, [kq], [("WIDX", 8)])
            elif own:
                TS("dve", WIDX[:, i, :], tq[:, 64:80], 1.0 / 32.0, None, ALU.mult, None, [kq], [("WIDX", i)])
        s3 = src[:, 0:nd].rearrange("p (h d) -> p h d", h=nh)
        x1 = s3[:, :, 0:rh]
        x2 = s3[:, :, rh:2 * rh]
        if rh == 16:
            cT, sT = COS32[:, tt, :], SIN32[:, tt, :]
        else:
            cT, sT = COS16[:, tt, :], SIN16[:, tt, :]
        cb = cT.unsqueeze(1).to_broadcast([128, nh, rh])
        sbb = sT.unsqueeze(1).to_broadcast([128, nh, rh])
        r4 = [rt_[:, j, 0:nh * rh].rearrange("p (h r) -> p h r", h=nh) for j in range(4)]
        TT("dve", r4[0], x1, cb, ALU.mult, [ksrc] + CK, [krt])
        TT("dve", r4[1], x2, sbb, ALU.mult, [ksrc] + CK, [krt])
        yield
        TT("dve", r4[2], x2, cb, ALU.mult, [ksrc] + CK, [krt])
        TT("dve", r4[3], x1, sbb, ALU.mult, [ksrc] + CK, [krt])
        yield
        TT("dve", x1, r4[0], r4[1], ALU.subtract, [krt], [ksrc])
        TT("dve", x2, r4[2], r4[3], ALU.add, [krt], [ksrc])
        yield
        if kind == "kiw":
            P.op("pool", lambda e: e.memset(kb_[:, 64:192], 0.0), [], [kkb])
            CP("dve", kb_[:, 0:64], src[:, 0:64], [ksrc], [kkb])
            ACT(kb_[:, 192:256], src[:, 0:64], AF.Copy, [ksrc], [kkb])
            nblk = 2
        else:
            CP("dve", kb_[:, 0:256], src[:, 0:256], [ksrc], [kkb])
            nblk = 2
        yield
        tb = 2 + trc[0] % 2
        trc[0] += 1
        for j in range(nblk):
            TR(psb[tb][:, j * 128:(j + 1) * 128], kb_[:, j * 128:(j + 1) * 128], ident_b, [kkb, "identb"], ["ps%d" % tb])
        for j in range(nblk):
            if kind == "k":
                dst, wk = (KT_s[:, j, 0:32], ("KTs", j)) if tt == 16 else (KT[:, j, tt * 128:(tt + 1) * 128], ("KT", tt))
            elif kind == "kiw":
                if tt == 16:
                    if j == 1:
                        continue
                    dst, wk = KIT_s[:, 0:32], "KITs"
                else:
                    dst, wk = KIT[:, j, tt * 128:(tt + 1) * 128], ("KIT", tt, j)
            elif kind == "q":
                hh = 2 * cidx + j
                dst, wk = (QT_s[:, hh, 0:32], ("QTs", hh)) if tt == 16 else (QT[:, hh, i * 128:(i + 1) * 128], ("QT", i, hh))
            else:
                hh = 2 * cidx + j
                dst, wk = (QIT_s[:, hh, 0:32], ("QITs", hh)) if tt == 16 else (QIT[:, hh, i * 128:(i + 1) * 128], ("QIT", i, hh))
            ncp = dst.shape[-1]
            ACT(dst, psb[tb][:, j * 128:j * 128 + ncp], AF.Copy, ["ps%d" % tb], [wk])
        if own and kind in ("k", "kiw"):
            for _ in range(6):
                yield
            if kind == "k":
                if tt == 16:
                    DMA("sync", k_s, src[0:4, 0:256], "ko%d" % js, [ksrc], [])
                else:
                    DMA("sync", k_own[i * 128:(i + 1) * 128, :], src[:, 0:256], "ko%d" % js, [ksrc], [])
            else:
                if tt == 16:
                    DMA("sync", ki_s, src[0:4, 0:64], "ko%d" % js, [ksrc], [])
                else:
                    DMA("sync", ki_own[i * 128:(i + 1) * 128, :], src[:, 0:64], "ko%d" % js, [ksrc], [])

    def all_jobs():
        for tt in all_tiles:
            for ci in range(3):
                yield (ci, tt)
        for ci in range(3, len(chunks)):
            for tt in chunks[ci][3]:
                yield (ci, tt)

    bkB = [0]
    wB = [0]

    def gen_B():
        oth_keys = [("xnT", tt) for tt in range(1, 16, 2)]
        tok_chunks = [(0, 512), (512, 512), (1024, 32)]

        def fproj(s, sub, c0, n):
            bk = 4 + bkB[0] % 4
            bkB[0] += 1
            tts = [("xnT", 2 * i) for i in range(8)] if c0 < 1024 else [("xnT", 16)]
            for kt in range(16):
                MM(ps[bk][:, 0:n], Wr[s][:, kt, sub * 128:(sub + 1) * 128], xnT_own[:, kt, c0:c0 + n], kt == 0, kt == 15,
                   tts + ["W%d" % s], ["ps%d" % bk])
            return bk

        for f2 in range(4):
            s = 2 + wB[0] % 2
            wB[0] += 1
            DMA("pool", Wr[s][:, :, 0:256], win_cols(C_ZA + 256 * f2, 256), "w%d" % s, [], ["W%d" % s])
            for sub in range(2):
                f = 2 * f2 + sub
                for (c0, n) in tok_chunks:
                    bk = fproj(s, sub, c0, n)
                    if c0 < 1024:
                        ACT(SZA[:, f, c0:c0 + n], ps[bk][:, 0:n], AF.Silu, ["ps%d" % bk] + oth_keys, [("SZA", f, c0)])
                        yield
                    else:
                        ACT(SZA_s[:, f, 0:32], ps[bk][:, 0:32], AF.Silu, ["ps%d" % bk], [("SZAs", f)])
                    yield

        sA, sB = 2, 3

        def load_A(f_):
            DMA("pool", Wr[sA][:, :, 0:128], win_cols(C_H + 128 * f_, 128), "w%d" % sA, [], ["W%d" % sA])
            DMA("pool", Wr[sA][:, :, 128:256], win_cols(C_C + 128 * f_, 128), "w%d" % sA, [], ["W%d" % sA])

        def load_B(f_):
            DMA("pool", Wr[sB][:, :, 0:128], win_cols(C_B + 128 * f_, 128), "w%d" % sB, [], ["W%d" % sB])
            DMA("pool", Wr[sB][:, :, 128:256], win_cols(C_ZC + 128 * f_, 128), "w%d" % sB, [], ["W%d" % sB])

        load_A(0)
        load_B(0)
        for f in range(8):
            for ci, (c0, n) in enumerate(tok_chunks):
                bh = fproj(sA, 0, c0, n)
                yield
                bc = fproj(sA, 1, c0, n)
                yield
                a = tac[0] % 2
                tac[0] += 1
                ACT(tA[a][:, 0:n], ps[bh][:, 0:n], AF.Copy, ["ps%d" % bh], ["tA%d" % a])
                if c0 < 1024:
                    TT("dve", U[:, 4 * ci:4 * ci + 4, 2:130], tA[a][:, 0:512].rearrange("p (i t) -> p i t", i=4),
                       ps[bc][:, 0:512].rearrange("p (i t) -> p i t", i=4), ALU.mult, ["tA%d" % a, "ps%d" % bc], [("U", ci)])
                else:
                    TT("dve", tB[a][:, 0:32], tA[a][:, 0:32], ps[bc][:, 0:32], ALU.mult, ["tA%d" % a, "ps%d" % bc], ["tB%d" % a])
                    CP("dve", U[:, :, 0:2], tB[a][:, 4:20].rearrange("p (i two) -> p i two", two=2), ["tB%d" % a], [("U", "h")])
                    CP("dve", Us[:, 2:6], tB[a][:, 0:4], ["tB%d" % a], ["Us"])
                    CP("dve", Us[:, 0:2], SCT[:, :, f], ["SCT"], ["Us"])
            if f + 1 < 8:
                load_A(f + 1)
            ukeys = [("U", 0), ("U", 1), ("U", "h")]
            TS("dve", CV, U[:, :, 0:128], wcT[:, 0, f:f + 1], None, ALU.mult, None, ukeys + ["wcT"], ["CV"])
            STT(CV, U[:, :, 1:129], wcT[:, 1, f:f + 1], CV, ALU.mult, ALU.add, ukeys + ["wcT", "CV"], ["CV"])
            STT(CV, U[:, :, 2:130], wcT[:, 2, f:f + 1], CV, ALU.mult, ALU.add, ukeys + ["wcT", "CV"], ["CV"])
            TS("dve", cvs[:, 0:4], Us[:, 0:4], wcT[:, 0, f:f + 1], None, ALU.mult, None, ["Us", "wcT"], ["cvs"])
            STT(cvs[:, 0:4], Us[:, 1:5], wcT[:, 1, f:f + 1], cvs[:, 0:4], ALU.mult, ALU.add, ["Us", "wcT", "cvs"], ["cvs"])
            STT(cvs[:, 0:4], Us[:, 2:6], wcT[:, 2, f:f + 1], cvs[:, 0:4], ALU.mult, ALU.add, ["Us", "wcT", "cvs"], ["cvs"])
            CP("dve", CO[:, :, f], U[:, 7, 128:130], ukeys, [("CO", f)])
            CP("dve", COs[:, :, f], Us[:, 4:6], ["Us"], [("COs", f)])
            for ci, (c0, n) in enumerate(tok_chunks):
                bb = fproj(sB, 0, c0, n)
                yield
                bz = fproj(sB, 1, c0, n)
                yield
                a = tac[0] % 2
                tac[0] += 1
                ACT(tA[a][:, 0:n], ps[bz][:, 0:n], AF.Silu, ["ps%d" % bz], ["tA%d" % a])
                if c0 < 1024:
                    TT("dve", tB[a][:, 0:512], CV[:, 4 * ci:4 * ci + 4, :].rearrange("p i t -> p (i t)"), ps[bb][:, 0:512], ALU.mult,
                       ["CV", "ps%d" % bb], ["tB%d" % a])
                    TT("dve", MIXC[:, f, c0:c0 + 512], tB[a][:, 0:512], tA[a][:, 0:512], ALU.mult,
                       ["tA%d" % a, "tB%d" % a] + oth_keys, [("MIXC", f, ci)])
                else:
                    TT("dve", tB[a][:, 0:4], cvs[:, 0:4], ps[bb][:, 0:4], ALU.mult, ["cvs", "ps%d" % bb], ["tB%d" % a])
                    TT("dve", MIX_s[:, 8 + f, 0:4], tB[a][:, 0:4], tA[a][:, 0:4], ALU.mult, ["tA%d" % a, "tB%d" % a], [("MIXs", 8 + f)])
            if f + 1 < 8:
                load_B(f + 1)
        DMA("sync", conv_p, CO, "cvo", [("CO", f) for f in range(8)], [])
        DMA("sync", conv_s, COs, "cvo2", [("COs", f) for f in range(8)], [])


    issue_w(0)
    issue_w(1)
    issue_w(2)
    p1_upto(2)
    load_consts()
    free_slots = list(range(NJ))
    active = []
    p1_alias_done = [False]
    gB = [None]
    gB_done = [False]
    want_B = [False]

    def make_B():
        okeys = [("xnT", t_) for t_ in range(1, 16, 2)]
        for f_ in range(8):
            for c_ in (0, 512):
                P.alias(("SZA", f_, c_), okeys)
            for c_ in range(2):
                P.alias(("MIXC", f_, c_), okeys)
        gB[0] = gen_B()

    jit = all_jobs()
    pending = next(jit, None)
    while pending is not None or active:
        while pending is not None and free_slots:
            ci, tt = pending
            p1_upto(tt + 1)
            if p1_state[0] > 16 and not p1_alias_done[0]:
                p1_alias_done[0] = True
                for _k in ("tA0", "tA1", "tB0", "tB1"):
                    P.alias(_k, ["xst0"])
                P.alias("W3", ["xst1"])
                for _k in (("U", 0), ("U", 1), ("U", "h")):
                    P.alias(_k, ["xs1"])
                P.alias("CV", ["xs0"])
            if ci >= 3:
                want_B[0] = True
            if ci >= 3:
                issue_w(ci)
                issue_w(ci + 1)
            js = free_slots.pop(0)
            g_ = job(ci, tt, js)
            try:
                next(g_)
                active.append((g_, js))
            except StopIteration:
                free_slots.append(js)
            pending = next(jit, None)
        for ent in list(active):
            g, js = ent
            try:
                next(g)
            except StopIteration:
                active.remove(ent)
                free_slots.append(js)
        if want_B[0] and gB[0] is None:
            make_B()
        if gB[0] is not None and not gB_done[0]:
            try:
                next(gB[0])
            except StopIteration:
                gB_done[0] = True
    if gB[0] is None:
        make_B()
    if not gB_done[0]:
        for _ in gB[0]:
            pass

    if stop < 4:
        P.wait_all_dma('sync'); P.emit(); return nc
    P.barrier()

    R2 = Region()
    IS = [R2.sb([128, 2048], F32) for _ in range(3)]
    NBm = [R2.sb([128, 2048], BF16) for _ in range(3)]
    junk = R2.sb([128, 2048], BF16)
    Rh = [R2.sb([128, 512], BF16) for _ in range(8)]
    Dg = [R2.sb([128, 16, 128], BF16) for _ in range(2)]
    PT = [R2.sb([128, 512], BF16) for _ in range(7)]
    rz = R2.sb([128, 512], F32)
    otmp = R2.sb([128, 512], F32)
    t256 = R2.sb([128, 256], F32)
    cmaskb = R2.sb([128, 512], BF16)
    Zacc = [R2.sb([128, 512], F32) for _ in range(2)]
    ONESF = R2.sb([128, 128], F32)
    otm = [R2.sb([128, 512], F32) for _ in range(2)]
    rzl = [R2.sb([128, 512], F32) for _ in range(2)]
    rzz = [R2.sb([128, 512], F32) for _ in range(2)]
    rhc = [0]
    ptc = [0]
    dbk = [0]
    stc_ = [0]
    SCALE = float(HD) ** -0.5
    CP("pool", cmaskb, cmask, ["cmask"], ["cmaskb"])
    P.op("pool", lambda e: e.memset(ONESF, 1.0), [], ["ONESF"])

    def indexer(i):
        ib = i % 2
        isb = i % 3
        nk = 256 * (i + 1)
        par = i % 2
        for h in range(16):
            TS("pool", Dg[ib][:, h, :], ident_b, WIDX[:, i, h:h + 1], 0.0, ALU.mult, ALU.add, ["identb", ("WIDX", i)], [("Dg", ib)])
        nch = (nk + 511) // 512
        for c in range(nch):
            k0 = 512 * c
            ncol = min(512, nk - k0)
            m0 = nk - 256
            has_mask = (k0 + ncol > m0)
            prev = None
            for h in range(16):
                hp = h % 2
                bk = (0, 1, 7)[dbk[0] % 3]
                dbk[0] += 1
                MM(ps[bk][:, 0:ncol], QIT[:, h // 2, i * 128:(i + 1) * 128],
                   KIT[:, hp, k0:k0 + ncol], True, True,
                   [("QIT", i, h // 2)] + [("KIT", tt, hp) for tt in range(k0 // 128, (k0 + ncol) // 128)], ["ps%d" % bk])
                r = rhc[0] % 8
                rhc[0] += 1
                ACT(Rh[r][:, 0:ncol], ps[bk][:, 0:ncol], AF.Relu, ["ps%d" % bk], ["Rh%d" % r])
                if prev is not None:
                    ph, pr = prev
                    MM(ps[2][:, 0:ncol], Dg[ib][:, ph, :], Rh[pr][:, 0:ncol], ph == 0, False, [("Dg", ib), "Rh%d" % pr], ["ps2"])
                prev = (h, r)
                yield
            ph, pr = prev
            MM(ps[2][:, 0:ncol], Dg[ib][:, ph, :], Rh[pr][:, 0:ncol], False, not has_mask, [("Dg", ib), "Rh%d" % pr], ["ps2"])
            if has_mask:
                lo = m0 - k0
                MM(ps[2][:, lo:lo + 256], ident_b, cmaskb[:, par * 256:(par + 1) * 256], False, True, ["identb", "cmaskb"], ["ps2"])
            ACT(IS[isb][:, k0:k0 + ncol], ps[2][:, 0:ncol], AF.Copy, ["ps2"], [("IS", isb)])

    def bisect(i):
        ib = i % 3
        nk = 256 * (i + 1)
        par = i % 2
        isk = ("IS", ib)
        X = IS[ib][:, 0:nk]
        RED(bis[:, 0:1], X, ALU.max, [isk], ["bis"])
        TT("dve", t256, IS[ib][:, nk - 256:nk], pmask[:, par * 256:(par + 1) * 256], ALU.add, [isk, "pmask"], ["t256"])
        RED(bis[:, 1:2], t256, ALU.min, ["t256"], ["bis"])
        if nk > 256:
            RED(bis[:, 2:3], IS[ib][:, 0:nk - 256], ALU.min, [isk], ["bis"])
            TT("dve", bis[:, 1:2], bis[:, 1:2], bis[:, 2:3], ALU.min, ["bis"], ["bis"])
        TT("dve", bis[:, 3:4], bis[:, 0:1], bis[:, 1:2], ALU.subtract, ["bis"], ["bis"])
        STT(bis[:, 4:5], bis[:, 3:4], -(2.0 ** -10), bis[:, 1:2], ALU.mult, ALU.add, ["bis"], ["bis"])
        TS("dve", bis[:, 4:5], bis[:, 4:5], -1e-6, None, ALU.add, None, ["bis"], ["bis"])
        TT("dve", bis[:, 5:6], bis[:, 0:1], bis[:, 4:5], ALU.subtract, ["bis"], ["bis"])
        TS("dve", Wt, pow2, bis[:, 5:6], None, ALU.mult, None, ["pow2", "bis"], ["Wt"])
        TT("dve", bis[:, 6:7], bis[:, 4:5], Wt[:, 0:1], ALU.add, ["bis", "Wt"], ["bis"])
        for k in range(KBIS):
            TS("dve", junk[:, 0:nk], X, bis[:, 6:7], None, ALU.is_ge, ALU.add, [isk, "bis"], ["junk", "bis"], accum=bis[:, 7:8])
            TS("dve", bis[:, 8:9], bis[:, 7:8], float(TOPK), 0.5, ALU.is_ge, ALU.subtract, ["bis"], ["bis"])
            if k < KBIS - 1:
                STT(bis[:, 6:7], bis[:, 8:9], Wt[:, k:k + 1], bis[:, 6:7], ALU.mult, ALU.add, ["bis", "Wt"], ["bis"])
            else:
                TS("dve", bis[:, 8:9], bis[:, 8:9], -0.5, None, ALU.add, None, ["bis"], ["bis"])
                STT(bis[:, 9:10], bis[:, 8:9], Wt[:, k:k + 1], bis[:, 6:7], ALU.mult, ALU.add, ["bis", "Wt"], ["bis"])
        TS("dve", NBm[ib][:, 0:nk], X, bis[:, 9:10], NEG, ALU.is_lt, ALU.mult, [isk, "bis"], [("NB", ib)])

    def attend_main(i):
        ib = i % 3
        nsl = 2 * i + 2
        seq = [(g, s_) for g in range(2) for s_ in range(nsl)]
        pend = None

        def s_mm(g, s_):
            sb_ = (3, 6)[stc_[0] % 2]
            stc_[0] += 1
            MM(ps[sb_].rearrange("p (h t) -> p h t", h=4), KT[:, g, s_ * 128:(s_ + 1) * 128],
               QT[:, 4 * g:4 * g + 4, i * 128:(i + 1) * 128], True, False,
               [("KT", s_)] + [("QT", i, 4 * g + hh) for hh in range(4)], ["ps%d" % sb_])
            MM(ps[sb_], NBm[ib][:, s_ * 128:(s_ + 1) * 128], ID4, False, True, [("NB", ib), "ID4"], ["ps%d" % sb_])
            r = ptc[0] % 7
            ptc[0] += 1
            ACT(PT[r], ps[sb_], AF.Exp, ["ps%d" % sb_], ["PT%d" % r], scale=SCALE)
            return r

        def pv_mm(g, s_, r):
            MM(ps[4], Vb[:, s_, g * 128:(g + 1) * 128], PT[r], s_ == 0, s_ == nsl - 1, [("V", s_), "PT%d" % r], ["ps4"])
            if s_ % 2 == 0:
                MM(ps[5], ONES, PT[r], s_ == 0, False, ["ONES", "PT%d" % r], ["ps5"])
            elif s_ == 1:
                CP("pool", Zacc[g], PT[r], ["PT%d" % r], ["Zacc%d" % g])
            else:
                TT("pool", Zacc[g], Zacc[g], PT[r], ALU.add, ["PT%d" % r, "Zacc%d" % g], ["Zacc%d" % g])
            if s_ == nsl - 1:
                MM(ps[5], ONESF, Zacc[g], False, True, ["ONESF", "Zacc%d" % g], ["ps5"])
                ACT(otm[g], ps[4], AF.Copy, ["ps4"], ["otm%d" % g])
                ACT(rzl[g], ps[5], AF.Ln, ["ps5"], ["rzl%d" % g])
                ACT(rzz[g], rzl[g], AF.Exp, ["rzl%d" % g], ["rzz%d" % g], scale=-1.0)
                TT("pool", otm[g], otm[g], rzz[g], ALU.mult, ["otm%d" % g, "rzz%d" % g], ["otm%d" % g])
                TT("pool", MIXA[:, 4 * g:4 * g + 4, i * 128:(i + 1) * 128], otm[g].rearrange("p (h t) -> p h t", h=4),
                   SZA[:, 4 * g:4 * g + 4, i * 128:(i + 1) * 128], ALU.mult,
                   ["otm%d" % g] + [("SZA", 4 * g + hh, (i // 4) * 512) for hh in range(4)], [("MIXA", i, g)])

        for (g, s_) in seq:
            r = s_mm(g, s_)
            if pend is not None:
                pv_mm(*pend)
            pend = (g, s_, r)
            yield
        pv_mm(*pend)

    def attend_epi(i):
        pass

    def drain(g_):
        for _ in g_:
            pass

    drain(indexer(0))
    bisect(0)
    drain(indexer(1))
    drain(indexer(2))
    for i in range(8):
        if i + 1 < 8:
            bisect(i + 1)
        ga = attend_main(i)
        if i + 3 < 8:
            gi = indexer(i + 3)
            n_att = 2 * (2 * i + 2)
            n_idx = 16 * ((256 * (i + 4) + 511) // 512)
            gi_alive = True
            acc = 0
            for _ in ga:
                acc += n_idx
                while acc >= n_att:
                    acc -= n_att
                    if gi_alive:
                        try:
                            next(gi)
                        except StopIteration:
                            gi_alive = False
            if gi_alive:
                drain(gi)
        else:
            drain(ga)

    if stop < 5:
        P.wait_all_dma('sync'); P.emit(); return nc
    P.barrier()

    R4 = Region()
    I_all = R4.sb([128, 129, 4], F32)
    NBs = R4.sb([128, 129, 4], F32)
    junk_s = R4.sb([128, 132], F32)
    idx16 = R4.sb([128, 16], I32)
    ptb = R4.sb([128, 1], I32)
    Q64 = R4.sb([128, 8, 2, 4], BF16)
    Q64z = R4.sb([128, 2, 64], BF16)
    cnt_b = R4.sb([128, 4], BF16)
    idx4 = R4.sb([128, 4], I32)
    sm5 = R4.sb([128, 64], F32)
    Dw = R4.sb([128, 64], F32)
    Wbc = R4.sb([128, 64], F32)
    dg4 = R4.sb([128, 8], F32)
    ones_f = R4.sb([128, 128], F32)
    cm4 = R4.sb([128, 4], F32)
    Vn = R4.sb([128, 2, 130], BF16)
    Pn = R4.sb([128, 2, 16], BF16)
    Sn = R4.sb([128, 2, 16], F32)
    att = R4.sb([128, 128], F32)
    o_sub = R4.o
    KI = R4.sb([128, 128, 64], F32)
    KIb = R4.sb([128, 128, 64], BF16)
    kiTg = [R4.sb([128, 8, 128], BF16) for _ in range(2)]
    tmpi = R4.sb([128, 8, 64], F32)
    R5 = Region()
    R5.o = o_sub
    Kc = [R5.sb([128, 8, 256], F32) for _ in range(3)]
    Vc = [R5.sb([128, 8, 256], F32) for _ in range(3)]
    Kb = [R5.sb([128, 8, 256], BF16) for _ in range(2)]
    KTc = [R5.sb([128, 8, 2, 128], BF16) for _ in range(2)]
    Vb2 = [R5.sb([128, 8, 2, 130], BF16) for _ in range(3)]
    Sm = R5.sb([128, 8, 2, 16], F32)
    Pb = [R5.sb([128, 8, 2, 16], BF16) for _ in range(2)]

    def IDMA(out, in_, idx_ap, dsem, r, w):
        P.dma("pool", lambda e: e.indirect_dma_start(out=out, out_offset=None, in_=in_,
                                                     in_offset=bass.IndirectOffsetOnAxis(ap=idx_ap, axis=0)), dsem, r, w)

    sstage = [0]

    def gen_sample():
        DMA("sync", ptb, ptab, "s0", [], ["ptb"])
        DMA("sync", cm4[0:4, :], cm4_d, "s1", [], ["cm4"])
        cki4 = cache_ki.rearrange("p (c x) -> (p c) x", c=4)
        for q4 in range(4):
            TS("dve", idx4[:, q4:q4 + 1], ptb[:, 0:1], 4.0, float(q4), ALU.mult, ALU.add, ["ptb"], [("idx4", q4)])
            IDMA(KI[:, q4 * 32:(q4 + 1) * 32, :].rearrange("p j d -> p (j d)"), cki4, idx4[:, q4:q4 + 1], "s2_%d" % q4,
                 [("idx4", q4)], [("KI", q4)])
        for c in range(16):
            TS("dve", idx16[:, c:c + 1], ptb[:, 0:1], 16.0, float(c), ALU.mult, ALU.add, ["ptb"], ["idx16"])
        P.op("pool", lambda e: e.memset(ones_f, 1.0), [], ["ones_f"])
        P.op("pool", lambda e: e.memset(I_all[:, 128, :], -1e30), [], ["I128"])
        P.op("pool", lambda e: e.memset(Vn, 1.0), [], ["Vn"])
        CP("dve", Q64[0:64, :, 0, :], QIT_s[0:64, :, 0:4], [("QITs", j) for j in range(8)], ["Q64a"])
        DMA("sync", Q64[0:64, :, 1, :], QIT_s[64:128, :, 0:4], "s3", [("QITs", j) for j in range(8)], ["Q64b"])
        P.op("pool", lambda e: e.memset(Q64z, 0.0), [], ["Q64z"])
        CP("dve", Q64z[0:64, 0, :], Q64[0:64].rearrange("p a b t -> p (a b t)"), ["Q64a", "Q64b", "Q64z"], ["Q64z"])
        DMA("sync", Q64z[64:128, 1, :], Q64[0:64].rearrange("p a b t -> p (a b t)"), "s3b", ["Q64a", "Q64b", "Q64z"], ["Q64zb"])
        TT("dve", Dw[0:4, :].rearrange("p (h t) -> p h t", h=16), WIDX[0:4, 8, :].unsqueeze(2).to_broadcast([4, 16, 4]),
           ident_f[0:4, 0:4].unsqueeze(1).to_broadcast([4, 16, 4]), ALU.mult, [("WIDX", 8), "identf"], ["Dw"])
        MM(ps[4][:, 0:64], ones_f[0:4, :], Dw[0:4, :], True, True, ["ones_f", "Dw"], ["ps4"])
        CP("dve", Wbc, ps[4][:, 0:64], ["ps4"], ["Wbc"])
        for q4 in range(4):
            if q4 % 2 == 1:
                CP("pool", KIb[:, q4 * 32:(q4 + 1) * 32, :], KI[:, q4 * 32:(q4 + 1) * 32, :], [("KI", q4)], [("KIb", q4)])
            else:
                ACT(KIb[:, q4 * 32:(q4 + 1) * 32, :], KI[:, q4 * 32:(q4 + 1) * 32, :], AF.Copy, [("KI", q4)], [("KIb", q4)])
        wb3 = Wbc.unsqueeze(1).to_broadcast([128, 8, 64])
        for gi in range(16):
            j0 = gi * 8
            tb = gi % 2
            for jp in range(4):
                TR(psb[tb][:, jp * 128:(jp + 1) * 128], KIb[:, j0 + 2 * jp:j0 + 2 * jp + 2, :].rearrange("p a d -> p (a d)"), ident_b,
                   [("KIb", j0 // 32), "identb"], ["ps%d" % tb])
            ACT(kiTg[tb][:, 0:4, :], psb[tb][:, 0:512].rearrange("p (j t) -> p j t", j=4), AF.Copy, ["ps%d" % tb], ["kiTg%d" % tb])
            db = 2 + gi % 2
            for jj in range(8):
                MM(ps[db][:, jj * 64:(jj + 1) * 64], kiTg[tb][:, jj // 2, :], Q64z[:, jj % 2, :], True, True,
                   ["kiTg%d" % tb, "Q64z", "Q64zb"], ["ps%d" % db])
            STT(tmpi, ps[db].rearrange("p (j c) -> p j c", j=8), 0.0, wb3, ALU.max, ALU.mult, ["ps%d" % db, "Wbc"], ["tmpi"])
            RED(I_all[:, j0:j0 + 8, :], tmpi.rearrange("p j (h t) -> p j t h", h=16), ALU.add, ["tmpi"], [("Ia", gi)])
            yield
        MM(ps[4][0:4, 0:64], KIT_s[:, 0:4], Q64z[:, 0, :], True, True, ["KITs", "Q64z", "Q64zb"], ["ps4"])
        STT(tmpi[0:4, 0, :], ps[4][0:4, 0:64], 0.0, Wbc[0:4, :], ALU.max, ALU.mult, ["ps4", "Wbc"], ["tmpi"])
        RED(sm5[0:4, 0:4], tmpi[0:4, 0, :].rearrange("p (h t) -> p t h", h=16), ALU.add, ["tmpi"], ["sm5"])
        TT("dve", I_all[0:4, 128, :], sm5[0:4, 0:4], cm4[0:4, :], ALU.add, ["sm5", "cm4", "I128"], ["I128"])
        IA_KEYS = [("Ia", gi) for gi in range(16)] + ["I128"]

        def bcast4(dst, src_col4, negate=False):
            P.op("pe", lambda e: e.transpose(out=ps[5][0:4, 0:128], in_=src_col4, identity=ident_f), ["sm5", "identf"], ["ps5"])
            RED(dg4[0:4, 0:1], ps[5][0:4, 0:128], ALU.max, ["ps5"], ["dg4"])
            TS("dve", dg4[0:4, 4:8], ident_f[0:4, 0:4], dg4[0:4, 0:1], None, ALU.mult, None, ["dg4", "identf"], ["dg4b"])
            MM(ps[5][:, 256:260], ones_f[0:4, :], dg4[0:4, 4:8], True, True, ["ones_f", "dg4b"], ["ps5"])
            TS("dve", dst, ps[5][:, 256:260], -1.0 if negate else 1.0, None, ALU.mult, None, ["ps5"], ["sm5"])

        Iv = I_all.rearrange("p s t -> p t s")
        RED(sm5[:, 8:12], Iv, ALU.max, IA_KEYS, ["sm5"])
        bcast4(sm5[:, 12:16], sm5[:, 8:12])
        P.op("dve", lambda e: e.tensor_reduce(out=sm5[:, 8:12], in_=Iv[:, :, 0:128], axis=AX.X, op=ALU.min, negate=True), IA_KEYS + ["sm5"], ["sm5"])
        bcast4(sm5[:, 16:20], sm5[:, 8:12], negate=True)
        TT("dve", sm5[:, 20:24], sm5[:, 12:16], sm5[:, 16:20], ALU.subtract, ["sm5"], ["sm5"])
        STT(sm5[:, 24:28], sm5[:, 20:24], -(2.0 ** -10), sm5[:, 16:20], ALU.mult, ALU.add, ["sm5"], ["sm5"])
        TS("dve", sm5[:, 24:28], sm5[:, 24:28], -1e-6, None, ALU.add, None, ["sm5"], ["sm5"])
        TT("dve", sm5[:, 28:32], sm5[:, 12:16], sm5[:, 24:28], ALU.subtract, ["sm5"], ["sm5"])
        STT(sm5[:, 32:36], sm5[:, 28:32], 0.5, sm5[:, 24:28], ALU.mult, ALU.add, ["sm5"], ["sm5"])
        def bisect_s():
            KS = 14
            for k in range(KS):
                for t in range(4):
                    TS("dve", junk_s[:, 0:129], I_all[:, :, t], sm5[:, 32 + t:33 + t], None, ALU.is_ge, ALU.add, IA_KEYS + ["sm5"],
                       ["junk_s", "cntp"], accum=cnt_b[:, t:t + 1])
                MM(ps[4][:, 0:4], ONES, cnt_b, True, True, ["ONES", "cntp"], ["ps4"])
                TS("dve", sm5[:, 40:44], ps[4][:, 0:4], float(TOPK), 0.5, ALU.is_ge, ALU.subtract, ["ps4"], ["sm5"])
                TT("dve", sm5[:, 44:48], sm5[:, 40:44], sm5[:, 28:32], ALU.mult, ["sm5"], ["sm5"])
                yield
                if k < KS - 1:
                    STT(sm5[:, 32:36], sm5[:, 44:48], 2.0 ** -(k + 1), sm5[:, 32:36], ALU.mult, ALU.add, ["sm5"], ["sm5"])
                else:
                    STT(sm5[:, 48:52], sm5[:, 28:32], -0.5, sm5[:, 44:48], ALU.mult, ALU.add, ["sm5"], ["sm5"])
                    STT(sm5[:, 52:56], sm5[:, 48:52], 2.0 ** -(k + 1), sm5[:, 32:36], ALU.mult, ALU.add, ["sm5"], ["sm5"])
            for t in range(4):
                TS("dve", NBs[:, :, t], I_all[:, :, t], sm5[:, 52 + t:53 + t], NEG, ALU.is_lt, ALU.mult, IA_KEYS + ["sm5"], ["NBs"])


        if DUMP:
            DMA('sync', dbg_I, I_all.rearrange('p s t -> p (s t)'), 'dd0', IA_KEYS, [])
            DMA('sync', dbg_sm5, sm5, 'dd1', ['sm5'], [])
            DMA('sync', dbg_NB, NBs.rearrange('p s t -> p (s t)'), 'dd2', ['NBs'], [])
            DMA('sync', dbg_Wbc, Wbc, 'dd3', ['Wbc'], [])
        NEWK = ["Kc0", "Kc1", "Kc2", "Vc0", "Vc1", "Vc2", "Kb0", "Kb1", ("KTc", 0, 0), ("KTc", 0, 1), ("KTc", 1, 0), ("KTc", 1, 1),
                "Vb2_0", "Vb2_1", "Vb2_2", "Sm", "Pb0", "Pb1"]
        OLDK = [("KI", 0), ("KI", 1), ("KI", 2), ("KI", 3), ("KIb", 0), ("KIb", 1), ("KIb", 2), ("KIb", 3), "kiTg0", "kiTg1", "tmpi"]
        for _nk in NEWK:
            P.alias(_nk, OLDK)
        for c3 in range(3):
            P.op("pool", (lambda c3: (lambda e: e.memset(Vb2[c3], 1.0)))(c3), [], ["Vb2_%d" % c3])
        ck2 = cache_k.rearrange("p (c x) -> (p c) x", c=16)
        cv2 = cache_v.rearrange("p (c x) -> (p c) x", c=16)
        QsR = [QT_s[:, 4 * g:4 * g + 4, 0:4] for g in range(2)]

        def front(c):
            b2, b3 = c % 2, c % 3
            IDMA(Kc[b3].rearrange("p j d -> p (j d)"), ck2, idx16[:, c:c + 1], "s4_%d" % b3, ["idx16"], ["Kc%d" % b3])
            IDMA(Vc[b3].rearrange("p j d -> p (j d)"), cv2, idx16[:, c:c + 1], "s5_%d" % b3, ["idx16"], ["Vc%d" % b3])
            ACT(Kb[b2], Kc[b3], AF.Copy, ["Kc%d" % b3], ["Kb%d" % b2])
            CP("dve", Vb2[b3][:, :, :, 0:128], Vc[b3].rearrange("p j (g d) -> p j g d", g=2), ["Vc%d" % b3, "Vb2_%d" % b3], ["Vb2_%d" % b3])
            yield
            for half in range(2):
                tb = half
                for jj in range(4):
                    for g in range(2):
                        TR(psb[tb][:, (jj * 2 + g) * 128:(jj * 2 + g + 1) * 128], Kb[b2][:, half * 4 + jj, g * 128:(g + 1) * 128], ident_b,
                           ["Kb%d" % b2, "identb"], ["ps%d" % tb])
                ACT(KTc[b2][:, half * 4:half * 4 + 4, :, :], psb[tb].rearrange("p (j g t) -> p j g t", j=4, g=2), AF.Copy,
                    ["ps%d" % tb], [("KTc", b2, half)])
                yield
            sbk = 2 + b2
            for jj in range(8):
                for g in range(2):
                    MM(ps[sbk][:, (jj * 2 + g) * 16:(jj * 2 + g + 1) * 16].rearrange("p (h t) -> p h t", h=4), KTc[b2][:, jj, g, :], QsR[g], True, True,
                       [("KTc", b2, jj // 4)] + [("QTs", 4 * g + hh) for hh in range(4)], ["ps%d" % sbk])
            yield

        def back(c):
            b2, b3 = c % 2, c % 3
            sbk = 2 + b2
            TT("dve", Sm.rearrange("p j g (h t) -> p j (g h) t", h=4), ps[sbk][:, 0:256].rearrange("p (j gh t) -> p j gh t", j=8, t=4),
               NBs[:, c * 8:(c + 1) * 8, :].unsqueeze(2).to_broadcast([128, 8, 8, 4]), ALU.add, ["ps%d" % sbk, "NBs"], ["Sm"])
            ACT(Pb[b2], Sm, AF.Exp, ["Sm"], ["Pb%d" % b2], scale=SCALE)
            for jj in range(8):
                for g in range(2):
                    MM(ps[4 + g][0:16, 0:129], Pb[b2][:, jj, g, :], Vb2[b3][:, jj, g, 0:129], c == 0 and jj == 0, False,
                       ["Pb%d" % b2, "Vb2_%d" % b3], ["ps%d" % (4 + g)])
            yield

        yield from front(0)
        yield from front(1)
        yield from bisect_s()
        sstage[0] = 1
        for c in range(16):
            yield from back(c)
            if c + 2 < 16:
                yield from front(c + 2)
        for g in range(2):
            MM(ps[2][0:4, g * 16:(g + 1) * 16].rearrange("p (h t) -> p h t", h=4), KT_s[:, g, 0:4], QsR[g], True, True,
               [("KTs", g)] + [("QTs", 4 * g + hh) for hh in range(4)], ["ps2"])
            CP("dve", Vn[0:4, g, 0:128], V_s[0:4, g * 128:(g + 1) * 128], ["Vs", "Vn"], ["Vn"])
        TT("dve", Sn[0:4].rearrange("p g (h t) -> p (g h) t", h=4), ps[2][0:4, 0:32].rearrange("p (gh t) -> p gh t", t=4),
           NBs[0:4, 128, :].unsqueeze(1).to_broadcast([4, 8, 4]), ALU.add, ["ps2", "NBs"], ["Sn"])
        ACT(Pn[0:4], Sn[0:4], AF.Exp, ["Sn"], ["Pn"], scale=SCALE)
        for g in range(2):
            MM(ps[4 + g][0:16, 0:129], Pn[0:4, g, :], Vn[0:4, g, 0:129], False, True, ["Pn", "Vn"], ["ps%d" % (4 + g)])
            RCP(sm5[0:16, 60:61], ps[4 + g][0:16, 128:129], ["ps%d" % (4 + g)], ["sm5r"])
            TS("dve", att[0:16, :], ps[4 + g][0:16, 0:128], sm5[0:16, 60:61], None, ALU.mult, None, ["ps%d" % (4 + g), "sm5r"], ["att"])
            P.op("pe", lambda e: e.transpose(out=ps[0][:, 0:16], in_=att[0:16, :], identity=ident_f[0:16, 0:16]), ["att", "identf"], ["ps0"])
            TT("dve", MIX_s[:, 4 * g:4 * g + 4, 0:4], ps[0][:, 0:16].rearrange("p (h t) -> p h t", h=4), SZA_s[:, 4 * g:4 * g + 4, 0:4], ALU.mult,
               ["ps0"] + [("SZAs", 4 * g + hh) for hh in range(4)], [("MIXs", 4 * g + hh) for hh in range(4)])

        if DUMP:
            DMA('sync', dbg_att, att, 'dd4', ['att'], [])
            CP('dve', junk_s[0:16, 0:132], ps[5][0:16, 0:132], ['ps5'], ['junk_s'])
            DMA('sync', dbg_O, junk_s[0:16, 0:132], 'dd5', ['junk_s'], [])
            DMA('sync', dbg_Sm, Sm.rearrange('p j g c -> p (j g c)'), 'dd6', ['Sm'], [])

    if stop < 6:
        for _ in gen_sample():
            pass
        P.wait_all_dma('sync'); P.emit(); return nc

    R3 = Region()
    R3.o = o_QT
    WO = [R3.sb([128, 16, 512], BF16) for _ in range(3)]
    xr = [R3.sb([128, 512], F32) for _ in range(2)]
    yst = [R3.sb([128, 512], F32) for _ in range(2)]
    wo_slot = {0: 0, 1: 1, 2: 2, 3: 0}
    assert R3.o <= o_MIXA, (R3.o, o_MIXA)
    xc = [0]
    obk = [0]

    step_list = [(c_, i_) for c_ in range(4) for i_ in range(8)] + [(1, 8), (2, 8), (3, 8), (0, 8)]
    xr_buf = {}

    def issue_xr(c, i):
        if (c, i) in xr_buf:
            return
        a_ = xc[0] % 2
        xc[0] += 1
        xr_buf[(c, i)] = a_
        if i < 8:
            DMA("sync", xr[a_], xall[(2 * i) * 128:(2 * i + 1) * 128, c * 512:(c + 1) * 512], "xr%d" % a_, [], ["xr%d" % a_])
        else:
            DMA("sync", xr[a_][0:4, :], xall[2048:2052, c * 512:(c + 1) * 512], "xr%d" % a_, [], ["xr%d" % a_])

    def out_step(c, i):
        s_ = wo_slot[c]
        bk = 6 + obk[0] % 2
        obk[0] += 1
        issue_xr(c, i)
        a_ = xr_buf[(c, i)]
        k_ = step_list.index((c, i))
        if k_ + 1 < len(step_list):
            issue_xr(*step_list[k_ + 1])
        np_ = 128 if i < 8 else 4
        for ft in range(16):
            if i < 8:
                lhs = (MIXA if ft < 8 else MIXC)[:, ft % 8, i * 128:(i + 1) * 128]
                rk = [("MIXA", i, 0), ("MIXA", i, 1)] if ft < 8 else [("MIXC", ft - 8, i // 4)]
            else:
                lhs = MIX_s[:, ft, 0:4]
                rk = [("MIXs", ft)]
            MM(ps[bk][:np_, :], lhs, WO[s_][:, ft, :], ft == 0, ft == 15, rk + ["WO%d" % s_], ["ps%d" % bk])
            if ft % 4 == 3 and ft < 15:
                yield
        TT("dve", yst[a_][:np_, :], ps[bk][:np_, :], xr[a_][:np_, :], ALU.add, ["ps%d" % bk, "xr%d" % a_], ["yst%d" % a_])
        if i < 8:
            DMA("sync", y_own[i * 128:(i + 1) * 128, c * 512:(c + 1) * 512], yst[a_], "yo%d" % a_, ["yst%d" % a_], [])
        else:
            DMA("sync", y_s[:, c * 512:(c + 1) * 512], yst[a_][0:4, :], "yo%d" % a_, ["yst%d" % a_], [])
        yield

    def load_wo(c):
        s_ = wo_slot[c]
        DMA("pool", WO[s_], w_out[:, c * 512:(c + 1) * 512].rearrange("(ft p) c -> p ft c", p=128), "wo%d" % s_, [], ["WO%d" % s_])

    def gen_out():
        load_wo(0)
        load_wo(1)
        load_wo(2)
        for c in range(4):
            for i in range(8):
                yield from out_step(c, i)
            if c == 0:
                load_wo(3)

    gs = gen_sample()
    go = gen_out()
    gs_alive, go_alive = True, True
    while gs_alive or go_alive:
        if gs_alive:
            try:
                next(gs)
            except StopIteration:
                gs_alive = False
        n_out = 3 if (sstage[0] == 0 and gs_alive) else 1
        for _ in range(n_out):
            if go_alive:
                try:
                    next(go)
                except StopIteration:
                    go_alive = False
    for _ in out_step(1, 8):
        pass
    wo_slot[0] = 1
    load_wo(0)
    for c_ in (2, 3, 0):
        for _ in out_step(c_, 8):
            pass

    P.wait_all_dma("sync")
    P.emit()
    return nc


_NC_CACHE = {}


def _rope_tables(pos, half):
    inv = (np.float32(THETA) ** (-(np.arange(half, dtype=np.float32) / np.float32(half)))).astype(np.float32)
    ang = pos.astype(np.float32)[:, None] * inv[None, :]
    return np.cos(ang).astype(np.float32), np.sin(ang).astype(np.float32)


def kernel(x_prompt, x_sample, cache_k, cache_v, cache_kidx, state_conv, page_table,
           norm_in, w_in, g_q, g_k, g_kidx, w_conv, w_out):
    x_prompt = np.asarray(x_prompt, np.float32)
    x_sample = np.asarray(x_sample, np.float32)
    if "nc" not in _NC_CACHE:
        _NC_CACHE["nc"] = build_program()
    nc = _NC_CACHE["nc"]
    ck = np.ascontiguousarray(np.asarray(cache_k, np.float32)[0].reshape(1280, PAGE * 256))
    cv = np.ascontiguousarray(np.asarray(cache_v, np.float32)[0].reshape(1280, PAGE * 256))
    cki = np.ascontiguousarray(np.asarray(cache_kidx, np.float32)[0].reshape(1280, PAGE * 64))
    w_in0 = np.ascontiguousarray(np.asarray(w_in, np.float32)[0])
    w_out0 = np.ascontiguousarray(np.asarray(w_out, np.float32)[0])
    pow2 = (2.0 ** -(np.arange(KBIS, dtype=np.float32) + 1)).astype(np.float32)
    in_maps = []
    slots_all = []
    for c in range(8):
        b, j = c // 2, c % 2
        own = T_OWN[j]
        oth = T_OWN[1 - j]
        slots = []
        for i in range(8):
            slots += [own[i], oth[i]]
        slots_all.append(slots)
        xall = np.zeros((2176, D), np.float32)
        pos = np.zeros(2176, np.int64)
        for s_, t in enumerate(slots):
            xall[s_ * 128:(s_ + 1) * 128] = x_prompt[b, t * 128:(t + 1) * 128]
            pos[s_ * 128:(s_ + 1) * 128] = np.arange(t * 128, (t + 1) * 128)
        xall[2048:2052] = x_sample[c]
        pos[2048:2052] = PAST + np.arange(4)
        for i in range(8):
            t = own[i]
            if t > 0:
                xall[2052 + 2 * i:2054 + 2 * i] = x_prompt[b, t * 128 - 2:t * 128]
        c32, s32 = _rope_tables(pos, 16)
        c16, s16 = _rope_tables(pos, 8)
        cm = np.zeros((128, 512), np.float32)
        tri = np.where(np.arange(128)[None, :] <= np.arange(128)[:, None], 0.0, -30000.0).astype(np.float32)
        for par in range(2):
            cm[:, par * 256:par * 256 + 128] = tri
            other_before = oth[par] < own[par]
            cm[:, par * 256 + 128:par * 256 + 256] = 0.0 if other_before else -30000.0
        pm = (-2.0 * cm).astype(np.float32)
        in_maps.append(dict(
            xall=xall, w_in=w_in0, w_out=w_out0,
            norm_in=np.ascontiguousarray(np.asarray(norm_in, np.float32)[0].reshape(16, 128).T),
            g_q=np.ascontiguousarray(np.asarray(g_q, np.float32)[0]),
            g_k=np.ascontiguousarray(np.asarray(g_k, np.float32)[0]),
            g_ki=np.ascontiguousarray(np.asarray(g_kidx, np.float32)[0]),
            w_conv=np.ascontiguousarray(np.asarray(w_conv, np.float32)[0].reshape(3, 8, 128).transpose(2, 0, 1)),
            cos32=c32, sin32=s32, cos16=c16, sin16=s16, cmask=cm, pmask=pm, pow2=pow2,
            stc=np.ascontiguousarray(np.asarray(state_conv, np.float32)[0, c].reshape(2, 8, 128).transpose(2, 0, 1)),
            ptab=np.ascontiguousarray(np.asarray(page_table, np.int32)[c].reshape(128, 1)),
            cache_k=ck, cache_v=cv, cache_ki=cki,
            cm4=np.where(np.arange(4)[:, None] <= np.arange(4)[None, :], 0.0, -1e30).astype(np.float32),
        ))
    res = run_bass_kernel_spmd(nc, in_maps, core_ids=list(range(8)))
    R = res.results
    y_prompt = np.zeros((NB_, S, D), np.float32)
    y_sample = np.zeros((DEC_B, DEC_T, D), np.float32)
    k_p = np.zeros((1, NB_, S, NKV, HD), np.float32)
    v_p = np.zeros((1, NB_, S, NKV, HD), np.float32)
    ki_p = np.zeros((1, NB_, S, IDXD), np.float32)
    conv_p = np.zeros((1, NB_, 2, 1024), np.float32)
    k_s = np.zeros((1, DEC_B, DEC_T, NKV, HD), np.float32)
    v_s = np.zeros((1, DEC_B, DEC_T, NKV, HD), np.float32)
    ki_s = np.zeros((1, DEC_B, DEC_T, IDXD), np.float32)
    conv_s = np.zeros((1, DEC_B, 2, 1024), np.float32)
    for c in range(8):
        b, j = c // 2, c % 2
        r = R[c]
        for i, t in enumerate(T_OWN[j]):
            sl = slice(t * 128, (t + 1) * 128)
            y_prompt[b, sl] = r["y_own"][i * 128:(i + 1) * 128]
            k_p[0, b, sl] = r["k_own"][i * 128:(i + 1) * 128].reshape(128, NKV, HD)
            v_p[0, b, sl] = r["v_own"][i * 128:(i + 1) * 128].reshape(128, NKV, HD)
            ki_p[0, b, sl] = r["ki_own"][i * 128:(i + 1) * 128]
        if j == 0:
            conv_p[0, b] = r["conv_p"].transpose(1, 2, 0).reshape(2, 1024)
        y_sample[c] = r["y_s"]
        k_s[0, c] = r["k_s"].reshape(4, NKV, HD)
        v_s[0, c] = r["v_s"].reshape(4, NKV, HD)
        ki_s[0, c] = r["ki_s"]
        conv_s[0, c] = r["conv_s"].transpose(1, 2, 0).reshape(2, 1024)
    return (y_prompt, y_sample, k_p, v_p, ki_p, conv_p, k_s, v_s, ki_s, conv_s)
```
